# Optimizing a Trainium2 kernel written in Bass

```python
import jax, jax.numpy as jnp
from jax import lax
import numpy as np

D_MODEL = 1024
BATCH = 2
SEQ = 8192
DEPTH = 2

N_MIXERS = 2
N_HEADS = 16
HEAD_DIM = D_MODEL // N_HEADS
N_KV_GROUPS = 2
HEADS_PER_GROUP = N_HEADS // N_KV_GROUPS
KV_DIM = N_KV_GROUPS * HEAD_DIM
N_BRANCH = 3
CMP_STRIDE = 16
CMP_BLOCK = 2 * CMP_STRIDE
CMP_HIDDEN = 256
SEL_BLOCK = 64
SEL_TOPK = 16
WINDOW = 512
Q_BLOCK = 128
ROPE_THETA = 10000.0
IN_SIZES = (D_MODEL,) + (KV_DIM,) * 6 + (N_BRANCH * N_HEADS,)
IN_COLS = sum(IN_SIZES)
POOL_WINDOWS = (2, 4, 8, 16)
N_POOL_GROUPS = len(POOL_WINDOWS)
POOL_GROUP_DIM = D_MODEL // N_POOL_GROUPS
D_FF = 2816
CONV_WIDTH = 3
NORM_EPS = 1e-6
NEG_INF = -1e30
FORCE_BONUS = 1e4

kernel_name = 'hybrid_nsa_pool_convffn'


def rmsnorm(x, g):
    xf = x.astype(jnp.float32)
    y = xf * lax.rsqrt(jnp.mean(xf * xf, axis=-1, keepdims=True) + NORM_EPS)
    return (y * g.astype(jnp.float32)).astype(x.dtype)


def rope_tables(pos):
    inv = 1.0 / (ROPE_THETA ** (jnp.arange(0, HEAD_DIM, 2, dtype=jnp.float32) / HEAD_DIM))
    ang = pos.astype(jnp.float32)[:, None] * inv[None, :]
    return jnp.cos(ang), jnp.sin(ang)


def apply_rope(a, cos, sin):
    a1, a2 = jnp.split(a.astype(jnp.float32), 2, axis=-1)
    c, s = cos[:, None, :], sin[:, None, :]
    return jnp.concatenate([a1 * c - a2 * s, a1 * s + a2 * c], axis=-1).astype(a.dtype)


def masked_softmax(s, mask):
    s = jnp.where(mask, s.astype(jnp.float32), NEG_INF)
    return jax.nn.softmax(s, axis=-1) * jnp.any(mask, axis=-1, keepdims=True)


def compress_blocks(a, pos_emb, w1, b1, w2, b2):
    B, T, G, dh = a.shape
    c = a.reshape(B, T // CMP_STRIDE, CMP_STRIDE, G, dh)
    blocks = jnp.concatenate([c[:, :-1], c[:, 1:]], axis=2) + pos_emb[:, None, :]
    flat = blocks.transpose(0, 1, 3, 2, 4).reshape(B, T // CMP_STRIDE - 1, G, CMP_BLOCK * dh)
    return jax.nn.gelu(flat @ w1 + b1) @ w2 + b2


def cmp_sel_overlap(n_cmp, n_blk):
    j = np.arange(n_cmp)[:, None]
    s = np.arange(n_blk)[None, :]
    lo = np.maximum(j * CMP_STRIDE, s * SEL_BLOCK)
    hi = np.minimum(j * CMP_STRIDE + CMP_BLOCK, (s + 1) * SEL_BLOCK)
    return jnp.asarray(np.clip(hi - lo, 0, None) / CMP_BLOCK, dtype=jnp.float32)


def gather_blocks(blocks, sel):
    return jax.vmap(jax.vmap(lambda bl, idx: bl[idx]))(blocks, sel)


def nsa_mixer(h, w_in, ck_pos, ck_w1, ck_b1, ck_w2, ck_b2,
              cv_pos, cv_w1, cv_b1, cv_w2, cv_b2, w_out):
    B, T, _ = h.shape
    G, HG, dh = N_KV_GROUPS, HEADS_PER_GROUP, HEAD_DIM
    n_cmp = T // CMP_STRIDE - 1
    n_blk = T // SEL_BLOCK
    n_sel = min(SEL_TOPK, n_blk)
    n_chunks = T // Q_BLOCK
    scale = HEAD_DIM ** -0.5
    split_at = [int(v) for v in np.cumsum(IN_SIZES)[:-1]]
    q, k_c, v_c, k_s, v_s, k_w, v_w, g_logit = jnp.split(h @ w_in, split_at, axis=-1)
    q = q.reshape(B, T, N_HEADS, dh)
    k_c, v_c, k_s, v_s, k_w, v_w = [a.reshape(B, T, G, dh) for a in (k_c, v_c, k_s, v_s, k_w, v_w)]
    cos, sin = rope_tables(jnp.arange(T))
    q = apply_rope(q, cos, sin)
    k_s = apply_rope(k_s, cos, sin)
    k_w = apply_rope(k_w, cos, sin)
    kc = compress_blocks(k_c, ck_pos, ck_w1, ck_b1, ck_w2, ck_b2)
    vc = compress_blocks(v_c, cv_pos, cv_w1, cv_b1, cv_w2, cv_b2)
    cmp_end = jnp.arange(n_cmp) * CMP_STRIDE + (CMP_BLOCK - 1)
    ccos, csin = rope_tables(cmp_end)
    kc = apply_rope(kc, ccos, csin)
    qg = q.reshape(B, T, G, HG, dh).transpose(0, 2, 3, 1, 4)
    kc = kc.transpose(0, 2, 1, 3)
    vc = vc.transpose(0, 2, 1, 3)
    kb = k_s.transpose(0, 2, 1, 3).reshape(B, G, n_blk, SEL_BLOCK, dh)
    vb = v_s.transpose(0, 2, 1, 3).reshape(B, G, n_blk, SEL_BLOCK, dh)
    pad_w = ((0, 0), (0, 0), (WINDOW, 0), (0, 0))
    kw = jnp.pad(k_w.transpose(0, 2, 1, 3), pad_w)
    vw = jnp.pad(v_w.transpose(0, 2, 1, 3), pad_w)
    gates = jax.nn.sigmoid(g_logit).reshape(B, T, G, HG, N_BRANCH).transpose(0, 2, 3, 1, 4)
    overlap = cmp_sel_overlap(n_cmp, n_blk)
    blk_ids = jnp.arange(n_blk)
    win_off = jnp.arange(Q_BLOCK + WINDOW)
    sel_off = jnp.arange(SEL_BLOCK)

    def chunk(i):
        s0 = i * Q_BLOCK
        t = s0 + jnp.arange(Q_BLOCK)
        qb = lax.dynamic_slice_in_dim(qg, s0, Q_BLOCK, axis=3)
        m_c = cmp_end[None, :] <= t[:, None]
        p_c = masked_softmax(jnp.einsum('bghqd,bgnd->bghqn', qb, kc) * scale, m_c)
        o_c = jnp.einsum('bghqn,bgnd->bghqd', p_c.astype(vc.dtype), vc)
        imp = jnp.einsum('bghqn,ns->bgqs', p_c, overlap)
        cur = t // SEL_BLOCK
        forced = (blk_ids == 0) | (blk_ids == cur[:, None]) | (blk_ids == cur[:, None] - 1)
        score = jnp.where(blk_ids <= cur[:, None], imp + FORCE_BONUS * forced, NEG_INF)
        _, sel = lax.top_k(score, n_sel)
        kg = gather_blocks(kb, sel)
        vg = gather_blocks(vb, sel)
        tok = sel[..., None] * SEL_BLOCK + sel_off
        m_s = (tok <= t[:, None, None]).reshape(B, G, 1, Q_BLOCK, n_sel * SEL_BLOCK)
        s_s = jnp.einsum('bghqd,bgqnkd->bghqnk', qb, kg).reshape(B, G, HG, Q_BLOCK, n_sel * SEL_BLOCK)
        p_s = masked_softmax(s_s * scale, m_s).reshape(B, G, HG, Q_BLOCK, n_sel, SEL_BLOCK)
        o_s = jnp.einsum('bghqnk,bgqnkd->bghqd', p_s.astype(vg.dtype), vg)
        kwb = lax.dynamic_slice_in_dim(kw, s0, Q_BLOCK + WINDOW, axis=2)
        vwb = lax.dynamic_slice_in_dim(vw, s0, Q_BLOCK + WINDOW, axis=2)
        kpos = s0 - WINDOW + win_off
        m_w = (kpos[None, :] <= t[:, None]) & (kpos[None, :] > t[:, None] - WINDOW) & (kpos[None, :] >= 0)
        p_w = masked_softmax(jnp.einsum('bghqd,bgkd->bghqk', qb, kwb) * scale, m_w)
        o_w = jnp.einsum('bghqk,bgkd->bghqd', p_w.astype(vwb.dtype), vwb)
        gb = lax.dynamic_slice_in_dim(gates, s0, Q_BLOCK, axis=3)
        return gb[..., 0:1] * o_c + gb[..., 1:2] * o_s + gb[..., 2:3] * o_w

    o = lax.map(chunk, jnp.arange(n_chunks))
    o = o.transpose(1, 0, 4, 2, 3, 5).reshape(B, T, D_MODEL)
    return o @ w_out


def pool_mixer(h, pool_w, pool_b, pool_scale):
    B, T, _ = h.shape
    hf = h.astype(jnp.float32)
    csum = jnp.pad(jnp.cumsum(hf, axis=1), ((0, 0), (1, 0), (0, 0)))
    count_base = jnp.arange(1, T + 1, dtype=jnp.float32)[:, None]
    outs = []
    for g, w in enumerate(POOL_WINDOWS):
        sl = slice(g * POOL_GROUP_DIM, (g + 1) * POOL_GROUP_DIM)
        c = csum[..., sl]
        lag = jnp.pad(c, ((0, 0), (w - 1, 0), (0, 0)))[:, :T]
        count = jnp.minimum(count_base, float(w))
        outs.append((c[:, 1:] - lag) / count - hf[..., sl])
    pooled = jnp.stack(outs, axis=2)
    y = jnp.einsum('btgc,gcd->btgd', pooled, pool_w.astype(jnp.float32)) + pool_b
    return (y.reshape(B, T, D_MODEL) * pool_scale).astype(h.dtype)


def conv_ffn(h, w_up, conv_w, conv_b, w_down):
    u = h @ w_up
    T = u.shape[1]
    up = jnp.pad(u, ((0, 0), (CONV_WIDTH - 1, 0), (0, 0)))
    conv = sum(conv_w[k] * up[:, k:k + T] for k in range(CONV_WIDTH)) + conv_b
    gate, val = jnp.split(conv, 2, axis=-1)
    return (jax.nn.silu(gate) * val) @ w_down


def setup_inputs(seed: int = 0) -> dict:
    key = jax.random.key(seed)
    ks = iter(jax.random.split(key, 40))
    f32 = jnp.float32

    def w(shape, fan_in):
        return jax.random.normal(next(ks), shape, f32) * (fan_in ** -0.5)

    def gain(n):
        return 1.0 + 0.02 * jax.random.normal(next(ks), (n,), f32)

    def bias(shape):
        return 0.01 * jax.random.normal(next(ks), shape, f32)

    def ffn_params():
        return (gain(D_MODEL), w((D_MODEL, 2 * D_FF), D_MODEL),
                w((CONV_WIDTH, 2 * D_FF), CONV_WIDTH), bias((2 * D_FF,)),
                w((D_FF, D_MODEL), D_FF))

    x = jax.random.normal(next(ks), (BATCH, SEQ, D_MODEL), f32)
    norm_mix_0 = gain(D_MODEL)
    nsa_w_in = w((D_MODEL, IN_COLS), D_MODEL)
    cmp_k_pos = 0.02 * jax.random.normal(next(ks), (CMP_BLOCK, HEAD_DIM), f32)
    cmp_k_w1 = w((CMP_BLOCK * HEAD_DIM, CMP_HIDDEN), CMP_BLOCK * HEAD_DIM)
    cmp_k_b1 = bias((CMP_HIDDEN,))
    cmp_k_w2 = w((CMP_HIDDEN, HEAD_DIM), CMP_HIDDEN)
    cmp_k_b2 = bias((HEAD_DIM,))
    cmp_v_pos = 0.02 * jax.random.normal(next(ks), (CMP_BLOCK, HEAD_DIM), f32)
    cmp_v_w1 = w((CMP_BLOCK * HEAD_DIM, CMP_HIDDEN), CMP_BLOCK * HEAD_DIM)
    cmp_v_b1 = bias((CMP_HIDDEN,))
    cmp_v_w2 = w((CMP_HIDDEN, HEAD_DIM), CMP_HIDDEN)
    cmp_v_b2 = bias((HEAD_DIM,))
    nsa_w_out = w((D_MODEL, D_MODEL), D_MODEL)
    norm_ffn_0, ffn_up_0, ffn_conv_w_0, ffn_conv_b_0, ffn_down_0 = ffn_params()
    norm_mix_1 = gain(D_MODEL)
    pool_w = w((N_POOL_GROUPS, POOL_GROUP_DIM, POOL_GROUP_DIM), POOL_GROUP_DIM)
    pool_b = bias((N_POOL_GROUPS, POOL_GROUP_DIM))
    pool_scale = 1.0 + 0.1 * jax.random.normal(next(ks), (D_MODEL,), f32)
    norm_ffn_1, ffn_up_1, ffn_conv_w_1, ffn_conv_b_1, ffn_down_1 = ffn_params()
    norm_final = gain(D_MODEL)
    return {
        'x': x, 'norm_mix_0': norm_mix_0, 'nsa_w_in': nsa_w_in,
        'cmp_k_pos': cmp_k_pos, 'cmp_k_w1': cmp_k_w1, 'cmp_k_b1': cmp_k_b1,
        'cmp_k_w2': cmp_k_w2, 'cmp_k_b2': cmp_k_b2,
        'cmp_v_pos': cmp_v_pos, 'cmp_v_w1': cmp_v_w1, 'cmp_v_b1': cmp_v_b1,
        'cmp_v_w2': cmp_v_w2, 'cmp_v_b2': cmp_v_b2, 'nsa_w_out': nsa_w_out,
        'norm_ffn_0': norm_ffn_0, 'ffn_up_0': ffn_up_0, 'ffn_conv_w_0': ffn_conv_w_0,
        'ffn_conv_b_0': ffn_conv_b_0, 'ffn_down_0': ffn_down_0,
        'norm_mix_1': norm_mix_1, 'pool_w': pool_w, 'pool_b': pool_b, 'pool_scale': pool_scale,
        'norm_ffn_1': norm_ffn_1, 'ffn_up_1': ffn_up_1, 'ffn_conv_w_1': ffn_conv_w_1,
        'ffn_conv_b_1': ffn_conv_b_1, 'ffn_down_1': ffn_down_1, 'norm_final': norm_final,
    }


def reference(x, norm_mix_0, nsa_w_in, cmp_k_pos, cmp_k_w1, cmp_k_b1, cmp_k_w2, cmp_k_b2,
              cmp_v_pos, cmp_v_w1, cmp_v_b1, cmp_v_w2, cmp_v_b2, nsa_w_out,
              norm_ffn_0, ffn_up_0, ffn_conv_w_0, ffn_conv_b_0, ffn_down_0,
              norm_mix_1, pool_w, pool_b, pool_scale,
              norm_ffn_1, ffn_up_1, ffn_conv_w_1, ffn_conv_b_1, ffn_down_1, norm_final):
    mixers = [
        lambda h: nsa_mixer(h, nsa_w_in, cmp_k_pos, cmp_k_w1, cmp_k_b1, cmp_k_w2, cmp_k_b2,
                            cmp_v_pos, cmp_v_w1, cmp_v_b1, cmp_v_w2, cmp_v_b2, nsa_w_out),
        lambda h: pool_mixer(h, pool_w, pool_b, pool_scale),
    ]
    mix_norms = [norm_mix_0, norm_mix_1]
    ffns = [(norm_ffn_0, ffn_up_0, ffn_conv_w_0, ffn_conv_b_0, ffn_down_0),
            (norm_ffn_1, ffn_up_1, ffn_conv_w_1, ffn_conv_b_1, ffn_down_1)]
    for i in range(DEPTH):
        x = x + mixers[i % N_MIXERS](rmsnorm(x, mix_norms[i]))
        g, up, cw, cb, down = ffns[i]
        x = x + conv_ffn(rmsnorm(x, g), up, cw, cb, down)
    return rmsnorm(x, norm_final)
```

```python
import contextlib
import numpy as np
import ml_dtypes
import concourse.bass as bass
import concourse.mybir as mybir
from concourse.bass_utils import run_bass_kernel_spmd

F32 = mybir.dt.float32
BF16 = mybir.dt.bfloat16
AF = mybir.ActivationFunctionType
ALU = mybir.AluOpType
AX = mybir.AxisListType
NPBF = ml_dtypes.bfloat16

D = 1024
KC = 8
T = 8192
NSLOT = 64
OWN0 = 48
NOWN = 16
HALO = 20
NTOK = HALO + 128 * NOWN
DFF = 2816
NFC = 22
EPS = 1e-6
SCALE = 0.125
NEGB = -30000.0
BLK = [(47, 108, HALO, 0)] + [(OWN0 + m, 0, 128, HALO + 128 * m) for m in range(NOWN)]
NBLK = len(BLK)
TG = [(0, HALO)] + [(HALO + 512 * i, 512) for i in range(4)]
FPASS = [list(range(0, 6)), list(range(6, 12)), list(range(12, 17)), list(range(17, 22))]

ENGS = ("pe", "act", "dve", "pool", "sp")


class Op:
    __slots__ = ("eng", "fn", "dma", "waits", "signal", "sigidx", "idx")

    def __init__(self, eng, fn, dma):
        self.eng = eng
        self.fn = fn
        self.dma = dma
        self.waits = []
        self.signal = False
        self.sigidx = None


class Phase:
    def __init__(self, nc, name):
        self.nc = nc
        self.name = name
        self.ops = {e: [] for e in ENGS}
        self.lastw = {}
        self.readers = {}
        self.dma_count = {}
        self.n = 0

    def add(self, eng, fn, r=(), w=(), dma=None):
        op = Op(eng, fn, dma)
        op.idx = self.n
        self.n += 1
        deps = []
        for k in r:
            x = self.lastw.get(k)
            if x is not None:
                deps.append(x)
        for k in w:
            x = self.lastw.get(k)
            if x is not None:
                deps.append(x)
            deps.extend(self.readers.get(k, {}).values())
        seen = set()
        for d in deps:
            if d is op or id(d) in seen:
                continue
            seen.add(id(d))
            if d.dma is not None:
                op.waits.append(("dma", d.dma, 16 * self.dma_count[d.dma]))
            else:
                if d.eng == "pe" and eng == "pe" and dma is None:
                    continue
                d.signal = True
                op.waits.append(("eng", d, None))
        rk = eng if dma is None else ("dma", op.idx)
        for k in r:
            self.readers.setdefault(k, {})[rk] = op
        for k in w:
            self.lastw[k] = op
            self.readers[k] = {}
        if dma is not None:
            self.dma_count[dma] = self.dma_count.get(dma, 0) + 1
        self.ops[eng].append(op)
        return op

    def emit(self):
        nc = self.nc
        for e in ENGS:
            k = 0
            for op in self.ops[e]:
                if op.dma is None and op.signal:
                    k += 1
                    op.sigidx = k
        with contextlib.ExitStack() as st:
            esem = {e: st.enter_context(nc.semaphore(f"{self.name}_s_{e}")) for e in ENGS}
            dsem = {k: st.enter_context(nc.semaphore(f"{self.name}_d_{i}"))
                    for i, k in enumerate(self.dma_count)}
            block = st.enter_context(nc.Block())
            final_dma = dict(self.dma_count)

            def run(e, eng):
                seen = {}
                for op in self.ops[e]:
                    for kind, obj, val in op.waits:
                        if kind == "dma":
                            sem, v = dsem[obj], val
                        else:
                            sem, v = esem[obj.eng], obj.sigidx
                        if seen.get(id(sem), 0) >= v:
                            continue
                        seen[id(sem)] = v
                        eng.wait_ge(sem, v)
                    inst = op.fn(eng)
                    if op.dma is not None:
                        inst.then_inc(dsem[op.dma], 16)
                    elif op.signal:
                        inst.then_inc(esem[e], 1)
                mine = []
                for op in self.ops[e]:
                    if op.dma is not None and op.dma not in mine:
                        mine.append(op.dma)
                for k in mine:
                    v = 16 * final_dma[k]
                    if seen.get(id(dsem[k]), 0) < v:
                        eng.wait_ge(dsem[k], v)

            block.tensor(lambda eng: run("pe", eng))
            block.scalar(lambda eng: run("act", eng))
            block.vector(lambda eng: run("dve", eng))
            block.gpsimd(lambda eng: run("pool", eng))
            block.sync(lambda eng: run("sp", eng))


def _c(a, dt=np.float32):
    return np.ascontiguousarray(a).astype(dt, copy=False)


def _pk(v):
    return _c(np.asarray(v).reshape(-1, 128).T)


def _rope_tab(pos):
    inv = (1.0 / (10000.0 ** (np.arange(0, 64, 2, dtype=np.float32) / np.float32(64)))).astype(np.float32)
    ang = pos.astype(np.float32)[:, None] * inv[None, :]
    c = np.cos(ang).astype(np.float32)
    s = np.sin(ang).astype(np.float32)
    idx = np.arange(128) % 32
    return _c(c[:, idx].T), _c(s[:, idx].T)


def _shared_inputs(inp):
    sh = {}
    w_in = np.asarray(inp["nsa_w_in"])
    sh["wq"] = _c(w_in[:, :1024].reshape(1024, 2, 8, 64).transpose(0, 2, 1, 3).reshape(1024, 1024))
    sh["wkv"] = _c(w_in[:, 1024:1792])
    sh["wg"] = _c(w_in[:, 1792:1840].reshape(1024, 2, 8, 3).transpose(0, 3, 1, 2).reshape(1024, 48))
    sh["wout"] = _c(np.asarray(inp["nsa_w_out"]).reshape(2, 8, 64, 1024).transpose(1, 0, 2, 3).reshape(1024, 1024))
    for x in ("k", "v"):
        sh[f"c{x}_w1"] = _c(inp[f"cmp_{x}_w1"])
        sh[f"c{x}_w2"] = _c(inp[f"cmp_{x}_w2"])
        pos = np.asarray(inp[f"cmp_{x}_pos"])
        pt = np.zeros((128, 32, 2), np.float32)
        pt[0:64, :, 0] = pos.T
        pt[64:128, :, 0] = pos.T
        sh[f"c{x}_posT"] = _c(pt)
        sh[f"c{x}_b1"] = _c(np.asarray(inp[f"cmp_{x}_b1"]).reshape(2, 128).T)
        sh[f"c{x}_b2"] = _c(np.tile(np.asarray(inp[f"cmp_{x}_b2"]), 2)[None, :])
    for i in (0, 1):
        sh[f"gmix{i}"] = _pk(inp[f"norm_mix_{i}"])
        sh[f"gffn{i}"] = _pk(inp[f"norm_ffn_{i}"])
        sh[f"wup{i}"] = _c(inp[f"ffn_up_{i}"])
        sh[f"wdn{i}"] = _c(inp[f"ffn_down_{i}"])
        cw = np.asarray(inp[f"ffn_conv_w_{i}"])
        sh[f"cw{i}"] = _c(cw.reshape(3, 44, 128).transpose(2, 0, 1))
        sh[f"cb{i}"] = _c(np.asarray(inp[f"ffn_conv_b_{i}"]).reshape(44, 128).T)
    sh["gfin"] = _pk(inp["norm_final"])
    sh["poolw"] = _c(np.asarray(inp["pool_w"]).reshape(4, 2, 128, 256).transpose(2, 0, 1, 3).reshape(128, 8, 256))
    sh["poolb"] = _pk(np.asarray(inp["pool_b"]).reshape(-1))
    sh["pools"] = _pk(inp["pool_scale"])
    sh["identb"] = _c(np.eye(128), NPBF)
    sh["identf"] = _c(np.eye(128))
    ef = np.zeros((128, 16, 128), np.float32)
    for p in range(128):
        for v in range(16):
            if p % 32 == 2 * v:
                ef[p, v, 0:64] = 1
            if p % 32 == 2 * v + 1:
                ef[p, v, 64:128] = 1
    sh["ef"] = _c(ef, NPBF)
    n = np.arange(512)[:, None]
    s = np.arange(128)[None, :]
    lo = np.maximum(n * 16, s * 64)
    hi = np.minimum(n * 16 + 32, (s + 1) * 64)
    ov = np.clip(hi - lo, 0, None) / 32.0
    ova = np.ones((512, 129), np.float32)
    ova[:, :128] = ov
    sh["ovl"] = _c(ova.reshape(4, 128, 129).transpose(1, 0, 2), NPBF)
    mk = np.zeros((NBLK, 128, 6, 128), np.float32)
    k = np.arange(128)[:, None]
    q = np.arange(128)[None, :]
    for bi, (S, off, nq, col0) in enumerate(BLK):
        mk[bi, :, 0, :] = (k <= off + q)
        mk[bi, :, 1, :] = (k > off + q)
        tq = 128 * S + off + q
        for c in range(4):
            mk[bi, :, 2 + c, :] = (16 * (c * 128 + k) + 31 <= tq)
    sh["mk"] = _c(mk, NPBF)
    return sh


def _core_inputs(inp, core):
    b, j = core // 4, core % 4
    SH = OWN0 - 16 * j
    x = np.asarray(inp["x"])[b]
    ci = {}
    xs = np.zeros((T, D), np.float32)
    nreal = (NSLOT - SH) * 128
    xs[SH * 128:] = x[:nreal]
    ci["xs"] = xs
    tp = np.arange(T)
    pos = np.maximum(tp - 128 * SH, 0)
    ci["cosK"], ci["sinK"] = _rope_tab(pos)
    qpos = np.zeros((NBLK, 128), np.int64)
    bonus = np.zeros((NBLK, 128, 128), np.float32)
    sblk = np.arange(128)[None, :]
    for bi, (S, off, nq, col0) in enumerate(BLK):
        tq = 128 * S + off + np.arange(128)
        qpos[bi] = np.maximum(tq - 128 * SH, 0)
        cur = (tq // 64)[:, None]
        forced = (sblk == cur) | (sblk == cur - 1) | (sblk == 2 * SH)
        bonus[bi] = np.where(sblk <= cur, 1e4 * forced, -1e30)
    cq, sq = _rope_tab(qpos.reshape(-1))
    ci["cosQ"], ci["sinQ"] = cq, sq
    ci["bonus"] = _c(bonus.transpose(1, 0, 2))
    nn = np.arange(512)
    cend = np.maximum(16 * (nn - 8 * SH) + 31, 0)
    ci["cosC"], ci["sinC"] = _rope_tab(cend)
    validc = (nn >= 8 * SH) & (nn <= 510)
    ci["biasC"] = _c(np.where(validc, 0.0, NEGB).reshape(4, 128).T)
    vk = (np.arange(NSLOT) >= SH).astype(np.float32)
    ci["validk"] = _c(np.tile(vk[None, :], (128, 1)), NPBF)
    ci["vq"] = _c(np.full((128, 1), 0.0 if j == 0 else 1.0))
    ps = np.zeros((128, 8, 16), np.float32)
    for kc in range(8):
        w = [2, 4, 8, 16][kc // 2]
        for qq in range(16):
            t = 128 * 16 * j + qq
            ps[:, kc, qq] = 1.0 / min(t + 1, w)
    ci["pscale"] = ps
    return ci


def build(dbg=()):
    nc = bass.Bass("TRN2", target_bir_lowering=False)
    dr = {}

    def din(name, shape, dt=F32):
        dr[name] = nc.dram_tensor(name, list(shape), dt, kind="ExternalInput").ap()
        return dr[name]

    din("xs", [T, D])
    din("cosK", [128, T]); din("sinK", [128, T])
    din("cosQ", [128, NBLK * 128]); din("sinQ", [128, NBLK * 128])
    din("bonus", [128, NBLK, 128])
    din("cosC", [128, 512]); din("sinC", [128, 512])
    din("biasC", [128, 4]); din("validk", [128, NSLOT], BF16); din("vq", [128, 1]); din("pscale", [128, 8, 16])
    din("wq", [D, D]); din("wkv", [D, 768]); din("wg", [D, 48]); din("wout", [D, D])
    for x in ("k", "v"):
        din(f"c{x}_w1", [2048, 256]); din(f"c{x}_w2", [256, 64]); din(f"c{x}_posT", [128, 32, 2])
        din(f"c{x}_b1", [128, 2]); din(f"c{x}_b2", [1, 128])
    for i in (0, 1):
        din(f"gmix{i}", [128, 8]); din(f"gffn{i}", [128, 8])
        din(f"wup{i}", [D, 2 * DFF]); din(f"wdn{i}", [DFF, D])
        din(f"cw{i}", [128, 3, 44]); din(f"cb{i}", [128, 44])
    din("gfin", [128, 8]); din("poolw", [128, 8, 256]); din("poolb", [128, 8]); din("pools", [128, 8])
    din("identb", [128, 128], BF16); din("identf", [128, 128])
    din("ef", [128, 16, 128], BF16); din("ovl", [128, 4, 129], BF16); din("mk", [NBLK, 128, 6, 128], BF16)
    out = nc.dram_tensor("out", [128 * NOWN, D], F32, kind="ExternalOutput").ap()
    gscr = nc.dram_tensor("gscr", [NBLK, 48, 128], F32).ap()
    dbgout = {}
    for name, shape, dt in dbg:
        if shape is None:
            dbgout[name] = dt
            continue
        dbgout[name] = nc.dram_tensor("dbg_" + name, list(shape), dt, kind="ExternalOutput").ap()

    with contextlib.ExitStack() as top:
        def SB(st, name, shape, dt):
            return st.enter_context(nc.sbuf_tensor("s_" + name, list(shape), dt))

        def PS(st, name, shape, dt):
            return st.enter_context(nc.psum_tensor("p_" + name, list(shape), dt))

        identb = SB(top, "identb", [128, 128], BF16)
        identf = SB(top, "identf", [128, 128], F32)
        ones32 = SB(top, "ones32", [128, 512], F32)
        onesb = SB(top, "onesb", [128, 128], BF16)
        gsb = SB(top, "gsb", [128, 5, 8], F32)
        vq = SB(top, "vq", [128, 1], F32)
        ph = Phase(nc, "K")
        ph.add("sp", lambda e: e.dma_start(out=identb[:], in_=dr["identb"]), w=["identb"], dma="c")
        ph.add("sp", lambda e: e.dma_start(out=identf[:], in_=dr["identf"]), w=["identf"], dma="c")
        for i, nm in enumerate(("gmix0", "gffn0", "gmix1", "gffn1", "gfin")):
            ph.add("sp", lambda e, i=i, nm=nm: e.dma_start(out=gsb[:, i, :], in_=dr[nm]), w=["gsb"], dma="c")
        ph.add("sp", lambda e: e.dma_start(out=vq[:], in_=dr["vq"]), w=["vq"], dma="c")
        ph.add("dve", lambda e: e.memset(ones32[:], 1.0), w=["ones32"])
        ph.add("dve", lambda e: e.memset(onesb[:], 1.0), w=["onesb"])
        ph.emit()

        oscr = nc.dram_tensor("oscr", [128, 8, NTOK], BF16).ap()
        with contextlib.ExitStack() as att:
            KsT = SB(att, "KsT", [128, T], BF16)
            KwT = SB(att, "KwT", [128, 24 * 128], BF16)
            Vs = SB(att, "Vs", [128, NSLOT, 194], BF16)
            Vw = SB(att, "Vw", [128, 24, 194], BF16)
            kcT = SB(att, "kcT", [128, 512], BF16)
            RCv = SB(att, "RCv", [128, 4, 194], BF16)
            biasC = SB(att, "biasC", [128, 4], F32)
            P = dict(KsT=KsT, KwT=KwT, Vs=Vs, Vw=Vw, kcT=kcT, RCv=RCv, biasC=biasC,
                     identb=identb, ones32=ones32, gsb=gsb, vq=vq)
            with contextlib.ExitStack() as st:
                _phase_A(nc, st, SB, PS, dr, P, dbgout)
            if "stopA" not in dbgout:
                with contextlib.ExitStack() as st:
                    _phase_B(nc, st, SB, PS, dr, gscr, oscr, P, dbgout)
        if "stopA" in dbgout or "stopB" in dbgout:
            return nc
        with contextlib.ExitStack() as rest:
            xres = SB(rest, "xres", [128, 8, NTOK], F32)
            P = dict(xres=xres, onesb=onesb, gsb=gsb, vq=vq, identf=identf, identb=identb)
            with contextlib.ExitStack() as st:
                _phase_C(nc, st, SB, PS, dr, oscr, P, dbgout)
            with contextlib.ExitStack() as st:
                _phase_ffn(nc, st, SB, PS, dr, 0, P, dbgout)
            with contextlib.ExitStack() as st:
                _phase_pool(nc, st, SB, PS, dr, P, dbgout)
            with contextlib.ExitStack() as st:
                _phase_ffn(nc, st, SB, PS, dr, 1, P, dbgout)
            with contextlib.ExitStack() as st:
                _phase_out(nc, st, SB, PS, dr, out, P, dbgout)
    return nc


def _phase_A(nc, st, SB, PS, dr, P, dbgout):
    KsT, KwT, Vs, Vw, kcT, RCv = P["KsT"], P["KwT"], P["Vs"], P["Vw"], P["kcT"], P["RCv"]
    identb, ones32, gsb = P["identb"], P["ones32"], P["gsb"]
    rawK = SB(st, "rawK", [128, T], BF16)
    rawV = SB(st, "rawV", [128, T], BF16)
    t1a = SB(st, "t1a", [128, 512], F32)
    t1b = SB(st, "t1b", [128, 512], F32)
    ist = contextlib.ExitStack()
    WA = SB(ist, "WA", [128, 8, 768], BF16)
    WAr = SB(ist, "WAr", [128, 8, 256], BF16)
    validk = SB(ist, "validk", [128, NSLOT], BF16)
    xt = [SB(ist, f"xt{i}", [128, D], F32) for i in range(3)]
    junk = SB(ist, "junk", [128, D], BF16)
    ssq = [SB(ist, f"ssq{i}", [128, 1], F32) for i in range(3)]
    rstd = [SB(ist, f"rstd{i}", [128, 1], F32) for i in range(3)]
    xn = [SB(ist, f"xn{i}", [128, D], BF16) for i in range(2)]
    hT = [SB(ist, f"hT{i}", [128, 8, 512], BF16) for i in range(2)]
    cs = [SB(ist, f"cs{i}", [128, 512], F32) for i in range(2)]
    sn = [SB(ist, f"sn{i}", [128, 512], F32) for i in range(2)]
    pst = contextlib.ExitStack()
    tp = [PS(pst, f"tp{i}", [128, D], BF16) for i in range(2)]
    pk = [PS(pst, f"pk{i}", [128, 512], F32) for i in range(4)]
    pv = PS(pst, "pv", [128, 512], F32)

    ph = Phase(nc, "A")
    A = ph.add
    A("pool", lambda e: e.dma_start(out=WA[:], in_=dr["wkv"].rearrange("(k p) n -> p k n", p=128)), w=["WA"], dma="w")
    A("sp", lambda e: e.dma_start(out=validk[:], in_=dr["validk"]), w=["validk"], dma="c")
    A("sp", lambda e: e.dma_start(out=P["biasC"][:], in_=dr["biasC"]), w=["biasC"], dma="c")
    for i, c0 in enumerate((256, 512)):
        for g in range(2):
            A("dve", lambda e, i=i, c0=c0, g=g: e.tensor_scalar(
                out=WAr[:, :, i * 128 + g * 64: i * 128 + g * 64 + 32], in0=WA[:, :, c0 + g * 64 + 32: c0 + g * 64 + 64],
                scalar1=-1.0, scalar2=None, op0=ALU.mult), r=["WA"], w=[("WAr", i, g, 0)])
            A("dve", lambda e, i=i, c0=c0, g=g: e.tensor_copy(
                out=WAr[:, :, i * 128 + g * 64 + 32: i * 128 + g * 64 + 64], in_=WA[:, :, c0 + g * 64: c0 + g * 64 + 32]),
              r=["WA"], w=[("WAr", i, g, 1)])
    WArk = [("WAr", i, g, h) for i in range(2) for g in range(2) for h in range(2)]
    A("pool", lambda e: e.memset(Vs[:], 0.0), w=["Vs0"])
    A("pool", lambda e: e.memset(Vw[:], 0.0), w=["Vw0"])
    A("pool", lambda e: e.memset(RCv[:], 0.0), w=["RCv"])
    A("dve", lambda e: e.tensor_copy(out=Vs[:, :, 64:65], in_=validk[:].unsqueeze(2)), r=["validk", "Vs0"], w=["Vs1"])
    A("dve", lambda e: e.tensor_copy(out=Vs[:, :, 66:67], in_=validk[:].unsqueeze(2)), r=["validk", "Vs0"], w=["Vs2"])
    A("dve", lambda e: e.tensor_copy(out=Vw[:, :, 64:65], in_=validk[:, 40:64].unsqueeze(2)), r=["validk", "Vw0"], w=["Vw1"])
    A("dve", lambda e: e.tensor_copy(out=Vw[:, :, 66:67], in_=validk[:, 40:64].unsqueeze(2)), r=["validk", "Vw0"], w=["Vw2"])

    for G in range(16):
        hb = hT[G % 2]
        hk = ("hT", G % 2)
        A("sp", lambda e, G=G: e.dma_start(out=cs[G % 2][:], in_=dr["cosK"][:, G * 512:(G + 1) * 512]),
          w=[("cs", G % 2)], dma=("cs", G % 2))
        A("sp", lambda e, G=G: e.dma_start(out=sn[G % 2][:], in_=dr["sinK"][:, G * 512:(G + 1) * 512]),
          w=[("sn", G % 2)], dma=("cs", G % 2))
        for si in range(4):
            s = 4 * G + si
            b3, b2 = s % 3, s % 2
            A("sp", lambda e, s=s, b3=b3: e.dma_start(out=xt[b3][:], in_=dr["xs"][s * 128:(s + 1) * 128, :]),
              w=[("xt", b3)], dma=("x", b3))
            A("act", lambda e, b3=b3: e.activation(out=junk[:], in_=xt[b3][:], func=AF.Square, accum_out=ssq[b3][:]),
              r=[("xt", b3)], w=["junk", ("ssq", b3)])
            A("act", lambda e, b3=b3: e.activation(out=rstd[b3][:], in_=ssq[b3][:], func=AF.Sqrt, bias=EPS, scale=1.0 / D),
              r=[("ssq", b3)], w=[("rstd", b3)])
            A("dve", lambda e, b3=b3: e.reciprocal(out=rstd[b3][:], in_=rstd[b3][:]), r=[("rstd", b3)], w=[("rstd", b3)])
            A("dve", lambda e, b3=b3, b2=b2: e.tensor_scalar(out=xn[b2][:], in0=xt[b3][:], scalar1=rstd[b3][:], scalar2=None,
                                                           op0=ALU.mult), r=[("xt", b3), ("rstd", b3)], w=[("xn", b2)])
            for kc in range(8):
                A("pe", lambda e, kc=kc, b2=b2: e.transpose(out=tp[b2][:, kc * 128:(kc + 1) * 128],
                                                            in_=xn[b2][:, kc * 128:(kc + 1) * 128], identity=identb[:]),
                  r=[("xn", b2)], w=[("tp", b2)])
            A("dve", lambda e, b2=b2, hb=hb, si=si: e.tensor_tensor(
                out=hb[:, :, si * 128:(si + 1) * 128], in0=tp[b2][:].rearrange("p (k t) -> p k t", k=8),
                in1=gsb[:, 0, :].unsqueeze(2).to_broadcast([128, 8, 128]), op=ALU.mult),
              r=[("tp", b2)], w=[hk + (si,)])
        hks = [hk + (si,) for si in range(4)]

        def proj(W, c0, bank, bk, wk, hb=hb, hks=hks):
            for kc in range(8):
                A("pe", lambda e, kc=kc, W=W, c0=c0, bank=bank, hb=hb: e.matmul(bank[:], lhsT=W[:, kc, c0:c0 + 128], rhs=hb[:, kc, :],
                                                                          start=(kc == 0), stop=(kc == 7)),
                  r=hks + wk, w=[bk])
        proj(WA, 0, pk[0], "pk0", ["WA"])
        proj(WA, 128, pk[1], "pk1", ["WA"])
        proj(WA, 256, pk[2], "pk2", ["WA"])
        proj(WAr, 0, pk[3], "pk3", WArk)
        A("act", lambda e, G=G: e.copy(out=rawK[:, G * 512:(G + 1) * 512], in_=pk[0][:]), r=["pk0"], w=[("rawK", G)])
        A("act", lambda e, G=G: e.copy(out=rawV[:, G * 512:(G + 1) * 512], in_=pk[1][:]), r=["pk1"], w=[("rawV", G)])

        def ropeevac(b0, b0k, b1, b1k, dst, dk, G=G):
            A("dve", lambda e: e.tensor_tensor(out=t1a[:], in0=b0[:], in1=cs[G % 2][:], op=ALU.mult),
              r=[b0k, ("cs", G % 2)], w=["t1a"])
            A("dve", lambda e: e.tensor_tensor(out=t1b[:], in0=b1[:], in1=sn[G % 2][:], op=ALU.mult),
              r=[b1k, ("sn", G % 2)], w=["t1b"])
            A("dve", lambda e: e.tensor_tensor(out=dst, in0=t1a[:], in1=t1b[:], op=ALU.add), r=["t1a", "t1b"], w=[dk])
        ropeevac(pk[2], "pk2", pk[3], "pk3", KsT[:, G * 512:(G + 1) * 512], ("KsT", G))
        if G >= 10:
            proj(WA, 512, pk[0], "pk0", ["WA"])
            proj(WAr, 128, pk[1], "pk1", WArk)
            ropeevac(pk[0], "pk0", pk[1], "pk1", KwT[:, (G - 10) * 512:(G - 9) * 512], ("KwT", G))
        for si in range(4):
            s = 4 * G + si
            nv = 2 if G >= 10 else 1
            for vi in range(nv):
                c0 = 384 if vi == 0 else 640
                for kc in range(8):
                    A("pe", lambda e, kc=kc, si=si, vi=vi, c0=c0, hb=hb: e.matmul(
                        pv[:, vi * 128:(vi + 1) * 128], lhsT=hb[:, kc, si * 128:(si + 1) * 128], rhs=WA[:, kc, c0:c0 + 128],
                        start=(kc == 0), stop=(kc == 7)), r=[hk + (si,), "WA"], w=["pv"])
            A("act", lambda e, s=s: e.copy(out=Vs[:, s, 0:64], in_=pv[:, 0:64]), r=["pv", "Vs0"], w=[("Vs", s, 0)])
            A("act", lambda e, s=s: e.copy(out=Vs[:, s, 130:194], in_=pv[:, 64:128]), r=["pv", "Vs0"], w=[("Vs", s, 1)])
            if G >= 10:
                A("act", lambda e, s=s: e.copy(out=Vw[:, s - 40, 0:64], in_=pv[:, 128:192]), r=["pv", "Vw0"], w=[("Vw", s, 0)])
                A("act", lambda e, s=s: e.copy(out=Vw[:, s - 40, 130:194], in_=pv[:, 192:256]), r=["pv", "Vw0"], w=[("Vw", s, 1)])
    if "KsT" in dbgout:
        A("sp", lambda e: e.dma_start(out=dbgout["KsT"], in_=KsT[:]), r=[("KsT", G) for G in range(16)], dma="dbg")
        A("sp", lambda e: e.dma_start(out=dbgout["Vs"], in_=Vs[:]), r=[("Vs", s, i) for s in range(64) for i in range(2)] + ["Vs1", "Vs2"], dma="dbg")
        A("sp", lambda e: e.dma_start(out=dbgout["KwT"], in_=KwT[:]), r=[("KwT", G) for G in range(10, 16)], dma="dbg")
    ph.emit()
    pst.close()
    ist.close()
    _phase_A2(nc, st, SB, PS, dr, P, dbgout, rawK, rawV, t1a, t1b)


def _phase_A2(nc, st0, SB, PS, dr, P, dbgout, rawK, rawV, t1a, t1b):
    kcT, RCv, ones32 = P["kcT"], P["RCv"], P["ones32"]
    with contextlib.ExitStack() as st:
        w1 = [SB(st, f"w1{x}", [128, 32, 256], BF16) for x in range(2)]
        w2 = [SB(st, f"w2{x}", [128, 2, 64], BF16) for x in range(2)]
        posT = [SB(st, f"posT{x}", [128, 32, 2], BF16) for x in range(2)]
        b1 = [SB(st, f"b1{x}", [128, 2], F32) for x in range(2)]
        b1e = [SB(st, f"b1e{x}", [128, 2], F32) for x in range(2)]
        b2row = [SB(st, f"b2row{x}", [1, 128], F32) for x in range(2)]
        b2rrow = SB(st, "b2rrow", [1, 128], F32)
        w2pad = [SB(st, f"w2pad{g}", [128, 2, 128], BF16) for g in range(2)]
        w2padr = [SB(st, f"w2padr{g}", [128, 2, 128], BF16) for g in range(2)]
        hid = [[SB(st, f"hid{x}{g}", [128, 2, 512], BF16) for g in range(2)] for x in range(2)]
        u = SB(st, "u", [128, 512], F32)
        u2 = SB(st, "u2", [128, 512], F32)
        th = SB(st, "th", [128, 512], F32)
        csC = SB(st, "csC", [128, 512], F32)
        snC = SB(st, "snC", [128, 512], F32)
        pb = PS(st, "pb", [128, 512], F32)
        ph_ = [PS(st, f"ph{i}", [128, 512], F32) for i in range(2)]
        po = [PS(st, f"po{i}", [128, 512], F32) for i in range(2)]
        ph = Phase(nc, "A2")
        A = ph.add
        NCMP = 511
        for x, nm in enumerate(("k", "v")):
            src = dr[f"c{nm}_w1"].rearrange("(p d) h -> d p h", d=64)
            A("pool", lambda e, x=x, src=src: e.dma_start(out=w1[x][0:64], in_=src), w=[("w1", x, 0)], dma="w")
            A("pool", lambda e, x=x, src=src: e.dma_start(out=w1[x][64:128], in_=src), w=[("w1", x, 1)], dma="w")
            A("pool", lambda e, x=x, nm=nm: e.dma_start(out=w2[x][:], in_=dr[f"c{nm}_w2"].rearrange("(k p) d -> p k d", p=128)),
              w=[("w2", x)], dma="w")
            A("pool", lambda e, x=x, nm=nm: e.dma_start(out=posT[x][:], in_=dr[f"c{nm}_posT"]), w=[("posT", x)], dma="w")
            A("sp", lambda e, x=x, nm=nm: e.dma_start(out=b1[x][:], in_=dr[f"c{nm}_b1"]), w=[("b1", x)], dma="c")
            A("sp", lambda e, x=x, nm=nm: e.dma_start(out=b2row[x][:], in_=dr[f"c{nm}_b2"]), w=[("b2row", x)], dma="c")
        A("sp", lambda e: e.dma_start(out=csC[:], in_=dr["cosC"]), w=["csC"], dma="c")
        A("sp", lambda e: e.dma_start(out=snC[:], in_=dr["sinC"]), w=["snC"], dma="c")
        for g in range(2):
            A("dve", lambda e, g=g: e.memset(w2pad[g][:], 0.0), w=[("w2pad", g)])
            A("dve", lambda e, g=g: e.memset(w2padr[g][:], 0.0), w=[("w2padr", g)])
            A("dve", lambda e, g=g: e.tensor_copy(out=w2pad[g][:, :, g * 64:(g + 1) * 64], in_=w2[0][:]),
              r=[("w2", 0)], w=[("w2pad", g)])
            A("dve", lambda e, g=g: e.tensor_scalar(out=w2padr[g][:, :, g * 64:g * 64 + 32], in0=w2[0][:, :, 32:64],
                                                     scalar1=-1.0, scalar2=None, op0=ALU.mult), r=[("w2", 0)], w=[("w2padr", g)])
            A("dve", lambda e, g=g: e.tensor_copy(out=w2padr[g][:, :, g * 64 + 32:g * 64 + 64], in_=w2[0][:, :, 0:32]),
              r=[("w2", 0)], w=[("w2padr", g)])
            A("dve", lambda e, g=g: e.tensor_scalar(out=b2rrow[0:1, g * 64:g * 64 + 32], in0=b2row[0][0:1, g * 64 + 32:g * 64 + 64],
                                                     scalar1=-1.0, scalar2=None, op0=ALU.mult), r=[("b2row", 0)], w=["b2rrow"])
            A("dve", lambda e, g=g: e.tensor_copy(out=b2rrow[0:1, g * 64 + 32:g * 64 + 64], in_=b2row[0][0:1, g * 64:g * 64 + 32]),
              r=[("b2row", 0)], w=["b2rrow"])
        for x in range(2):
            raw = rawK if x == 0 else rawV
            for half in range(2):
                for p in range(32):
                    A("pe", lambda e, x=x, half=half, p=p: e.matmul(
                        pb[:, half * 2:half * 2 + 2], lhsT=w1[x][0:64, p, half * 128:(half + 1) * 128], rhs=posT[x][0:64, p, :],
                        start=(p == 0), stop=(p == 31)), r=[("w1", x, 0), ("posT", x)], w=["pb"])
            A("dve", lambda e, x=x: e.tensor_tensor(out=b1e[x][:], in0=pb[:, 0:4:2], in1=b1[x][:], op=ALU.add),
              r=["pb", ("b1", x)], w=[("b1e", x)])
            for g in range(2):
                for half in range(2):
                    bank = ph_[half]
                    for p in range(32):
                        A("pe", lambda e, x=x, g=g, half=half, p=p, raw=raw, bank=bank: e.matmul(
                            bank[:, 0:NCMP], lhsT=w1[x][64 * g:64 * g + 64, p, half * 128:(half + 1) * 128],
                            rhs=raw[64 * g:64 * g + 64, p:p + 16 * (NCMP - 1) + 1:16],
                            start=(p == 0), stop=(p == 31)), r=[("w1", x, g)], w=[("ph", half)])
                    A("act", lambda e, x=x, half=half, bank=bank: e.activation(out=u[:, 0:NCMP], in_=bank[:, 0:NCMP], func=AF.Identity,
                                                                               bias=b1e[x][:, half:half + 1], scale=1.0),
                      r=[("ph", half), ("b1e", x)], w=["u"])
                    A("dve", lambda e: e.tensor_tensor(out=u2[:, 0:NCMP], in0=u[:, 0:NCMP], in1=u[:, 0:NCMP], op=ALU.mult), r=["u"], w=["u2"])
                    A("dve", lambda e: e.tensor_scalar(out=u2[:, 0:NCMP], in0=u2[:, 0:NCMP], scalar1=0.044715, scalar2=1.0,
                                                        op0=ALU.mult, op1=ALU.add), r=["u2"], w=["u2"])
                    A("dve", lambda e: e.tensor_tensor(out=u2[:, 0:NCMP], in0=u2[:, 0:NCMP], in1=u[:, 0:NCMP], op=ALU.mult), r=["u2", "u"], w=["u2"])
                    A("act", lambda e: e.activation(out=th[:, 0:NCMP], in_=u2[:, 0:NCMP], func=AF.Tanh, scale=0.7978845608028654),
                      r=["u2"], w=["th"])
                    A("dve", lambda e: e.tensor_scalar(out=th[:, 0:NCMP], in0=th[:, 0:NCMP], scalar1=0.5, scalar2=0.5,
                                                        op0=ALU.mult, op1=ALU.add), r=["th"], w=["th"])
                    A("dve", lambda e, x=x, g=g, half=half: e.tensor_tensor(out=hid[x][g][:, half, 0:NCMP], in0=th[:, 0:NCMP],
                                                                             in1=u[:, 0:NCMP], op=ALU.mult),
                      r=["th", "u"], w=[("hid", x, g, half)])
        for r_, (pads, brow, bank, bk) in enumerate(((w2pad, b2row[0], po[0], "po0"), (w2padr, b2rrow, po[1], "po1"))):
            first = True
            for g in range(2):
                for half in range(2):
                    A("pe", lambda e, g=g, half=half, pads=pads, bank=bank, first=first: e.matmul(
                        bank[:, 0:NCMP], lhsT=pads[g][:, half, :], rhs=hid[0][g][:, half, 0:NCMP], start=first, stop=False),
                      r=[("hid", 0, g, half), ("w2pad", g), ("w2padr", g)], w=[bk])
                    first = False
            A("pe", lambda e, brow=brow, bank=bank: e.matmul(bank[:, 0:NCMP], lhsT=brow[0:1, 0:128], rhs=ones32[0:1, 0:NCMP],
                                                              start=False, stop=True), r=[("b2row", 0), "b2rrow"], w=[bk])
        A("dve", lambda e: e.tensor_tensor(out=t1a[:, 0:NCMP], in0=po[0][:, 0:NCMP], in1=csC[:, 0:NCMP], op=ALU.mult),
          r=["po0", "csC"], w=["t1a"])
        A("dve", lambda e: e.tensor_tensor(out=t1b[:, 0:NCMP], in0=po[1][:, 0:NCMP], in1=snC[:, 0:NCMP], op=ALU.mult),
          r=["po1", "snC"], w=["t1b"])
        A("dve", lambda e: e.memset(kcT[:], 0.0), w=["kcT"])
        A("dve", lambda e: e.tensor_tensor(out=kcT[:, 0:NCMP], in0=t1a[:, 0:NCMP], in1=t1b[:, 0:NCMP], op=ALU.add),
          r=["t1a", "t1b"], w=["kcT"])
        for c in range(4):
            n = 128 if c < 3 else NCMP - 384
            bank = po[c % 2]
            bk = f"po{c % 2}"
            for g in range(2):
                for half in range(2):
                    A("pe", lambda e, c=c, n=n, g=g, half=half, bank=bank: e.matmul(
                        bank[0:n, g * 64:(g + 1) * 64], lhsT=hid[1][g][:, half, c * 128:c * 128 + n], rhs=w2[1][:, half, :],
                        start=(half == 0), stop=False), r=[("hid", 1, g, half), ("w2", 1)], w=[bk])
                A("pe", lambda e, n=n, g=g, bank=bank: e.matmul(
                    bank[0:n, g * 64:(g + 1) * 64], lhsT=ones32[0:1, 0:n], rhs=b2row[1][0:1, g * 64:(g + 1) * 64],
                    start=False, stop=True), r=[("b2row", 1)], w=[bk])
            A("act", lambda e, c=c, n=n, bank=bank: e.copy(out=RCv[0:n, c, 0:64], in_=bank[0:n, 0:64]), r=[bk, "RCv"], w=[("RCv", c, 0)])
            A("act", lambda e, c=c, n=n, bank=bank: e.copy(out=RCv[0:n, c, 130:194], in_=bank[0:n, 64:128]), r=[bk, "RCv"], w=[("RCv", c, 1)])
            A("dve", lambda e, c=c, n=n: e.memset(RCv[0:n, c, 64:65], 1.0), r=["RCv"], w=[("RCv", c, 2)])
            A("dve", lambda e, c=c, n=n: e.memset(RCv[0:n, c, 66:67], 1.0), r=["RCv"], w=[("RCv", c, 3)])
        if "kcT" in dbgout:
            A("sp", lambda e: e.dma_start(out=dbgout["kcT"], in_=kcT[:]), r=["kcT"], dma="dbg")
            A("sp", lambda e: e.dma_start(out=dbgout["RCv"], in_=RCv[:]), r=[("RCv", c, i) for c in range(4) for i in range(4)], dma="dbg")
        ph.emit()


def _phase_B(nc, st, SB, PS, dr, gscr, oscr, P, dbgout):
    KsT, KwT, Vs, Vw, kcT, RCv, biasC = P["KsT"], P["KwT"], P["Vs"], P["Vw"], P["kcT"], P["RCv"], P["biasC"]
    ones32, gsb, vq = P["ones32"], P["gsb"], P["vq"]
    identf = SB(st, "identfB", [128, 128], F32)
    Wq = SB(st, "Wq", [128, 8, 1024], BF16)
    Wqr = SB(st, "Wqr", [128, 8, 1024], BF16)
    Wg = SB(st, "Wg", [128, 8, 48], BF16)
    EF = SB(st, "EF", [128, 16, 128], BF16)
    ovl = SB(st, "ovl", [128, 4, 129], BF16)
    xq1 = SB(st, "xq0", [128, D], F32)
    xq = [xq1, xq1]
    xnf = SB(st, "xnf", [128, D], F32)
    ssq = SB(st, "ssqB", [128, 1], F32)
    rstd = SB(st, "rstdB", [128, 1], F32)
    hTq = SB(st, "hTq", [128, 8, 128], BF16)
    cq1 = SB(st, "cq0", [128, 128], F32)
    sq1 = SB(st, "sq0", [128, 128], F32)
    cq = [cq1, cq1]
    sq = [sq1, sq1]
    bon3 = [SB(st, f"bon{i}", [128, 128], F32) for i in range(3)]
    mkb3 = [SB(st, f"mkb{i}", [128, 6, 128], BF16) for i in range(3)]
    QT = [SB(st, f"QT{i}", [128, 8, 128], BF16) for i in range(2)]
    gsig = [SB(st, f"gsig{i}", [48, 128], F32) for i in range(2)]
    selT = [[SB(st, f"selT{i}{g}", [128, 128], BF16) for g in range(2)] for i in range(2)]
    acc = [SB(st, f"acc{i}", [128, 8, 128], F32) for i in range(2)]
    Ec = [SB(st, f"Ec{c}", [128, 8, 128], BF16) for c in range(4)]
    NBUF = 5
    E = [SB(st, f"E{i}", [128, 8, 128], BF16) for i in range(NBUF)]
    Pb = [SB(st, f"Pb{i}", [128, 8, 128], BF16) for i in range(NBUF)]
    oasb = [SB(st, f"oasb{i}", [128, 1024], F32) for i in range(2)]
    msk = [SB(st, f"msk{i}", [128, 128], BF16) for i in range(NBUF)]
    t1 = SB(st, "t1B", [128, 8, 128], F32)
    t2 = SB(st, "t2B", [128, 8, 128], F32)
    dsb = SB(st, "dsb", [65, 1024], F32)
    rdb = SB(st, "rdb", [128, 4, 128], F32)
    tmpf = SB(st, "tmpf", [128, 4, 128], F32)
    grow = [SB(st, f"grow{i}", [65, 1024], F32) for i in range(2)]
    dsb2 = SB(st, "dsb2", [65, 1024], F32)
    cbc = SB(st, "cbc", [128, 1024], F32)
    crow = nc.dram_tensor("crow", [8, 1024], F32).ap()
    score = SB(st, "score", [128, 128], F32)
    work = SB(st, "work", [128, 128], F32)
    selq = SB(st, "selq", [128, 128], F32)
    m8a = SB(st, "m8a", [128, 8], F32)
    m8b = SB(st, "m8b", [128, 8], F32)
    thr = SB(st, "thr", [128, 1], F32)
    rc = SB(st, "rc", [128, 8], F32)
    obf = SB(st, "obf", [128, 8, 128], BF16)
    scA = PS(st, "scA", [128, 1024], F32)
    scB = PS(st, "scB", [128, 1024], F32)
    oa = PS(st, "oa", [128, 1024], F32)
    mx = PS(st, "mx", [128, 512], F32)
    msc = PS(st, "msc", [128, 512], F32)
    gs2 = gscr.rearrange("b r q -> b (r q)")
    print("[phase B] sbuf bytes remaining:", nc.sbuf_bytes_remaining)

    ph = Phase(nc, "B")
    A = ph.add
    A("pool", lambda e: e.dma_start(out=Wq[:], in_=dr["wq"].rearrange("(k p) n -> p k n", p=128)), w=["Wq"], dma="w")
    A("pool", lambda e: e.dma_start(out=Wg[:], in_=dr["wg"].rearrange("(k p) n -> p k n", p=128)), w=["Wg"], dma="w")
    A("sp", lambda e: e.dma_start(out=EF[:], in_=dr["ef"]), w=["EF"], dma="c")
    A("sp", lambda e: e.dma_start(out=ovl[:], in_=dr["ovl"]), w=["ovl"], dma="c")
    A("sp", lambda e: e.dma_start(out=identf[:], in_=dr["identf"]), w=["identf"], dma="c")
    for kc in range(8):
        v = Wq[:, kc, :].rearrange("p (h two d) -> p h two d", two=2, d=32)
        vr = Wqr[:, kc, :].rearrange("p (h two d) -> p h two d", two=2, d=32)
        A("dve", lambda e, v=v, vr=vr: e.tensor_scalar(out=vr[:, :, 0, :], in0=v[:, :, 1, :], scalar1=-1.0, scalar2=None, op0=ALU.mult),
          r=["Wq"], w=[("Wqr", kc, 0)])
        A("dve", lambda e, v=v, vr=vr: e.tensor_copy(out=vr[:, :, 1, :], in_=v[:, :, 0, :]), r=["Wq"], w=[("Wqr", kc, 1)])
    Wqrk = [("Wqr", kc, i) for kc in range(8) for i in range(2)]

    def v8(t, nq):
        return t[:].rearrange("p (h q) -> p h q", q=128)[:, :, 0:nq]

    def v4(t, u, nq, p0=0, p1=128):
        return t[p0:p1, u * 512:(u + 1) * 512].rearrange("p (h q) -> p h q", q=128)[:, :, 0:nq]

    def bc(ap2, n, nq):
        return ap2.unsqueeze(1).to_broadcast([ap2.shape[0], n, nq])

    def stage_load(bi):
        S, off, nq, col0 = BLK[bi]
        t0 = 128 * S + off
        b3 = bi % 3
        A("sp", lambda e: e.dma_start(out=xq1[0:nq, :], in_=dr["xs"][t0:t0 + nq, :]), w=["xqb"], dma="ldq")
        A("sp", lambda e: e.dma_start(out=cq1[:, 0:nq], in_=dr["cosQ"][:, bi * 128:bi * 128 + nq]), w=["cqb"], dma="ldq")
        A("sp", lambda e: e.dma_start(out=sq1[:, 0:nq], in_=dr["sinQ"][:, bi * 128:bi * 128 + nq]), w=["sqb"], dma="ldq")
        A("sp", lambda e: e.dma_start(out=bon3[b3][0:nq, :], in_=dr["bonus"][0:nq, bi, :]), w=[("bon", b3)], dma=("ldm", b3))
        A("sp", lambda e: e.dma_start(out=mkb3[b3][:], in_=dr["mk"][bi]), w=[("mkb", b3)], dma=("ldm", b3))

    def stage_q(bi):
        S, off, nq, col0 = BLK[bi]
        pb = bi % 2
        A("act", lambda e: e.activation(out=xnf[0:nq, :], in_=xq[pb][0:nq, :], func=AF.Square, accum_out=ssq[0:nq, :]),
          r=["xqb"], w=["xnf", "ssq"])
        A("act", lambda e: e.activation(out=rstd[0:nq, :], in_=ssq[0:nq, :], func=AF.Sqrt, bias=EPS, scale=1.0 / D), r=["ssq"], w=["rstd"])
        A("dve", lambda e: e.reciprocal(out=rstd[0:nq, :], in_=rstd[0:nq, :]), r=["rstd"], w=["rstd"])
        A("dve", lambda e: e.tensor_scalar(out=xnf[0:nq, :], in0=xq[pb][0:nq, :], scalar1=rstd[0:nq, :], scalar2=None, op0=ALU.mult),
          r=["xqb", "rstd"], w=["xnf"])
        for kc in range(8):
            A("pe", lambda e, kc=kc: e.transpose(out=scA[:, kc * 128:kc * 128 + nq], in_=xnf[0:nq, kc * 128:(kc + 1) * 128],
                                                 identity=identf[0:nq, 0:nq]), r=["xnf", "identf"], w=[("scA", kc // 4)])
        A("dve", lambda e: e.tensor_tensor(out=hTq[:, :, 0:nq], in0=v8(scA, nq), in1=gsb[:, 0, :].unsqueeze(2).to_broadcast([128, 8, nq]),
                                           op=ALU.mult), r=[("scA", 0), ("scA", 1)], w=["hTq"])
        for hl in range(8):
            for kc in range(8):
                A("pe", lambda e, hl=hl, kc=kc: e.matmul(scB[:, hl * 128:hl * 128 + nq], lhsT=Wq[:, kc, hl * 128:(hl + 1) * 128],
                                                         rhs=hTq[:, kc, 0:nq], start=(kc == 0), stop=(kc == 7)),
                  r=["Wq", "hTq"], w=[("scB", hl // 4)])
        for hl in range(8):
            for kc in range(8):
                A("pe", lambda e, hl=hl, kc=kc: e.matmul(oa[:, hl * 128:hl * 128 + nq], lhsT=Wqr[:, kc, hl * 128:(hl + 1) * 128],
                                                         rhs=hTq[:, kc, 0:nq], start=(kc == 0), stop=(kc == 7)),
                  r=Wqrk + ["hTq"], w=[("oa", hl // 4)])
        A("dve", lambda e: e.tensor_tensor(out=t1[:, :, 0:nq], in0=v8(scB, nq), in1=bc(cq[pb][:, 0:nq], 8, nq), op=ALU.mult),
          r=[("scB", 0), ("scB", 1), "cqb"], w=[("t1", 0), ("t1", 3), ("t1", 6)])
        A("dve", lambda e: e.tensor_tensor(out=t2[:, :, 0:nq], in0=v8(oa, nq), in1=bc(sq[pb][:, 0:nq], 8, nq), op=ALU.mult),
          r=[("oa", 0), ("oa", 1), "sqb"], w=["t2"])
        A("dve", lambda e: e.tensor_tensor(out=QT[pb][:, :, 0:nq], in0=t1[:, :, 0:nq], in1=t2[:, :, 0:nq], op=ALU.add),
          r=[("t1", 0), ("t1", 3), ("t1", 6), "t2"], w=[("QT", pb)])
        for kc in range(8):
            A("pe", lambda e, kc=kc: e.matmul(mx[0:48, 0:nq], lhsT=Wg[:, kc, :], rhs=hTq[:, kc, 0:nq], start=(kc == 0), stop=(kc == 7)),
              r=["Wg", "hTq"], w=MXALL)
        A("act", lambda e: e.activation(out=gsig[pb][:, 0:nq], in_=mx[0:48, 0:nq], func=AF.Sigmoid), r=MXALL, w=[("gsig", pb)])
        A("sp", lambda e: e.dma_start(out=gscr[bi, :, 0:nq], in_=gsig[pb][:, 0:nq]), r=[("gsig", pb)], w=[("gscr", bi)], dma=("gs", pb))

    growi = [0]
    grpi = [0]
    MXALL = ["mx"]

    DEFER = True
    pending = []
    stepc = [0]

    def flush(force=False):
        while pending and (force or pending[0][0] <= stepc[0]):
            pending.pop(0)[1]()

    fini = [0]

    def finalize(bi, g, br, first, src, srck, delay):
        S, off, nq, col0 = BLK[bi]
        pb = bi % 2
        p0 = 64 * g
        dp = 64 if g == 0 else 0
        flush(force=True)
        fi = fini[0]
        fini[0] += 1
        X, xk = (dsb, "dsb") if fi % 2 == 0 else (dsb2, "dsb2")
        ri = fi % 8
        gi = growi[0] % 2
        growi[0] += 1
        r0 = br * 16 + g * 8
        A("sp", lambda e: e.dma_start(out=grow[gi][dp:dp + 1, :], in_=gs2[bi:bi + 1, r0 * 128:r0 * 128 + 1024]),
          r=[("gscr", bi)], w=[("grow", gi)], dma=("gr", gi))
        A("dve", lambda e: e.tensor_scalar(out=X[dp:dp + 1, :], in0=src[dp:dp + 1, :], scalar1=1.0e-30, scalar2=None, op0=ALU.max),
          r=[srck(0), srck(1)], w=[xk])
        A("act", lambda e: e.activation(out=X[dp:dp + 1, :], in_=X[dp:dp + 1, :], func=AF.Ln), r=[xk], w=[xk])
        A("act", lambda e: e.activation(out=X[dp:dp + 1, :], in_=X[dp:dp + 1, :], func=AF.Exp, scale=-1.0), r=[xk], w=[xk])
        A("dve", lambda e: e.tensor_tensor(out=X[dp:dp + 1, :], in0=X[dp:dp + 1, :], in1=grow[gi][dp:dp + 1, :], op=ALU.mult),
          r=[xk, ("grow", gi)], w=[xk])
        A("sp", lambda e: e.dma_start(out=crow[ri:ri + 1, :], in_=X[dp:dp + 1, :]), r=[xk], w=[("crow", ri)], dma=("cr", ri % 2))
        A("sp", lambda e: e.dma_start(out=cbc[p0:p0 + 64, :], in_=crow[ri:ri + 1, :].partition_broadcast(64)),
          r=[("crow", ri)], w=[("cbc", g)], dma=("cb", g))
        def tail():
            for u in range(2):
                dst = acc[pb][p0:p0 + 64, 4 * u:4 * u + 4, 0:nq]
                if first:
                    A("dve", lambda e, u=u, dst=dst: e.tensor_tensor(out=dst, in0=v4(src, u, nq, p0, p0 + 64), in1=v4(cbc, u, nq, p0, p0 + 64),
                                                                     op=ALU.mult), r=[srck(u), ("cbc", g)], w=[("acc", pb, g, u)])
                else:
                    A("dve", lambda e, u=u: e.tensor_tensor(out=tmpf[p0:p0 + 64, :, 0:nq], in0=v4(src, u, nq, p0, p0 + 64),
                                                            in1=v4(cbc, u, nq, p0, p0 + 64), op=ALU.mult), r=[srck(u), ("cbc", g)], w=["tmpf"])
                    A("dve", lambda e, dst=dst: e.tensor_tensor(out=dst, in0=dst, in1=tmpf[p0:p0 + 64, :, 0:nq], op=ALU.add),
                      r=["tmpf", ("acc", pb, g, u)], w=[("acc", pb, g, u)])
        if DEFER:
            pending.append((stepc[0] + delay, tail))
        else:
            tail()

    def vaug(Vt, idx, g):
        return Vt[:, idx, 0:128] if g == 0 else Vt[:, idx, 66:194]

    def stage_cmp(bi):
        fins = [stage_cmp_g(bi, g) for g in range(2)]
        for g, (ob, obk) in enumerate(fins):
            finalize(bi, g, 0, True, ob, lambda u, obk=obk: obk + (u,), 4)

    def stage_cmp_g(bi, g):
        S, off, nq, col0 = BLK[bi]
        pb = bi % 2
        M = [128, 128]
        if True:
            for c in range(4):
                sc, sk = (scA, "scA") if c % 2 == 0 else (scB, "scB")
                for u in range(2):
                    A("pe", lambda e, c=c, u=u, sc=sc: e.matmul(v4(sc, u, nq), lhsT=kcT[64 * g:64 * g + 64, c * 128:(c + 1) * 128],
                                                              rhs=QT[pb][64 * g:64 * g + 64, 4 * u:4 * u + 4, 0:nq], start=True, stop=True),
                      r=["kcT", ("QT", pb)], w=[(sk, u)])
                A("act", lambda e, c=c, sc=sc: e.activation(out=Ec[c][:, :, 0:nq], in_=v8(sc, nq), func=AF.Exp, bias=biasC[:, c:c + 1],
                                                            scale=SCALE), r=[(sk, 0), (sk, 1), "biasC"], w=[("Ec", c)])
                A("dve", lambda e, c=c: e.tensor_tensor(out=Ec[c][:, :, 0:nq], in0=Ec[c][:, :, 0:nq],
                                                        in1=bc(mkb3[bi % 3][:, 2 + c, 0:nq], 8, nq), op=ALU.mult),
                  r=[("Ec", c), ("mkb", bi % 3)], w=[("Ec", c)])
            for u in range(2):
                for c in range(4):
                    A("pe", lambda e, c=c, u=u: e.matmul(v4(oa, u, nq, 0, M[g]), lhsT=vaug(RCv, c, g), rhs=Ec[c][:, 4 * u:4 * u + 4, 0:nq],
                                                         start=(c == 0), stop=(c == 3)), r=[("Ec", c), "RCv"], w=[("oa", u)])
            regs = []
            for hl in range(8):
                j, o = hl // 3, (hl % 3) * 129
                tt, tk = [(scA, ("scA", 0)), (scA, ("scA", 1)), (scB, ("scB", 0))][j]
                base = 512 if j == 1 else 0
                regs.append((tt, tk, base + o))
                for c in range(4):
                    A("pe", lambda e, hl=hl, c=c, tt=tt, base=base, o=o: e.matmul(
                        tt[0:nq, base + o:base + o + 129], lhsT=Ec[c][:, hl, 0:nq], rhs=ovl[:, c, :], start=(c == 0), stop=(c == 3)),
                      r=[("Ec", c), "ovl"], w=[tk])
            banks = [(scA, ("scA", 0), 0, 3, 0), (scA, ("scA", 1), 512, 3, 3), (scB, ("scB", 0), 0, 2, 6)]
            for tt, tk, base, nh, h0 in banks:
                A("dve", lambda e, tt=tt, base=base, nh=nh, h0=h0: e.tensor_scalar(
                    out=rc[0:nq, h0:h0 + nh], in0=tt[0:nq, base + 128:base + 128 + 129 * (nh - 1) + 1:129], scalar1=1.0e-30,
                    scalar2=None, op0=ALU.max), r=[tk], w=[("rc", h0)])
            A("dve", lambda e: e.reciprocal(out=rc[0:nq, :], in_=rc[0:nq, :]), r=[("rc", 0), ("rc", 3), ("rc", 6)], w=["rc"])
            for tt, tk, base, nh, h0 in banks:
                A("dve", lambda e, tt=tt, base=base, nh=nh, h0=h0: e.tensor_tensor(
                    out=t1[0:nq, h0:h0 + nh, :], in0=tt[0:nq, base:base + 129 * nh].rearrange("p (h c) -> p h c", c=129)[:, :, 0:128],
                    in1=rc[0:nq, h0:h0 + nh].unsqueeze(2).to_broadcast([nq, nh, 128]), op=ALU.mult), r=[tk, "rc"], w=[("t1", h0)])
            A("dve", lambda e: e.tensor_reduce(out=score[0:nq, :], in_=t1[0:nq, :, :].rearrange("p h s -> p s h"), axis=AX.X, op=ALU.add),
              r=[("t1", 0), ("t1", 3), ("t1", 6)], w=["score"])
            A("dve", lambda e: e.tensor_tensor(out=score[0:nq, :], in0=score[0:nq, :], in1=bon3[bi % 3][0:nq, :], op=ALU.add),
              r=["score", ("bon", bi % 3)], w=["score"])
            A("dve", lambda e: e.max(out=m8a[0:nq, :], in_=score[0:nq, :]), r=["score"], w=["m8a"])
            A("dve", lambda e: e.match_replace(out=work[0:nq, :], in_to_replace=m8a[0:nq, :], in_values=score[0:nq, :], imm_value=-3.0e38),
              r=["score", "m8a"], w=["work"])
            A("dve", lambda e: e.max(out=m8b[0:nq, :], in_=work[0:nq, :]), r=["work"], w=["m8b"])
            A("dve", lambda e: e.tensor_scalar(out=thr[0:nq, :], in0=m8b[0:nq, 7:8], scalar1=-1.0e29, scalar2=None, op0=ALU.max),
              r=["m8b"], w=["thr"])
            A("dve", lambda e: e.tensor_scalar(out=selq[0:nq, :], in0=score[0:nq, :], scalar1=thr[0:nq, :], scalar2=None, op0=ALU.is_ge),
              r=["score", "thr"], w=["selq"])
            A("pe", lambda e: e.transpose(out=mx[:, 0:nq], in_=selq[0:nq, :], identity=identf[0:nq, 0:nq]), r=["selq", "identf"], w=MXALL)
            A("act", lambda e, g=g: e.copy(out=selT[pb][g][:, 0:nq], in_=mx[:, 0:nq]), r=MXALL, w=[("selT", pb, g)])
            flush(force=True)
            ob = oasb[grpi[0] % 2]
            obk = ("oasb", grpi[0] % 2)
            grpi[0] += 1
            for u in range(2):
                A("act", lambda e, u=u, ob=ob: e.copy(out=ob[:, u * 512:(u + 1) * 512], in_=oa[:, u * 512:(u + 1) * 512]),
                  r=[("oa", u)], w=[obk + (u,)])
            return ob, obk

    LAG = 4
    FILL = 1
    WARM = 16
    XFILL = 12

    def stage_attn(bi):
        S, off, nq, col0 = BLK[bi]
        pb = bi % 2
        M = [128, 128]
        items = []
        gidx = []
        for g in range(2):
            for br in (1, 2):
                kts = list(range(0, S + 1)) if br == 1 else list(range(S - 4, S + 1))
                for idx, kt in enumerate(kts):
                    items.append((g, br, kt, idx == 0, idx == len(kts) - 1))
                    gidx.append(idx)
        N = len(items)
        srcs = [None] * N
        mids = [None] * N

        def front(i):
            g, br, kt, isfirst, islast = items[i]
            KT, Vt, koff = (KsT, Vs, 0) if br == 1 else (KwT, Vw, 40)
            sc, sk = (scA, "scA") if i % 2 == 0 else (scB, "scB")
            Eb, ek = E[i % NBUF], ("E", i % NBUF)
            Pq, pk_ = Pb[i % NBUF], ("Pb", i % NBUF)
            mb, mbk = msk[i % NBUF], ("msk", i % NBUF)
            masked = True
            if br == 1:
                a, v = kt // 16, kt % 16
                kw = dict(tile_position=(96, 0)) if a == 3 else {}
                A("pe", lambda e: e.matmul(mx[:, 0:nq], lhsT=EF[32 * a:32 * a + 32, v, :], rhs=selT[pb][g][32 * a:32 * a + 32, 0:nq],
                                           start=True, stop=True, **kw), r=["EF", ("selT", pb, g)], w=["mx"])
                if kt == S:
                    A("dve", lambda e: e.tensor_tensor(out=mb[:, 0:nq], in0=mx[:, 0:nq], in1=mkb3[bi % 3][:, 0, 0:nq], op=ALU.mult),
                      r=["mx", ("mkb", bi % 3)], w=[mbk])
                else:
                    A("dve", lambda e: e.tensor_copy(out=mb[:, 0:nq], in_=mx[:, 0:nq]), r=["mx"], w=[mbk])
                mask_ap, mask_r = mb[:, 0:nq], [mbk]
            else:
                if kt == S - 4:
                    mask_ap, mask_r = mkb3[bi % 3][:, 1, 0:nq], [("mkb", bi % 3)]
                elif kt == S:
                    mask_ap, mask_r = mkb3[bi % 3][:, 0, 0:nq], [("mkb", bi % 3)]
                else:
                    masked = False
            for u in range(2):
                A("pe", lambda e, u=u: e.matmul(v4(sc, u, nq), lhsT=KT[64 * g:64 * g + 64, (kt - koff) * 128:(kt - koff + 1) * 128],
                                                rhs=QT[pb][64 * g:64 * g + 64, 4 * u:4 * u + 4, 0:nq], start=True, stop=True),
                  r=[("QT", pb)], w=[(sk, u)])
            A("act", lambda e: e.activation(out=Eb[:, :, 0:nq], in_=v8(sc, nq), func=AF.Exp, scale=SCALE), r=[(sk, 0), (sk, 1)], w=[ek])
            for _ in range(FILL + (1 if gidx[i] < XFILL else 0)):
                A("pe", lambda e: e.matmul(msc[:], lhsT=Wq[:, 0, 0:128], rhs=Wq[:, 1, 0:512], start=True, stop=True), r=["Wq"], w=["msc"])
            if masked:
                mids[i] = (Eb, ek, Pq, pk_, mask_ap, mask_r)
                srcs[i] = (Pq, pk_)
            else:
                srcs[i] = (Eb, ek)

        def mid(i):
            if mids[i] is None:
                return
            Eb, ek, Pq, pk_, mask_ap, mask_r = mids[i]
            A("dve", lambda e: e.tensor_tensor(out=Pq[:, :, 0:nq], in0=Eb[:, :, 0:nq], in1=bc(mask_ap, 8, nq), op=ALU.mult),
              r=[ek] + mask_r, w=[pk_])

        def back(i):
            g, br, kt, isfirst, islast = items[i]
            KT, Vt, koff = (KsT, Vs, 0) if br == 1 else (KwT, Vw, 40)
            src, srck = srcs[i]
            for u in range(2):
                A("pe", lambda e, u=u: e.matmul(v4(oa, u, nq, 0, M[g]), lhsT=vaug(Vt, kt - koff, g), rhs=src[:, 4 * u:4 * u + 4, 0:nq],
                                                start=isfirst, stop=islast), r=[srck], w=[("oa", u)])
            if islast:
                flush(force=True)
                ob = oasb[grpi[0] % 2]
                obk = ("oasb", grpi[0] % 2)
                grpi[0] += 1
                for u in range(2):
                    A("act", lambda e, u=u: e.copy(out=ob[:, u * 512:(u + 1) * 512], in_=oa[:, u * 512:(u + 1) * 512]),
                      r=[("oa", u)], w=[obk + (u,)])
                finalize(bi, g, br, False, ob, lambda u: obk + (u,), 4)

        for _ in range(WARM):
            A("pe", lambda e: e.matmul(msc[:], lhsT=Wq[:, 0, 0:128], rhs=Wq[:, 1, 0:512], start=True, stop=True), r=["Wq"], w=["msc"])
        for i in range(N + LAG):
            stepc[0] += 1
            flush()
            if i < N:
                front(i)
            if 0 <= i - 1 < N:
                mid(i - 1)
            if i - LAG >= 0:
                back(i - LAG)
        if DEFER:
            pending.append((stepc[0] + 4, lambda: stage_store(bi)))
        else:
            stage_store(bi)

    def stage_store(bi):
        S, off, nq, col0 = BLK[bi]
        pb = bi % 2
        rk = [("acc", pb, g, u) for g in range(2) for u in range(2)]
        if bi == 0:
            A("dve", lambda e: e.tensor_scalar(out=obf[:, :, 0:nq], in0=acc[pb][:, :, 0:nq], scalar1=vq[:, 0:1], scalar2=None, op0=ALU.mult),
              r=rk + ["vq"], w=["obf"])
        else:
            A("dve", lambda e: e.tensor_copy(out=obf[:, :, 0:nq], in_=acc[pb][:, :, 0:nq]), r=rk, w=["obf"])
        A("sp", lambda e: e.dma_start(out=oscr[:, :, col0:col0 + nq], in_=obf[:, :, 0:nq]), r=["obf"], w=["oscr"], dma="os")
        if "selT" in dbgout and bi == dbgout["_blk"]:
            for g in range(2):
                A("sp", lambda e, g=g: e.dma_start(out=dbgout["selT"][g], in_=selT[pb][g][:]), r=[("selT", pb, g)], dma="dbg")
            A("sp", lambda e: e.dma_start(out=dbgout["QT"], in_=QT[pb][:]), r=[("QT", pb)], dma="dbg")
            A("sp", lambda e: e.dma_start(out=dbgout["acc"], in_=acc[pb][:]), r=rk, dma="dbg")

    nb = dbgout.get("_nblk", NBLK)
    stage_load(0)
    stage_q(0)
    if nb > 1:
        stage_load(1)
    stage_cmp(0)
    for bi in range(nb):
        if bi + 1 < nb:
            stage_q(bi + 1)
            if bi + 2 < nb:
                stage_load(bi + 2)
            stage_cmp(bi + 1)
        stage_attn(bi)
    flush(force=True)
    ph.emit()


def _norm_group(A, xres, c0, n, gcol, onesb, sqb, pn, rs, dst, dkey, tag):
    A("act", lambda e: e.activation(out=sqb[:, :, 0:n], in_=xres[:, :, c0:c0 + n], func=AF.Square), r=["xres"], w=["sqb"])
    for kc in range(8):
        A("pe", lambda e, kc=kc: e.matmul(pn[:, 0:n], lhsT=onesb[:], rhs=sqb[:, kc, 0:n], start=(kc == 0), stop=(kc == 7)),
          r=["sqb"], w=["pn"])
    A("act", lambda e: e.activation(out=rs[:, 0:n], in_=pn[:, 0:n], func=AF.Sqrt, bias=EPS, scale=1.0 / D), r=["pn"], w=["rs"])
    A("dve", lambda e: e.reciprocal(out=rs[:, 0:n], in_=rs[:, 0:n]), r=["rs"], w=["rs"])
    for kc in range(8):
        A("dve", lambda e, kc=kc: e.scalar_tensor_tensor(out=dst[:, kc, 0:n], in0=xres[:, kc, c0:c0 + n], scalar=gcol[:, kc:kc + 1],
                                                        in1=rs[:, 0:n], op0=ALU.mult, op1=ALU.mult), r=["xres", "rs"], w=[dkey])


def _phase_C(nc, st, SB, PS, dr, oscr, P, dbgout):
    xres, identf = P["xres"], P["identf"]
    Wo = SB(st, "Wo", [128, 8, 1024], BF16)
    xq = [SB(st, f"xqC{i}", [128, D], F32) for i in range(2)]
    ot = [SB(st, f"otC{i}", [128, 8, 512], BF16) for i in range(2)]
    pa = [PS(st, f"paC{i}", [128, 1024], F32) for i in range(2)]
    py = [PS(st, f"pyC{i}", [128, 512], F32) for i in range(2)]
    ph = Phase(nc, "C")
    A = ph.add
    A("pool", lambda e: e.dma_start(out=Wo[:], in_=dr["wout"].rearrange("(k p) n -> p k n", p=128)), w=["Wo"], dma="w")
    for bi, (S, off, nq, col0) in enumerate(BLK):
        pb = bi % 2
        t0 = 128 * S + off
        A("sp", lambda e, pb=pb, t0=t0, nq=nq: e.dma_start(out=xq[pb][0:nq, :], in_=dr["xs"][t0:t0 + nq, :]), w=[("xq", pb)], dma=("xq", pb))
        for kc in range(8):
            A("pe", lambda e, kc=kc, pb=pb, nq=nq: e.transpose(out=pa[pb][:, kc * 128:kc * 128 + nq], in_=xq[pb][0:nq, kc * 128:(kc + 1) * 128],
                                                              identity=identf[0:nq, 0:nq]), r=[("xq", pb)], w=[("pa", pb)])
        A("act", lambda e, pb=pb, nq=nq, col0=col0: e.copy(out=xres[:, :, col0:col0 + nq],
                                                           in_=pa[pb][:].rearrange("p (k q) -> p k q", q=128)[:, :, 0:nq]),
          r=[("pa", pb)], w=["xres"])
    for ti, (c0, n) in enumerate(TG):
        tb = ti % 2
        A("sp", lambda e, tb=tb, c0=c0, n=n: e.dma_start(out=ot[tb][:, :, 0:n], in_=oscr[:, :, c0:c0 + n]), w=[("ot", tb)], dma=("ot", tb))
        for dc in range(8):
            for hl in range(8):
                A("pe", lambda e, dc=dc, hl=hl, tb=tb, n=n: e.matmul(py[dc % 2][:, 0:n], lhsT=Wo[:, hl, dc * 128:(dc + 1) * 128],
                                                                    rhs=ot[tb][:, hl, 0:n], start=(hl == 0), stop=(hl == 7)),
                  r=["Wo", ("ot", tb)], w=[("py", dc % 2)])
            A("dve", lambda e, dc=dc, c0=c0, n=n: e.tensor_tensor(out=xres[:, dc, c0:c0 + n], in0=xres[:, dc, c0:c0 + n],
                                                                  in1=py[dc % 2][:, 0:n], op=ALU.add),
              r=[("py", dc % 2), "xres"], w=["xres"])
    if "x0mix" in dbgout:
        A("sp", lambda e: e.dma_start(out=dbgout["x0mix"], in_=xres[:]), r=["xres"], dma="dbg")
    ph.emit()


def _phase_ffn(nc, st, SB, PS, dr, L, P, dbgout):
    xres, onesb, gsb, vq = P["xres"], P["onesb"], P["gsb"], P["vq"]
    gcol = gsb[:, 1 + 2 * L, :]
    hTall = SB(st, f"hTall{L}", [128, 8, NTOK], BF16)
    with contextlib.ExitStack() as nst:
        sqb = SB(nst, f"sqbf{L}", [128, 8, 512], BF16)
        rs = SB(nst, f"rsf{L}", [128, 512], F32)
        pn = PS(nst, f"pnf{L}", [128, 512], F32)
        phn = Phase(nc, f"N{L}")
        for ti, (c0, n) in enumerate(TG):
            _norm_group(phn.add, xres, c0, n, gcol, onesb, sqb, pn, rs, hTall[:, :, c0:c0 + n], "hT", f"f{L}")
        phn.emit()
    wu = [SB(st, f"wu{L}{i}", [128, 8, 6, 256], BF16) for i in range(2)]
    wd = [SB(st, f"wd{L}{i}", [128, 6, 1024], BF16) for i in range(2)]
    cw = SB(st, f"cw{L}", [128, 3, 44], F32)
    cb = SB(st, f"cb{L}", [128, 44], F32)
    carry = SB(st, f"carry{L}", [128, 44, 2], F32)
    ub = [SB(st, f"ub{L}{i}", [128, 514], F32) for i in range(2)]
    cbuf = [SB(st, f"cbuf{L}{i}", [128, 512], F32) for i in range(2)]
    sg = SB(st, f"sg{L}", [128, 512], F32)
    act2 = [SB(st, f"act{L}{i}", [128, 6, 512], BF16) for i in range(2)]
    pu = [[PS(st, f"pu{L}{a}{b}", [128, 512], F32) for b in range(2)] for a in range(2)]
    py = [PS(st, f"pyf{L}{i}", [128, 512], F32) for i in range(2)]
    ph = Phase(nc, f"F{L}")
    A = ph.add
    A("sp", lambda e: e.dma_start(out=cw[:], in_=dr[f"cw{L}"]), w=["cw"], dma="c")
    A("sp", lambda e: e.dma_start(out=cb[:], in_=dr[f"cb{L}"]), w=["cb"], dma="c")
    A("dve", lambda e: e.memset(carry[:], 0.0), w=[("carry", ch) for ch in range(44)])

    def load_pass(p):
        wb = p % 2
        for i, fc in enumerate(FPASS[p]):
            A("pool", lambda e, i=i, fc=fc, wb=wb: e.dma_start(
                out=wu[wb][:, :, i, 0:128], in_=dr[f"wup{L}"][:, fc * 128:(fc + 1) * 128].rearrange("(k p) n -> p k n", p=128)),
              w=[("wu", wb, i)], dma=("w", wb))
            A("pool", lambda e, i=i, fc=fc, wb=wb: e.dma_start(
                out=wu[wb][:, :, i, 128:256], in_=dr[f"wup{L}"][:, DFF + fc * 128:DFF + (fc + 1) * 128].rearrange("(k p) n -> p k n", p=128)),
              w=[("wu", wb, i)], dma=("w", wb))
            A("pool", lambda e, i=i, fc=fc, wb=wb: e.dma_start(out=wd[wb][:, i, :], in_=dr[f"wdn{L}"][fc * 128:(fc + 1) * 128, :]),
              w=[("wd", wb, i)], dma=("w", wb))

    def up_stage(p, ti):
        wb = p % 2
        c0, n = TG[ti]
        ab = (p * len(TG) + ti) % 2
        for i, fc in enumerate(FPASS[p]):
            for part in range(2):
                bank = pu[part][i % 2]
                bk = ("pu", part, i % 2)
                ch = fc + 22 * part
                for kc in range(8):
                    A("pe", lambda e, kc=kc, i=i, part=part, bank=bank: e.matmul(
                        bank[:, 0:n], lhsT=wu[wb][:, kc, i, part * 128:(part + 1) * 128], rhs=hTall[:, kc, c0:c0 + n],
                        start=(kc == 0), stop=(kc == 7)), r=[("wu", wb, i)], w=[bk])
                A("act", lambda e, part=part, bank=bank: e.copy(out=ub[part][:, 2:2 + n], in_=bank[:, 0:n]), r=[bk], w=[("ub", part)])
                A("pool", lambda e, part=part, ch=ch: e.tensor_copy(out=ub[part][:, 0:2], in_=carry[:, ch, :]), r=[("carry", ch)], w=[("ubc", part)])
                A("act", lambda e, part=part, bank=bank, ch=ch: e.activation(
                    out=cbuf[part][:, 0:n], in_=bank[:, 0:n], func=AF.Identity, bias=cb[:, ch:ch + 1], scale=cw[:, 2, ch:ch + 1]),
                  r=[bk, "cw", "cb"], w=[("cbuf", part)])
                for k in (1, 0):
                    A("dve", lambda e, part=part, ch=ch, k=k: e.scalar_tensor_tensor(
                        out=cbuf[part][:, 0:n], in0=ub[part][:, k:k + n], scalar=cw[:, k, ch:ch + 1], in1=cbuf[part][:, 0:n],
                        op0=ALU.mult, op1=ALU.add), r=[("ub", part), ("ubc", part), ("cbuf", part), "cw"], w=[("cbuf", part)])
                A("pool", lambda e, part=part, ch=ch: e.tensor_copy(out=carry[:, ch, :], in_=ub[part][:, n:n + 2]), r=[("ub", part)], w=[("carry", ch)])
            A("act", lambda e: e.activation(out=sg[:, 0:n], in_=cbuf[0][:, 0:n], func=AF.Silu), r=[("cbuf", 0)], w=["sg"])
            A("dve", lambda e, i=i: e.tensor_tensor(out=act2[ab][:, i, 0:n], in0=sg[:, 0:n], in1=cbuf[1][:, 0:n], op=ALU.mult),
              r=["sg", ("cbuf", 1)], w=[("act", ab, i)])

    def down_stage(p, ti):
        wb = p % 2
        c0, n = TG[ti]
        ab = (p * len(TG) + ti) % 2
        nf = len(FPASS[p])
        for dc in range(8):
            for i in range(nf):
                A("pe", lambda e, dc=dc, i=i: e.matmul(py[dc % 2][:, 0:n], lhsT=wd[wb][:, i, dc * 128:(dc + 1) * 128], rhs=act2[ab][:, i, 0:n],
                                                       start=(i == 0), stop=(i == nf - 1)), r=[("wd", wb, i), ("act", ab, i)], w=[("py", dc % 2)])
            A("dve", lambda e, dc=dc: e.tensor_tensor(out=xres[:, dc, c0:c0 + n], in0=xres[:, dc, c0:c0 + n], in1=py[dc % 2][:, 0:n], op=ALU.add),
              r=[("py", dc % 2), "xres"], w=["xres"])

    steps = [(p, ti) for p in range(len(FPASS)) for ti in range(len(TG))]
    load_pass(0)
    load_pass(1)
    for k in range(len(steps) + 1):
        if k < len(steps):
            up_stage(*steps[k])
        if k >= 1:
            pp, pti = steps[k - 1]
            down_stage(pp, pti)
            if pti == len(TG) - 1 and pp + 2 < len(FPASS):
                load_pass(pp + 2)
    A("dve", lambda e: e.tensor_scalar(out=xres[:, :, 0:HALO], in0=xres[:, :, 0:HALO], scalar1=vq[:, 0:1], scalar2=None, op0=ALU.mult),
      r=["xres"], w=["xres"])
    if f"xffn{L}" in dbgout:
        A("sp", lambda e: e.dma_start(out=dbgout[f"xffn{L}"], in_=xres[:]), r=["xres"], dma="dbg")
    ph.emit()


def _phase_pool(nc, st, SB, PS, dr, P, dbgout):
    xres, onesb, gsb, vq = P["xres"], P["onesb"], P["gsb"], P["vq"]
    gcol = gsb[:, 2, :]
    hf = SB(st, "hf", [128, 8, NTOK], F32)
    pl = SB(st, "pl", [128, 8, NTOK], BF16)
    wa = SB(st, "wa", [128, NTOK], F32)
    wb_ = SB(st, "wb", [128, NTOK], F32)
    sqb = SB(st, "sqbp", [128, 8, 512], BF16)
    rs = SB(st, "rsp", [128, 512], F32)
    pw = SB(st, "pw", [128, 8, 256], BF16)
    pbias = SB(st, "pbias", [128, 8], F32)
    pscl = SB(st, "pscl", [128, 8], F32)
    psc16 = SB(st, "psc16", [128, 8, 16], F32)
    tmp16 = SB(st, "tmp16", [128, 16], F32)
    ytmp = SB(st, "ytmp", [128, 512], F32)
    pn = PS(st, "pnp", [128, 512], F32)
    py = [PS(st, f"pyp{i}", [128, 512], F32) for i in range(2)]
    ph = Phase(nc, "P")
    A = ph.add
    A("pool", lambda e: e.dma_start(out=pw[:], in_=dr["poolw"]), w=["pw"], dma="w")
    A("sp", lambda e: e.dma_start(out=pbias[:], in_=dr["poolb"]), w=["pbias"], dma="c")
    A("sp", lambda e: e.dma_start(out=pscl[:], in_=dr["pools"]), w=["pscl"], dma="c")
    A("sp", lambda e: e.dma_start(out=psc16[:], in_=dr["pscale"]), w=["psc16"], dma="c")
    for ti, (c0, n) in enumerate(TG):
        _norm_group(A, xres, c0, n, gcol, onesb, sqb, pn, rs, hf[:, :, c0:c0 + n], "hf", "p")
    for kc in range(8):
        nsteps = kc // 2 + 1
        w = 2 ** nsteps
        src, sk = hf[:, kc, :], "hf"
        bufs = [(wa, "wa"), (wb_, "wb")]
        for sidx in range(nsteps):
            d = 2 ** sidx
            dst, dk = bufs[sidx % 2]
            A("dve", lambda e, src=src, dst=dst, d=d: e.tensor_tensor(out=dst[:, d:NTOK], in0=src[:, d:NTOK], in1=src[:, 0:NTOK - d], op=ALU.add),
              r=[sk], w=[dk])
            A("act", lambda e, src=src, dst=dst, d=d: e.copy(out=dst[:, 0:d], in_=src[:, 0:d]), r=[sk], w=[dk])
            src, sk = dst[:], dk
        A("dve", lambda e, src=src, kc=kc, w=w: e.scalar_tensor_tensor(out=pl[:, kc, :], in0=src, scalar=1.0 / w, in1=hf[:, kc, :],
                                                                      op0=ALU.mult, op1=ALU.subtract), r=[sk, "hf"], w=[("pl", kc)])
        A("dve", lambda e, src=src, kc=kc: e.tensor_tensor(out=tmp16[:], in0=src[:, HALO:HALO + 16], in1=psc16[:, kc, :], op=ALU.mult),
          r=[sk, "psc16"], w=["tmp16"])
        A("dve", lambda e, kc=kc: e.tensor_tensor(out=pl[:, kc, HALO:HALO + 16], in0=tmp16[:], in1=hf[:, kc, HALO:HALO + 16], op=ALU.subtract),
          r=["tmp16", "hf", ("pl", kc)], w=[("pl", kc)])
    for ti, (c0, n) in enumerate(TG):
        for oc in range(8):
            g, oh = oc // 2, oc % 2
            for kh in range(2):
                A("pe", lambda e, oc=oc, g=g, oh=oh, kh=kh, c0=c0, n=n: e.matmul(
                    py[oc % 2][:, 0:n], lhsT=pw[:, g * 2 + kh, oh * 128:(oh + 1) * 128], rhs=pl[:, g * 2 + kh, c0:c0 + n],
                    start=(kh == 0), stop=(kh == 1)), r=["pw", ("pl", g * 2 + kh)], w=[("py", oc % 2)])
            A("dve", lambda e, oc=oc, n=n: e.tensor_scalar(out=ytmp[:, 0:n], in0=py[oc % 2][:, 0:n], scalar1=pbias[:, oc:oc + 1],
                                                          scalar2=pscl[:, oc:oc + 1], op0=ALU.add, op1=ALU.mult),
              r=[("py", oc % 2), "pbias", "pscl"], w=["ytmp"])
            A("dve", lambda e, oc=oc, c0=c0, n=n: e.tensor_tensor(out=xres[:, oc, c0:c0 + n], in0=xres[:, oc, c0:c0 + n], in1=ytmp[:, 0:n],
                                                                  op=ALU.add), r=["ytmp", "xres"], w=["xres"])
    A("dve", lambda e: e.tensor_scalar(out=xres[:, :, 0:HALO], in0=xres[:, :, 0:HALO], scalar1=vq[:, 0:1], scalar2=None, op0=ALU.mult),
      r=["xres"], w=["xres"])
    if "xpool" in dbgout:
        A("sp", lambda e: e.dma_start(out=dbgout["xpool"], in_=xres[:]), r=["xres"], dma="dbg")
    ph.emit()


def _phase_out(nc, st, SB, PS, dr, out, P, dbgout):
    xres, onesb, gsb, identf = P["xres"], P["onesb"], P["gsb"], P["identf"]
    gcol = gsb[:, 4, :]
    of = SB(st, "of", [128, 8, 512], F32)
    sqb = SB(st, "sqbo", [128, 8, 512], BF16)
    rs = SB(st, "rso", [128, 512], F32)
    ot = [SB(st, f"oto{i}", [128, D], F32) for i in range(2)]
    pn = PS(st, "pno", [128, 512], F32)
    pt = [PS(st, f"pto{i}", [128, 1024], F32) for i in range(2)]
    ph = Phase(nc, "O")
    A = ph.add
    for gi in range(4):
        c0 = HALO + 512 * gi
        _norm_group(A, xres, c0, 512, gcol, onesb, sqb, pn, rs, of, "of", "o")
        for tt in range(4):
            tb = tt % 2
            for kc in range(8):
                A("pe", lambda e, tt=tt, tb=tb, kc=kc: e.transpose(out=pt[tb][:, kc * 128:(kc + 1) * 128],
                                                                 in_=of[:, kc, tt * 128:(tt + 1) * 128], identity=identf[:]),
                  r=["of"], w=[("pt", tb)])
            A("act", lambda e, tb=tb: e.copy(out=ot[tb][:], in_=pt[tb][:]), r=[("pt", tb)], w=[("ot", tb)])
            row = (gi * 4 + tt) * 128
            A("sp", lambda e, tb=tb, row=row: e.dma_start(out=out[row:row + 128, :], in_=ot[tb][:]), r=[("ot", tb)], dma=("st", tb))
    ph.emit()


_CACHE = {}


def kernel(**inputs):
    inp = {k: np.asarray(v) for k, v in inputs.items()}
    if "nc" not in _CACHE:
        _CACHE["nc"] = build()
    nc = _CACHE["nc"]
    sh = _shared_inputs(inp)
    maps = []
    for c in range(8):
        m = dict(sh)
        m.update(_core_inputs(inp, c))
        maps.append(m)
    res = run_bass_kernel_spmd(nc, maps, core_ids=list(range(8)))
    outp = np.zeros((2, T, D), np.float32)
    for c in range(8):
        b, j = c // 4, c % 4
        outp[b, 2048 * j:2048 * (j + 1)] = np.asarray(res.results[c]["out"], dtype=np.float32)
    return outp
```

```python
import contextlib
import numpy as np
import ml_dtypes
import concourse.bass as bass
import concourse.mybir as mybir
from concourse.bass_utils import run_bass_kernel_spmd

F32 = mybir.dt.float32
BF16 = mybir.dt.bfloat16
AF = mybir.ActivationFunctionType
ALU = mybir.AluOpType
AX = mybir.AxisListType
NPBF = ml_dtypes.bfloat16

D = 1024
KC = 8
T = 8192
NSLOT = 64
OWN0 = 48
NOWN = 16
HALO = 20
NTOK = HALO + 128 * NOWN
DFF = 2816
NFC = 22
EPS = 1e-6
SCALE = 0.125
NEGB = -30000.0
BLK = [(47, 108, HALO, 0)] + [(OWN0 + m, 0, 128, HALO + 128 * m) for m in range(NOWN)]
NBLK = len(BLK)
TG = [(0, HALO)] + [(HALO + 512 * i, 512) for i in range(4)]
FPASS = [list(range(0, 6)), list(range(6, 12)), list(range(12, 17)), list(range(17, 22))]

ENGS = ("pe", "act", "dve", "pool", "sp")


class Op:
    __slots__ = ("eng", "fn", "dma", "waits", "signal", "sigidx", "idx")

    def __init__(self, eng, fn, dma):
        self.eng = eng
        self.fn = fn
        self.dma = dma
        self.waits = []
        self.signal = False
        self.sigidx = None


class Phase:
    def __init__(self, nc, name):
        self.nc = nc
        self.name = name
        self.ops = {e: [] for e in ENGS}
        self.lastw = {}
        self.readers = {}
        self.dma_count = {}
        self.n = 0

    def add(self, eng, fn, r=(), w=(), dma=None):
        op = Op(eng, fn, dma)
        op.idx = self.n
        self.n += 1
        deps = []
        for k in r:
            x = self.lastw.get(k)
            if x is not None:
                deps.append(x)
        for k in w:
            x = self.lastw.get(k)
            if x is not None:
                deps.append(x)
            deps.extend(self.readers.get(k, {}).values())
        seen = set()
        for d in deps:
            if d is op or id(d) in seen:
                continue
            seen.add(id(d))
            if d.dma is not None:
                op.waits.append(("dma", d.dma, 16 * self.dma_count[d.dma]))
            else:
                if d.eng == "pe" and eng == "pe" and dma is None:
                    continue
                d.signal = True
                op.waits.append(("eng", d, None))
        rk = eng if dma is None else ("dma", op.idx)
        for k in r:
            self.readers.setdefault(k, {})[rk] = op
        for k in w:
            self.lastw[k] = op
            self.readers[k] = {}
        if dma is not None:
            self.dma_count[dma] = self.dma_count.get(dma, 0) + 1
        self.ops[eng].append(op)
        return op

    def emit(self):
        nc = self.nc
        for e in ENGS:
            k = 0
            for op in self.ops[e]:
                if op.dma is None and op.signal:
                    k += 1
                    op.sigidx = k
        with contextlib.ExitStack() as st:
            esem = {e: st.enter_context(nc.semaphore(f"{self.name}_s_{e}")) for e in ENGS}
            dsem = {k: st.enter_context(nc.semaphore(f"{self.name}_d_{i}"))
                    for i, k in enumerate(self.dma_count)}
            block = st.enter_context(nc.Block())
            final_dma = dict(self.dma_count)

            def run(e, eng):
                seen = {}
                for op in self.ops[e]:
                    for kind, obj, val in op.waits:
                        if kind == "dma":
                            sem, v = dsem[obj], val
                        else:
                            sem, v = esem[obj.eng], obj.sigidx
                        if seen.get(id(sem), 0) >= v:
                            continue
                        seen[id(sem)] = v
                        eng.wait_ge(sem, v)
                    inst = op.fn(eng)
                    if op.dma is not None:
                        inst.then_inc(dsem[op.dma], 16)
                    elif op.signal:
                        inst.then_inc(esem[e], 1)
                mine = []
                for op in self.ops[e]:
                    if op.dma is not None and op.dma not in mine:
                        mine.append(op.dma)
                for k in mine:
                    v = 16 * final_dma[k]
                    if seen.get(id(dsem[k]), 0) < v:
                        eng.wait_ge(dsem[k], v)

            block.tensor(lambda eng: run("pe", eng))
            block.scalar(lambda eng: run("act", eng))
            block.vector(lambda eng: run("dve", eng))
            block.gpsimd(lambda eng: run("pool", eng))
            block.sync(lambda eng: run("sp", eng))


def _c(a, dt=np.float32):
    return np.ascontiguousarray(a).astype(dt, copy=False)


def _pk(v):
    return _c(np.asarray(v).reshape(-1, 128).T)


def _rope_tab(pos):
    inv = (1.0 / (10000.0 ** (np.arange(0, 64, 2, dtype=np.float32) / np.float32(64)))).astype(np.float32)
    ang = pos.astype(np.float32)[:, None] * inv[None, :]
    c = np.cos(ang).astype(np.float32)
    s = np.sin(ang).astype(np.float32)
    idx = np.arange(128) % 32
    return _c(c[:, idx].T), _c(s[:, idx].T)


def _shared_inputs(inp):
    sh = {}
    w_in = np.asarray(inp["nsa_w_in"])
    sh["wq"] = _c(w_in[:, :1024].reshape(1024, 2, 8, 64).transpose(0, 2, 1, 3).reshape(1024, 1024))
    sh["wkv"] = _c(w_in[:, 1024:1792])
    sh["wg"] = _c(w_in[:, 1792:1840].reshape(1024, 2, 8, 3).transpose(0, 3, 1, 2).reshape(1024, 48))
    sh["wout"] = _c(np.asarray(inp["nsa_w_out"]).reshape(2, 8, 64, 1024).transpose(1, 0, 2, 3).reshape(1024, 1024))
    for x in ("k", "v"):
        sh[f"c{x}_w1"] = _c(inp[f"cmp_{x}_w1"])
        sh[f"c{x}_w2"] = _c(inp[f"cmp_{x}_w2"])
        pos = np.asarray(inp[f"cmp_{x}_pos"])
        pt = np.zeros((128, 32, 2), np.float32)
        pt[0:64, :, 0] = pos.T
        pt[64:128, :, 0] = pos.T
        sh[f"c{x}_posT"] = _c(pt)
        sh[f"c{x}_b1"] = _c(np.asarray(inp[f"cmp_{x}_b1"]).reshape(2, 128).T)
        sh[f"c{x}_b2"] = _c(np.tile(np.asarray(inp[f"cmp_{x}_b2"]), 2)[None, :])
    for i in (0, 1):
        sh[f"gmix{i}"] = _pk(inp[f"norm_mix_{i}"])
        sh[f"gffn{i}"] = _pk(inp[f"norm_ffn_{i}"])
        sh[f"wup{i}"] = _c(inp[f"ffn_up_{i}"])
        sh[f"wdn{i}"] = _c(inp[f"ffn_down_{i}"])
        cw = np.asarray(inp[f"ffn_conv_w_{i}"])
        sh[f"cw{i}"] = _c(cw.reshape(3, 44, 128).transpose(2, 0, 1))
        sh[f"cb{i}"] = _c(np.asarray(inp[f"ffn_conv_b_{i}"]).reshape(44, 128).T)
    sh["gfin"] = _pk(inp["norm_final"])
    sh["poolw"] = _c(np.asarray(inp["pool_w"]).reshape(4, 2, 128, 256).transpose(2, 0, 1, 3).reshape(128, 8, 256))
    sh["poolb"] = _pk(np.asarray(inp["pool_b"]).reshape(-1))
    sh["pools"] = _pk(inp["pool_scale"])
    sh["identb"] = _c(np.eye(128), NPBF)
    sh["identf"] = _c(np.eye(128))
    ef = np.zeros((128, 16, 128), np.float32)
    for p in range(128):
        for v in range(16):
            if p % 32 == 2 * v:
                ef[p, v, 0:64] = 1
            if p % 32 == 2 * v + 1:
                ef[p, v, 64:128] = 1
    sh["ef"] = _c(ef, NPBF)
    n = np.arange(512)[:, None]
    s = np.arange(128)[None, :]
    lo = np.maximum(n * 16, s * 64)
    hi = np.minimum(n * 16 + 32, (s + 1) * 64)
    ov = np.clip(hi - lo, 0, None) / 32.0
    ova = np.ones((512, 129), np.float32)
    ova[:, :128] = ov
    sh["ovl"] = _c(ova.reshape(4, 128, 129).transpose(1, 0, 2), NPBF)
    mk = np.zeros((NBLK, 128, 6, 128), np.float32)
    k = np.arange(128)[:, None]
    q = np.arange(128)[None, :]
    for bi, (S, off, nq, col0) in enumerate(BLK):
        mk[bi, :, 0, :] = (k <= off + q)
        mk[bi, :, 1, :] = (k > off + q)
        tq = 128 * S + off + q
        for c in range(4):
            mk[bi, :, 2 + c, :] = (16 * (c * 128 + k) + 31 <= tq)
    sh["mk"] = _c(mk, NPBF)
    return sh


def _core_inputs(inp, core):
    b, j = core // 4, core % 4
    SH = OWN0 - 16 * j
    x = np.asarray(inp["x"])[b]
    ci = {}
    xs = np.zeros((T, D), np.float32)
    nreal = (NSLOT - SH) * 128
    xs[SH * 128:] = x[:nreal]
    ci["xs"] = xs
    tp = np.arange(T)
    pos = np.maximum(tp - 128 * SH, 0)
    ci["cosK"], ci["sinK"] = _rope_tab(pos)
    qpos = np.zeros((NBLK, 128), np.int64)
    bonus = np.zeros((NBLK, 128, 128), np.float32)
    sblk = np.arange(128)[None, :]
    for bi, (S, off, nq, col0) in enumerate(BLK):
        tq = 128 * S + off + np.arange(128)
        qpos[bi] = np.maximum(tq - 128 * SH, 0)
        cur = (tq // 64)[:, None]
        forced = (sblk == cur) | (sblk == cur - 1) | (sblk == 2 * SH)
        bonus[bi] = np.where(sblk <= cur, 1e4 * forced, -1e30)
    cq, sq = _rope_tab(qpos.reshape(-1))
    ci["cosQ"], ci["sinQ"] = cq, sq
    ci["bonus"] = _c(bonus.transpose(1, 0, 2))
    nn = np.arange(512)
    cend = np.maximum(16 * (nn - 8 * SH) + 31, 0)
    ci["cosC"], ci["sinC"] = _rope_tab(cend)
    validc = (nn >= 8 * SH) & (nn <= 510)
    ci["biasC"] = _c(np.where(validc, 0.0, NEGB).reshape(4, 128).T)
    vk = (np.arange(NSLOT) >= SH).astype(np.float32)
    ci["validk"] = _c(np.tile(vk[None, :], (128, 1)), NPBF)
    ci["vq"] = _c(np.full((128, 1), 0.0 if j == 0 else 1.0))
    ps = np.zeros((128, 8, 16), np.float32)
    for kc in range(8):
        w = [2, 4, 8, 16][kc // 2]
        for qq in range(16):
            t = 128 * 16 * j + qq
            ps[:, kc, qq] = 1.0 / min(t + 1, w)
    ci["pscale"] = ps
    return ci


def build(dbg=()):
    nc = bass.Bass("TRN2", target_bir_lowering=False)
    dr = {}

    def din(name, shape, dt=F32):
        dr[name] = nc.dram_tensor(name, list(shape), dt, kind="ExternalInput").ap()
        return dr[name]

    din("xs", [T, D])
    din("cosK", [128, T]); din("sinK", [128, T])
    din("cosQ", [128, NBLK * 128]); din("sinQ", [128, NBLK * 128])
    din("bonus", [128, NBLK, 128])
    din("cosC", [128, 512]); din("sinC", [128, 512])
    din("biasC", [128, 4]); din("validk", [128, NSLOT], BF16); din("vq", [128, 1]); din("pscale", [128, 8, 16])
    din("wq", [D, D]); din("wkv", [D, 768]); din("wg", [D, 48]); din("wout", [D, D])
    for x in ("k", "v"):
        din(f"c{x}_w1", [2048, 256]); din(f"c{x}_w2", [256, 64]); din(f"c{x}_posT", [128, 32, 2])
        din(f"c{x}_b1", [128, 2]); din(f"c{x}_b2", [1, 128])
    for i in (0, 1):
        din(f"gmix{i}", [128, 8]); din(f"gffn{i}", [128, 8])
        din(f"wup{i}", [D, 2 * DFF]); din(f"wdn{i}", [DFF, D])
        din(f"cw{i}", [128, 3, 44]); din(f"cb{i}", [128, 44])
    din("gfin", [128, 8]); din("poolw", [128, 8, 256]); din("poolb", [128, 8]); din("pools", [128, 8])
    din("identb", [128, 128], BF16); din("identf", [128, 128])
    din("ef", [128, 16, 128], BF16); din("ovl", [128, 4, 129], BF16); din("mk", [NBLK, 128, 6, 128], BF16)
    out = nc.dram_tensor("out", [128 * NOWN, D], F32, kind="ExternalOutput").ap()
    gscr = nc.dram_tensor("gscr", [NBLK, 48, 128], F32).ap()
    dbgout = {}
    for name, shape, dt in dbg:
        if shape is None:
            dbgout[name] = dt
            continue
        dbgout[name] = nc.dram_tensor("dbg_" + name, list(shape), dt, kind="ExternalOutput").ap()

    with contextlib.ExitStack() as top:
        def SB(st, name, shape, dt):
            return st.enter_context(nc.sbuf_tensor("s_" + name, list(shape), dt))

        def PS(st, name, shape, dt):
            return st.enter_context(nc.psum_tensor("p_" + name, list(shape), dt))

        identb = SB(top, "identb", [128, 128], BF16)
        identf = SB(top, "identf", [128, 128], F32)
        ones32 = SB(top, "ones32", [128, 512], F32)
        onesb = SB(top, "onesb", [128, 128], BF16)
        gsb = SB(top, "gsb", [128, 5, 8], F32)
        vq = SB(top, "vq", [128, 1], F32)
        ph = Phase(nc, "K")
        ph.add("sp", lambda e: e.dma_start(out=identb[:], in_=dr["identb"]), w=["identb"], dma="c")
        ph.add("sp", lambda e: e.dma_start(out=identf[:], in_=dr["identf"]), w=["identf"], dma="c")
        for i, nm in enumerate(("gmix0", "gffn0", "gmix1", "gffn1", "gfin")):
            ph.add("sp", lambda e, i=i, nm=nm: e.dma_start(out=gsb[:, i, :], in_=dr[nm]), w=["gsb"], dma="c")
        ph.add("sp", lambda e: e.dma_start(out=vq[:], in_=dr["vq"]), w=["vq"], dma="c")
        ph.add("dve", lambda e: e.memset(ones32[:], 1.0), w=["ones32"])
        ph.add("dve", lambda e: e.memset(onesb[:], 1.0), w=["onesb"])
        ph.emit()

        oscr = nc.dram_tensor("oscr", [128, 8, NTOK], BF16).ap()
        with contextlib.ExitStack() as att:
            KsT = SB(att, "KsT", [128, T], BF16)
            KwT = SB(att, "KwT", [128, 24 * 128], BF16)
            Vs = SB(att, "Vs", [128, NSLOT, 194], BF16)
            Vw = SB(att, "Vw", [128, 24, 194], BF16)
            kcT = SB(att, "kcT", [128, 512], BF16)
            RCv = SB(att, "RCv", [128, 4, 194], BF16)
            biasC = SB(att, "biasC", [128, 4], F32)
            P = dict(KsT=KsT, KwT=KwT, Vs=Vs, Vw=Vw, kcT=kcT, RCv=RCv, biasC=biasC,
                     identb=identb, ones32=ones32, gsb=gsb, vq=vq)
            with contextlib.ExitStack() as st:
                _phase_A(nc, st, SB, PS, dr, P, dbgout)
            if "stopA" not in dbgout:
                with contextlib.ExitStack() as st:
                    _phase_B(nc, st, SB, PS, dr, gscr, oscr, P, dbgout)
        if "stopA" in dbgout or "stopB" in dbgout:
            return nc
        with contextlib.ExitStack() as rest:
            xres = SB(rest, "xres", [128, 8, NTOK], F32)
            P = dict(xres=xres, onesb=onesb, gsb=gsb, vq=vq, identf=identf, identb=identb)
            with contextlib.ExitStack() as st:
                _phase_C(nc, st, SB, PS, dr, oscr, P, dbgout)
            with contextlib.ExitStack() as st:
                _phase_ffn(nc, st, SB, PS, dr, 0, P, dbgout)
            with contextlib.ExitStack() as st:
                _phase_pool(nc, st, SB, PS, dr, P, dbgout)
            with contextlib.ExitStack() as st:
                _phase_ffn(nc, st, SB, PS, dr, 1, P, dbgout)
            with contextlib.ExitStack() as st:
                _phase_out(nc, st, SB, PS, dr, out, P, dbgout)
    return nc


def _phase_A(nc, st, SB, PS, dr, P, dbgout):
    KsT, KwT, Vs, Vw, kcT, RCv = P["KsT"], P["KwT"], P["Vs"], P["Vw"], P["kcT"], P["RCv"]
    identb, ones32, gsb = P["identb"], P["ones32"], P["gsb"]
    rawK = SB(st, "rawK", [128, T], BF16)
    rawV = SB(st, "rawV", [128, T], BF16)
    t1a = SB(st, "t1a", [128, 512], F32)
    t1b = SB(st, "t1b", [128, 512], F32)
    ist = contextlib.ExitStack()
    WA = SB(ist, "WA", [128, 8, 768], BF16)
    WAr = SB(ist, "WAr", [128, 8, 256], BF16)
    validk = SB(ist, "validk", [128, NSLOT], BF16)
    xt = [SB(ist, f"xt{i}", [128, D], F32) for i in range(3)]
    junk = SB(ist, "junk", [128, D], BF16)
    ssq = [SB(ist, f"ssq{i}", [128, 1], F32) for i in range(3)]
    rstd = [SB(ist, f"rstd{i}", [128, 1], F32) for i in range(3)]
    xn = [SB(ist, f"xn{i}", [128, D], BF16) for i in range(2)]
    hT = [SB(ist, f"hT{i}", [128, 8, 512], BF16) for i in range(2)]
    cs = [SB(ist, f"cs{i}", [128, 512], F32) for i in range(2)]
    sn = [SB(ist, f"sn{i}", [128, 512], F32) for i in range(2)]
    pst = contextlib.ExitStack()
    tp = [PS(pst, f"tp{i}", [128, D], BF16) for i in range(2)]
    pk = [PS(pst, f"pk{i}", [128, 512], F32) for i in range(4)]
    pv = PS(pst, "pv", [128, 512], F32)

    ph = Phase(nc, "A")
    A = ph.add
    A("pool", lambda e: e.dma_start(out=WA[:], in_=dr["wkv"].rearrange("(k p) n -> p k n", p=128)), w=["WA"], dma="w")
    A("sp", lambda e: e.dma_start(out=validk[:], in_=dr["validk"]), w=["validk"], dma="c")
    A("sp", lambda e: e.dma_start(out=P["biasC"][:], in_=dr["biasC"]), w=["biasC"], dma="c")
    for i, c0 in enumerate((256, 512)):
        for g in range(2):
            A("dve", lambda e, i=i, c0=c0, g=g: e.tensor_scalar(
                out=WAr[:, :, i * 128 + g * 64: i * 128 + g * 64 + 32], in0=WA[:, :, c0 + g * 64 + 32: c0 + g * 64 + 64],
                scalar1=-1.0, scalar2=None, op0=ALU.mult), r=["WA"], w=[("WAr", i, g, 0)])
            A("dve", lambda e, i=i, c0=c0, g=g: e.tensor_copy(
                out=WAr[:, :, i * 128 + g * 64 + 32: i * 128 + g * 64 + 64], in_=WA[:, :, c0 + g * 64: c0 + g * 64 + 32]),
              r=["WA"], w=[("WAr", i, g, 1)])
    WArk = [("WAr", i, g, h) for i in range(2) for g in range(2) for h in range(2)]
    A("pool", lambda e: e.memset(Vs[:], 0.0), w=["Vs0"])
    A("pool", lambda e: e.memset(Vw[:], 0.0), w=["Vw0"])
    A("pool", lambda e: e.memset(RCv[:], 0.0), w=["RCv"])
    A("dve", lambda e: e.tensor_copy(out=Vs[:, :, 64:65], in_=validk[:].unsqueeze(2)), r=["validk", "Vs0"], w=["Vs1"])
    A("dve", lambda e: e.tensor_copy(out=Vs[:, :, 66:67], in_=validk[:].unsqueeze(2)), r=["validk", "Vs0"], w=["Vs2"])
    A("dve", lambda e: e.tensor_copy(out=Vw[:, :, 64:65], in_=validk[:, 40:64].unsqueeze(2)), r=["validk", "Vw0"], w=["Vw1"])
    A("dve", lambda e: e.tensor_copy(out=Vw[:, :, 66:67], in_=validk[:, 40:64].unsqueeze(2)), r=["validk", "Vw0"], w=["Vw2"])

    for G in range(16):
        hb = hT[G % 2]
        hk = ("hT", G % 2)
        A("sp", lambda e, G=G: e.dma_start(out=cs[G % 2][:], in_=dr["cosK"][:, G * 512:(G + 1) * 512]),
          w=[("cs", G % 2)], dma=("cs", G % 2))
        A("sp", lambda e, G=G: e.dma_start(out=sn[G % 2][:], in_=dr["sinK"][:, G * 512:(G + 1) * 512]),
          w=[("sn", G % 2)], dma=("cs", G % 2))
        for si in range(4):
            s = 4 * G + si
            b3, b2 = s % 3, s % 2
            A("sp", lambda e, s=s, b3=b3: e.dma_start(out=xt[b3][:], in_=dr["xs"][s * 128:(s + 1) * 128, :]),
              w=[("xt", b3)], dma=("x", b3))
            A("act", lambda e, b3=b3: e.activation(out=junk[:], in_=xt[b3][:], func=AF.Square, accum_out=ssq[b3][:]),
              r=[("xt", b3)], w=["junk", ("ssq", b3)])
            A("act", lambda e, b3=b3: e.activation(out=rstd[b3][:], in_=ssq[b3][:], func=AF.Sqrt, bias=EPS, scale=1.0 / D),
              r=[("ssq", b3)], w=[("rstd", b3)])
            A("dve", lambda e, b3=b3: e.reciprocal(out=rstd[b3][:], in_=rstd[b3][:]), r=[("rstd", b3)], w=[("rstd", b3)])
            A("dve", lambda e, b3=b3, b2=b2: e.tensor_scalar(out=xn[b2][:], in0=xt[b3][:], scalar1=rstd[b3][:], scalar2=None,
                                                           op0=ALU.mult), r=[("xt", b3), ("rstd", b3)], w=[("xn", b2)])
            for kc in range(8):
                A("pe", lambda e, kc=kc, b2=b2: e.transpose(out=tp[b2][:, kc * 128:(kc + 1) * 128],
                                                            in_=xn[b2][:, kc * 128:(kc + 1) * 128], identity=identb[:]),
                  r=[("xn", b2)], w=[("tp", b2)])
            A("dve", lambda e, b2=b2, hb=hb, si=si: e.tensor_tensor(
                out=hb[:, :, si * 128:(si + 1) * 128], in0=tp[b2][:].rearrange("p (k t) -> p k t", k=8),
                in1=gsb[:, 0, :].unsqueeze(2).to_broadcast([128, 8, 128]), op=ALU.mult),
              r=[("tp", b2)], w=[hk + (si,)])
        hks = [hk + (si,) for si in range(4)]

        def proj(W, c0, bank, bk, wk, hb=hb, hks=hks):
            for kc in range(8):
                A("pe", lambda e, kc=kc, W=W, c0=c0, bank=bank, hb=hb: e.matmul(bank[:], lhsT=W[:, kc, c0:c0 + 128], rhs=hb[:, kc, :],
                                                                          start=(kc == 0), stop=(kc == 7)),
                  r=hks + wk, w=[bk])
        proj(WA, 0, pk[0], "pk0", ["WA"])
        proj(WA, 128, pk[1], "pk1", ["WA"])
        proj(WA, 256, pk[2], "pk2", ["WA"])
        proj(WAr, 0, pk[3], "pk3", WArk)
        A("act", lambda e, G=G: e.copy(out=rawK[:, G * 512:(G + 1) * 512], in_=pk[0][:]), r=["pk0"], w=[("rawK", G)])
        A("act", lambda e, G=G: e.copy(out=rawV[:, G * 512:(G + 1) * 512], in_=pk[1][:]), r=["pk1"], w=[("rawV", G)])

        def ropeevac(b0, b0k, b1, b1k, dst, dk, G=G):
            A("dve", lambda e: e.tensor_tensor(out=t1a[:], in0=b0[:], in1=cs[G % 2][:], op=ALU.mult),
              r=[b0k, ("cs", G % 2)], w=["t1a"])
            A("dve", lambda e: e.tensor_tensor(out=t1b[:], in0=b1[:], in1=sn[G % 2][:], op=ALU.mult),
              r=[b1k, ("sn", G % 2)], w=["t1b"])
            A("dve", lambda e: e.tensor_tensor(out=dst, in0=t1a[:], in1=t1b[:], op=ALU.add), r=["t1a", "t1b"], w=[dk])
        ropeevac(pk[2], "pk2", pk[3], "pk3", KsT[:, G * 512:(G + 1) * 512], ("KsT", G))
        if G >= 10:
            proj(WA, 512, pk[0], "pk0", ["WA"])
            proj(WAr, 128, pk[1], "pk1", WArk)
            ropeevac(pk[0], "pk0", pk[1], "pk1", KwT[:, (G - 10) * 512:(G - 9) * 512], ("KwT", G))
        for si in range(4):
            s = 4 * G + si
            nv = 2 if G >= 10 else 1
            for vi in range(nv):
                c0 = 384 if vi == 0 else 640
                for kc in range(8):
                    A("pe", lambda e, kc=kc, si=si, vi=vi, c0=c0, hb=hb: e.matmul(
                        pv[:, vi * 128:(vi + 1) * 128], lhsT=hb[:, kc, si * 128:(si + 1) * 128], rhs=WA[:, kc, c0:c0 + 128],
                        start=(kc == 0), stop=(kc == 7)), r=[hk + (si,), "WA"], w=["pv"])
            A("act", lambda e, s=s: e.copy(out=Vs[:, s, 0:64], in_=pv[:, 0:64]), r=["pv", "Vs0"], w=[("Vs", s, 0)])
            A("act", lambda e, s=s: e.copy(out=Vs[:, s, 130:194], in_=pv[:, 64:128]), r=["pv", "Vs0"], w=[("Vs", s, 1)])
            if G >= 10:
                A("act", lambda e, s=s: e.copy(out=Vw[:, s - 40, 0:64], in_=pv[:, 128:192]), r=["pv", "Vw0"], w=[("Vw", s, 0)])
                A("act", lambda e, s=s: e.copy(out=Vw[:, s - 40, 130:194], in_=pv[:, 192:256]), r=["pv", "Vw0"], w=[("Vw", s, 1)])
    if "KsT" in dbgout:
        A("sp", lambda e: e.dma_start(out=dbgout["KsT"], in_=KsT[:]), r=[("KsT", G) for G in range(16)], dma="dbg")
        A("sp", lambda e: e.dma_start(out=dbgout["Vs"], in_=Vs[:]), r=[("Vs", s, i) for s in range(64) for i in range(2)] + ["Vs1", "Vs2"], dma="dbg")
        A("sp", lambda e: e.dma_start(out=dbgout["KwT"], in_=KwT[:]), r=[("KwT", G) for G in range(10, 16)], dma="dbg")
    ph.emit()
    pst.close()
    ist.close()
    _phase_A2(nc, st, SB, PS, dr, P, dbgout, rawK, rawV, t1a, t1b)


def _phase_A2(nc, st0, SB, PS, dr, P, dbgout, rawK, rawV, t1a, t1b):
    kcT, RCv, ones32 = P["kcT"], P["RCv"], P["ones32"]
    with contextlib.ExitStack() as st:
        w1 = [SB(st, f"w1{x}", [128, 32, 256], BF16) for x in range(2)]
        w2 = [SB(st, f"w2{x}", [128, 2, 64], BF16) for x in range(2)]
        posT = [SB(st, f"posT{x}", [128, 32, 2], BF16) for x in range(2)]
        b1 = [SB(st, f"b1{x}", [128, 2], F32) for x in range(2)]
        b1e = [SB(st, f"b1e{x}", [128, 2], F32) for x in range(2)]
        b2row = [SB(st, f"b2row{x}", [1, 128], F32) for x in range(2)]
        b2rrow = SB(st, "b2rrow", [1, 128], F32)
        w2pad = [SB(st, f"w2pad{g}", [128, 2, 128], BF16) for g in range(2)]
        w2padr = [SB(st, f"w2padr{g}", [128, 2, 128], BF16) for g in range(2)]
        hid = [[SB(st, f"hid{x}{g}", [128, 2, 512], BF16) for g in range(2)] for x in range(2)]
        u = SB(st, "u", [128, 512], F32)
        u2 = SB(st, "u2", [128, 512], F32)
        th = SB(st, "th", [128, 512], F32)
        csC = SB(st, "csC", [128, 512], F32)
        snC = SB(st, "snC", [128, 512], F32)
        pb = PS(st, "pb", [128, 512], F32)
        ph_ = [PS(st, f"ph{i}", [128, 512], F32) for i in range(2)]
        po = [PS(st, f"po{i}", [128, 512], F32) for i in range(2)]
        ph = Phase(nc, "A2")
        A = ph.add
        NCMP = 511
        for x, nm in enumerate(("k", "v")):
            src = dr[f"c{nm}_w1"].rearrange("(p d) h -> d p h", d=64)
            A("pool", lambda e, x=x, src=src: e.dma_start(out=w1[x][0:64], in_=src), w=[("w1", x, 0)], dma="w")
            A("pool", lambda e, x=x, src=src: e.dma_start(out=w1[x][64:128], in_=src), w=[("w1", x, 1)], dma="w")
            A("pool", lambda e, x=x, nm=nm: e.dma_start(out=w2[x][:], in_=dr[f"c{nm}_w2"].rearrange("(k p) d -> p k d", p=128)),
              w=[("w2", x)], dma="w")
            A("pool", lambda e, x=x, nm=nm: e.dma_start(out=posT[x][:], in_=dr[f"c{nm}_posT"]), w=[("posT", x)], dma="w")
            A("sp", lambda e, x=x, nm=nm: e.dma_start(out=b1[x][:], in_=dr[f"c{nm}_b1"]), w=[("b1", x)], dma="c")
            A("sp", lambda e, x=x, nm=nm: e.dma_start(out=b2row[x][:], in_=dr[f"c{nm}_b2"]), w=[("b2row", x)], dma="c")
        A("sp", lambda e: e.dma_start(out=csC[:], in_=dr["cosC"]), w=["csC"], dma="c")
        A("sp", lambda e: e.dma_start(out=snC[:], in_=dr["sinC"]), w=["snC"], dma="c")
        for g in range(2):
            A("dve", lambda e, g=g: e.memset(w2pad[g][:], 0.0), w=[("w2pad", g)])
            A("dve", lambda e, g=g: e.memset(w2padr[g][:], 0.0), w=[("w2padr", g)])
            A("dve", lambda e, g=g: e.tensor_copy(out=w2pad[g][:, :, g * 64:(g + 1) * 64], in_=w2[0][:]),
              r=[("w2", 0)], w=[("w2pad", g)])
            A("dve", lambda e, g=g: e.tensor_scalar(out=w2padr[g][:, :, g * 64:g * 64 + 32], in0=w2[0][:, :, 32:64],
                                                     scalar1=-1.0, scalar2=None, op0=ALU.mult), r=[("w2", 0)], w=[("w2padr", g)])
            A("dve", lambda e, g=g: e.tensor_copy(out=w2padr[g][:, :, g * 64 + 32:g * 64 + 64], in_=w2[0][:, :, 0:32]),
              r=[("w2", 0)], w=[("w2padr", g)])
            A("dve", lambda e, g=g: e.tensor_scalar(out=b2rrow[0:1, g * 64:g * 64 + 32], in0=b2row[0][0:1, g * 64 + 32:g * 64 + 64],
                                                     scalar1=-1.0, scalar2=None, op0=ALU.mult), r=[("b2row", 0)], w=["b2rrow"])
            A("dve", lambda e, g=g: e.tensor_copy(out=b2rrow[0:1, g * 64 + 32:g * 64 + 64], in_=b2row[0][0:1, g * 64:g * 64 + 32]),
              r=[("b2row", 0)], w=["b2rrow"])
        for x in range(2):
            raw = rawK if x == 0 else rawV
            for half in range(2):
                for p in range(32):
                    A("pe", lambda e, x=x, half=half, p=p: e.matmul(
                        pb[:, half * 2:half * 2 + 2], lhsT=w1[x][0:64, p, half * 128:(half + 1) * 128], rhs=posT[x][0:64, p, :],
                        start=(p == 0), stop=(p == 31)), r=[("w1", x, 0), ("posT", x)], w=["pb"])
            A("dve", lambda e, x=x: e.tensor_tensor(out=b1e[x][:], in0=pb[:, 0:4:2], in1=b1[x][:], op=ALU.add),
              r=["pb", ("b1", x)], w=[("b1e", x)])
            for g in range(2):
                for half in range(2):
                    bank = ph_[half]
                    for p in range(32):
                        A("pe", lambda e, x=x, g=g, half=half, p=p, raw=raw, bank=bank: e.matmul(
                            bank[:, 0:NCMP], lhsT=w1[x][64 * g:64 * g + 64, p, half * 128:(half + 1) * 128],
                            rhs=raw[64 * g:64 * g + 64, p:p + 16 * (NCMP - 1) + 1:16],
                            start=(p == 0), stop=(p == 31)), r=[("w1", x, g)], w=[("ph", half)])
                    A("act", lambda e, x=x, half=half, bank=bank: e.activation(out=u[:, 0:NCMP], in_=bank[:, 0:NCMP], func=AF.Identity,
                                                                               bias=b1e[x][:, half:half + 1], scale=1.0),
                      r=[("ph", half), ("b1e", x)], w=["u"])
                    A("dve", lambda e: e.tensor_tensor(out=u2[:, 0:NCMP], in0=u[:, 0:NCMP], in1=u[:, 0:NCMP], op=ALU.mult), r=["u"], w=["u2"])
                    A("dve", lambda e: e.tensor_scalar(out=u2[:, 0:NCMP], in0=u2[:, 0:NCMP], scalar1=0.044715, scalar2=1.0,
                                                        op0=ALU.mult, op1=ALU.add), r=["u2"], w=["u2"])
                    A("dve", lambda e: e.tensor_tensor(out=u2[:, 0:NCMP], in0=u2[:, 0:NCMP], in1=u[:, 0:NCMP], op=ALU.mult), r=["u2", "u"], w=["u2"])
                    A("act", lambda e: e.activation(out=th[:, 0:NCMP], in_=u2[:, 0:NCMP], func=AF.Tanh, scale=0.7978845608028654),
                      r=["u2"], w=["th"])
                    A("dve", lambda e: e.tensor_scalar(out=th[:, 0:NCMP], in0=th[:, 0:NCMP], scalar1=0.5, scalar2=0.5,
                                                        op0=ALU.mult, op1=ALU.add), r=["th"], w=["th"])
                    A("dve", lambda e, x=x, g=g, half=half: e.tensor_tensor(out=hid[x][g][:, half, 0:NCMP], in0=th[:, 0:NCMP],
                                                                             in1=u[:, 0:NCMP], op=ALU.mult),
                      r=["th", "u"], w=[("hid", x, g, half)])
        for r_, (pads, brow, bank, bk) in enumerate(((w2pad, b2row[0], po[0], "po0"), (w2padr, b2rrow, po[1], "po1"))):
            first = True
            for g in range(2):
                for half in range(2):
                    A("pe", lambda e, g=g, half=half, pads=pads, bank=bank, first=first: e.matmul(
                        bank[:, 0:NCMP], lhsT=pads[g][:, half, :], rhs=hid[0][g][:, half, 0:NCMP], start=first, stop=False),
                      r=[("hid", 0, g, half), ("w2pad", g), ("w2padr", g)], w=[bk])
                    first = False
            A("pe", lambda e, brow=brow, bank=bank: e.matmul(bank[:, 0:NCMP], lhsT=brow[0:1, 0:128], rhs=ones32[0:1, 0:NCMP],
                                                              start=False, stop=True), r=[("b2row", 0), "b2rrow"], w=[bk])
        A("dve", lambda e: e.tensor_tensor(out=t1a[:, 0:NCMP], in0=po[0][:, 0:NCMP], in1=csC[:, 0:NCMP], op=ALU.mult),
          r=["po0", "csC"], w=["t1a"])
        A("dve", lambda e: e.tensor_tensor(out=t1b[:, 0:NCMP], in0=po[1][:, 0:NCMP], in1=snC[:, 0:NCMP], op=ALU.mult),
          r=["po1", "snC"], w=["t1b"])
        A("dve", lambda e: e.memset(kcT[:], 0.0), w=["kcT"])
        A("dve", lambda e: e.tensor_tensor(out=kcT[:, 0:NCMP], in0=t1a[:, 0:NCMP], in1=t1b[:, 0:NCMP], op=ALU.add),
          r=["t1a", "t1b"], w=["kcT"])
        for c in range(4):
            n = 128 if c < 3 else NCMP - 384
            bank = po[c % 2]
            bk = f"po{c % 2}"
            for g in range(2):
                for half in range(2):
                    A("pe", lambda e, c=c, n=n, g=g, half=half, bank=bank: e.matmul(
                        bank[0:n, g * 64:(g + 1) * 64], lhsT=hid[1][g][:, half, c * 128:c * 128 + n], rhs=w2[1][:, half, :],
                        start=(half == 0), stop=False), r=[("hid", 1, g, half), ("w2", 1)], w=[bk])
                A("pe", lambda e, n=n, g=g, bank=bank: e.matmul(
                    bank[0:n, g * 64:(g + 1) * 64], lhsT=ones32[0:1, 0:n], rhs=b2row[1][0:1, g * 64:(g + 1) * 64],
                    start=False, stop=True), r=[("b2row", 1)], w=[bk])
            A("act", lambda e, c=c, n=n, bank=bank: e.copy(out=RCv[0:n, c, 0:64], in_=bank[0:n, 0:64]), r=[bk, "RCv"], w=[("RCv", c, 0)])
            A("act", lambda e, c=c, n=n, bank=bank: e.copy(out=RCv[0:n, c, 130:194], in_=bank[0:n, 64:128]), r=[bk, "RCv"], w=[("RCv", c, 1)])
            A("dve", lambda e, c=c, n=n: e.memset(RCv[0:n, c, 64:65], 1.0), r=["RCv"], w=[("RCv", c, 2)])
            A("dve", lambda e, c=c, n=n: e.memset(RCv[0:n, c, 66:67], 1.0), r=["RCv"], w=[("RCv", c, 3)])
        if "kcT" in dbgout:
            A("sp", lambda e: e.dma_start(out=dbgout["kcT"], in_=kcT[:]), r=["kcT"], dma="dbg")
            A("sp", lambda e: e.dma_start(out=dbgout["RCv"], in_=RCv[:]), r=[("RCv", c, i) for c in range(4) for i in range(4)], dma="dbg")
        ph.emit()


def _phase_B(nc, st, SB, PS, dr, gscr, oscr, P, dbgout):
    KsT, KwT, Vs, Vw, kcT, RCv, biasC = P["KsT"], P["KwT"], P["Vs"], P["Vw"], P["kcT"], P["RCv"], P["biasC"]
    ones32, gsb, vq = P["ones32"], P["gsb"], P["vq"]
    identf = SB(st, "identfB", [128, 128], F32)
    Wq = SB(st, "Wq", [128, 8, 1024], BF16)
    Wqr = SB(st, "Wqr", [128, 8, 1024], BF16)
    Wg = SB(st, "Wg", [128, 8, 48], BF16)
    EF = SB(st, "EF", [128, 16, 128], BF16)
    ovl = SB(st, "ovl", [128, 4, 129], BF16)
    xq1 = SB(st, "xq0", [128, D], F32)
    xq = [xq1, xq1]
    xnf = SB(st, "xnf", [128, D], F32)
    ssq = SB(st, "ssqB", [128, 1], F32)
    rstd = SB(st, "rstdB", [128, 1], F32)
    hTq = SB(st, "hTq", [128, 8, 128], BF16)
    cq1 = SB(st, "cq0", [128, 128], F32)
    sq1 = SB(st, "sq0", [128, 128], F32)
    cq = [cq1, cq1]
    sq = [sq1, sq1]
    bon3 = [SB(st, f"bon{i}", [128, 128], F32) for i in range(3)]
    mkb3 = [SB(st, f"mkb{i}", [128, 6, 128], BF16) for i in range(3)]
    QT = [SB(st, f"QT{i}", [128, 8, 128], BF16) for i in range(2)]
    gsig = [SB(st, f"gsig{i}", [48, 128], F32) for i in range(2)]
    selT = [[SB(st, f"selT{i}{g}", [128, 128], BF16) for g in range(2)] for i in range(2)]
    acc = [SB(st, f"acc{i}", [128, 8, 128], F32) for i in range(2)]
    Ec = [SB(st, f"Ec{c}", [128, 8, 128], BF16) for c in range(4)]
    NBUF = 5
    E = [SB(st, f"E{i}", [128, 8, 128], BF16) for i in range(NBUF)]
    Pb = [SB(st, f"Pb{i}", [128, 8, 128], BF16) for i in range(NBUF)]
    oasb = [SB(st, f"oasb{i}", [128, 1024], F32) for i in range(2)]
    msk = [SB(st, f"msk{i}", [128, 128], BF16) for i in range(NBUF)]
    t1 = SB(st, "t1B", [128, 8, 128], F32)
    t2 = SB(st, "t2B", [128, 8, 128], F32)
    dsb = SB(st, "dsb", [65, 1024], F32)
    rdb = SB(st, "rdb", [128, 4, 128], F32)
    tmpf = SB(st, "tmpf", [128, 4, 128], F32)
    grow = [SB(st, f"grow{i}", [65, 1024], F32) for i in range(2)]
    dsb2 = SB(st, "dsb2", [65, 1024], F32)
    cbc = SB(st, "cbc", [128, 1024], F32)
    crow = nc.dram_tensor("crow", [8, 1024], F32).ap()
    score = SB(st, "score", [128, 128], F32)
    work = SB(st, "work", [128, 128], F32)
    selq = SB(st, "selq", [128, 128], F32)
    m8a = SB(st, "m8a", [128, 8], F32)
    m8b = SB(st, "m8b", [128, 8], F32)
    thr = SB(st, "thr", [128, 1], F32)
    rc = SB(st, "rc", [128, 8], F32)
    obf = SB(st, "obf", [128, 8, 128], BF16)
    scA = PS(st, "scA", [128, 1024], F32)
    scB = PS(st, "scB", [128, 1024], F32)
    oa = PS(st, "oa", [128, 1024], F32)
    mx = PS(st, "mx", [128, 512], F32)
    msc = PS(st, "msc", [128, 512], F32)
    gs2 = gscr.rearrange("b r q -> b (r q)")
    print("[phase B] sbuf bytes remaining:", nc.sbuf_bytes_remaining)

    ph = Phase(nc, "B")
    A = ph.add
    A("pool", lambda e: e.dma_start(out=Wq[:], in_=dr["wq"].rearrange("(k p) n -> p k n", p=128)), w=["Wq"], dma="w")
    A("pool", lambda e: e.dma_start(out=Wg[:], in_=dr["wg"].rearrange("(k p) n -> p k n", p=128)), w=["Wg"], dma="w")
    A("sp", lambda e: e.dma_start(out=EF[:], in_=dr["ef"]), w=["EF"], dma="c")
    A("sp", lambda e: e.dma_start(out=ovl[:], in_=dr["ovl"]), w=["ovl"], dma="c")
    A("sp", lambda e: e.dma_start(out=identf[:], in_=dr["identf"]), w=["identf"], dma="c")
    for kc in range(8):
        v = Wq[:, kc, :].rearrange("p (h two d) -> p h two d", two=2, d=32)
        vr = Wqr[:, kc, :].rearrange("p (h two d) -> p h two d", two=2, d=32)
        A("dve", lambda e, v=v, vr=vr: e.tensor_scalar(out=vr[:, :, 0, :], in0=v[:, :, 1, :], scalar1=-1.0, scalar2=None, op0=ALU.mult),
          r=["Wq"], w=[("Wqr", kc, 0)])
        A("dve", lambda e, v=v, vr=vr: e.tensor_copy(out=vr[:, :, 1, :], in_=v[:, :, 0, :]), r=["Wq"], w=[("Wqr", kc, 1)])
    Wqrk = [("Wqr", kc, i) for kc in range(8) for i in range(2)]

    def v8(t, nq):
        return t[:].rearrange("p (h q) -> p h q", q=128)[:, :, 0:nq]

    def v4(t, u, nq, p0=0, p1=128):
        return t[p0:p1, u * 512:(u + 1) * 512].rearrange("p (h q) -> p h q", q=128)[:, :, 0:nq]

    def bc(ap2, n, nq):
        return ap2.unsqueeze(1).to_broadcast([ap2.shape[0], n, nq])

    def stage_load(bi):
        S, off, nq, col0 = BLK[bi]
        t0 = 128 * S + off
        b3 = bi % 3
        A("sp", lambda e: e.dma_start(out=xq1[0:nq, :], in_=dr["xs"][t0:t0 + nq, :]), w=["xqb"], dma="ldq")
        A("sp", lambda e: e.dma_start(out=cq1[:, 0:nq], in_=dr["cosQ"][:, bi * 128:bi * 128 + nq]), w=["cqb"], dma="ldq")
        A("sp", lambda e: e.dma_start(out=sq1[:, 0:nq], in_=dr["sinQ"][:, bi * 128:bi * 128 + nq]), w=["sqb"], dma="ldq")
        A("sp", lambda e: e.dma_start(out=bon3[b3][0:nq, :], in_=dr["bonus"][0:nq, bi, :]), w=[("bon", b3)], dma=("ldm", b3))
        A("sp", lambda e: e.dma_start(out=mkb3[b3][:], in_=dr["mk"][bi]), w=[("mkb", b3)], dma=("ldm", b3))

    def stage_q(bi):
        S, off, nq, col0 = BLK[bi]
        pb = bi % 2
        A("act", lambda e: e.activation(out=xnf[0:nq, :], in_=xq[pb][0:nq, :], func=AF.Square, accum_out=ssq[0:nq, :]),
          r=["xqb"], w=["xnf", "ssq"])
        A("act", lambda e: e.activation(out=rstd[0:nq, :], in_=ssq[0:nq, :], func=AF.Sqrt, bias=EPS, scale=1.0 / D), r=["ssq"], w=["rstd"])
        A("dve", lambda e: e.reciprocal(out=rstd[0:nq, :], in_=rstd[0:nq, :]), r=["rstd"], w=["rstd"])
        A("dve", lambda e: e.tensor_scalar(out=xnf[0:nq, :], in0=xq[pb][0:nq, :], scalar1=rstd[0:nq, :], scalar2=None, op0=ALU.mult),
          r=["xqb", "rstd"], w=["xnf"])
        for kc in range(8):
            A("pe", lambda e, kc=kc: e.transpose(out=scA[:, kc * 128:kc * 128 + nq], in_=xnf[0:nq, kc * 128:(kc + 1) * 128],
                                                 identity=identf[0:nq, 0:nq]), r=["xnf", "identf"], w=[("scA", kc // 4)])
        A("dve", lambda e: e.tensor_tensor(out=hTq[:, :, 0:nq], in0=v8(scA, nq), in1=gsb[:, 0, :].unsqueeze(2).to_broadcast([128, 8, nq]),
                                           op=ALU.mult), r=[("scA", 0), ("scA", 1)], w=["hTq"])
        for hl in range(8):
            for kc in range(8):
                A("pe", lambda e, hl=hl, kc=kc: e.matmul(scB[:, hl * 128:hl * 128 + nq], lhsT=Wq[:, kc, hl * 128:(hl + 1) * 128],
                                                         rhs=hTq[:, kc, 0:nq], start=(kc == 0), stop=(kc == 7)),
                  r=["Wq", "hTq"], w=[("scB", hl // 4)])
        for hl in range(8):
            for kc in range(8):
                A("pe", lambda e, hl=hl, kc=kc: e.matmul(oa[:, hl * 128:hl * 128 + nq], lhsT=Wqr[:, kc, hl * 128:(hl + 1) * 128],
                                                         rhs=hTq[:, kc, 0:nq], start=(kc == 0), stop=(kc == 7)),
                  r=Wqrk + ["hTq"], w=[("oa", hl // 4)])
        A("dve", lambda e: e.tensor_tensor(out=t1[:, :, 0:nq], in0=v8(scB, nq), in1=bc(cq[pb][:, 0:nq], 8, nq), op=ALU.mult),
          r=[("scB", 0), ("scB", 1), "cqb"], w=[("t1", 0), ("t1", 3), ("t1", 6)])
        A("dve", lambda e: e.tensor_tensor(out=t2[:, :, 0:nq], in0=v8(oa, nq), in1=bc(sq[pb][:, 0:nq], 8, nq), op=ALU.mult),
          r=[("oa", 0), ("oa", 1), "sqb"], w=["t2"])
        A("dve", lambda e: e.tensor_tensor(out=QT[pb][:, :, 0:nq], in0=t1[:, :, 0:nq], in1=t2[:, :, 0:nq], op=ALU.add),
          r=[("t1", 0), ("t1", 3), ("t1", 6), "t2"], w=[("QT", pb)])
        for kc in range(8):
            A("pe", lambda e, kc=kc: e.matmul(mx[0:48, 0:nq], lhsT=Wg[:, kc, :], rhs=hTq[:, kc, 0:nq], start=(kc == 0), stop=(kc == 7)),
              r=["Wg", "hTq"], w=MXALL)
        A("act", lambda e: e.activation(out=gsig[pb][:, 0:nq], in_=mx[0:48, 0:nq], func=AF.Sigmoid), r=MXALL, w=[("gsig", pb)])
        A("sp", lambda e: e.dma_start(out=gscr[bi, :, 0:nq], in_=gsig[pb][:, 0:nq]), r=[("gsig", pb)], w=[("gscr", bi)], dma=("gs", pb))

    growi = [0]
    grpi = [0]
    MXALL = ["mx"]

    DEFER = True
    pending = []
    stepc = [0]

    def flush(force=False):
        while pending and (force or pending[0][0] <= stepc[0]):
            pending.pop(0)[1]()

    fini = [0]

    def finalize(bi, g, br, first, src, srck, delay):
        S, off, nq, col0 = BLK[bi]
        pb = bi % 2
        p0 = 64 * g
        dp = 64 if g == 0 else 0
        flush(force=True)
        fi = fini[0]
        fini[0] += 1
        X, xk = (dsb, "dsb") if fi % 2 == 0 else (dsb2, "dsb2")
        ri = fi % 8
        gi = growi[0] % 2
        growi[0] += 1
        r0 = br * 16 + g * 8
        A("sp", lambda e: e.dma_start(out=grow[gi][dp:dp + 1, :], in_=gs2[bi:bi + 1, r0 * 128:r0 * 128 + 1024]),
          r=[("gscr", bi)], w=[("grow", gi)], dma=("gr", gi))
        A("dve", lambda e: e.tensor_scalar(out=X[dp:dp + 1, :], in0=src[dp:dp + 1, :], scalar1=1.0e-30, scalar2=None, op0=ALU.max),
          r=[srck(0), srck(1)], w=[xk])
        A("act", lambda e: e.activation(out=X[dp:dp + 1, :], in_=X[dp:dp + 1, :], func=AF.Ln), r=[xk], w=[xk])
        A("act", lambda e: e.activation(out=X[dp:dp + 1, :], in_=X[dp:dp + 1, :], func=AF.Exp, scale=-1.0), r=[xk], w=[xk])
        A("dve", lambda e: e.tensor_tensor(out=X[dp:dp + 1, :], in0=X[dp:dp + 1, :], in1=grow[gi][dp:dp + 1, :], op=ALU.mult),
          r=[xk, ("grow", gi)], w=[xk])
        A("sp", lambda e: e.dma_start(out=crow[ri:ri + 1, :], in_=X[dp:dp + 1, :]), r=[xk], w=[("crow", ri)], dma=("cr", ri % 2))
        A("sp", lambda e: e.dma_start(out=cbc[p0:p0 + 64, :], in_=crow[ri:ri + 1, :].partition_broadcast(64)),
          r=[("crow", ri)], w=[("cbc", g)], dma=("cb", g))
        def tail():
            for u in range(2):
                dst = acc[pb][p0:p0 + 64, 4 * u:4 * u + 4, 0:nq]
                if first:
                    A("dve", lambda e, u=u, dst=dst: e.tensor_tensor(out=dst, in0=v4(src, u, nq, p0, p0 + 64), in1=v4(cbc, u, nq, p0, p0 + 64),
                                                                     op=ALU.mult), r=[srck(u), ("cbc", g)], w=[("acc", pb, g, u)])
                else:
                    A("dve", lambda e, u=u: e.tensor_tensor(out=tmpf[p0:p0 + 64, :, 0:nq], in0=v4(src, u, nq, p0, p0 + 64),
                                                            in1=v4(cbc, u, nq, p0, p0 + 64), op=ALU.mult), r=[srck(u), ("cbc", g)], w=["tmpf"])
                    A("dve", lambda e, dst=dst: e.tensor_tensor(out=dst, in0=dst, in1=tmpf[p0:p0 + 64, :, 0:nq], op=ALU.add),
                      r=["tmpf", ("acc", pb, g, u)], w=[("acc", pb, g, u)])
        if DEFER:
            pending.append((stepc[0] + delay, tail))
        else:
            tail()

    def vaug(Vt, idx, g):
        return Vt[:, idx, 0:128] if g == 0 else Vt[:, idx, 66:194]

    def stage_cmp(bi):
        fins = [stage_cmp_g(bi, g) for g in range(2)]
        for g, (ob, obk) in enumerate(fins):
            finalize(bi, g, 0, True, ob, lambda u, obk=obk: obk + (u,), 4)

    def stage_cmp_g(bi, g):
        S, off, nq, col0 = BLK[bi]
        pb = bi % 2
        M = [128, 128]
        if True:
            for c in range(4):
                sc, sk = (scA, "scA") if c % 2 == 0 else (scB, "scB")
                for u in range(2):
                    A("pe", lambda e, c=c, u=u, sc=sc: e.matmul(v4(sc, u, nq), lhsT=kcT[64 * g:64 * g + 64, c * 128:(c + 1) * 128],
                                                              rhs=QT[pb][64 * g:64 * g + 64, 4 * u:4 * u + 4, 0:nq], start=True, stop=True),
                      r=["kcT", ("QT", pb)], w=[(sk, u)])
                A("act", lambda e, c=c, sc=sc: e.activation(out=Ec[c][:, :, 0:nq], in_=v8(sc, nq), func=AF.Exp, bias=biasC[:, c:c + 1],
                                                            scale=SCALE), r=[(sk, 0), (sk, 1), "biasC"], w=[("Ec", c)])
                A("dve", lambda e, c=c: e.tensor_tensor(out=Ec[c][:, :, 0:nq], in0=Ec[c][:, :, 0:nq],
                                                        in1=bc(mkb3[bi % 3][:, 2 + c, 0:nq], 8, nq), op=ALU.mult),
                  r=[("Ec", c), ("mkb", bi % 3)], w=[("Ec", c)])
            for u in range(2):
                for c in range(4):
                    A("pe", lambda e, c=c, u=u: e.matmul(v4(oa, u, nq, 0, M[g]), lhsT=vaug(RCv, c, g), rhs=Ec[c][:, 4 * u:4 * u + 4, 0:nq],
                                                         start=(c == 0), stop=(c == 3)), r=[("Ec", c), "RCv"], w=[("oa", u)])
            regs = []
            for hl in range(8):
                j, o = hl // 3, (hl % 3) * 129
                tt, tk = [(scA, ("scA", 0)), (scA, ("scA", 1)), (scB, ("scB", 0))][j]
                base = 512 if j == 1 else 0
                regs.append((tt, tk, base + o))
                for c in range(4):
                    A("pe", lambda e, hl=hl, c=c, tt=tt, base=base, o=o: e.matmul(
                        tt[0:nq, base + o:base + o + 129], lhsT=Ec[c][:, hl, 0:nq], rhs=ovl[:, c, :], start=(c == 0), stop=(c == 3)),
                      r=[("Ec", c), "ovl"], w=[tk])
            banks = [(scA, ("scA", 0), 0, 3, 0), (scA, ("scA", 1), 512, 3, 3), (scB, ("scB", 0), 0, 2, 6)]
            for tt, tk, base, nh, h0 in banks:
                A("dve", lambda e, tt=tt, base=base, nh=nh, h0=h0: e.tensor_scalar(
                    out=rc[0:nq, h0:h0 + nh], in0=tt[0:nq, base + 128:base + 128 + 129 * (nh - 1) + 1:129], scalar1=1.0e-30,
                    scalar2=None, op0=ALU.max), r=[tk], w=[("rc", h0)])
            A("dve", lambda e: e.reciprocal(out=rc[0:nq, :], in_=rc[0:nq, :]), r=[("rc", 0), ("rc", 3), ("rc", 6)], w=["rc"])
            for tt, tk, base, nh, h0 in banks:
                A("dve", lambda e, tt=tt, base=base, nh=nh, h0=h0: e.tensor_tensor(
                    out=t1[0:nq, h0:h0 + nh, :], in0=tt[0:nq, base:base + 129 * nh].rearrange("p (h c) -> p h c", c=129)[:, :, 0:128],
                    in1=rc[0:nq, h0:h0 + nh].unsqueeze(2).to_broadcast([nq, nh, 128]), op=ALU.mult), r=[tk, "rc"], w=[("t1", h0)])
            A("dve", lambda e: e.tensor_reduce(out=score[0:nq, :], in_=t1[0:nq, :, :].rearrange("p h s -> p s h"), axis=AX.X, op=ALU.add),
              r=[("t1", 0), ("t1", 3), ("t1", 6)], w=["score"])
            A("dve", lambda e: e.tensor_tensor(out=score[0:nq, :], in0=score[0:nq, :], in1=bon3[bi % 3][0:nq, :], op=ALU.add),
              r=["score", ("bon", bi % 3)], w=["score"])
            A("dve", lambda e: e.max(out=m8a[0:nq, :], in_=score[0:nq, :]), r=["score"], w=["m8a"])
            A("dve", lambda e: e.match_replace(out=work[0:nq, :], in_to_replace=m8a[0:nq, :], in_values=score[0:nq, :], imm_value=-3.0e38),
              r=["score", "m8a"], w=["work"])
            A("dve", lambda e: e.max(out=m8b[0:nq, :], in_=work[0:nq, :]), r=["work"], w=["m8b"])
            A("dve", lambda e: e.tensor_scalar(out=thr[0:nq, :], in0=m8b[0:nq, 7:8], scalar1=-1.0e29, scalar2=None, op0=ALU.max),
              r=["m8b"], w=["thr"])
            A("dve", lambda e: e.tensor_scalar(out=selq[0:nq, :], in0=score[0:nq, :], scalar1=thr[0:nq, :], scalar2=None, op0=ALU.is_ge),
              r=["score", "thr"], w=["selq"])
            A("pe", lambda e: e.transpose(out=mx[:, 0:nq], in_=selq[0:nq, :], identity=identf[0:nq, 0:nq]), r=["selq", "identf"], w=MXALL)
            A("act", lambda e, g=g: e.copy(out=selT[pb][g][:, 0:nq], in_=mx[:, 0:nq]), r=MXALL, w=[("selT", pb, g)])
            flush(force=True)
            ob = oasb[grpi[0] % 2]
            obk = ("oasb", grpi[0] % 2)
            grpi[0] += 1
            for u in range(2):
                A("act", lambda e, u=u, ob=ob: e.copy(out=ob[:, u * 512:(u + 1) * 512], in_=oa[:, u * 512:(u + 1) * 512]),
                  r=[("oa", u)], w=[obk + (u,)])
            return ob, obk

    LAG = 4
    FILL = 1
    WARM = 0
    XFILL = 16

    def stage_attn(bi):
        S, off, nq, col0 = BLK[bi]
        pb = bi % 2
        M = [128, 128]
        items = []
        gidx = []
        for g in range(2):
            for br in (1, 2):
                kts = list(range(0, S + 1)) if br == 1 else list(range(S - 4, S + 1))
                for idx, kt in enumerate(kts):
                    items.append((g, br, kt, idx == 0, idx == len(kts) - 1))
                    gidx.append(idx)
        N = len(items)
        srcs = [None] * N
        mids = [None] * N

        def front(i):
            g, br, kt, isfirst, islast = items[i]
            KT, Vt, koff = (KsT, Vs, 0) if br == 1 else (KwT, Vw, 40)
            sc, sk = (scA, "scA") if i % 2 == 0 else (scB, "scB")
            Eb, ek = E[i % NBUF], ("E", i % NBUF)
            Pq, pk_ = Pb[i % NBUF], ("Pb", i % NBUF)
            mb, mbk = msk[i % NBUF], ("msk", i % NBUF)
            masked = True
            if br == 1:
                a, v = kt // 16, kt % 16
                kw = dict(tile_position=(96, 0)) if a == 3 else {}
                A("pe", lambda e: e.matmul(mx[:, 0:nq], lhsT=EF[32 * a:32 * a + 32, v, :], rhs=selT[pb][g][32 * a:32 * a + 32, 0:nq],
                                           start=True, stop=True, **kw), r=["EF", ("selT", pb, g)], w=["mx"])
                if kt == S:
                    A("dve", lambda e: e.tensor_tensor(out=mb[:, 0:nq], in0=mx[:, 0:nq], in1=mkb3[bi % 3][:, 0, 0:nq], op=ALU.mult),
                      r=["mx", ("mkb", bi % 3)], w=[mbk])
                else:
                    A("dve", lambda e: e.tensor_copy(out=mb[:, 0:nq], in_=mx[:, 0:nq]), r=["mx"], w=[mbk])
                mask_ap, mask_r = mb[:, 0:nq], [mbk]
            else:
                if kt == S - 4:
                    mask_ap, mask_r = mkb3[bi % 3][:, 1, 0:nq], [("mkb", bi % 3)]
                elif kt == S:
                    mask_ap, mask_r = mkb3[bi % 3][:, 0, 0:nq], [("mkb", bi % 3)]
                else:
                    masked = False
            for u in range(2):
                A("pe", lambda e, u=u: e.matmul(v4(sc, u, nq), lhsT=KT[64 * g:64 * g + 64, (kt - koff) * 128:(kt - koff + 1) * 128],
                                                rhs=QT[pb][64 * g:64 * g + 64, 4 * u:4 * u + 4, 0:nq], start=True, stop=True),
                  r=[("QT", pb)], w=[(sk, u)])
            A("act", lambda e: e.activation(out=Eb[:, :, 0:nq], in_=v8(sc, nq), func=AF.Exp, scale=SCALE), r=[(sk, 0), (sk, 1)], w=[ek])
            for _ in range(FILL + (1 if gidx[i] < XFILL else 0)):
                A("pe", lambda e: e.matmul(msc[:], lhsT=Wq[:, 0, 0:128], rhs=Wq[:, 1, 0:512], start=True, stop=True), r=["Wq"], w=["msc"])
            if masked:
                mids[i] = (Eb, ek, Pq, pk_, mask_ap, mask_r)
                srcs[i] = (Pq, pk_)
            else:
                srcs[i] = (Eb, ek)

        def mid(i):
            if mids[i] is None:
                return
            Eb, ek, Pq, pk_, mask_ap, mask_r = mids[i]
            A("dve", lambda e: e.tensor_tensor(out=Pq[:, :, 0:nq], in0=Eb[:, :, 0:nq], in1=bc(mask_ap, 8, nq), op=ALU.mult),
              r=[ek] + mask_r, w=[pk_])

        def back(i):
            g, br, kt, isfirst, islast = items[i]
            KT, Vt, koff = (KsT, Vs, 0) if br == 1 else (KwT, Vw, 40)
            src, srck = srcs[i]
            for u in range(2):
                A("pe", lambda e, u=u: e.matmul(v4(oa, u, nq, 0, M[g]), lhsT=vaug(Vt, kt - koff, g), rhs=src[:, 4 * u:4 * u + 4, 0:nq],
                                                start=isfirst, stop=islast), r=[srck], w=[("oa", u)])
            if islast:
                flush(force=True)
                ob = oasb[grpi[0] % 2]
                obk = ("oasb", grpi[0] % 2)
                grpi[0] += 1
                for u in range(2):
                    A("act", lambda e, u=u: e.copy(out=ob[:, u * 512:(u + 1) * 512], in_=oa[:, u * 512:(u + 1) * 512]),
                      r=[("oa", u)], w=[obk + (u,)])
                finalize(bi, g, br, False, ob, lambda u: obk + (u,), 4)

        for _ in range(WARM):
            A("pe", lambda e: e.matmul(msc[:], lhsT=Wq[:, 0, 0:128], rhs=Wq[:, 1, 0:512], start=True, stop=True), r=["Wq"], w=["msc"])
        for i in range(N + LAG):
            stepc[0] += 1
            flush()
            if i < N:
                front(i)
            if 0 <= i - 1 < N:
                mid(i - 1)
            if i - LAG >= 0:
                back(i - LAG)
        if DEFER:
            pending.append((stepc[0] + 4, lambda: stage_store(bi)))
        else:
            stage_store(bi)

    def stage_store(bi):
        S, off, nq, col0 = BLK[bi]
        pb = bi % 2
        rk = [("acc", pb, g, u) for g in range(2) for u in range(2)]
        if bi == 0:
            A("dve", lambda e: e.tensor_scalar(out=obf[:, :, 0:nq], in0=acc[pb][:, :, 0:nq], scalar1=vq[:, 0:1], scalar2=None, op0=ALU.mult),
              r=rk + ["vq"], w=["obf"])
        else:
            A("dve", lambda e: e.tensor_copy(out=obf[:, :, 0:nq], in_=acc[pb][:, :, 0:nq]), r=rk, w=["obf"])
        A("sp", lambda e: e.dma_start(out=oscr[:, :, col0:col0 + nq], in_=obf[:, :, 0:nq]), r=["obf"], w=["oscr"], dma="os")
        if "selT" in dbgout and bi == dbgout["_blk"]:
            for g in range(2):
                A("sp", lambda e, g=g: e.dma_start(out=dbgout["selT"][g], in_=selT[pb][g][:]), r=[("selT", pb, g)], dma="dbg")
            A("sp", lambda e: e.dma_start(out=dbgout["QT"], in_=QT[pb][:]), r=[("QT", pb)], dma="dbg")
            A("sp", lambda e: e.dma_start(out=dbgout["acc"], in_=acc[pb][:]), r=rk, dma="dbg")

    nb = dbgout.get("_nblk", NBLK)
    stage_load(0)
    stage_q(0)
    if nb > 1:
        stage_load(1)
    stage_cmp(0)
    for bi in range(nb):
        if bi + 1 < nb:
            stage_q(bi + 1)
            if bi + 2 < nb:
                stage_load(bi + 2)
            stage_cmp(bi + 1)
        stage_attn(bi)
    flush(force=True)
    ph.emit()


def _norm_group(A, xres, c0, n, gcol, onesb, sqb, pn, rs, dst, dkey, tag):
    A("act", lambda e: e.activation(out=sqb[:, :, 0:n], in_=xres[:, :, c0:c0 + n], func=AF.Square), r=["xres"], w=["sqb"])
    for kc in range(8):
        A("pe", lambda e, kc=kc: e.matmul(pn[:, 0:n], lhsT=onesb[:], rhs=sqb[:, kc, 0:n], start=(kc == 0), stop=(kc == 7)),
          r=["sqb"], w=["pn"])
    A("act", lambda e: e.activation(out=rs[:, 0:n], in_=pn[:, 0:n], func=AF.Sqrt, bias=EPS, scale=1.0 / D), r=["pn"], w=["rs"])
    A("dve", lambda e: e.reciprocal(out=rs[:, 0:n], in_=rs[:, 0:n]), r=["rs"], w=["rs"])
    for kc in range(8):
        A("dve", lambda e, kc=kc: e.scalar_tensor_tensor(out=dst[:, kc, 0:n], in0=xres[:, kc, c0:c0 + n], scalar=gcol[:, kc:kc + 1],
                                                        in1=rs[:, 0:n], op0=ALU.mult, op1=ALU.mult), r=["xres", "rs"], w=[dkey])


def _phase_C(nc, st, SB, PS, dr, oscr, P, dbgout):
    xres, identf = P["xres"], P["identf"]
    Wo = SB(st, "Wo", [128, 8, 1024], BF16)
    xq = [SB(st, f"xqC{i}", [128, D], F32) for i in range(2)]
    ot = [SB(st, f"otC{i}", [128, 8, 512], BF16) for i in range(2)]
    pa = [PS(st, f"paC{i}", [128, 1024], F32) for i in range(2)]
    py = [PS(st, f"pyC{i}", [128, 512], F32) for i in range(2)]
    ph = Phase(nc, "C")
    A = ph.add
    A("pool", lambda e: e.dma_start(out=Wo[:], in_=dr["wout"].rearrange("(k p) n -> p k n", p=128)), w=["Wo"], dma="w")
    for bi, (S, off, nq, col0) in enumerate(BLK):
        pb = bi % 2
        t0 = 128 * S + off
        A("sp", lambda e, pb=pb, t0=t0, nq=nq: e.dma_start(out=xq[pb][0:nq, :], in_=dr["xs"][t0:t0 + nq, :]), w=[("xq", pb)], dma=("xq", pb))
        for kc in range(8):
            A("pe", lambda e, kc=kc, pb=pb, nq=nq: e.transpose(out=pa[pb][:, kc * 128:kc * 128 + nq], in_=xq[pb][0:nq, kc * 128:(kc + 1) * 128],
                                                              identity=identf[0:nq, 0:nq]), r=[("xq", pb)], w=[("pa", pb)])
        A("act", lambda e, pb=pb, nq=nq, col0=col0: e.copy(out=xres[:, :, col0:col0 + nq],
                                                           in_=pa[pb][:].rearrange("p (k q) -> p k q", q=128)[:, :, 0:nq]),
          r=[("pa", pb)], w=["xres"])
    for ti, (c0, n) in enumerate(TG):
        tb = ti % 2
        A("sp", lambda e, tb=tb, c0=c0, n=n: e.dma_start(out=ot[tb][:, :, 0:n], in_=oscr[:, :, c0:c0 + n]), w=[("ot", tb)], dma=("ot", tb))
        for dc in range(8):
            for hl in range(8):
                A("pe", lambda e, dc=dc, hl=hl, tb=tb, n=n: e.matmul(py[dc % 2][:, 0:n], lhsT=Wo[:, hl, dc * 128:(dc + 1) * 128],
                                                                    rhs=ot[tb][:, hl, 0:n], start=(hl == 0), stop=(hl == 7)),
                  r=["Wo", ("ot", tb)], w=[("py", dc % 2)])
            A("dve", lambda e, dc=dc, c0=c0, n=n: e.tensor_tensor(out=xres[:, dc, c0:c0 + n], in0=xres[:, dc, c0:c0 + n],
                                                                  in1=py[dc % 2][:, 0:n], op=ALU.add),
              r=[("py", dc % 2), "xres"], w=["xres"])
    if "x0mix" in dbgout:
        A("sp", lambda e: e.dma_start(out=dbgout["x0mix"], in_=xres[:]), r=["xres"], dma="dbg")
    ph.emit()


def _phase_ffn(nc, st, SB, PS, dr, L, P, dbgout):
    xres, onesb, gsb, vq = P["xres"], P["onesb"], P["gsb"], P["vq"]
    gcol = gsb[:, 1 + 2 * L, :]
    hTall = SB(st, f"hTall{L}", [128, 8, NTOK], BF16)
    with contextlib.ExitStack() as nst:
        sqb = SB(nst, f"sqbf{L}", [128, 8, 512], BF16)
        rs = SB(nst, f"rsf{L}", [128, 512], F32)
        pn = PS(nst, f"pnf{L}", [128, 512], F32)
        phn = Phase(nc, f"N{L}")
        for ti, (c0, n) in enumerate(TG):
            _norm_group(phn.add, xres, c0, n, gcol, onesb, sqb, pn, rs, hTall[:, :, c0:c0 + n], "hT", f"f{L}")
        phn.emit()
    wu = [SB(st, f"wu{L}{i}", [128, 8, 6, 256], BF16) for i in range(2)]
    wd = [SB(st, f"wd{L}{i}", [128, 6, 1024], BF16) for i in range(2)]
    cw = SB(st, f"cw{L}", [128, 3, 44], F32)
    cb = SB(st, f"cb{L}", [128, 44], F32)
    carry = SB(st, f"carry{L}", [128, 44, 2], F32)
    ub = [SB(st, f"ub{L}{i}", [128, 514], F32) for i in range(2)]
    cbuf = [SB(st, f"cbuf{L}{i}", [128, 512], F32) for i in range(2)]
    sg = SB(st, f"sg{L}", [128, 512], F32)
    act2 = [SB(st, f"act{L}{i}", [128, 6, 512], BF16) for i in range(2)]
    pu = [[PS(st, f"pu{L}{a}{b}", [128, 512], F32) for b in range(2)] for a in range(2)]
    py = [PS(st, f"pyf{L}{i}", [128, 512], F32) for i in range(2)]
    ph = Phase(nc, f"F{L}")
    A = ph.add
    A("sp", lambda e: e.dma_start(out=cw[:], in_=dr[f"cw{L}"]), w=["cw"], dma="c")
    A("sp", lambda e: e.dma_start(out=cb[:], in_=dr[f"cb{L}"]), w=["cb"], dma="c")
    A("dve", lambda e: e.memset(carry[:], 0.0), w=["carry"])

    def load_pass(p):
        wb = p % 2
        for i, fc in enumerate(FPASS[p]):
            A("pool", lambda e, i=i, fc=fc, wb=wb: e.dma_start(
                out=wu[wb][:, :, i, 0:128], in_=dr[f"wup{L}"][:, fc * 128:(fc + 1) * 128].rearrange("(k p) n -> p k n", p=128)),
              w=[("wu", wb, i)], dma=("w", wb))
            A("pool", lambda e, i=i, fc=fc, wb=wb: e.dma_start(
                out=wu[wb][:, :, i, 128:256], in_=dr[f"wup{L}"][:, DFF + fc * 128:DFF + (fc + 1) * 128].rearrange("(k p) n -> p k n", p=128)),
              w=[("wu", wb, i)], dma=("w", wb))
            A("pool", lambda e, i=i, fc=fc, wb=wb: e.dma_start(out=wd[wb][:, i, :], in_=dr[f"wdn{L}"][fc * 128:(fc + 1) * 128, :]),
              w=[("wd", wb, i)], dma=("w", wb))

    def up_stage(p, ti):
        wb = p % 2
        c0, n = TG[ti]
        ab = (p * len(TG) + ti) % 2
        for i, fc in enumerate(FPASS[p]):
            for part in range(2):
                bank = pu[part][i % 2]
                bk = ("pu", part, i % 2)
                ch = fc + 22 * part
                for kc in range(8):
                    A("pe", lambda e, kc=kc, i=i, part=part, bank=bank: e.matmul(
                        bank[:, 0:n], lhsT=wu[wb][:, kc, i, part * 128:(part + 1) * 128], rhs=hTall[:, kc, c0:c0 + n],
                        start=(kc == 0), stop=(kc == 7)), r=[("wu", wb, i)], w=[bk])
                A("act", lambda e, part=part, bank=bank: e.copy(out=ub[part][:, 2:2 + n], in_=bank[:, 0:n]), r=[bk], w=[("ub", part)])
                A("dve", lambda e, part=part, ch=ch: e.tensor_copy(out=ub[part][:, 0:2], in_=carry[:, ch, :]), r=["carry"], w=[("ub", part)])
                A("act", lambda e, part=part, bank=bank, ch=ch: e.activation(
                    out=cbuf[part][:, 0:n], in_=bank[:, 0:n], func=AF.Identity, bias=cb[:, ch:ch + 1], scale=cw[:, 2, ch:ch + 1]),
                  r=[bk, "cw", "cb"], w=[("cbuf", part)])
                for k in (1, 0):
                    A("dve", lambda e, part=part, ch=ch, k=k: e.scalar_tensor_tensor(
                        out=cbuf[part][:, 0:n], in0=ub[part][:, k:k + n], scalar=cw[:, k, ch:ch + 1], in1=cbuf[part][:, 0:n],
                        op0=ALU.mult, op1=ALU.add), r=[("ub", part), ("cbuf", part), "cw"], w=[("cbuf", part)])
                A("dve", lambda e, part=part, ch=ch: e.tensor_copy(out=carry[:, ch, :], in_=ub[part][:, n:n + 2]), r=[("ub", part)], w=["carry"])
            A("act", lambda e: e.activation(out=sg[:, 0:n], in_=cbuf[0][:, 0:n], func=AF.Silu), r=[("cbuf", 0)], w=["sg"])
            A("dve", lambda e, i=i: e.tensor_tensor(out=act2[ab][:, i, 0:n], in0=sg[:, 0:n], in1=cbuf[1][:, 0:n], op=ALU.mult),
              r=["sg", ("cbuf", 1)], w=[("act", ab, i)])

    def down_stage(p, ti):
        wb = p % 2
        c0, n = TG[ti]
        ab = (p * len(TG) + ti) % 2
        nf = len(FPASS[p])
        for dc in range(8):
            for i in range(nf):
                A("pe", lambda e, dc=dc, i=i: e.matmul(py[dc % 2][:, 0:n], lhsT=wd[wb][:, i, dc * 128:(dc + 1) * 128], rhs=act2[ab][:, i, 0:n],
                                                       start=(i == 0), stop=(i == nf - 1)), r=[("wd", wb, i), ("act", ab, i)], w=[("py", dc % 2)])
            A("dve", lambda e, dc=dc: e.tensor_tensor(out=xres[:, dc, c0:c0 + n], in0=xres[:, dc, c0:c0 + n], in1=py[dc % 2][:, 0:n], op=ALU.add),
              r=[("py", dc % 2), "xres"], w=["xres"])

    steps = [(p, ti) for p in range(len(FPASS)) for ti in range(len(TG))]
    load_pass(0)
    load_pass(1)
    for k in range(len(steps) + 1):
        if k < len(steps):
            up_stage(*steps[k])
        if k >= 1:
            pp, pti = steps[k - 1]
            down_stage(pp, pti)
            if pti == len(TG) - 1 and pp + 2 < len(FPASS):
                load_pass(pp + 2)
    A("dve", lambda e: e.tensor_scalar(out=xres[:, :, 0:HALO], in0=xres[:, :, 0:HALO], scalar1=vq[:, 0:1], scalar2=None, op0=ALU.mult),
      r=["xres"], w=["xres"])
    if f"xffn{L}" in dbgout:
        A("sp", lambda e: e.dma_start(out=dbgout[f"xffn{L}"], in_=xres[:]), r=["xres"], dma="dbg")
    ph.emit()


def _phase_pool(nc, st, SB, PS, dr, P, dbgout):
    xres, onesb, gsb, vq = P["xres"], P["onesb"], P["gsb"], P["vq"]
    gcol = gsb[:, 2, :]
    hf = SB(st, "hf", [128, 8, NTOK], F32)
    pl = SB(st, "pl", [128, 8, NTOK], BF16)
    wa = SB(st, "wa", [128, NTOK], F32)
    wb_ = SB(st, "wb", [128, NTOK], F32)
    sqb = SB(st, "sqbp", [128, 8, 512], BF16)
    rs = SB(st, "rsp", [128, 512], F32)
    pw = SB(st, "pw", [128, 8, 256], BF16)
    pbias = SB(st, "pbias", [128, 8], F32)
    pscl = SB(st, "pscl", [128, 8], F32)
    psc16 = SB(st, "psc16", [128, 8, 16], F32)
    tmp16 = SB(st, "tmp16", [128, 16], F32)
    ytmp = SB(st, "ytmp", [128, 512], F32)
    pn = PS(st, "pnp", [128, 512], F32)
    py = [PS(st, f"pyp{i}", [128, 512], F32) for i in range(2)]
    ph = Phase(nc, "P")
    A = ph.add
    A("pool", lambda e: e.dma_start(out=pw[:], in_=dr["poolw"]), w=["pw"], dma="w")
    A("sp", lambda e: e.dma_start(out=pbias[:], in_=dr["poolb"]), w=["pbias"], dma="c")
    A("sp", lambda e: e.dma_start(out=pscl[:], in_=dr["pools"]), w=["pscl"], dma="c")
    A("sp", lambda e: e.dma_start(out=psc16[:], in_=dr["pscale"]), w=["psc16"], dma="c")
    for ti, (c0, n) in enumerate(TG):
        _norm_group(A, xres, c0, n, gcol, onesb, sqb, pn, rs, hf[:, :, c0:c0 + n], "hf", "p")
    for kc in range(8):
        nsteps = kc // 2 + 1
        w = 2 ** nsteps
        src, sk = hf[:, kc, :], "hf"
        bufs = [(wa, "wa"), (wb_, "wb")]
        for sidx in range(nsteps):
            d = 2 ** sidx
            dst, dk = bufs[sidx % 2]
            A("dve", lambda e, src=src, dst=dst, d=d: e.tensor_tensor(out=dst[:, d:NTOK], in0=src[:, d:NTOK], in1=src[:, 0:NTOK - d], op=ALU.add),
              r=[sk], w=[dk])
            A("act", lambda e, src=src, dst=dst, d=d: e.copy(out=dst[:, 0:d], in_=src[:, 0:d]), r=[sk], w=[dk])
            src, sk = dst[:], dk
        A("dve", lambda e, src=src, kc=kc, w=w: e.scalar_tensor_tensor(out=pl[:, kc, :], in0=src, scalar=1.0 / w, in1=hf[:, kc, :],
                                                                      op0=ALU.mult, op1=ALU.subtract), r=[sk, "hf"], w=[("pl", kc)])
        A("dve", lambda e, src=src, kc=kc: e.tensor_tensor(out=tmp16[:], in0=src[:, HALO:HALO + 16], in1=psc16[:, kc, :], op=ALU.mult),
          r=[sk, "psc16"], w=["tmp16"])
        A("dve", lambda e, kc=kc: e.tensor_tensor(out=pl[:, kc, HALO:HALO + 16], in0=tmp16[:], in1=hf[:, kc, HALO:HALO + 16], op=ALU.subtract),
          r=["tmp16", "hf", ("pl", kc)], w=[("pl", kc)])
    for ti, (c0, n) in enumerate(TG):
        for oc in range(8):
            g, oh = oc // 2, oc % 2
            for kh in range(2):
                A("pe", lambda e, oc=oc, g=g, oh=oh, kh=kh, c0=c0, n=n: e.matmul(
                    py[oc % 2][:, 0:n], lhsT=pw[:, g * 2 + kh, oh * 128:(oh + 1) * 128], rhs=pl[:, g * 2 + kh, c0:c0 + n],
                    start=(kh == 0), stop=(kh == 1)), r=["pw", ("pl", g * 2 + kh)], w=[("py", oc % 2)])
            A("dve", lambda e, oc=oc, n=n: e.tensor_scalar(out=ytmp[:, 0:n], in0=py[oc % 2][:, 0:n], scalar1=pbias[:, oc:oc + 1],
                                                          scalar2=pscl[:, oc:oc + 1], op0=ALU.add, op1=ALU.mult),
              r=[("py", oc % 2), "pbias", "pscl"], w=["ytmp"])
            A("dve", lambda e, oc=oc, c0=c0, n=n: e.tensor_tensor(out=xres[:, oc, c0:c0 + n], in0=xres[:, oc, c0:c0 + n], in1=ytmp[:, 0:n],
                                                                  op=ALU.add), r=["ytmp", "xres"], w=["xres"])
    A("dve", lambda e: e.tensor_scalar(out=xres[:, :, 0:HALO], in0=xres[:, :, 0:HALO], scalar1=vq[:, 0:1], scalar2=None, op0=ALU.mult),
      r=["xres"], w=["xres"])
    if "xpool" in dbgout:
        A("sp", lambda e: e.dma_start(out=dbgout["xpool"], in_=xres[:]), r=["xres"], dma="dbg")
    ph.emit()


def _phase_out(nc, st, SB, PS, dr, out, P, dbgout):
    xres, onesb, gsb, identf = P["xres"], P["onesb"], P["gsb"], P["identf"]
    gcol = gsb[:, 4, :]
    of = SB(st, "of", [128, 8, 512], F32)
    sqb = SB(st, "sqbo", [128, 8, 512], BF16)
    rs = SB(st, "rso", [128, 512], F32)
    ot = [SB(st, f"oto{i}", [128, D], F32) for i in range(2)]
    pn = PS(st, "pno", [128, 512], F32)
    pt = [PS(st, f"pto{i}", [128, 1024], F32) for i in range(2)]
    ph = Phase(nc, "O")
    A = ph.add
    for gi in range(4):
        c0 = HALO + 512 * gi
        _norm_group(A, xres, c0, 512, gcol, onesb, sqb, pn, rs, of, "of", "o")
        for tt in range(4):
            tb = tt % 2
            for kc in range(8):
                A("pe", lambda e, tt=tt, tb=tb, kc=kc: e.transpose(out=pt[tb][:, kc * 128:(kc + 1) * 128],
                                                                 in_=of[:, kc, tt * 128:(tt + 1) * 128], identity=identf[:]),
                  r=["of"], w=[("pt", tb)])
            A("act", lambda e, tb=tb: e.copy(out=ot[tb][:], in_=pt[tb][:]), r=[("pt", tb)], w=[("ot", tb)])
            row = (gi * 4 + tt) * 128
            A("sp", lambda e, tb=tb, row=row: e.dma_start(out=out[row:row + 128, :], in_=ot[tb][:]), r=[("ot", tb)], dma=("st", tb))
    ph.emit()


_CACHE = {}


def kernel(**inputs):
    inp = {k: np.asarray(v) for k, v in inputs.items()}
    if "nc" not in _CACHE:
        _CACHE["nc"] = build()
    nc = _CACHE["nc"]
    sh = _shared_inputs(inp)
    maps = []
    for c in range(8):
        m = dict(sh)
        m.update(_core_inputs(inp, c))
        maps.append(m)
    res = run_bass_kernel_spmd(nc, maps, core_ids=list(range(8)))
    outp = np.zeros((2, T, D), np.float32)
    for c in range(8):
        b, j = c // 4, c % 4
        outp[b, 2048 * j:2048 * (j + 1)] = np.asarray(res.results[c]["out"], dtype=np.float32)
    return outp
```

```python
import contextlib
import numpy as np
import ml_dtypes
import concourse.bass as bass
import concourse.mybir as mybir
from concourse.bass_utils import run_bass_kernel_spmd

F32 = mybir.dt.float32
BF16 = mybir.dt.bfloat16
AF = mybir.ActivationFunctionType
ALU = mybir.AluOpType
AX = mybir.AxisListType
NPBF = ml_dtypes.bfloat16

D = 1024
KC = 8
T = 8192
NSLOT = 64
OWN0 = 48
NOWN = 16
HALO = 20
NTOK = HALO + 128 * NOWN
DFF = 2816
NFC = 22
EPS = 1e-6
SCALE = 0.125
NEGB = -30000.0
BLK = [(47, 108, HALO, 0)] + [(OWN0 + m, 0, 128, HALO + 128 * m) for m in range(NOWN)]
NBLK = len(BLK)
TG = [(0, HALO)] + [(HALO + 512 * i, 512) for i in range(4)]
FPASS = [list(range(0, 6)), list(range(6, 12)), list(range(12, 17)), list(range(17, 22))]

ENGS = ("pe", "act", "dve", "pool", "sp")


class Op:
    __slots__ = ("eng", "fn", "dma", "waits", "signal", "sigidx", "idx")

    def __init__(self, eng, fn, dma):
        self.eng = eng
        self.fn = fn
        self.dma = dma
        self.waits = []
        self.signal = False
        self.sigidx = None


class Phase:
    def __init__(self, nc, name):
        self.nc = nc
        self.name = name
        self.ops = {e: [] for e in ENGS}
        self.lastw = {}
        self.readers = {}
        self.dma_count = {}
        self.n = 0

    def add(self, eng, fn, r=(), w=(), dma=None):
        op = Op(eng, fn, dma)
        op.idx = self.n
        self.n += 1
        deps = []
        for k in r:
            x = self.lastw.get(k)
            if x is not None:
                deps.append(x)
        for k in w:
            x = self.lastw.get(k)
            if x is not None:
                deps.append(x)
            deps.extend(self.readers.get(k, {}).values())
        seen = set()
        for d in deps:
            if d is op or id(d) in seen:
                continue
            seen.add(id(d))
            if d.dma is not None:
                op.waits.append(("dma", d.dma, 16 * self.dma_count[d.dma]))
            else:
                if d.eng == "pe" and eng == "pe" and dma is None:
                    continue
                d.signal = True
                op.waits.append(("eng", d, None))
        rk = eng if dma is None else ("dma", op.idx)
        for k in r:
            self.readers.setdefault(k, {})[rk] = op
        for k in w:
            self.lastw[k] = op
            self.readers[k] = {}
        if dma is not None:
            self.dma_count[dma] = self.dma_count.get(dma, 0) + 1
        self.ops[eng].append(op)
        return op

    def emit(self):
        nc = self.nc
        for e in ENGS:
            k = 0
            for op in self.ops[e]:
                if op.dma is None and op.signal:
                    k += 1
                    op.sigidx = k
        with contextlib.ExitStack() as st:
            esem = {e: st.enter_context(nc.semaphore(f"{self.name}_s_{e}")) for e in ENGS}
            dsem = {k: st.enter_context(nc.semaphore(f"{self.name}_d_{i}"))
                    for i, k in enumerate(self.dma_count)}
            block = st.enter_context(nc.Block())
            final_dma = dict(self.dma_count)

            def run(e, eng):
                seen = {}
                for op in self.ops[e]:
                    for kind, obj, val in op.waits:
                        if kind == "dma":
                            sem, v = dsem[obj], val
                        else:
                            sem, v = esem[obj.eng], obj.sigidx
                        if seen.get(id(sem), 0) >= v:
                            continue
                        seen[id(sem)] = v
                        eng.wait_ge(sem, v)
                    inst = op.fn(eng)
                    if op.dma is not None:
                        inst.then_inc(dsem[op.dma], 16)
                    elif op.signal:
                        inst.then_inc(esem[e], 1)
                mine = []
                for op in self.ops[e]:
                    if op.dma is not None and op.dma not in mine:
                        mine.append(op.dma)
                for k in mine:
                    v = 16 * final_dma[k]
                    if seen.get(id(dsem[k]), 0) < v:
                        eng.wait_ge(dsem[k], v)

            block.tensor(lambda eng: run("pe", eng))
            block.scalar(lambda eng: run("act", eng))
            block.vector(lambda eng: run("dve", eng))
            block.gpsimd(lambda eng: run("pool", eng))
            block.sync(lambda eng: run("sp", eng))


def _c(a, dt=np.float32):
    return np.ascontiguousarray(a).astype(dt, copy=False)


def _pk(v):
    return _c(np.asarray(v).reshape(-1, 128).T)


def _rope_tab(pos):
    inv = (1.0 / (10000.0 ** (np.arange(0, 64, 2, dtype=np.float32) / np.float32(64)))).astype(np.float32)
    ang = pos.astype(np.float32)[:, None] * inv[None, :]
    c = np.cos(ang).astype(np.float32)
    s = np.sin(ang).astype(np.float32)
    idx = np.arange(128) % 32
    return _c(c[:, idx].T), _c(s[:, idx].T)


def _shared_inputs(inp):
    sh = {}
    w_in = np.asarray(inp["nsa_w_in"])
    sh["wq"] = _c(w_in[:, :1024].reshape(1024, 2, 8, 64).transpose(0, 2, 1, 3).reshape(1024, 1024))
    sh["wkv"] = _c(w_in[:, 1024:1792])
    sh["wg"] = _c(w_in[:, 1792:1840].reshape(1024, 2, 8, 3).transpose(0, 3, 1, 2).reshape(1024, 48))
    sh["wout"] = _c(np.asarray(inp["nsa_w_out"]).reshape(2, 8, 64, 1024).transpose(1, 0, 2, 3).reshape(1024, 1024))
    for x in ("k", "v"):
        sh[f"c{x}_w1"] = _c(inp[f"cmp_{x}_w1"])
        sh[f"c{x}_w2"] = _c(inp[f"cmp_{x}_w2"])
        pos = np.asarray(inp[f"cmp_{x}_pos"])
        pt = np.zeros((128, 32, 2), np.float32)
        pt[0:64, :, 0] = pos.T
        pt[64:128, :, 0] = pos.T
        sh[f"c{x}_posT"] = _c(pt)
        sh[f"c{x}_b1"] = _c(np.asarray(inp[f"cmp_{x}_b1"]).reshape(2, 128).T)
        sh[f"c{x}_b2"] = _c(np.tile(np.asarray(inp[f"cmp_{x}_b2"]), 2)[None, :])
    for i in (0, 1):
        sh[f"gmix{i}"] = _pk(inp[f"norm_mix_{i}"])
        sh[f"gffn{i}"] = _pk(inp[f"norm_ffn_{i}"])
        sh[f"wup{i}"] = _c(inp[f"ffn_up_{i}"])
        sh[f"wdn{i}"] = _c(inp[f"ffn_down_{i}"])
        cw = np.asarray(inp[f"ffn_conv_w_{i}"])
        sh[f"cw{i}"] = _c(cw.reshape(3, 44, 128).transpose(2, 0, 1))
        sh[f"cb{i}"] = _c(np.asarray(inp[f"ffn_conv_b_{i}"]).reshape(44, 128).T)
    sh["gfin"] = _pk(inp["norm_final"])
    sh["poolw"] = _c(np.asarray(inp["pool_w"]).reshape(4, 2, 128, 256).transpose(2, 0, 1, 3).reshape(128, 8, 256))
    sh["poolb"] = _pk(np.asarray(inp["pool_b"]).reshape(-1))
    sh["pools"] = _pk(inp["pool_scale"])
    sh["identb"] = _c(np.eye(128), NPBF)
    sh["identf"] = _c(np.eye(128))
    ef = np.zeros((128, 16, 128), np.float32)
    for p in range(128):
        for v in range(16):
            if p % 32 == 2 * v:
                ef[p, v, 0:64] = 1
            if p % 32 == 2 * v + 1:
                ef[p, v, 64:128] = 1
    sh["ef"] = _c(ef, NPBF)
    n = np.arange(512)[:, None]
    s = np.arange(128)[None, :]
    lo = np.maximum(n * 16, s * 64)
    hi = np.minimum(n * 16 + 32, (s + 1) * 64)
    ov = np.clip(hi - lo, 0, None) / 32.0
    ova = np.ones((512, 129), np.float32)
    ova[:, :128] = ov
    sh["ovl"] = _c(ova.reshape(4, 128, 129).transpose(1, 0, 2), NPBF)
    mk = np.zeros((NBLK, 128, 6, 128), np.float32)
    k = np.arange(128)[:, None]
    q = np.arange(128)[None, :]
    for bi, (S, off, nq, col0) in enumerate(BLK):
        mk[bi, :, 0, :] = (k <= off + q)
        mk[bi, :, 1, :] = (k > off + q)
        tq = 128 * S + off + q
        for c in range(4):
            mk[bi, :, 2 + c, :] = (16 * (c * 128 + k) + 31 <= tq)
    sh["mk"] = _c(mk, NPBF)
    return sh


def _core_inputs(inp, core):
    b, j = core // 4, core % 4
    SH = OWN0 - 16 * j
    x = np.asarray(inp["x"])[b]
    ci = {}
    xs = np.zeros((T, D), np.float32)
    nreal = (NSLOT - SH) * 128
    xs[SH * 128:] = x[:nreal]
    ci["xs"] = xs
    tp = np.arange(T)
    pos = np.maximum(tp - 128 * SH, 0)
    ci["cosK"], ci["sinK"] = _rope_tab(pos)
    qpos = np.zeros((NBLK, 128), np.int64)
    bonus = np.zeros((NBLK, 128, 128), np.float32)
    sblk = np.arange(128)[None, :]
    for bi, (S, off, nq, col0) in enumerate(BLK):
        tq = 128 * S + off + np.arange(128)
        qpos[bi] = np.maximum(tq - 128 * SH, 0)
        cur = (tq // 64)[:, None]
        forced = (sblk == cur) | (sblk == cur - 1) | (sblk == 2 * SH)
        bonus[bi] = np.where(sblk <= cur, 1e4 * forced, -1e30)
    cq, sq = _rope_tab(qpos.reshape(-1))
    ci["cosQ"], ci["sinQ"] = cq, sq
    ci["bonus"] = _c(bonus.transpose(1, 0, 2))
    nn = np.arange(512)
    cend = np.maximum(16 * (nn - 8 * SH) + 31, 0)
    ci["cosC"], ci["sinC"] = _rope_tab(cend)
    validc = (nn >= 8 * SH) & (nn <= 510)
    ci["biasC"] = _c(np.where(validc, 0.0, NEGB).reshape(4, 128).T)
    vk = (np.arange(NSLOT) >= SH).astype(np.float32)
    ci["validk"] = _c(np.tile(vk[None, :], (128, 1)), NPBF)
    ci["vq"] = _c(np.full((128, 1), 0.0 if j == 0 else 1.0))
    ps = np.zeros((128, 8, 16), np.float32)
    for kc in range(8):
        w = [2, 4, 8, 16][kc // 2]
        for qq in range(16):
            t = 128 * 16 * j + qq
            ps[:, kc, qq] = 1.0 / min(t + 1, w)
    ci["pscale"] = ps
    return ci


def build(dbg=()):
    nc = bass.Bass("TRN2", target_bir_lowering=False)
    dr = {}

    def din(name, shape, dt=F32):
        dr[name] = nc.dram_tensor(name, list(shape), dt, kind="ExternalInput").ap()
        return dr[name]

    din("xs", [T, D])
    din("cosK", [128, T]); din("sinK", [128, T])
    din("cosQ", [128, NBLK * 128]); din("sinQ", [128, NBLK * 128])
    din("bonus", [128, NBLK, 128])
    din("cosC", [128, 512]); din("sinC", [128, 512])
    din("biasC", [128, 4]); din("validk", [128, NSLOT], BF16); din("vq", [128, 1]); din("pscale", [128, 8, 16])
    din("wq", [D, D]); din("wkv", [D, 768]); din("wg", [D, 48]); din("wout", [D, D])
    for x in ("k", "v"):
        din(f"c{x}_w1", [2048, 256]); din(f"c{x}_w2", [256, 64]); din(f"c{x}_posT", [128, 32, 2])
        din(f"c{x}_b1", [128, 2]); din(f"c{x}_b2", [1, 128])
    for i in (0, 1):
        din(f"gmix{i}", [128, 8]); din(f"gffn{i}", [128, 8])
        din(f"wup{i}", [D, 2 * DFF]); din(f"wdn{i}", [DFF, D])
        din(f"cw{i}", [128, 3, 44]); din(f"cb{i}", [128, 44])
    din("gfin", [128, 8]); din("poolw", [128, 8, 256]); din("poolb", [128, 8]); din("pools", [128, 8])
    din("identb", [128, 128], BF16); din("identf", [128, 128])
    din("ef", [128, 16, 128], BF16); din("ovl", [128, 4, 129], BF16); din("mk", [NBLK, 128, 6, 128], BF16)
    out = nc.dram_tensor("out", [128 * NOWN, D], F32, kind="ExternalOutput").ap()
    gscr = nc.dram_tensor("gscr", [NBLK, 48, 128], F32).ap()
    dbgout = {}
    for name, shape, dt in dbg:
        if shape is None:
            dbgout[name] = dt
            continue
        dbgout[name] = nc.dram_tensor("dbg_" + name, list(shape), dt, kind="ExternalOutput").ap()

    with contextlib.ExitStack() as top:
        def SB(st, name, shape, dt):
            return st.enter_context(nc.sbuf_tensor("s_" + name, list(shape), dt))

        def PS(st, name, shape, dt):
            return st.enter_context(nc.psum_tensor("p_" + name, list(shape), dt))

        identb = SB(top, "identb", [128, 128], BF16)
        identf = SB(top, "identf", [128, 128], F32)
        ones32 = SB(top, "ones32", [128, 512], F32)
        onesb = SB(top, "onesb", [128, 128], BF16)
        gsb = SB(top, "gsb", [128, 5, 8], F32)
        vq = SB(top, "vq", [128, 1], F32)
        ph = Phase(nc, "K")
        ph.add("sp", lambda e: e.dma_start(out=identb[:], in_=dr["identb"]), w=["identb"], dma="c")
        ph.add("sp", lambda e: e.dma_start(out=identf[:], in_=dr["identf"]), w=["identf"], dma="c")
        for i, nm in enumerate(("gmix0", "gffn0", "gmix1", "gffn1", "gfin")):
            ph.add("sp", lambda e, i=i, nm=nm: e.dma_start(out=gsb[:, i, :], in_=dr[nm]), w=["gsb"], dma="c")
        ph.add("sp", lambda e: e.dma_start(out=vq[:], in_=dr["vq"]), w=["vq"], dma="c")
        ph.add("dve", lambda e: e.memset(ones32[:], 1.0), w=["ones32"])
        ph.add("dve", lambda e: e.memset(onesb[:], 1.0), w=["onesb"])
        ph.emit()

        oscr = nc.dram_tensor("oscr", [128, 8, NTOK], BF16).ap()
        with contextlib.ExitStack() as att:
            KsT = SB(att, "KsT", [128, T], BF16)
            KwT = SB(att, "KwT", [128, 24 * 128], BF16)
            Vs = SB(att, "Vs", [128, NSLOT, 194], BF16)
            Vw = SB(att, "Vw", [128, 24, 194], BF16)
            kcT = SB(att, "kcT", [128, 512], BF16)
            RCv = SB(att, "RCv", [128, 4, 194], BF16)
            biasC = SB(att, "biasC", [128, 4], F32)
            P = dict(KsT=KsT, KwT=KwT, Vs=Vs, Vw=Vw, kcT=kcT, RCv=RCv, biasC=biasC,
                     identb=identb, ones32=ones32, gsb=gsb, vq=vq)
            with contextlib.ExitStack() as st:
                _phase_A(nc, st, SB, PS, dr, P, dbgout)
            if "stopA" not in dbgout:
                with contextlib.ExitStack() as st:
                    _phase_B(nc, st, SB, PS, dr, gscr, oscr, P, dbgout)
        if "stopA" in dbgout or "stopB" in dbgout:
            return nc
        with contextlib.ExitStack() as rest:
            xres = SB(rest, "xres", [128, 8, NTOK], F32)
            P = dict(xres=xres, onesb=onesb, gsb=gsb, vq=vq, identf=identf, identb=identb)
            with contextlib.ExitStack() as st:
                _phase_C(nc, st, SB, PS, dr, oscr, P, dbgout)
            with contextlib.ExitStack() as st:
                _phase_ffn(nc, st, SB, PS, dr, 0, P, dbgout)
            with contextlib.ExitStack() as st:
                _phase_pool(nc, st, SB, PS, dr, P, dbgout)
            with contextlib.ExitStack() as st:
                _phase_ffn(nc, st, SB, PS, dr, 1, P, dbgout)
            with contextlib.ExitStack() as st:
                _phase_out(nc, st, SB, PS, dr, out, P, dbgout)
    return nc


def _phase_A(nc, st, SB, PS, dr, P, dbgout):
    KsT, KwT, Vs, Vw, kcT, RCv = P["KsT"], P["KwT"], P["Vs"], P["Vw"], P["kcT"], P["RCv"]
    identb, ones32, gsb = P["identb"], P["ones32"], P["gsb"]
    rawK = SB(st, "rawK", [128, T], BF16)
    rawV = SB(st, "rawV", [128, T], BF16)
    t1a = SB(st, "t1a", [128, 512], F32)
    t1b = SB(st, "t1b", [128, 512], F32)
    ist = contextlib.ExitStack()
    WA = SB(ist, "WA", [128, 8, 768], BF16)
    WAr = SB(ist, "WAr", [128, 8, 256], BF16)
    validk = SB(ist, "validk", [128, NSLOT], BF16)
    xt = [SB(ist, f"xt{i}", [128, D], F32) for i in range(3)]
    junk = SB(ist, "junk", [128, D], BF16)
    ssq = [SB(ist, f"ssq{i}", [128, 1], F32) for i in range(3)]
    rstd = [SB(ist, f"rstd{i}", [128, 1], F32) for i in range(3)]
    xn = [SB(ist, f"xn{i}", [128, D], BF16) for i in range(2)]
    hT = [SB(ist, f"hT{i}", [128, 8, 512], BF16) for i in range(2)]
    cs = [SB(ist, f"cs{i}", [128, 512], F32) for i in range(2)]
    sn = [SB(ist, f"sn{i}", [128, 512], F32) for i in range(2)]
    pst = contextlib.ExitStack()
    tp = [PS(pst, f"tp{i}", [128, D], BF16) for i in range(2)]
    pk = [PS(pst, f"pk{i}", [128, 512], F32) for i in range(4)]
    pv = PS(pst, "pv", [128, 512], F32)

    ph = Phase(nc, "A")
    A = ph.add
    A("pool", lambda e: e.dma_start(out=WA[:], in_=dr["wkv"].rearrange("(k p) n -> p k n", p=128)), w=["WA"], dma="w")
    A("sp", lambda e: e.dma_start(out=validk[:], in_=dr["validk"]), w=["validk"], dma="c")
    A("sp", lambda e: e.dma_start(out=P["biasC"][:], in_=dr["biasC"]), w=["biasC"], dma="c")
    for i, c0 in enumerate((256, 512)):
        for g in range(2):
            A("dve", lambda e, i=i, c0=c0, g=g: e.tensor_scalar(
                out=WAr[:, :, i * 128 + g * 64: i * 128 + g * 64 + 32], in0=WA[:, :, c0 + g * 64 + 32: c0 + g * 64 + 64],
                scalar1=-1.0, scalar2=None, op0=ALU.mult), r=["WA"], w=[("WAr", i, g, 0)])
            A("dve", lambda e, i=i, c0=c0, g=g: e.tensor_copy(
                out=WAr[:, :, i * 128 + g * 64 + 32: i * 128 + g * 64 + 64], in_=WA[:, :, c0 + g * 64: c0 + g * 64 + 32]),
              r=["WA"], w=[("WAr", i, g, 1)])
    WArk = [("WAr", i, g, h) for i in range(2) for g in range(2) for h in range(2)]
    A("pool", lambda e: e.memset(Vs[:], 0.0), w=["Vs0"])
    A("pool", lambda e: e.memset(Vw[:], 0.0), w=["Vw0"])
    A("pool", lambda e: e.memset(RCv[:], 0.0), w=["RCv"])
    A("dve", lambda e: e.tensor_copy(out=Vs[:, :, 64:65], in_=validk[:].unsqueeze(2)), r=["validk", "Vs0"], w=["Vs1"])
    A("dve", lambda e: e.tensor_copy(out=Vs[:, :, 66:67], in_=validk[:].unsqueeze(2)), r=["validk", "Vs0"], w=["Vs2"])
    A("dve", lambda e: e.tensor_copy(out=Vw[:, :, 64:65], in_=validk[:, 40:64].unsqueeze(2)), r=["validk", "Vw0"], w=["Vw1"])
    A("dve", lambda e: e.tensor_copy(out=Vw[:, :, 66:67], in_=validk[:, 40:64].unsqueeze(2)), r=["validk", "Vw0"], w=["Vw2"])

    for G in range(16):
        hb = hT[G % 2]
        hk = ("hT", G % 2)
        A("sp", lambda e, G=G: e.dma_start(out=cs[G % 2][:], in_=dr["cosK"][:, G * 512:(G + 1) * 512]),
          w=[("cs", G % 2)], dma=("cs", G % 2))
        A("sp", lambda e, G=G: e.dma_start(out=sn[G % 2][:], in_=dr["sinK"][:, G * 512:(G + 1) * 512]),
          w=[("sn", G % 2)], dma=("cs", G % 2))
        for si in range(4):
            s = 4 * G + si
            b3, b2 = s % 3, s % 2
            A("sp", lambda e, s=s, b3=b3: e.dma_start(out=xt[b3][:], in_=dr["xs"][s * 128:(s + 1) * 128, :]),
              w=[("xt", b3)], dma=("x", b3))
            A("act", lambda e, b3=b3: e.activation(out=junk[:], in_=xt[b3][:], func=AF.Square, accum_out=ssq[b3][:]),
              r=[("xt", b3)], w=["junk", ("ssq", b3)])
            A("act", lambda e, b3=b3: e.activation(out=rstd[b3][:], in_=ssq[b3][:], func=AF.Sqrt, bias=EPS, scale=1.0 / D),
              r=[("ssq", b3)], w=[("rstd", b3)])
            A("dve", lambda e, b3=b3: e.reciprocal(out=rstd[b3][:], in_=rstd[b3][:]), r=[("rstd", b3)], w=[("rstd", b3)])
            A("dve", lambda e, b3=b3, b2=b2: e.tensor_scalar(out=xn[b2][:], in0=xt[b3][:], scalar1=rstd[b3][:], scalar2=None,
                                                           op0=ALU.mult), r=[("xt", b3), ("rstd", b3)], w=[("xn", b2)])
            for kc in range(8):
                A("pe", lambda e, kc=kc, b2=b2: e.transpose(out=tp[b2][:, kc * 128:(kc + 1) * 128],
                                                            in_=xn[b2][:, kc * 128:(kc + 1) * 128], identity=identb[:]),
                  r=[("xn", b2)], w=[("tp", b2)])
            A("dve", lambda e, b2=b2, hb=hb, si=si: e.tensor_tensor(
                out=hb[:, :, si * 128:(si + 1) * 128], in0=tp[b2][:].rearrange("p (k t) -> p k t", k=8),
                in1=gsb[:, 0, :].unsqueeze(2).to_broadcast([128, 8, 128]), op=ALU.mult),
              r=[("tp", b2)], w=[hk + (si,)])
        hks = [hk + (si,) for si in range(4)]

        def proj(W, c0, bank, bk, wk, hb=hb, hks=hks):
            for kc in range(8):
                A("pe", lambda e, kc=kc, W=W, c0=c0, bank=bank, hb=hb: e.matmul(bank[:], lhsT=W[:, kc, c0:c0 + 128], rhs=hb[:, kc, :],
                                                                          start=(kc == 0), stop=(kc == 7)),
                  r=hks + wk, w=[bk])
        proj(WA, 0, pk[0], "pk0", ["WA"])
        proj(WA, 128, pk[1], "pk1", ["WA"])
        proj(WA, 256, pk[2], "pk2", ["WA"])
        proj(WAr, 0, pk[3], "pk3", WArk)
        A("act", lambda e, G=G: e.copy(out=rawK[:, G * 512:(G + 1) * 512], in_=pk[0][:]), r=["pk0"], w=[("rawK", G)])
        A("act", lambda e, G=G: e.copy(out=rawV[:, G * 512:(G + 1) * 512], in_=pk[1][:]), r=["pk1"], w=[("rawV", G)])

        def ropeevac(b0, b0k, b1, b1k, dst, dk, G=G):
            A("dve", lambda e: e.tensor_tensor(out=t1a[:], in0=b0[:], in1=cs[G % 2][:], op=ALU.mult),
              r=[b0k, ("cs", G % 2)], w=["t1a"])
            A("dve", lambda e: e.tensor_tensor(out=t1b[:], in0=b1[:], in1=sn[G % 2][:], op=ALU.mult),
              r=[b1k, ("sn", G % 2)], w=["t1b"])
            A("dve", lambda e: e.tensor_tensor(out=dst, in0=t1a[:], in1=t1b[:], op=ALU.add), r=["t1a", "t1b"], w=[dk])
        ropeevac(pk[2], "pk2", pk[3], "pk3", KsT[:, G * 512:(G + 1) * 512], ("KsT", G))
        if G >= 10:
            proj(WA, 512, pk[0], "pk0", ["WA"])
            proj(WAr, 128, pk[1], "pk1", WArk)
            ropeevac(pk[0], "pk0", pk[1], "pk1", KwT[:, (G - 10) * 512:(G - 9) * 512], ("KwT", G))
        for si in range(4):
            s = 4 * G + si
            nv = 2 if G >= 10 else 1
            for vi in range(nv):
                c0 = 384 if vi == 0 else 640
                for kc in range(8):
                    A("pe", lambda e, kc=kc, si=si, vi=vi, c0=c0, hb=hb: e.matmul(
                        pv[:, vi * 128:(vi + 1) * 128], lhsT=hb[:, kc, si * 128:(si + 1) * 128], rhs=WA[:, kc, c0:c0 + 128],
                        start=(kc == 0), stop=(kc == 7)), r=[hk + (si,), "WA"], w=["pv"])
            A("act", lambda e, s=s: e.copy(out=Vs[:, s, 0:64], in_=pv[:, 0:64]), r=["pv", "Vs0"], w=[("Vs", s, 0)])
            A("act", lambda e, s=s: e.copy(out=Vs[:, s, 130:194], in_=pv[:, 64:128]), r=["pv", "Vs0"], w=[("Vs", s, 1)])
            if G >= 10:
                A("act", lambda e, s=s: e.copy(out=Vw[:, s - 40, 0:64], in_=pv[:, 128:192]), r=["pv", "Vw0"], w=[("Vw", s, 0)])
                A("act", lambda e, s=s: e.copy(out=Vw[:, s - 40, 130:194], in_=pv[:, 192:256]), r=["pv", "Vw0"], w=[("Vw", s, 1)])
    if "KsT" in dbgout:
        A("sp", lambda e: e.dma_start(out=dbgout["KsT"], in_=KsT[:]), r=[("KsT", G) for G in range(16)], dma="dbg")
        A("sp", lambda e: e.dma_start(out=dbgout["Vs"], in_=Vs[:]), r=[("Vs", s, i) for s in range(64) for i in range(2)] + ["Vs1", "Vs2"], dma="dbg")
        A("sp", lambda e: e.dma_start(out=dbgout["KwT"], in_=KwT[:]), r=[("KwT", G) for G in range(10, 16)], dma="dbg")
    ph.emit()
    pst.close()
    ist.close()
    _phase_A2(nc, st, SB, PS, dr, P, dbgout, rawK, rawV, t1a, t1b)


def _phase_A2(nc, st0, SB, PS, dr, P, dbgout, rawK, rawV, t1a, t1b):
    kcT, RCv, ones32 = P["kcT"], P["RCv"], P["ones32"]
    with contextlib.ExitStack() as st:
        w1 = [SB(st, f"w1{x}", [128, 32, 256], BF16) for x in range(2)]
        w2 = [SB(st, f"w2{x}", [128, 2, 64], BF16) for x in range(2)]
        posT = [SB(st, f"posT{x}", [128, 32, 2], BF16) for x in range(2)]
        b1 = [SB(st, f"b1{x}", [128, 2], F32) for x in range(2)]
        b1e = [SB(st, f"b1e{x}", [128, 2], F32) for x in range(2)]
        b2row = [SB(st, f"b2row{x}", [1, 128], F32) for x in range(2)]
        b2rrow = SB(st, "b2rrow", [1, 128], F32)
        w2pad = [SB(st, f"w2pad{g}", [128, 2, 128], BF16) for g in range(2)]
        w2padr = [SB(st, f"w2padr{g}", [128, 2, 128], BF16) for g in range(2)]
        hid = [[SB(st, f"hid{x}{g}", [128, 2, 512], BF16) for g in range(2)] for x in range(2)]
        u = SB(st, "u", [128, 512], F32)
        u2 = SB(st, "u2", [128, 512], F32)
        th = SB(st, "th", [128, 512], F32)
        csC = SB(st, "csC", [128, 512], F32)
        snC = SB(st, "snC", [128, 512], F32)
        pb = PS(st, "pb", [128, 512], F32)
        ph_ = [PS(st, f"ph{i}", [128, 512], F32) for i in range(2)]
        po = [PS(st, f"po{i}", [128, 512], F32) for i in range(2)]
        ph = Phase(nc, "A2")
        A = ph.add
        NCMP = 511
        for x, nm in enumerate(("k", "v")):
            src = dr[f"c{nm}_w1"].rearrange("(p d) h -> d p h", d=64)
            A("pool", lambda e, x=x, src=src: e.dma_start(out=w1[x][0:64], in_=src), w=[("w1", x, 0)], dma="w")
            A("pool", lambda e, x=x, src=src: e.dma_start(out=w1[x][64:128], in_=src), w=[("w1", x, 1)], dma="w")
            A("pool", lambda e, x=x, nm=nm: e.dma_start(out=w2[x][:], in_=dr[f"c{nm}_w2"].rearrange("(k p) d -> p k d", p=128)),
              w=[("w2", x)], dma="w")
            A("pool", lambda e, x=x, nm=nm: e.dma_start(out=posT[x][:], in_=dr[f"c{nm}_posT"]), w=[("posT", x)], dma="w")
            A("sp", lambda e, x=x, nm=nm: e.dma_start(out=b1[x][:], in_=dr[f"c{nm}_b1"]), w=[("b1", x)], dma="c")
            A("sp", lambda e, x=x, nm=nm: e.dma_start(out=b2row[x][:], in_=dr[f"c{nm}_b2"]), w=[("b2row", x)], dma="c")
        A("sp", lambda e: e.dma_start(out=csC[:], in_=dr["cosC"]), w=["csC"], dma="c")
        A("sp", lambda e: e.dma_start(out=snC[:], in_=dr["sinC"]), w=["snC"], dma="c")
        for g in range(2):
            A("dve", lambda e, g=g: e.memset(w2pad[g][:], 0.0), w=[("w2pad", g)])
            A("dve", lambda e, g=g: e.memset(w2padr[g][:], 0.0), w=[("w2padr", g)])
            A("dve", lambda e, g=g: e.tensor_copy(out=w2pad[g][:, :, g * 64:(g + 1) * 64], in_=w2[0][:]),
              r=[("w2", 0)], w=[("w2pad", g)])
            A("dve", lambda e, g=g: e.tensor_scalar(out=w2padr[g][:, :, g * 64:g * 64 + 32], in0=w2[0][:, :, 32:64],
                                                     scalar1=-1.0, scalar2=None, op0=ALU.mult), r=[("w2", 0)], w=[("w2padr", g)])
            A("dve", lambda e, g=g: e.tensor_copy(out=w2padr[g][:, :, g * 64 + 32:g * 64 + 64], in_=w2[0][:, :, 0:32]),
              r=[("w2", 0)], w=[("w2padr", g)])
            A("dve", lambda e, g=g: e.tensor_scalar(out=b2rrow[0:1, g * 64:g * 64 + 32], in0=b2row[0][0:1, g * 64 + 32:g * 64 + 64],
                                                     scalar1=-1.0, scalar2=None, op0=ALU.mult), r=[("b2row", 0)], w=["b2rrow"])
            A("dve", lambda e, g=g: e.tensor_copy(out=b2rrow[0:1, g * 64 + 32:g * 64 + 64], in_=b2row[0][0:1, g * 64:g * 64 + 32]),
              r=[("b2row", 0)], w=["b2rrow"])
        for x in range(2):
            raw = rawK if x == 0 else rawV
            for half in range(2):
                for p in range(32):
                    A("pe", lambda e, x=x, half=half, p=p: e.matmul(
                        pb[:, half * 2:half * 2 + 2], lhsT=w1[x][0:64, p, half * 128:(half + 1) * 128], rhs=posT[x][0:64, p, :],
                        start=(p == 0), stop=(p == 31)), r=[("w1", x, 0), ("posT", x)], w=["pb"])
            A("dve", lambda e, x=x: e.tensor_tensor(out=b1e[x][:], in0=pb[:, 0:4:2], in1=b1[x][:], op=ALU.add),
              r=["pb", ("b1", x)], w=[("b1e", x)])
            for g in range(2):
                for half in range(2):
                    bank = ph_[half]
                    for p in range(32):
                        A("pe", lambda e, x=x, g=g, half=half, p=p, raw=raw, bank=bank: e.matmul(
                            bank[:, 0:NCMP], lhsT=w1[x][64 * g:64 * g + 64, p, half * 128:(half + 1) * 128],
                            rhs=raw[64 * g:64 * g + 64, p:p + 16 * (NCMP - 1) + 1:16],
                            start=(p == 0), stop=(p == 31)), r=[("w1", x, g)], w=[("ph", half)])
                    A("act", lambda e, x=x, half=half, bank=bank: e.activation(out=u[:, 0:NCMP], in_=bank[:, 0:NCMP], func=AF.Identity,
                                                                               bias=b1e[x][:, half:half + 1], scale=1.0),
                      r=[("ph", half), ("b1e", x)], w=["u"])
                    A("dve", lambda e: e.tensor_tensor(out=u2[:, 0:NCMP], in0=u[:, 0:NCMP], in1=u[:, 0:NCMP], op=ALU.mult), r=["u"], w=["u2"])
                    A("dve", lambda e: e.tensor_scalar(out=u2[:, 0:NCMP], in0=u2[:, 0:NCMP], scalar1=0.044715, scalar2=1.0,
                                                        op0=ALU.mult, op1=ALU.add), r=["u2"], w=["u2"])
                    A("dve", lambda e: e.tensor_tensor(out=u2[:, 0:NCMP], in0=u2[:, 0:NCMP], in1=u[:, 0:NCMP], op=ALU.mult), r=["u2", "u"], w=["u2"])
                    A("act", lambda e: e.activation(out=th[:, 0:NCMP], in_=u2[:, 0:NCMP], func=AF.Tanh, scale=0.7978845608028654),
                      r=["u2"], w=["th"])
                    A("dve", lambda e: e.tensor_scalar(out=th[:, 0:NCMP], in0=th[:, 0:NCMP], scalar1=0.5, scalar2=0.5,
                                                        op0=ALU.mult, op1=ALU.add), r=["th"], w=["th"])
                    A("dve", lambda e, x=x, g=g, half=half: e.tensor_tensor(out=hid[x][g][:, half, 0:NCMP], in0=th[:, 0:NCMP],
                                                                             in1=u[:, 0:NCMP], op=ALU.mult),
                      r=["th", "u"], w=[("hid", x, g, half)])
        for r_, (pads, brow, bank, bk) in enumerate(((w2pad, b2row[0], po[0], "po0"), (w2padr, b2rrow, po[1], "po1"))):
            first = True
            for g in range(2):
                for half in range(2):
                    A("pe", lambda e, g=g, half=half, pads=pads, bank=bank, first=first: e.matmul(
                        bank[:, 0:NCMP], lhsT=pads[g][:, half, :], rhs=hid[0][g][:, half, 0:NCMP], start=first, stop=False),
                      r=[("hid", 0, g, half), ("w2pad", g), ("w2padr", g)], w=[bk])
                    first = False
            A("pe", lambda e, brow=brow, bank=bank: e.matmul(bank[:, 0:NCMP], lhsT=brow[0:1, 0:128], rhs=ones32[0:1, 0:NCMP],
                                                              start=False, stop=True), r=[("b2row", 0), "b2rrow"], w=[bk])
        A("dve", lambda e: e.tensor_tensor(out=t1a[:, 0:NCMP], in0=po[0][:, 0:NCMP], in1=csC[:, 0:NCMP], op=ALU.mult),
          r=["po0", "csC"], w=["t1a"])
        A("dve", lambda e: e.tensor_tensor(out=t1b[:, 0:NCMP], in0=po[1][:, 0:NCMP], in1=snC[:, 0:NCMP], op=ALU.mult),
          r=["po1", "snC"], w=["t1b"])
        A("dve", lambda e: e.memset(kcT[:], 0.0), w=["kcT"])
        A("dve", lambda e: e.tensor_tensor(out=kcT[:, 0:NCMP], in0=t1a[:, 0:NCMP], in1=t1b[:, 0:NCMP], op=ALU.add),
          r=["t1a", "t1b"], w=["kcT"])
        for c in range(4):
            n = 128 if c < 3 else NCMP - 384
            bank = po[c % 2]
            bk = f"po{c % 2}"
            for g in range(2):
                for half in range(2):
                    A("pe", lambda e, c=c, n=n, g=g, half=half, bank=bank: e.matmul(
                        bank[0:n, g * 64:(g + 1) * 64], lhsT=hid[1][g][:, half, c * 128:c * 128 + n], rhs=w2[1][:, half, :],
                        start=(half == 0), stop=False), r=[("hid", 1, g, half), ("w2", 1)], w=[bk])
                A("pe", lambda e, n=n, g=g, bank=bank: e.matmul(
                    bank[0:n, g * 64:(g + 1) * 64], lhsT=ones32[0:1, 0:n], rhs=b2row[1][0:1, g * 64:(g + 1) * 64],
                    start=False, stop=True), r=[("b2row", 1)], w=[bk])
            A("act", lambda e, c=c, n=n, bank=bank: e.copy(out=RCv[0:n, c, 0:64], in_=bank[0:n, 0:64]), r=[bk, "RCv"], w=[("RCv", c, 0)])
            A("act", lambda e, c=c, n=n, bank=bank: e.copy(out=RCv[0:n, c, 130:194], in_=bank[0:n, 64:128]), r=[bk, "RCv"], w=[("RCv", c, 1)])
            A("dve", lambda e, c=c, n=n: e.memset(RCv[0:n, c, 64:65], 1.0), r=["RCv"], w=[("RCv", c, 2)])
            A("dve", lambda e, c=c, n=n: e.memset(RCv[0:n, c, 66:67], 1.0), r=["RCv"], w=[("RCv", c, 3)])
        if "kcT" in dbgout:
            A("sp", lambda e: e.dma_start(out=dbgout["kcT"], in_=kcT[:]), r=["kcT"], dma="dbg")
            A("sp", lambda e: e.dma_start(out=dbgout["RCv"], in_=RCv[:]), r=[("RCv", c, i) for c in range(4) for i in range(4)], dma="dbg")
        ph.emit()


def _phase_B(nc, st, SB, PS, dr, gscr, oscr, P, dbgout):
    KsT, KwT, Vs, Vw, kcT, RCv, biasC = P["KsT"], P["KwT"], P["Vs"], P["Vw"], P["kcT"], P["RCv"], P["biasC"]
    ones32, gsb, vq = P["ones32"], P["gsb"], P["vq"]
    identf = SB(st, "identfB", [128, 128], F32)
    Wq = SB(st, "Wq", [128, 8, 1024], BF16)
    Wqr = SB(st, "Wqr", [128, 8, 1024], BF16)
    Wg = SB(st, "Wg", [128, 8, 48], BF16)
    EF = SB(st, "EF", [128, 16, 128], BF16)
    ovl = SB(st, "ovl", [128, 4, 129], BF16)
    xq1 = SB(st, "xq0", [128, D], F32)
    xq = [xq1, xq1]
    xnf = SB(st, "xnf", [128, D], F32)
    ssq = SB(st, "ssqB", [128, 1], F32)
    rstd = SB(st, "rstdB", [128, 1], F32)
    hTq = SB(st, "hTq", [128, 8, 128], BF16)
    cq1 = SB(st, "cq0", [128, 128], F32)
    sq1 = SB(st, "sq0", [128, 128], F32)
    cq = [cq1, cq1]
    sq = [sq1, sq1]
    bon3 = [SB(st, f"bon{i}", [128, 128], F32) for i in range(3)]
    mkb3 = [SB(st, f"mkb{i}", [128, 6, 128], BF16) for i in range(3)]
    QT = [SB(st, f"QT{i}", [128, 8, 128], BF16) for i in range(2)]
    gsig = [SB(st, f"gsig{i}", [48, 128], F32) for i in range(2)]
    selT = [[SB(st, f"selT{i}{g}", [128, 128], BF16) for g in range(2)] for i in range(2)]
    acc = [SB(st, f"acc{i}", [128, 8, 128], F32) for i in range(2)]
    Ec = [SB(st, f"Ec{c}", [128, 8, 128], BF16) for c in range(4)]
    NBUF = 5
    E = [SB(st, f"E{i}", [128, 8, 128], BF16) for i in range(NBUF)]
    Pb = [SB(st, f"Pb{i}", [128, 8, 128], BF16) for i in range(NBUF)]
    oasb = [SB(st, f"oasb{i}", [128, 1024], F32) for i in range(2)]
    msk = [SB(st, f"msk{i}", [128, 128], BF16) for i in range(NBUF)]
    t1 = SB(st, "t1B", [128, 8, 128], F32)
    t2 = SB(st, "t2B", [128, 8, 128], F32)
    dsb = SB(st, "dsb", [65, 1024], F32)
    rdb = SB(st, "rdb", [128, 4, 128], F32)
    tmpf = SB(st, "tmpf", [128, 4, 128], F32)
    grow = [SB(st, f"grow{i}", [65, 1024], F32) for i in range(2)]
    dsb2 = SB(st, "dsb2", [65, 1024], F32)
    cbc = SB(st, "cbc", [128, 1024], F32)
    crow = nc.dram_tensor("crow", [8, 1024], F32).ap()
    score = SB(st, "score", [128, 128], F32)
    work = SB(st, "work", [128, 128], F32)
    selq = SB(st, "selq", [128, 128], F32)
    m8a = SB(st, "m8a", [128, 8], F32)
    m8b = SB(st, "m8b", [128, 8], F32)
    thr = SB(st, "thr", [128, 1], F32)
    rc = SB(st, "rc", [128, 8], F32)
    obf = SB(st, "obf", [128, 8, 128], BF16)
    scA = PS(st, "scA", [128, 1024], F32)
    scB = PS(st, "scB", [128, 1024], F32)
    oa = PS(st, "oa", [128, 1024], F32)
    mx = PS(st, "mx", [128, 512], F32)
    msc = PS(st, "msc", [128, 512], F32)
    gs2 = gscr.rearrange("b r q -> b (r q)")
    print("[phase B] sbuf bytes remaining:", nc.sbuf_bytes_remaining)

    ph = Phase(nc, "B")
    A = ph.add
    A("pool", lambda e: e.dma_start(out=Wq[:], in_=dr["wq"].rearrange("(k p) n -> p k n", p=128)), w=["Wq"], dma="w")
    A("pool", lambda e: e.dma_start(out=Wg[:], in_=dr["wg"].rearrange("(k p) n -> p k n", p=128)), w=["Wg"], dma="w")
    A("sp", lambda e: e.dma_start(out=EF[:], in_=dr["ef"]), w=["EF"], dma="c")
    A("sp", lambda e: e.dma_start(out=ovl[:], in_=dr["ovl"]), w=["ovl"], dma="c")
    A("sp", lambda e: e.dma_start(out=identf[:], in_=dr["identf"]), w=["identf"], dma="c")
    for kc in range(8):
        v = Wq[:, kc, :].rearrange("p (h two d) -> p h two d", two=2, d=32)
        vr = Wqr[:, kc, :].rearrange("p (h two d) -> p h two d", two=2, d=32)
        A("dve", lambda e, v=v, vr=vr: e.tensor_scalar(out=vr[:, :, 0, :], in0=v[:, :, 1, :], scalar1=-1.0, scalar2=None, op0=ALU.mult),
          r=["Wq"], w=[("Wqr", kc, 0)])
        A("dve", lambda e, v=v, vr=vr: e.tensor_copy(out=vr[:, :, 1, :], in_=v[:, :, 0, :]), r=["Wq"], w=[("Wqr", kc, 1)])
    Wqrk = [("Wqr", kc, i) for kc in range(8) for i in range(2)]

    def v8(t, nq):
        return t[:].rearrange("p (h q) -> p h q", q=128)[:, :, 0:nq]

    def v4(t, u, nq, p0=0, p1=128):
        return t[p0:p1, u * 512:(u + 1) * 512].rearrange("p (h q) -> p h q", q=128)[:, :, 0:nq]

    def bc(ap2, n, nq):
        return ap2.unsqueeze(1).to_broadcast([ap2.shape[0], n, nq])

    def stage_load(bi):
        S, off, nq, col0 = BLK[bi]
        t0 = 128 * S + off
        b3 = bi % 3
        A("sp", lambda e: e.dma_start(out=xq1[0:nq, :], in_=dr["xs"][t0:t0 + nq, :]), w=["xqb"], dma="ldq")
        A("sp", lambda e: e.dma_start(out=cq1[:, 0:nq], in_=dr["cosQ"][:, bi * 128:bi * 128 + nq]), w=["cqb"], dma="ldq")
        A("sp", lambda e: e.dma_start(out=sq1[:, 0:nq], in_=dr["sinQ"][:, bi * 128:bi * 128 + nq]), w=["sqb"], dma="ldq")
        A("sp", lambda e: e.dma_start(out=bon3[b3][0:nq, :], in_=dr["bonus"][0:nq, bi, :]), w=[("bon", b3)], dma=("ldm", b3))
        A("sp", lambda e: e.dma_start(out=mkb3[b3][:], in_=dr["mk"][bi]), w=[("mkb", b3)], dma=("ldm", b3))

    def stage_q(bi):
        S, off, nq, col0 = BLK[bi]
        pb = bi % 2
        A("act", lambda e: e.activation(out=xnf[0:nq, :], in_=xq[pb][0:nq, :], func=AF.Square, accum_out=ssq[0:nq, :]),
          r=["xqb"], w=["xnf", "ssq"])
        A("act", lambda e: e.activation(out=rstd[0:nq, :], in_=ssq[0:nq, :], func=AF.Sqrt, bias=EPS, scale=1.0 / D), r=["ssq"], w=["rstd"])
        A("dve", lambda e: e.reciprocal(out=rstd[0:nq, :], in_=rstd[0:nq, :]), r=["rstd"], w=["rstd"])
        A("dve", lambda e: e.tensor_scalar(out=xnf[0:nq, :], in0=xq[pb][0:nq, :], scalar1=rstd[0:nq, :], scalar2=None, op0=ALU.mult),
          r=["xqb", "rstd"], w=["xnf"])
        for kc in range(8):
            A("pe", lambda e, kc=kc: e.transpose(out=scA[:, kc * 128:kc * 128 + nq], in_=xnf[0:nq, kc * 128:(kc + 1) * 128],
                                                 identity=identf[0:nq, 0:nq]), r=["xnf", "identf"], w=[("scA", kc // 4)])
        A("dve", lambda e: e.tensor_tensor(out=hTq[:, :, 0:nq], in0=v8(scA, nq), in1=gsb[:, 0, :].unsqueeze(2).to_broadcast([128, 8, nq]),
                                           op=ALU.mult), r=[("scA", 0), ("scA", 1)], w=["hTq"])
        for hl in range(8):
            for kc in range(8):
                A("pe", lambda e, hl=hl, kc=kc: e.matmul(scB[:, hl * 128:hl * 128 + nq], lhsT=Wq[:, kc, hl * 128:(hl + 1) * 128],
                                                         rhs=hTq[:, kc, 0:nq], start=(kc == 0), stop=(kc == 7)),
                  r=["Wq", "hTq"], w=[("scB", hl // 4)])
        for hl in range(8):
            for kc in range(8):
                A("pe", lambda e, hl=hl, kc=kc: e.matmul(oa[:, hl * 128:hl * 128 + nq], lhsT=Wqr[:, kc, hl * 128:(hl + 1) * 128],
                                                         rhs=hTq[:, kc, 0:nq], start=(kc == 0), stop=(kc == 7)),
                  r=Wqrk + ["hTq"], w=[("oa", hl // 4)])
        A("dve", lambda e: e.tensor_tensor(out=t1[:, :, 0:nq], in0=v8(scB, nq), in1=bc(cq[pb][:, 0:nq], 8, nq), op=ALU.mult),
          r=[("scB", 0), ("scB", 1), "cqb"], w=[("t1", 0), ("t1", 3), ("t1", 6)])
        A("dve", lambda e: e.tensor_tensor(out=t2[:, :, 0:nq], in0=v8(oa, nq), in1=bc(sq[pb][:, 0:nq], 8, nq), op=ALU.mult),
          r=[("oa", 0), ("oa", 1), "sqb"], w=["t2"])
        A("dve", lambda e: e.tensor_tensor(out=QT[pb][:, :, 0:nq], in0=t1[:, :, 0:nq], in1=t2[:, :, 0:nq], op=ALU.add),
          r=[("t1", 0), ("t1", 3), ("t1", 6), "t2"], w=[("QT", pb)])
        for kc in range(8):
            A("pe", lambda e, kc=kc: e.matmul(mx[0:48, 0:nq], lhsT=Wg[:, kc, :], rhs=hTq[:, kc, 0:nq], start=(kc == 0), stop=(kc == 7)),
              r=["Wg", "hTq"], w=MXALL)
        A("act", lambda e: e.activation(out=gsig[pb][:, 0:nq], in_=mx[0:48, 0:nq], func=AF.Sigmoid), r=MXALL, w=[("gsig", pb)])
        A("sp", lambda e: e.dma_start(out=gscr[bi, :, 0:nq], in_=gsig[pb][:, 0:nq]), r=[("gsig", pb)], w=[("gscr", bi)], dma=("gs", pb))

    growi = [0]
    grpi = [0]
    MXALL = ["mx"]

    DEFER = True
    pending = []
    stepc = [0]

    def flush(force=False):
        while pending and (force or pending[0][0] <= stepc[0]):
            pending.pop(0)[1]()

    fini = [0]

    def finalize(bi, g, br, first, src, srck, delay):
        S, off, nq, col0 = BLK[bi]
        pb = bi % 2
        p0 = 64 * g
        dp = 64 if g == 0 else 0
        flush(force=True)
        fi = fini[0]
        fini[0] += 1
        X, xk = (dsb, "dsb") if fi % 2 == 0 else (dsb2, "dsb2")
        ri = fi % 8
        gi = growi[0] % 2
        growi[0] += 1
        r0 = br * 16 + g * 8
        A("sp", lambda e: e.dma_start(out=grow[gi][dp:dp + 1, :], in_=gs2[bi:bi + 1, r0 * 128:r0 * 128 + 1024]),
          r=[("gscr", bi)], w=[("grow", gi)], dma=("gr", gi))
        A("dve", lambda e: e.tensor_scalar(out=X[dp:dp + 1, :], in0=src[dp:dp + 1, :], scalar1=1.0e-30, scalar2=None, op0=ALU.max),
          r=[srck(0), srck(1)], w=[xk])
        A("act", lambda e: e.activation(out=X[dp:dp + 1, :], in_=X[dp:dp + 1, :], func=AF.Ln), r=[xk], w=[xk])
        A("act", lambda e: e.activation(out=X[dp:dp + 1, :], in_=X[dp:dp + 1, :], func=AF.Exp, scale=-1.0), r=[xk], w=[xk])
        A("dve", lambda e: e.tensor_tensor(out=X[dp:dp + 1, :], in0=X[dp:dp + 1, :], in1=grow[gi][dp:dp + 1, :], op=ALU.mult),
          r=[xk, ("grow", gi)], w=[xk])
        A("sp", lambda e: e.dma_start(out=crow[ri:ri + 1, :], in_=X[dp:dp + 1, :]), r=[xk], w=[("crow", ri)], dma=("cr", ri % 2))
        A("sp", lambda e: e.dma_start(out=cbc[p0:p0 + 64, :], in_=crow[ri:ri + 1, :].partition_broadcast(64)),
          r=[("crow", ri)], w=[("cbc", g)], dma=("cb", g))
        def tail():
            for u in range(2):
                dst = acc[pb][p0:p0 + 64, 4 * u:4 * u + 4, 0:nq]
                if first:
                    A("dve", lambda e, u=u, dst=dst: e.tensor_tensor(out=dst, in0=v4(src, u, nq, p0, p0 + 64), in1=v4(cbc, u, nq, p0, p0 + 64),
                                                                     op=ALU.mult), r=[srck(u), ("cbc", g)], w=[("acc", pb, g, u)])
                else:
                    A("dve", lambda e, u=u: e.tensor_tensor(out=tmpf[p0:p0 + 64, :, 0:nq], in0=v4(src, u, nq, p0, p0 + 64),
                                                            in1=v4(cbc, u, nq, p0, p0 + 64), op=ALU.mult), r=[srck(u), ("cbc", g)], w=["tmpf"])
                    A("dve", lambda e, dst=dst: e.tensor_tensor(out=dst, in0=dst, in1=tmpf[p0:p0 + 64, :, 0:nq], op=ALU.add),
                      r=["tmpf", ("acc", pb, g, u)], w=[("acc", pb, g, u)])
        if DEFER:
            pending.append((stepc[0] + delay, tail))
        else:
            tail()

    def vaug(Vt, idx, g):
        return Vt[:, idx, 0:128] if g == 0 else Vt[:, idx, 66:194]

    def stage_cmp(bi):
        fins = [stage_cmp_g(bi, g) for g in range(2)]
        for g, (ob, obk) in enumerate(fins):
            finalize(bi, g, 0, True, ob, lambda u, obk=obk: obk + (u,), 4)

    def stage_cmp_g(bi, g):
        S, off, nq, col0 = BLK[bi]
        pb = bi % 2
        M = [128, 128]
        if True:
            for c in range(4):
                sc, sk = (scA, "scA") if c % 2 == 0 else (scB, "scB")
                for u in range(2):
                    A("pe", lambda e, c=c, u=u, sc=sc: e.matmul(v4(sc, u, nq), lhsT=kcT[64 * g:64 * g + 64, c * 128:(c + 1) * 128],
                                                              rhs=QT[pb][64 * g:64 * g + 64, 4 * u:4 * u + 4, 0:nq], start=True, stop=True),
                      r=["kcT", ("QT", pb)], w=[(sk, u)])
                A("act", lambda e, c=c, sc=sc: e.activation(out=Ec[c][:, :, 0:nq], in_=v8(sc, nq), func=AF.Exp, bias=biasC[:, c:c + 1],
                                                            scale=SCALE), r=[(sk, 0), (sk, 1), "biasC"], w=[("Ec", c)])
                A("dve", lambda e, c=c: e.tensor_tensor(out=Ec[c][:, :, 0:nq], in0=Ec[c][:, :, 0:nq],
                                                        in1=bc(mkb3[bi % 3][:, 2 + c, 0:nq], 8, nq), op=ALU.mult),
                  r=[("Ec", c), ("mkb", bi % 3)], w=[("Ec", c)])
            for u in range(2):
                for c in range(4):
                    A("pe", lambda e, c=c, u=u: e.matmul(v4(oa, u, nq, 0, M[g]), lhsT=vaug(RCv, c, g), rhs=Ec[c][:, 4 * u:4 * u + 4, 0:nq],
                                                         start=(c == 0), stop=(c == 3)), r=[("Ec", c), "RCv"], w=[("oa", u)])
            regs = []
            for hl in range(8):
                j, o = hl // 3, (hl % 3) * 129
                tt, tk = [(scA, ("scA", 0)), (scA, ("scA", 1)), (scB, ("scB", 0))][j]
                base = 512 if j == 1 else 0
                regs.append((tt, tk, base + o))
                for c in range(4):
                    A("pe", lambda e, hl=hl, c=c, tt=tt, base=base, o=o: e.matmul(
                        tt[0:nq, base + o:base + o + 129], lhsT=Ec[c][:, hl, 0:nq], rhs=ovl[:, c, :], start=(c == 0), stop=(c == 3)),
                      r=[("Ec", c), "ovl"], w=[tk])
            banks = [(scA, ("scA", 0), 0, 3, 0), (scA, ("scA", 1), 512, 3, 3), (scB, ("scB", 0), 0, 2, 6)]
            for tt, tk, base, nh, h0 in banks:
                A("dve", lambda e, tt=tt, base=base, nh=nh, h0=h0: e.tensor_scalar(
                    out=rc[0:nq, h0:h0 + nh], in0=tt[0:nq, base + 128:base + 128 + 129 * (nh - 1) + 1:129], scalar1=1.0e-30,
                    scalar2=None, op0=ALU.max), r=[tk], w=[("rc", h0)])
            A("dve", lambda e: e.reciprocal(out=rc[0:nq, :], in_=rc[0:nq, :]), r=[("rc", 0), ("rc", 3), ("rc", 6)], w=["rc"])
            for tt, tk, base, nh, h0 in banks:
                A("dve", lambda e, tt=tt, base=base, nh=nh, h0=h0: e.tensor_tensor(
                    out=t1[0:nq, h0:h0 + nh, :], in0=tt[0:nq, base:base + 129 * nh].rearrange("p (h c) -> p h c", c=129)[:, :, 0:128],
                    in1=rc[0:nq, h0:h0 + nh].unsqueeze(2).to_broadcast([nq, nh, 128]), op=ALU.mult), r=[tk, "rc"], w=[("t1", h0)])
            A("dve", lambda e: e.tensor_reduce(out=score[0:nq, :], in_=t1[0:nq, :, :].rearrange("p h s -> p s h"), axis=AX.X, op=ALU.add),
              r=[("t1", 0), ("t1", 3), ("t1", 6)], w=["score"])
            A("dve", lambda e: e.tensor_tensor(out=score[0:nq, :], in0=score[0:nq, :], in1=bon3[bi % 3][0:nq, :], op=ALU.add),
              r=["score", ("bon", bi % 3)], w=["score"])
            A("dve", lambda e: e.max(out=m8a[0:nq, :], in_=score[0:nq, :]), r=["score"], w=["m8a"])
            A("dve", lambda e: e.match_replace(out=work[0:nq, :], in_to_replace=m8a[0:nq, :], in_values=score[0:nq, :], imm_value=-3.0e38),
              r=["score", "m8a"], w=["work"])
            A("dve", lambda e: e.max(out=m8b[0:nq, :], in_=work[0:nq, :]), r=["work"], w=["m8b"])
            A("dve", lambda e: e.tensor_scalar(out=thr[0:nq, :], in0=m8b[0:nq, 7:8], scalar1=-1.0e29, scalar2=None, op0=ALU.max),
              r=["m8b"], w=["thr"])
            A("dve", lambda e: e.tensor_scalar(out=selq[0:nq, :], in0=score[0:nq, :], scalar1=thr[0:nq, :], scalar2=None, op0=ALU.is_ge),
              r=["score", "thr"], w=["selq"])
            A("pe", lambda e: e.transpose(out=mx[:, 0:nq], in_=selq[0:nq, :], identity=identf[0:nq, 0:nq]), r=["selq", "identf"], w=MXALL)
            A("act", lambda e, g=g: e.copy(out=selT[pb][g][:, 0:nq], in_=mx[:, 0:nq]), r=MXALL, w=[("selT", pb, g)])
            flush(force=True)
            ob = oasb[grpi[0] % 2]
            obk = ("oasb", grpi[0] % 2)
            grpi[0] += 1
            for u in range(2):
                A("act", lambda e, u=u, ob=ob: e.copy(out=ob[:, u * 512:(u + 1) * 512], in_=oa[:, u * 512:(u + 1) * 512]),
                  r=[("oa", u)], w=[obk + (u,)])
            return ob, obk

    LAG = 4
    FILL = 1
    WARM = 0
    XFILL = 16

    def stage_attn(bi):
        S, off, nq, col0 = BLK[bi]
        pb = bi % 2
        M = [128, 128]
        items = []
        gidx = []
        for g in range(2):
            for br in (1, 2):
                kts = list(range(0, S + 1)) if br == 1 else list(range(S - 4, S + 1))
                for idx, kt in enumerate(kts):
                    items.append((g, br, kt, idx == 0, idx == len(kts) - 1))
                    gidx.append(idx)
        N = len(items)
        srcs = [None] * N
        mids = [None] * N

        def front(i):
            g, br, kt, isfirst, islast = items[i]
            KT, Vt, koff = (KsT, Vs, 0) if br == 1 else (KwT, Vw, 40)
            sc, sk = (scA, "scA") if i % 2 == 0 else (scB, "scB")
            Eb, ek = E[i % NBUF], ("E", i % NBUF)
            Pq, pk_ = Pb[i % NBUF], ("Pb", i % NBUF)
            mb, mbk = msk[i % NBUF], ("msk", i % NBUF)
            masked = True
            if br == 1:
                a, v = kt // 16, kt % 16
                kw = dict(tile_position=(96, 0)) if a == 3 else {}
                A("pe", lambda e: e.matmul(mx[:, 0:nq], lhsT=EF[32 * a:32 * a + 32, v, :], rhs=selT[pb][g][32 * a:32 * a + 32, 0:nq],
                                           start=True, stop=True, **kw), r=["EF", ("selT", pb, g)], w=["mx"])
                if kt == S:
                    A("dve", lambda e: e.tensor_tensor(out=mb[:, 0:nq], in0=mx[:, 0:nq], in1=mkb3[bi % 3][:, 0, 0:nq], op=ALU.mult),
                      r=["mx", ("mkb", bi % 3)], w=[mbk])
                else:
                    A("dve", lambda e: e.tensor_copy(out=mb[:, 0:nq], in_=mx[:, 0:nq]), r=["mx"], w=[mbk])
                mask_ap, mask_r = mb[:, 0:nq], [mbk]
            else:
                if kt == S - 4:
                    mask_ap, mask_r = mkb3[bi % 3][:, 1, 0:nq], [("mkb", bi % 3)]
                elif kt == S:
                    mask_ap, mask_r = mkb3[bi % 3][:, 0, 0:nq], [("mkb", bi % 3)]
                else:
                    masked = False
            for u in range(2):
                A("pe", lambda e, u=u: e.matmul(v4(sc, u, nq), lhsT=KT[64 * g:64 * g + 64, (kt - koff) * 128:(kt - koff + 1) * 128],
                                                rhs=QT[pb][64 * g:64 * g + 64, 4 * u:4 * u + 4, 0:nq], start=True, stop=True),
                  r=[("QT", pb)], w=[(sk, u)])
            A("act", lambda e: e.activation(out=Eb[:, :, 0:nq], in_=v8(sc, nq), func=AF.Exp, scale=SCALE), r=[(sk, 0), (sk, 1)], w=[ek])
            for _ in range(FILL + (1 if gidx[i] < XFILL else 0)):
                A("pe", lambda e: e.matmul(msc[:], lhsT=Wq[:, 0, 0:128], rhs=Wq[:, 1, 0:512], start=True, stop=True), r=["Wq"], w=["msc"])
            if masked:
                mids[i] = (Eb, ek, Pq, pk_, mask_ap, mask_r)
                srcs[i] = (Pq, pk_)
            else:
                srcs[i] = (Eb, ek)

        def mid(i):
            if mids[i] is None:
                return
            Eb, ek, Pq, pk_, mask_ap, mask_r = mids[i]
            A("dve", lambda e: e.tensor_tensor(out=Pq[:, :, 0:nq], in0=Eb[:, :, 0:nq], in1=bc(mask_ap, 8, nq), op=ALU.mult),
              r=[ek] + mask_r, w=[pk_])

        def back(i):
            g, br, kt, isfirst, islast = items[i]
            KT, Vt, koff = (KsT, Vs, 0) if br == 1 else (KwT, Vw, 40)
            src, srck = srcs[i]
            for u in range(2):
                A("pe", lambda e, u=u: e.matmul(v4(oa, u, nq, 0, M[g]), lhsT=vaug(Vt, kt - koff, g), rhs=src[:, 4 * u:4 * u + 4, 0:nq],
                                                start=isfirst, stop=islast), r=[srck], w=[("oa", u)])
            if islast:
                flush(force=True)
                ob = oasb[grpi[0] % 2]
                obk = ("oasb", grpi[0] % 2)
                grpi[0] += 1
                for u in range(2):
                    A("act", lambda e, u=u: e.copy(out=ob[:, u * 512:(u + 1) * 512], in_=oa[:, u * 512:(u + 1) * 512]),
                      r=[("oa", u)], w=[obk + (u,)])
                finalize(bi, g, br, False, ob, lambda u: obk + (u,), 4)

        for _ in range(WARM):
            A("pe", lambda e: e.matmul(msc[:], lhsT=Wq[:, 0, 0:128], rhs=Wq[:, 1, 0:512], start=True, stop=True), r=["Wq"], w=["msc"])
        for i in range(N + LAG):
            stepc[0] += 1
            flush()
            if i < N:
                front(i)
            if 0 <= i - 1 < N:
                mid(i - 1)
            if i - LAG >= 0:
                back(i - LAG)
        if DEFER:
            pending.append((stepc[0] + 4, lambda: stage_store(bi)))
        else:
            stage_store(bi)

    def stage_store(bi):
        S, off, nq, col0 = BLK[bi]
        pb = bi % 2
        rk = [("acc", pb, g, u) for g in range(2) for u in range(2)]
        if bi == 0:
            A("dve", lambda e: e.tensor_scalar(out=obf[:, :, 0:nq], in0=acc[pb][:, :, 0:nq], scalar1=vq[:, 0:1], scalar2=None, op0=ALU.mult),
              r=rk + ["vq"], w=["obf"])
        else:
            A("dve", lambda e: e.tensor_copy(out=obf[:, :, 0:nq], in_=acc[pb][:, :, 0:nq]), r=rk, w=["obf"])
        A("sp", lambda e: e.dma_start(out=oscr[:, :, col0:col0 + nq], in_=obf[:, :, 0:nq]), r=["obf"], w=["oscr"], dma="os")
        if "selT" in dbgout and bi == dbgout["_blk"]:
            for g in range(2):
                A("sp", lambda e, g=g: e.dma_start(out=dbgout["selT"][g], in_=selT[pb][g][:]), r=[("selT", pb, g)], dma="dbg")
            A("sp", lambda e: e.dma_start(out=dbgout["QT"], in_=QT[pb][:]), r=[("QT", pb)], dma="dbg")
            A("sp", lambda e: e.dma_start(out=dbgout["acc"], in_=acc[pb][:]), r=rk, dma="dbg")

    nb = dbgout.get("_nblk", NBLK)
    stage_load(0)
    stage_q(0)
    if nb > 1:
        stage_load(1)
    stage_cmp(0)
    for bi in range(nb):
        if bi + 1 < nb:
            stage_q(bi + 1)
            if bi + 2 < nb:
                stage_load(bi + 2)
            stage_cmp(bi + 1)
        stage_attn(bi)
    flush(force=True)
    ph.emit()


def _norm_group(A, xres, c0, n, gcol, onesb, sqb, pn, rs, dst, dkey, tag):
    A("act", lambda e: e.activation(out=sqb[:, :, 0:n], in_=xres[:, :, c0:c0 + n], func=AF.Square), r=["xres"], w=["sqb"])
    for kc in range(8):
        A("pe", lambda e, kc=kc: e.matmul(pn[:, 0:n], lhsT=onesb[:], rhs=sqb[:, kc, 0:n], start=(kc == 0), stop=(kc == 7)),
          r=["sqb"], w=["pn"])
    A("act", lambda e: e.activation(out=rs[:, 0:n], in_=pn[:, 0:n], func=AF.Sqrt, bias=EPS, scale=1.0 / D), r=["pn"], w=["rs"])
    A("dve", lambda e: e.reciprocal(out=rs[:, 0:n], in_=rs[:, 0:n]), r=["rs"], w=["rs"])
    for kc in range(8):
        A("dve", lambda e, kc=kc: e.scalar_tensor_tensor(out=dst[:, kc, 0:n], in0=xres[:, kc, c0:c0 + n], scalar=gcol[:, kc:kc + 1],
                                                        in1=rs[:, 0:n], op0=ALU.mult, op1=ALU.mult), r=["xres", "rs"], w=[dkey])


def _phase_C(nc, st, SB, PS, dr, oscr, P, dbgout):
    xres, identf = P["xres"], P["identf"]
    Wo = SB(st, "Wo", [128, 8, 1024], BF16)
    xq = [SB(st, f"xqC{i}", [128, D], F32) for i in range(2)]
    ot = [SB(st, f"otC{i}", [128, 8, 512], BF16) for i in range(2)]
    pa = [PS(st, f"paC{i}", [128, 1024], F32) for i in range(2)]
    py = [PS(st, f"pyC{i}", [128, 512], F32) for i in range(2)]
    ph = Phase(nc, "C")
    A = ph.add
    A("pool", lambda e: e.dma_start(out=Wo[:], in_=dr["wout"].rearrange("(k p) n -> p k n", p=128)), w=["Wo"], dma="w")
    for bi, (S, off, nq, col0) in enumerate(BLK):
        pb = bi % 2
        t0 = 128 * S + off
        A("sp", lambda e, pb=pb, t0=t0, nq=nq: e.dma_start(out=xq[pb][0:nq, :], in_=dr["xs"][t0:t0 + nq, :]), w=[("xq", pb)], dma=("xq", pb))
        for kc in range(8):
            A("pe", lambda e, kc=kc, pb=pb, nq=nq: e.transpose(out=pa[pb][:, kc * 128:kc * 128 + nq], in_=xq[pb][0:nq, kc * 128:(kc + 1) * 128],
                                                              identity=identf[0:nq, 0:nq]), r=[("xq", pb)], w=[("pa", pb)])
        A("act", lambda e, pb=pb, nq=nq, col0=col0: e.copy(out=xres[:, :, col0:col0 + nq],
                                                           in_=pa[pb][:].rearrange("p (k q) -> p k q", q=128)[:, :, 0:nq]),
          r=[("pa", pb)], w=["xres"])
    for ti, (c0, n) in enumerate(TG):
        tb = ti % 2
        A("sp", lambda e, tb=tb, c0=c0, n=n: e.dma_start(out=ot[tb][:, :, 0:n], in_=oscr[:, :, c0:c0 + n]), w=[("ot", tb)], dma=("ot", tb))
        for dc in range(8):
            for hl in range(8):
                A("pe", lambda e, dc=dc, hl=hl, tb=tb, n=n: e.matmul(py[dc % 2][:, 0:n], lhsT=Wo[:, hl, dc * 128:(dc + 1) * 128],
                                                                    rhs=ot[tb][:, hl, 0:n], start=(hl == 0), stop=(hl == 7)),
                  r=["Wo", ("ot", tb)], w=[("py", dc % 2)])
            A("dve", lambda e, dc=dc, c0=c0, n=n: e.tensor_tensor(out=xres[:, dc, c0:c0 + n], in0=xres[:, dc, c0:c0 + n],
                                                                  in1=py[dc % 2][:, 0:n], op=ALU.add),
              r=[("py", dc % 2), "xres"], w=["xres"])
    if "x0mix" in dbgout:
        A("sp", lambda e: e.dma_start(out=dbgout["x0mix"], in_=xres[:]), r=["xres"], dma="dbg")
    ph.emit()


def _phase_ffn(nc, st, SB, PS, dr, L, P, dbgout):
    xres, onesb, gsb, vq = P["xres"], P["onesb"], P["gsb"], P["vq"]
    gcol = gsb[:, 1 + 2 * L, :]
    hTall = SB(st, f"hTall{L}", [128, 8, NTOK], BF16)
    with contextlib.ExitStack() as nst:
        sqb = SB(nst, f"sqbf{L}", [128, 8, 512], BF16)
        rs = SB(nst, f"rsf{L}", [128, 512], F32)
        pn = PS(nst, f"pnf{L}", [128, 512], F32)
        phn = Phase(nc, f"N{L}")
        for ti, (c0, n) in enumerate(TG):
            _norm_group(phn.add, xres, c0, n, gcol, onesb, sqb, pn, rs, hTall[:, :, c0:c0 + n], "hT", f"f{L}")
        phn.emit()
    wu = [SB(st, f"wu{L}{i}", [128, 8, 6, 256], BF16) for i in range(2)]
    wd = [SB(st, f"wd{L}{i}", [128, 6, 1024], BF16) for i in range(2)]
    cw = SB(st, f"cw{L}", [128, 3, 44], F32)
    cb = SB(st, f"cb{L}", [128, 44], F32)
    carry = SB(st, f"carry{L}", [128, 44, 2], F32)
    ub = [SB(st, f"ub{L}{i}", [128, 514], F32) for i in range(2)]
    cbuf = [SB(st, f"cbuf{L}{i}", [128, 512], F32) for i in range(2)]
    sg = SB(st, f"sg{L}", [128, 512], F32)
    act2 = [SB(st, f"act{L}{i}", [128, 6, 512], BF16) for i in range(2)]
    pu = [[PS(st, f"pu{L}{a}{b}", [128, 512], F32) for b in range(2)] for a in range(2)]
    py = [PS(st, f"pyf{L}{i}", [128, 512], F32) for i in range(2)]
    ph = Phase(nc, f"F{L}")
    A = ph.add
    A("sp", lambda e: e.dma_start(out=cw[:], in_=dr[f"cw{L}"]), w=["cw"], dma="c")
    A("sp", lambda e: e.dma_start(out=cb[:], in_=dr[f"cb{L}"]), w=["cb"], dma="c")
    A("dve", lambda e: e.memset(carry[:], 0.0), w=[("carry", ch) for ch in range(44)])

    def load_pass(p):
        wb = p % 2
        for i, fc in enumerate(FPASS[p]):
            A("pool", lambda e, i=i, fc=fc, wb=wb: e.dma_start(
                out=wu[wb][:, :, i, 0:128], in_=dr[f"wup{L}"][:, fc * 128:(fc + 1) * 128].rearrange("(k p) n -> p k n", p=128)),
              w=[("wu", wb, i)], dma=("w", wb))
            A("pool", lambda e, i=i, fc=fc, wb=wb: e.dma_start(
                out=wu[wb][:, :, i, 128:256], in_=dr[f"wup{L}"][:, DFF + fc * 128:DFF + (fc + 1) * 128].rearrange("(k p) n -> p k n", p=128)),
              w=[("wu", wb, i)], dma=("w", wb))
            A("pool", lambda e, i=i, fc=fc, wb=wb: e.dma_start(out=wd[wb][:, i, :], in_=dr[f"wdn{L}"][fc * 128:(fc + 1) * 128, :]),
              w=[("wd", wb, i)], dma=("w", wb))

    def up_stage(p, ti):
        wb = p % 2
        c0, n = TG[ti]
        ab = (p * len(TG) + ti) % 2
        for i, fc in enumerate(FPASS[p]):
            for part in range(2):
                bank = pu[part][i % 2]
                bk = ("pu", part, i % 2)
                ch = fc + 22 * part
                for kc in range(8):
                    A("pe", lambda e, kc=kc, i=i, part=part, bank=bank: e.matmul(
                        bank[:, 0:n], lhsT=wu[wb][:, kc, i, part * 128:(part + 1) * 128], rhs=hTall[:, kc, c0:c0 + n],
                        start=(kc == 0), stop=(kc == 7)), r=[("wu", wb, i)], w=[bk])
                A("act", lambda e, part=part, bank=bank: e.copy(out=ub[part][:, 2:2 + n], in_=bank[:, 0:n]), r=[bk], w=[("ub", part)])
                A("act", lambda e, part=part, bank=bank, ch=ch: e.activation(
                    out=cbuf[part][:, 0:n], in_=bank[:, 0:n], func=AF.Identity, bias=cb[:, ch:ch + 1], scale=cw[:, 2, ch:ch + 1]),
                  r=[bk, "cw", "cb"], w=[("cbuf", part)])
                A("act", lambda e, part=part, ch=ch: e.copy(out=ub[part][:, 0:2], in_=carry[:, ch, :]), r=[("carry", ch)], w=[("ubc", part)])
                for k in (1, 0):
                    A("dve", lambda e, part=part, ch=ch, k=k: e.scalar_tensor_tensor(
                        out=cbuf[part][:, 0:n], in0=ub[part][:, k:k + n], scalar=cw[:, k, ch:ch + 1], in1=cbuf[part][:, 0:n],
                        op0=ALU.mult, op1=ALU.add), r=[("ub", part), ("ubc", part), ("cbuf", part), "cw"], w=[("cbuf", part)])
                A("act", lambda e, part=part, ch=ch: e.copy(out=carry[:, ch, :], in_=ub[part][:, n:n + 2]), r=[("ub", part)], w=[("carry", ch)])
            A("act", lambda e: e.activation(out=sg[:, 0:n], in_=cbuf[0][:, 0:n], func=AF.Silu), r=[("cbuf", 0)], w=["sg"])
            A("dve", lambda e, i=i: e.tensor_tensor(out=act2[ab][:, i, 0:n], in0=sg[:, 0:n], in1=cbuf[1][:, 0:n], op=ALU.mult),
              r=["sg", ("cbuf", 1)], w=[("act", ab, i)])

    def down_stage(p, ti):
        wb = p % 2
        c0, n = TG[ti]
        ab = (p * len(TG) + ti) % 2
        nf = len(FPASS[p])
        for dc in range(8):
            for i in range(nf):
                A("pe", lambda e, dc=dc, i=i: e.matmul(py[dc % 2][:, 0:n], lhsT=wd[wb][:, i, dc * 128:(dc + 1) * 128], rhs=act2[ab][:, i, 0:n],
                                                       start=(i == 0), stop=(i == nf - 1)), r=[("wd", wb, i), ("act", ab, i)], w=[("py", dc % 2)])
            A("dve", lambda e, dc=dc: e.tensor_tensor(out=xres[:, dc, c0:c0 + n], in0=xres[:, dc, c0:c0 + n], in1=py[dc % 2][:, 0:n], op=ALU.add),
              r=[("py", dc % 2), "xres"], w=["xres"])

    steps = [(p, ti) for p in range(len(FPASS)) for ti in range(len(TG))]
    load_pass(0)
    load_pass(1)
    for k in range(len(steps) + 1):
        if k < len(steps):
            up_stage(*steps[k])
        if k >= 1:
            pp, pti = steps[k - 1]
            down_stage(pp, pti)
            if pti == len(TG) - 1 and pp + 2 < len(FPASS):
                load_pass(pp + 2)
    A("dve", lambda e: e.tensor_scalar(out=xres[:, :, 0:HALO], in0=xres[:, :, 0:HALO], scalar1=vq[:, 0:1], scalar2=None, op0=ALU.mult),
      r=["xres"], w=["xres"])
    if f"xffn{L}" in dbgout:
        A("sp", lambda e: e.dma_start(out=dbgout[f"xffn{L}"], in_=xres[:]), r=["xres"], dma="dbg")
    ph.emit()


def _phase_pool(nc, st, SB, PS, dr, P, dbgout):
    xres, onesb, gsb, vq = P["xres"], P["onesb"], P["gsb"], P["vq"]
    gcol = gsb[:, 2, :]
    hf = SB(st, "hf", [128, 8, NTOK], F32)
    pl = SB(st, "pl", [128, 8, NTOK], BF16)
    wa = SB(st, "wa", [128, NTOK], F32)
    wb_ = SB(st, "wb", [128, NTOK], F32)
    sqb = SB(st, "sqbp", [128, 8, 512], BF16)
    rs = SB(st, "rsp", [128, 512], F32)
    pw = SB(st, "pw", [128, 8, 256], BF16)
    pbias = SB(st, "pbias", [128, 8], F32)
    pscl = SB(st, "pscl", [128, 8], F32)
    psc16 = SB(st, "psc16", [128, 8, 16], F32)
    tmp16 = SB(st, "tmp16", [128, 16], F32)
    ytmp = SB(st, "ytmp", [128, 512], F32)
    pn = PS(st, "pnp", [128, 512], F32)
    py = [PS(st, f"pyp{i}", [128, 512], F32) for i in range(2)]
    ph = Phase(nc, "P")
    A = ph.add
    A("pool", lambda e: e.dma_start(out=pw[:], in_=dr["poolw"]), w=["pw"], dma="w")
    A("sp", lambda e: e.dma_start(out=pbias[:], in_=dr["poolb"]), w=["pbias"], dma="c")
    A("sp", lambda e: e.dma_start(out=pscl[:], in_=dr["pools"]), w=["pscl"], dma="c")
    A("sp", lambda e: e.dma_start(out=psc16[:], in_=dr["pscale"]), w=["psc16"], dma="c")
    for ti, (c0, n) in enumerate(TG):
        _norm_group(A, xres, c0, n, gcol, onesb, sqb, pn, rs, hf[:, :, c0:c0 + n], "hf", "p")
    for kc in range(8):
        nsteps = kc // 2 + 1
        w = 2 ** nsteps
        src, sk = hf[:, kc, :], "hf"
        bufs = [(wa, "wa"), (wb_, "wb")]
        for sidx in range(nsteps):
            d = 2 ** sidx
            dst, dk = bufs[sidx % 2]
            A("dve", lambda e, src=src, dst=dst, d=d: e.tensor_tensor(out=dst[:, d:NTOK], in0=src[:, d:NTOK], in1=src[:, 0:NTOK - d], op=ALU.add),
              r=[sk], w=[dk])
            A("act", lambda e, src=src, dst=dst, d=d: e.copy(out=dst[:, 0:d], in_=src[:, 0:d]), r=[sk], w=[dk])
            src, sk = dst[:], dk
        A("dve", lambda e, src=src, kc=kc, w=w: e.scalar_tensor_tensor(out=pl[:, kc, :], in0=src, scalar=1.0 / w, in1=hf[:, kc, :],
                                                                      op0=ALU.mult, op1=ALU.subtract), r=[sk, "hf"], w=[("pl", kc)])
        A("dve", lambda e, src=src, kc=kc: e.tensor_tensor(out=tmp16[:], in0=src[:, HALO:HALO + 16], in1=psc16[:, kc, :], op=ALU.mult),
          r=[sk, "psc16"], w=["tmp16"])
        A("dve", lambda e, kc=kc: e.tensor_tensor(out=pl[:, kc, HALO:HALO + 16], in0=tmp16[:], in1=hf[:, kc, HALO:HALO + 16], op=ALU.subtract),
          r=["tmp16", "hf", ("pl", kc)], w=[("pl", kc)])
    for ti, (c0, n) in enumerate(TG):
        for oc in range(8):
            g, oh = oc // 2, oc % 2
            for kh in range(2):
                A("pe", lambda e, oc=oc, g=g, oh=oh, kh=kh, c0=c0, n=n: e.matmul(
                    py[oc % 2][:, 0:n], lhsT=pw[:, g * 2 + kh, oh * 128:(oh + 1) * 128], rhs=pl[:, g * 2 + kh, c0:c0 + n],
                    start=(kh == 0), stop=(kh == 1)), r=["pw", ("pl", g * 2 + kh)], w=[("py", oc % 2)])
            A("dve", lambda e, oc=oc, n=n: e.tensor_scalar(out=ytmp[:, 0:n], in0=py[oc % 2][:, 0:n], scalar1=pbias[:, oc:oc + 1],
                                                          scalar2=pscl[:, oc:oc + 1], op0=ALU.add, op1=ALU.mult),
              r=[("py", oc % 2), "pbias", "pscl"], w=["ytmp"])
            A("dve", lambda e, oc=oc, c0=c0, n=n: e.tensor_tensor(out=xres[:, oc, c0:c0 + n], in0=xres[:, oc, c0:c0 + n], in1=ytmp[:, 0:n],
                                                                  op=ALU.add), r=["ytmp", "xres"], w=["xres"])
    A("dve", lambda e: e.tensor_scalar(out=xres[:, :, 0:HALO], in0=xres[:, :, 0:HALO], scalar1=vq[:, 0:1], scalar2=None, op0=ALU.mult),
      r=["xres"], w=["xres"])
    if "xpool" in dbgout:
        A("sp", lambda e: e.dma_start(out=dbgout["xpool"], in_=xres[:]), r=["xres"], dma="dbg")
    ph.emit()


def _phase_out(nc, st, SB, PS, dr, out, P, dbgout):
    xres, onesb, gsb, identf = P["xres"], P["onesb"], P["gsb"], P["identf"]
    gcol = gsb[:, 4, :]
    of = SB(st, "of", [128, 8, 512], F32)
    sqb = SB(st, "sqbo", [128, 8, 512], BF16)
    rs = SB(st, "rso", [128, 512], F32)
    ot = [SB(st, f"oto{i}", [128, D], F32) for i in range(2)]
    pn = PS(st, "pno", [128, 512], F32)
    pt = [PS(st, f"pto{i}", [128, 1024], F32) for i in range(2)]
    ph = Phase(nc, "O")
    A = ph.add
    for gi in range(4):
        c0 = HALO + 512 * gi
        _norm_group(A, xres, c0, 512, gcol, onesb, sqb, pn, rs, of, "of", "o")
        for tt in range(4):
            tb = tt % 2
            for kc in range(8):
                A("pe", lambda e, tt=tt, tb=tb, kc=kc: e.transpose(out=pt[tb][:, kc * 128:(kc + 1) * 128],
                                                                 in_=of[:, kc, tt * 128:(tt + 1) * 128], identity=identf[:]),
                  r=["of"], w=[("pt", tb)])
            A("act", lambda e, tb=tb: e.copy(out=ot[tb][:], in_=pt[tb][:]), r=[("pt", tb)], w=[("ot", tb)])
            row = (gi * 4 + tt) * 128
            A("sp", lambda e, tb=tb, row=row: e.dma_start(out=out[row:row + 128, :], in_=ot[tb][:]), r=[("ot", tb)], dma=("st", tb))
    ph.emit()


_CACHE = {}


def kernel(**inputs):
    inp = {k: np.asarray(v) for k, v in inputs.items()}
    if "nc" not in _CACHE:
        _CACHE["nc"] = build()
    nc = _CACHE["nc"]
    sh = _shared_inputs(inp)
    maps = []
    for c in range(8):
        m = dict(sh)
        m.update(_core_inputs(inp, c))
        maps.append(m)
    res = run_bass_kernel_spmd(nc, maps, core_ids=list(range(8)))
    outp = np.zeros((2, T, D), np.float32)
    for c in range(8):
        b, j = c // 4, c % 4
        outp[b, 2048 * j:2048 * (j + 1)] = np.asarray(res.results[c]["out"], dtype=np.float32)
    return outp
```

```python
import contextlib
import numpy as np
import ml_dtypes
import concourse.bass as bass
import concourse.mybir as mybir
from concourse.bass_utils import run_bass_kernel_spmd

F32 = mybir.dt.float32
BF16 = mybir.dt.bfloat16
AF = mybir.ActivationFunctionType
ALU = mybir.AluOpType
AX = mybir.AxisListType
NPBF = ml_dtypes.bfloat16

D = 1024
KC = 8
T = 8192
NSLOT = 64
OWN0 = 48
NOWN = 16
HALO = 20
NTOK = HALO + 128 * NOWN
DFF = 2816
NFC = 22
EPS = 1e-6
SCALE = 0.125
NEGB = -30000.0
BLK = [(47, 108, HALO, 0)] + [(OWN0 + m, 0, 128, HALO + 128 * m) for m in range(NOWN)]
NBLK = len(BLK)
TG = [(0, HALO)] + [(HALO + 512 * i, 512) for i in range(4)]
FPASS = [list(range(0, 6)), list(range(6, 12)), list(range(12, 17)), list(range(17, 22))]

ENGS = ("pe", "act", "dve", "pool", "sp")


class Op:
    __slots__ = ("eng", "fn", "dma", "waits", "signal", "sigidx", "idx")

    def __init__(self, eng, fn, dma):
        self.eng = eng
        self.fn = fn
        self.dma = dma
        self.waits = []
        self.signal = False
        self.sigidx = None


class Phase:
    def __init__(self, nc, name):
        self.nc = nc
        self.name = name
        self.ops = {e: [] for e in ENGS}
        self.lastw = {}
        self.readers = {}
        self.dma_count = {}
        self.n = 0

    def add(self, eng, fn, r=(), w=(), dma=None):
        op = Op(eng, fn, dma)
        op.idx = self.n
        self.n += 1
        deps = []
        for k in r:
            x = self.lastw.get(k)
            if x is not None:
                deps.append(x)
        for k in w:
            x = self.lastw.get(k)
            if x is not None:
                deps.append(x)
            deps.extend(self.readers.get(k, {}).values())
        seen = set()
        for d in deps:
            if d is op or id(d) in seen:
                continue
            seen.add(id(d))
            if d.dma is not None:
                op.waits.append(("dma", d.dma, 16 * self.dma_count[d.dma]))
            else:
                if d.eng == "pe" and eng == "pe" and dma is None:
                    continue
                d.signal = True
                op.waits.append(("eng", d, None))
        rk = eng if dma is None else ("dma", op.idx)
        for k in r:
            self.readers.setdefault(k, {})[rk] = op
        for k in w:
            self.lastw[k] = op
            self.readers[k] = {}
        if dma is not None:
            self.dma_count[dma] = self.dma_count.get(dma, 0) + 1
        self.ops[eng].append(op)
        return op

    def emit(self):
        nc = self.nc
        for e in ENGS:
            k = 0
            for op in self.ops[e]:
                if op.dma is None and op.signal:
                    k += 1
                    op.sigidx = k
        with contextlib.ExitStack() as st:
            esem = {e: st.enter_context(nc.semaphore(f"{self.name}_s_{e}")) for e in ENGS}
            dsem = {k: st.enter_context(nc.semaphore(f"{self.name}_d_{i}"))
                    for i, k in enumerate(self.dma_count)}
            block = st.enter_context(nc.Block())
            final_dma = dict(self.dma_count)

            def run(e, eng):
                seen = {}
                for op in self.ops[e]:
                    for kind, obj, val in op.waits:
                        if kind == "dma":
                            sem, v = dsem[obj], val
                        else:
                            sem, v = esem[obj.eng], obj.sigidx
                        if seen.get(id(sem), 0) >= v:
                            continue
                        seen[id(sem)] = v
                        eng.wait_ge(sem, v)
                    inst = op.fn(eng)
                    if op.dma is not None:
                        inst.then_inc(dsem[op.dma], 16)
                    elif op.signal:
                        inst.then_inc(esem[e], 1)
                mine = []
                for op in self.ops[e]:
                    if op.dma is not None and op.dma not in mine:
                        mine.append(op.dma)
                for k in mine:
                    v = 16 * final_dma[k]
                    if seen.get(id(dsem[k]), 0) < v:
                        eng.wait_ge(dsem[k], v)

            block.tensor(lambda eng: run("pe", eng))
            block.scalar(lambda eng: run("act", eng))
            block.vector(lambda eng: run("dve", eng))
            block.gpsimd(lambda eng: run("pool", eng))
            block.sync(lambda eng: run("sp", eng))


def _c(a, dt=np.float32):
    return np.ascontiguousarray(a).astype(dt, copy=False)


def _pk(v):
    return _c(np.asarray(v).reshape(-1, 128).T)


def _rope_tab(pos):
    inv = (1.0 / (10000.0 ** (np.arange(0, 64, 2, dtype=np.float32) / np.float32(64)))).astype(np.float32)
    ang = pos.astype(np.float32)[:, None] * inv[None, :]
    c = np.cos(ang).astype(np.float32)
    s = np.sin(ang).astype(np.float32)
    idx = np.arange(128) % 32
    return _c(c[:, idx].T), _c(s[:, idx].T)


def _shared_inputs(inp):
    sh = {}
    w_in = np.asarray(inp["nsa_w_in"])
    sh["wq"] = _c(w_in[:, :1024].reshape(1024, 2, 8, 64).transpose(0, 2, 1, 3).reshape(1024, 1024))
    sh["wkv"] = _c(w_in[:, 1024:1792])
    sh["wg"] = _c(w_in[:, 1792:1840].reshape(1024, 2, 8, 3).transpose(0, 3, 1, 2).reshape(1024, 48))
    sh["wout"] = _c(np.asarray(inp["nsa_w_out"]).reshape(2, 8, 64, 1024).transpose(1, 0, 2, 3).reshape(1024, 1024))
    for x in ("k", "v"):
        sh[f"c{x}_w1"] = _c(inp[f"cmp_{x}_w1"])
        sh[f"c{x}_w2"] = _c(inp[f"cmp_{x}_w2"])
        pos = np.asarray(inp[f"cmp_{x}_pos"])
        pt = np.zeros((128, 32, 2), np.float32)
        pt[0:64, :, 0] = pos.T
        pt[64:128, :, 0] = pos.T
        sh[f"c{x}_posT"] = _c(pt)
        sh[f"c{x}_b1"] = _c(np.asarray(inp[f"cmp_{x}_b1"]).reshape(2, 128).T)
        sh[f"c{x}_b2"] = _c(np.tile(np.asarray(inp[f"cmp_{x}_b2"]), 2)[None, :])
    for i in (0, 1):
        sh[f"gmix{i}"] = _pk(inp[f"norm_mix_{i}"])
        sh[f"gffn{i}"] = _pk(inp[f"norm_ffn_{i}"])
        sh[f"wup{i}"] = _c(inp[f"ffn_up_{i}"])
        sh[f"wdn{i}"] = _c(inp[f"ffn_down_{i}"])
        cw = np.asarray(inp[f"ffn_conv_w_{i}"])
        sh[f"cw{i}"] = _c(cw.reshape(3, 44, 128).transpose(2, 0, 1))
        sh[f"cb{i}"] = _c(np.asarray(inp[f"ffn_conv_b_{i}"]).reshape(44, 128).T)
    sh["gfin"] = _pk(inp["norm_final"])
    sh["poolw"] = _c(np.asarray(inp["pool_w"]).reshape(4, 2, 128, 256).transpose(2, 0, 1, 3).reshape(128, 8, 256))
    sh["poolb"] = _pk(np.asarray(inp["pool_b"]).reshape(-1))
    sh["pools"] = _pk(inp["pool_scale"])
    sh["identb"] = _c(np.eye(128), NPBF)
    sh["identf"] = _c(np.eye(128))
    ef = np.zeros((128, 16, 128), np.float32)
    for p in range(128):
        for v in range(16):
            if p % 32 == 2 * v:
                ef[p, v, 0:64] = 1
            if p % 32 == 2 * v + 1:
                ef[p, v, 64:128] = 1
    sh["ef"] = _c(ef, NPBF)
    n = np.arange(512)[:, None]
    s = np.arange(128)[None, :]
    lo = np.maximum(n * 16, s * 64)
    hi = np.minimum(n * 16 + 32, (s + 1) * 64)
    ov = np.clip(hi - lo, 0, None) / 32.0
    ova = np.ones((512, 129), np.float32)
    ova[:, :128] = ov
    sh["ovl"] = _c(ova.reshape(4, 128, 129).transpose(1, 0, 2), NPBF)
    mk = np.zeros((NBLK, 128, 6, 128), np.float32)
    k = np.arange(128)[:, None]
    q = np.arange(128)[None, :]
    for bi, (S, off, nq, col0) in enumerate(BLK):
        mk[bi, :, 0, :] = (k <= off + q)
        mk[bi, :, 1, :] = (k > off + q)
        tq = 128 * S + off + q
        for c in range(4):
            mk[bi, :, 2 + c, :] = (16 * (c * 128 + k) + 31 <= tq)
    sh["mk"] = _c(mk, NPBF)
    return sh


def _core_inputs(inp, core):
    b, j = core // 4, core % 4
    SH = OWN0 - 16 * j
    x = np.asarray(inp["x"])[b]
    ci = {}
    xs = np.zeros((T, D), np.float32)
    nreal = (NSLOT - SH) * 128
    xs[SH * 128:] = x[:nreal]
    ci["xs"] = xs
    tp = np.arange(T)
    pos = np.maximum(tp - 128 * SH, 0)
    ci["cosK"], ci["sinK"] = _rope_tab(pos)
    qpos = np.zeros((NBLK, 128), np.int64)
    bonus = np.zeros((NBLK, 128, 128), np.float32)
    sblk = np.arange(128)[None, :]
    for bi, (S, off, nq, col0) in enumerate(BLK):
        tq = 128 * S + off + np.arange(128)
        qpos[bi] = np.maximum(tq - 128 * SH, 0)
        cur = (tq // 64)[:, None]
        forced = (sblk == cur) | (sblk == cur - 1) | (sblk == 2 * SH)
        bonus[bi] = np.where(sblk <= cur, 1e4 * forced, -1e30)
    cq, sq = _rope_tab(qpos.reshape(-1))
    ci["cosQ"], ci["sinQ"] = cq, sq
    ci["bonus"] = _c(bonus.transpose(1, 0, 2))
    nn = np.arange(512)
    cend = np.maximum(16 * (nn - 8 * SH) + 31, 0)
    ci["cosC"], ci["sinC"] = _rope_tab(cend)
    validc = (nn >= 8 * SH) & (nn <= 510)
    ci["biasC"] = _c(np.where(validc, 0.0, NEGB).reshape(4, 128).T)
    vk = (np.arange(NSLOT) >= SH).astype(np.float32)
    ci["validk"] = _c(np.tile(vk[None, :], (128, 1)), NPBF)
    ci["vq"] = _c(np.full((128, 1), 0.0 if j == 0 else 1.0))
    ps = np.zeros((128, 8, 16), np.float32)
    for kc in range(8):
        w = [2, 4, 8, 16][kc // 2]
        for qq in range(16):
            t = 128 * 16 * j + qq
            ps[:, kc, qq] = 1.0 / min(t + 1, w)
    ci["pscale"] = ps
    return ci


def build(dbg=()):
    nc = bass.Bass("TRN2", target_bir_lowering=False)
    dr = {}

    def din(name, shape, dt=F32):
        dr[name] = nc.dram_tensor(name, list(shape), dt, kind="ExternalInput").ap()
        return dr[name]

    din("xs", [T, D])
    din("cosK", [128, T]); din("sinK", [128, T])
    din("cosQ", [128, NBLK * 128]); din("sinQ", [128, NBLK * 128])
    din("bonus", [128, NBLK, 128])
    din("cosC", [128, 512]); din("sinC", [128, 512])
    din("biasC", [128, 4]); din("validk", [128, NSLOT], BF16); din("vq", [128, 1]); din("pscale", [128, 8, 16])
    din("wq", [D, D]); din("wkv", [D, 768]); din("wg", [D, 48]); din("wout", [D, D])
    for x in ("k", "v"):
        din(f"c{x}_w1", [2048, 256]); din(f"c{x}_w2", [256, 64]); din(f"c{x}_posT", [128, 32, 2])
        din(f"c{x}_b1", [128, 2]); din(f"c{x}_b2", [1, 128])
    for i in (0, 1):
        din(f"gmix{i}", [128, 8]); din(f"gffn{i}", [128, 8])
        din(f"wup{i}", [D, 2 * DFF]); din(f"wdn{i}", [DFF, D])
        din(f"cw{i}", [128, 3, 44]); din(f"cb{i}", [128, 44])
    din("gfin", [128, 8]); din("poolw", [128, 8, 256]); din("poolb", [128, 8]); din("pools", [128, 8])
    din("identb", [128, 128], BF16); din("identf", [128, 128])
    din("ef", [128, 16, 128], BF16); din("ovl", [128, 4, 129], BF16); din("mk", [NBLK, 128, 6, 128], BF16)
    out = nc.dram_tensor("out", [128 * NOWN, D], F32, kind="ExternalOutput").ap()
    gscr = nc.dram_tensor("gscr", [NBLK, 48, 128], F32).ap()
    dbgout = {}
    for name, shape, dt in dbg:
        if shape is None:
            dbgout[name] = dt
            continue
        dbgout[name] = nc.dram_tensor("dbg_" + name, list(shape), dt, kind="ExternalOutput").ap()

    with contextlib.ExitStack() as top:
        def SB(st, name, shape, dt):
            return st.enter_context(nc.sbuf_tensor("s_" + name, list(shape), dt))

        def PS(st, name, shape, dt):
            return st.enter_context(nc.psum_tensor("p_" + name, list(shape), dt))

        identb = SB(top, "identb", [128, 128], BF16)
        identf = SB(top, "identf", [128, 128], F32)
        ones32 = SB(top, "ones32", [128, 512], F32)
        onesb = SB(top, "onesb", [128, 128], BF16)
        gsb = SB(top, "gsb", [128, 5, 8], F32)
        vq = SB(top, "vq", [128, 1], F32)
        ph = Phase(nc, "K")
        ph.add("sp", lambda e: e.dma_start(out=identb[:], in_=dr["identb"]), w=["identb"], dma="c")
        ph.add("sp", lambda e: e.dma_start(out=identf[:], in_=dr["identf"]), w=["identf"], dma="c")
        for i, nm in enumerate(("gmix0", "gffn0", "gmix1", "gffn1", "gfin")):
            ph.add("sp", lambda e, i=i, nm=nm: e.dma_start(out=gsb[:, i, :], in_=dr[nm]), w=["gsb"], dma="c")
        ph.add("sp", lambda e: e.dma_start(out=vq[:], in_=dr["vq"]), w=["vq"], dma="c")
        ph.add("dve", lambda e: e.memset(ones32[:], 1.0), w=["ones32"])
        ph.add("dve", lambda e: e.memset(onesb[:], 1.0), w=["onesb"])
        ph.emit()

        oscr = nc.dram_tensor("oscr", [128, 8, NTOK], BF16).ap()
        with contextlib.ExitStack() as att:
            KsT = SB(att, "KsT", [128, T], BF16)
            KwT = SB(att, "KwT", [128, 24 * 128], BF16)
            Vs = SB(att, "Vs", [128, NSLOT, 194], BF16)
            Vw = SB(att, "Vw", [128, 24, 194], BF16)
            kcT = SB(att, "kcT", [128, 512], BF16)
            RCv = SB(att, "RCv", [128, 4, 194], BF16)
            biasC = SB(att, "biasC", [128, 4], F32)
            P = dict(KsT=KsT, KwT=KwT, Vs=Vs, Vw=Vw, kcT=kcT, RCv=RCv, biasC=biasC,
                     identb=identb, ones32=ones32, gsb=gsb, vq=vq)
            with contextlib.ExitStack() as st:
                _phase_A(nc, st, SB, PS, dr, P, dbgout)
            if "stopA" not in dbgout:
                with contextlib.ExitStack() as st:
                    _phase_B(nc, st, SB, PS, dr, gscr, oscr, P, dbgout)
        if "stopA" in dbgout or "stopB" in dbgout:
            return nc
        with contextlib.ExitStack() as rest:
            xres = SB(rest, "xres", [128, 8, NTOK], F32)
            P = dict(xres=xres, onesb=onesb, gsb=gsb, vq=vq, identf=identf, identb=identb)
            with contextlib.ExitStack() as st:
                _phase_C(nc, st, SB, PS, dr, oscr, P, dbgout)
            with contextlib.ExitStack() as st:
                _phase_ffn(nc, st, SB, PS, dr, 0, P, dbgout)
            with contextlib.ExitStack() as st:
                _phase_pool(nc, st, SB, PS, dr, P, dbgout)
            with contextlib.ExitStack() as st:
                _phase_ffn(nc, st, SB, PS, dr, 1, P, dbgout)
            with contextlib.ExitStack() as st:
                _phase_out(nc, st, SB, PS, dr, out, P, dbgout)
    return nc


def _phase_A(nc, st, SB, PS, dr, P, dbgout):
    KsT, KwT, Vs, Vw, kcT, RCv = P["KsT"], P["KwT"], P["Vs"], P["Vw"], P["kcT"], P["RCv"]
    identb, ones32, gsb = P["identb"], P["ones32"], P["gsb"]
    rawK = SB(st, "rawK", [128, T], BF16)
    rawV = SB(st, "rawV", [128, T], BF16)
    t1a = SB(st, "t1a", [128, 512], F32)
    t1b = SB(st, "t1b", [128, 512], F32)
    ist = contextlib.ExitStack()
    WA = SB(ist, "WA", [128, 8, 768], BF16)
    WAr = SB(ist, "WAr", [128, 8, 256], BF16)
    validk = SB(ist, "validk", [128, NSLOT], BF16)
    xt = [SB(ist, f"xt{i}", [128, D], F32) for i in range(4)]
    junk = SB(ist, "junk", [128, D], BF16)
    ssq = [SB(ist, f"ssq{i}", [128, 1], F32) for i in range(4)]
    rstd = [SB(ist, f"rstd{i}", [128, 1], F32) for i in range(4)]
    xn = [SB(ist, f"xn{i}", [128, D], BF16) for i in range(8)]
    hT = [SB(ist, f"hT{i}", [128, 8, 512], BF16) for i in range(2)]
    cs = [SB(ist, f"cs{i}", [128, 512], F32) for i in range(2)]
    sn = [SB(ist, f"sn{i}", [128, 512], F32) for i in range(2)]
    pst = contextlib.ExitStack()
    tp = [PS(pst, f"tp{i}", [128, D], BF16) for i in range(2)]
    pk = [PS(pst, f"pk{i}", [128, 512], F32) for i in range(4)]
    pv = PS(pst, "pv", [128, 512], F32)

    ph = Phase(nc, "A")
    A = ph.add
    A("pool", lambda e: e.dma_start(out=WA[:], in_=dr["wkv"].rearrange("(k p) n -> p k n", p=128)), w=["WA"], dma="w")
    A("sp", lambda e: e.dma_start(out=validk[:], in_=dr["validk"]), w=["validk"], dma="c")
    A("sp", lambda e: e.dma_start(out=P["biasC"][:], in_=dr["biasC"]), w=["biasC"], dma="c")
    for i, c0 in enumerate((256, 512)):
        for g in range(2):
            A("dve", lambda e, i=i, c0=c0, g=g: e.tensor_scalar(
                out=WAr[:, :, i * 128 + g * 64: i * 128 + g * 64 + 32], in0=WA[:, :, c0 + g * 64 + 32: c0 + g * 64 + 64],
                scalar1=-1.0, scalar2=None, op0=ALU.mult), r=["WA"], w=[("WAr", i, g, 0)])
            A("dve", lambda e, i=i, c0=c0, g=g: e.tensor_copy(
                out=WAr[:, :, i * 128 + g * 64 + 32: i * 128 + g * 64 + 64], in_=WA[:, :, c0 + g * 64: c0 + g * 64 + 32]),
              r=["WA"], w=[("WAr", i, g, 1)])
    WArk = [("WAr", i, g, h) for i in range(2) for g in range(2) for h in range(2)]
    A("pool", lambda e: e.memset(Vs[:], 0.0), w=["Vs0"])
    A("pool", lambda e: e.memset(Vw[:], 0.0), w=["Vw0"])
    A("pool", lambda e: e.memset(RCv[:], 0.0), w=["RCv"])
    A("dve", lambda e: e.tensor_copy(out=Vs[:, :, 64:65], in_=validk[:].unsqueeze(2)), r=["validk", "Vs0"], w=["Vs1"])
    A("dve", lambda e: e.tensor_copy(out=Vs[:, :, 66:67], in_=validk[:].unsqueeze(2)), r=["validk", "Vs0"], w=["Vs2"])
    A("dve", lambda e: e.tensor_copy(out=Vw[:, :, 64:65], in_=validk[:, 40:64].unsqueeze(2)), r=["validk", "Vw0"], w=["Vw1"])
    A("dve", lambda e: e.tensor_copy(out=Vw[:, :, 66:67], in_=validk[:, 40:64].unsqueeze(2)), r=["validk", "Vw0"], w=["Vw2"])

    def norm_stage(G):
        A("sp", lambda e: e.dma_start(out=cs[G % 2][:], in_=dr["cosK"][:, G * 512:(G + 1) * 512]), w=[("cs", G % 2)], dma=("cs", G % 2))
        A("sp", lambda e: e.dma_start(out=sn[G % 2][:], in_=dr["sinK"][:, G * 512:(G + 1) * 512]), w=[("sn", G % 2)], dma=("cs", G % 2))
        for si in range(4):
            s_ = 4 * G + si
            b3 = si
            bx = (G % 2) * 4 + si
            A("sp", lambda e, s_=s_, b3=b3: e.dma_start(out=xt[b3][:], in_=dr["xs"][s_ * 128:(s_ + 1) * 128, :]),
              w=[("xt", b3)], dma=("x", b3))
            A("act", lambda e, b3=b3: e.activation(out=junk[:], in_=xt[b3][:], func=AF.Square, accum_out=ssq[b3][:]),
              r=[("xt", b3)], w=["junk", ("ssq", b3)])
            A("act", lambda e, b3=b3: e.activation(out=rstd[b3][:], in_=ssq[b3][:], func=AF.Sqrt, bias=EPS, scale=1.0 / D),
              r=[("ssq", b3)], w=[("rstd", b3)])
            A("dve", lambda e, b3=b3: e.reciprocal(out=rstd[b3][:], in_=rstd[b3][:]), r=[("rstd", b3)], w=[("rstd", b3)])
            A("dve", lambda e, b3=b3, bx=bx: e.tensor_scalar(out=xn[bx][:], in0=xt[b3][:], scalar1=rstd[b3][:], scalar2=None,
                                                           op0=ALU.mult), r=[("xt", b3), ("rstd", b3)], w=[("xn", bx)])

    norm_stage(0)
    for G in range(16):
        hb = hT[G % 2]
        hk = ("hT", G % 2)
        if G + 1 < 16:
            norm_stage(G + 1)
        for si in range(4):
            s = 4 * G + si
            b2 = s % 2
            bx = (G % 2) * 4 + si
            for kc in range(8):
                A("pe", lambda e, kc=kc, b2=b2, bx=bx: e.transpose(out=tp[b2][:, kc * 128:(kc + 1) * 128],
                                                                 in_=xn[bx][:, kc * 128:(kc + 1) * 128], identity=identb[:]),
                  r=[("xn", bx)], w=[("tp", b2)])
            A("dve", lambda e, b2=b2, hb=hb, si=si: e.tensor_tensor(
                out=hb[:, :, si * 128:(si + 1) * 128], in0=tp[b2][:].rearrange("p (k t) -> p k t", k=8),
                in1=gsb[:, 0, :].unsqueeze(2).to_broadcast([128, 8, 128]), op=ALU.mult),
              r=[("tp", b2)], w=[hk + (si,)])
        hks = [hk + (si,) for si in range(4)]

        def proj(W, c0, bank, bk, wk, hb=hb, hks=hks):
            for kc in range(8):
                A("pe", lambda e, kc=kc, W=W, c0=c0, bank=bank, hb=hb: e.matmul(bank[:], lhsT=W[:, kc, c0:c0 + 128], rhs=hb[:, kc, :],
                                                                          start=(kc == 0), stop=(kc == 7)),
                  r=hks + wk, w=[bk])
        proj(WA, 0, pk[0], "pk0", ["WA"])
        proj(WA, 128, pk[1], "pk1", ["WA"])
        proj(WA, 256, pk[2], "pk2", ["WA"])
        proj(WAr, 0, pk[3], "pk3", WArk)
        A("act", lambda e, G=G: e.copy(out=rawK[:, G * 512:(G + 1) * 512], in_=pk[0][:]), r=["pk0"], w=[("rawK", G)])
        A("act", lambda e, G=G: e.copy(out=rawV[:, G * 512:(G + 1) * 512], in_=pk[1][:]), r=["pk1"], w=[("rawV", G)])

        def ropeevac(b0, b0k, b1, b1k, dst, dk, G=G):
            A("dve", lambda e: e.tensor_tensor(out=t1a[:], in0=b0[:], in1=cs[G % 2][:], op=ALU.mult),
              r=[b0k, ("cs", G % 2)], w=["t1a"])
            A("dve", lambda e: e.tensor_tensor(out=t1b[:], in0=b1[:], in1=sn[G % 2][:], op=ALU.mult),
              r=[b1k, ("sn", G % 2)], w=["t1b"])
            A("dve", lambda e: e.tensor_tensor(out=dst, in0=t1a[:], in1=t1b[:], op=ALU.add), r=["t1a", "t1b"], w=[dk])
        ropeevac(pk[2], "pk2", pk[3], "pk3", KsT[:, G * 512:(G + 1) * 512], ("KsT", G))
        if G >= 10:
            proj(WA, 512, pk[0], "pk0", ["WA"])
            proj(WAr, 128, pk[1], "pk1", WArk)
            ropeevac(pk[0], "pk0", pk[1], "pk1", KwT[:, (G - 10) * 512:(G - 9) * 512], ("KwT", G))
        for si in range(4):
            s = 4 * G + si
            nv = 2 if G >= 10 else 1
            for vi in range(nv):
                c0 = 384 if vi == 0 else 640
                for kc in range(8):
                    A("pe", lambda e, kc=kc, si=si, vi=vi, c0=c0, hb=hb: e.matmul(
                        pv[:, vi * 128:(vi + 1) * 128], lhsT=hb[:, kc, si * 128:(si + 1) * 128], rhs=WA[:, kc, c0:c0 + 128],
                        start=(kc == 0), stop=(kc == 7)), r=[hk + (si,), "WA"], w=["pv"])
            A("act", lambda e, s=s: e.copy(out=Vs[:, s, 0:64], in_=pv[:, 0:64]), r=["pv", "Vs0"], w=[("Vs", s, 0)])
            A("act", lambda e, s=s: e.copy(out=Vs[:, s, 130:194], in_=pv[:, 64:128]), r=["pv", "Vs0"], w=[("Vs", s, 1)])
            if G >= 10:
                A("act", lambda e, s=s: e.copy(out=Vw[:, s - 40, 0:64], in_=pv[:, 128:192]), r=["pv", "Vw0"], w=[("Vw", s, 0)])
                A("act", lambda e, s=s: e.copy(out=Vw[:, s - 40, 130:194], in_=pv[:, 192:256]), r=["pv", "Vw0"], w=[("Vw", s, 1)])
    if "KsT" in dbgout:
        A("sp", lambda e: e.dma_start(out=dbgout["KsT"], in_=KsT[:]), r=[("KsT", G) for G in range(16)], dma="dbg")
        A("sp", lambda e: e.dma_start(out=dbgout["Vs"], in_=Vs[:]), r=[("Vs", s, i) for s in range(64) for i in range(2)] + ["Vs1", "Vs2"], dma="dbg")
        A("sp", lambda e: e.dma_start(out=dbgout["KwT"], in_=KwT[:]), r=[("KwT", G) for G in range(10, 16)], dma="dbg")
    ph.emit()
    pst.close()
    ist.close()
    _phase_A2(nc, st, SB, PS, dr, P, dbgout, rawK, rawV, t1a, t1b)


def _phase_A2(nc, st0, SB, PS, dr, P, dbgout, rawK, rawV, t1a, t1b):
    kcT, RCv, ones32 = P["kcT"], P["RCv"], P["ones32"]
    with contextlib.ExitStack() as st:
        w1 = [SB(st, f"w1{x}", [128, 32, 256], BF16) for x in range(2)]
        w2 = [SB(st, f"w2{x}", [128, 2, 64], BF16) for x in range(2)]
        posT = [SB(st, f"posT{x}", [128, 32, 2], BF16) for x in range(2)]
        b1 = [SB(st, f"b1{x}", [128, 2], F32) for x in range(2)]
        b1e = [SB(st, f"b1e{x}", [128, 2], F32) for x in range(2)]
        b2row = [SB(st, f"b2row{x}", [1, 128], F32) for x in range(2)]
        b2rrow = SB(st, "b2rrow", [1, 128], F32)
        w2pad = [SB(st, f"w2pad{g}", [128, 2, 128], BF16) for g in range(2)]
        w2padr = [SB(st, f"w2padr{g}", [128, 2, 128], BF16) for g in range(2)]
        hid = [[SB(st, f"hid{x}{g}", [128, 2, 512], BF16) for g in range(2)] for x in range(2)]
        u = SB(st, "u", [128, 512], F32)
        u2 = SB(st, "u2", [128, 512], F32)
        th = SB(st, "th", [128, 512], F32)
        csC = SB(st, "csC", [128, 512], F32)
        snC = SB(st, "snC", [128, 512], F32)
        pb = PS(st, "pb", [128, 512], F32)
        ph_ = [PS(st, f"ph{i}", [128, 512], F32) for i in range(2)]
        po = [PS(st, f"po{i}", [128, 512], F32) for i in range(2)]
        ph = Phase(nc, "A2")
        A = ph.add
        NCMP = 511
        for x, nm in enumerate(("k", "v")):
            src = dr[f"c{nm}_w1"].rearrange("(p d) h -> d p h", d=64)
            A("pool", lambda e, x=x, src=src: e.dma_start(out=w1[x][0:64], in_=src), w=[("w1", x, 0)], dma="w")
            A("pool", lambda e, x=x, src=src: e.dma_start(out=w1[x][64:128], in_=src), w=[("w1", x, 1)], dma="w")
            A("pool", lambda e, x=x, nm=nm: e.dma_start(out=w2[x][:], in_=dr[f"c{nm}_w2"].rearrange("(k p) d -> p k d", p=128)),
              w=[("w2", x)], dma="w")
            A("pool", lambda e, x=x, nm=nm: e.dma_start(out=posT[x][:], in_=dr[f"c{nm}_posT"]), w=[("posT", x)], dma="w")
            A("sp", lambda e, x=x, nm=nm: e.dma_start(out=b1[x][:], in_=dr[f"c{nm}_b1"]), w=[("b1", x)], dma="c")
            A("sp", lambda e, x=x, nm=nm: e.dma_start(out=b2row[x][:], in_=dr[f"c{nm}_b2"]), w=[("b2row", x)], dma="c")
        A("sp", lambda e: e.dma_start(out=csC[:], in_=dr["cosC"]), w=["csC"], dma="c")
        A("sp", lambda e: e.dma_start(out=snC[:], in_=dr["sinC"]), w=["snC"], dma="c")
        for g in range(2):
            A("dve", lambda e, g=g: e.memset(w2pad[g][:], 0.0), w=[("w2pad", g)])
            A("dve", lambda e, g=g: e.memset(w2padr[g][:], 0.0), w=[("w2padr", g)])
            A("dve", lambda e, g=g: e.tensor_copy(out=w2pad[g][:, :, g * 64:(g + 1) * 64], in_=w2[0][:]),
              r=[("w2", 0)], w=[("w2pad", g)])
            A("dve", lambda e, g=g: e.tensor_scalar(out=w2padr[g][:, :, g * 64:g * 64 + 32], in0=w2[0][:, :, 32:64],
                                                     scalar1=-1.0, scalar2=None, op0=ALU.mult), r=[("w2", 0)], w=[("w2padr", g)])
            A("dve", lambda e, g=g: e.tensor_copy(out=w2padr[g][:, :, g * 64 + 32:g * 64 + 64], in_=w2[0][:, :, 0:32]),
              r=[("w2", 0)], w=[("w2padr", g)])
            A("dve", lambda e, g=g: e.tensor_scalar(out=b2rrow[0:1, g * 64:g * 64 + 32], in0=b2row[0][0:1, g * 64 + 32:g * 64 + 64],
                                                     scalar1=-1.0, scalar2=None, op0=ALU.mult), r=[("b2row", 0)], w=["b2rrow"])
            A("dve", lambda e, g=g: e.tensor_copy(out=b2rrow[0:1, g * 64 + 32:g * 64 + 64], in_=b2row[0][0:1, g * 64:g * 64 + 32]),
              r=[("b2row", 0)], w=["b2rrow"])
        for x in range(2):
            raw = rawK if x == 0 else rawV
            for half in range(2):
                for p in range(32):
                    A("pe", lambda e, x=x, half=half, p=p: e.matmul(
                        pb[:, half * 2:half * 2 + 2], lhsT=w1[x][0:64, p, half * 128:(half + 1) * 128], rhs=posT[x][0:64, p, :],
                        start=(p == 0), stop=(p == 31)), r=[("w1", x, 0), ("posT", x)], w=["pb"])
            A("dve", lambda e, x=x: e.tensor_tensor(out=b1e[x][:], in0=pb[:, 0:4:2], in1=b1[x][:], op=ALU.add),
              r=["pb", ("b1", x)], w=[("b1e", x)])
            for g in range(2):
                for half in range(2):
                    bank = ph_[half]
                    for p in range(32):
                        A("pe", lambda e, x=x, g=g, half=half, p=p, raw=raw, bank=bank: e.matmul(
                            bank[:, 0:NCMP], lhsT=w1[x][64 * g:64 * g + 64, p, half * 128:(half + 1) * 128],
                            rhs=raw[64 * g:64 * g + 64, p:p + 16 * (NCMP - 1) + 1:16],
                            start=(p == 0), stop=(p == 31)), r=[("w1", x, g)], w=[("ph", half)])
                    A("act", lambda e, x=x, half=half, bank=bank: e.activation(out=u[:, 0:NCMP], in_=bank[:, 0:NCMP], func=AF.Identity,
                                                                               bias=b1e[x][:, half:half + 1], scale=1.0),
                      r=[("ph", half), ("b1e", x)], w=["u"])
                    A("dve", lambda e: e.tensor_tensor(out=u2[:, 0:NCMP], in0=u[:, 0:NCMP], in1=u[:, 0:NCMP], op=ALU.mult), r=["u"], w=["u2"])
                    A("dve", lambda e: e.tensor_scalar(out=u2[:, 0:NCMP], in0=u2[:, 0:NCMP], scalar1=0.044715, scalar2=1.0,
                                                        op0=ALU.mult, op1=ALU.add), r=["u2"], w=["u2"])
                    A("dve", lambda e: e.tensor_tensor(out=u2[:, 0:NCMP], in0=u2[:, 0:NCMP], in1=u[:, 0:NCMP], op=ALU.mult), r=["u2", "u"], w=["u2"])
                    A("act", lambda e: e.activation(out=th[:, 0:NCMP], in_=u2[:, 0:NCMP], func=AF.Tanh, scale=0.7978845608028654),
                      r=["u2"], w=["th"])
                    A("dve", lambda e: e.tensor_scalar(out=th[:, 0:NCMP], in0=th[:, 0:NCMP], scalar1=0.5, scalar2=0.5,
                                                        op0=ALU.mult, op1=ALU.add), r=["th"], w=["th"])
                    A("dve", lambda e, x=x, g=g, half=half: e.tensor_tensor(out=hid[x][g][:, half, 0:NCMP], in0=th[:, 0:NCMP],
                                                                             in1=u[:, 0:NCMP], op=ALU.mult),
                      r=["th", "u"], w=[("hid", x, g, half)])
        for r_, (pads, brow, bank, bk) in enumerate(((w2pad, b2row[0], po[0], "po0"), (w2padr, b2rrow, po[1], "po1"))):
            first = True
            for g in range(2):
                for half in range(2):
                    A("pe", lambda e, g=g, half=half, pads=pads, bank=bank, first=first: e.matmul(
                        bank[:, 0:NCMP], lhsT=pads[g][:, half, :], rhs=hid[0][g][:, half, 0:NCMP], start=first, stop=False),
                      r=[("hid", 0, g, half), ("w2pad", g), ("w2padr", g)], w=[bk])
                    first = False
            A("pe", lambda e, brow=brow, bank=bank: e.matmul(bank[:, 0:NCMP], lhsT=brow[0:1, 0:128], rhs=ones32[0:1, 0:NCMP],
                                                              start=False, stop=True), r=[("b2row", 0), "b2rrow"], w=[bk])
        A("dve", lambda e: e.tensor_tensor(out=t1a[:, 0:NCMP], in0=po[0][:, 0:NCMP], in1=csC[:, 0:NCMP], op=ALU.mult),
          r=["po0", "csC"], w=["t1a"])
        A("dve", lambda e: e.tensor_tensor(out=t1b[:, 0:NCMP], in0=po[1][:, 0:NCMP], in1=snC[:, 0:NCMP], op=ALU.mult),
          r=["po1", "snC"], w=["t1b"])
        A("dve", lambda e: e.memset(kcT[:], 0.0), w=["kcT"])
        A("dve", lambda e: e.tensor_tensor(out=kcT[:, 0:NCMP], in0=t1a[:, 0:NCMP], in1=t1b[:, 0:NCMP], op=ALU.add),
          r=["t1a", "t1b"], w=["kcT"])
        for c in range(4):
            n = 128 if c < 3 else NCMP - 384
            bank = po[c % 2]
            bk = f"po{c % 2}"
            for g in range(2):
                for half in range(2):
                    A("pe", lambda e, c=c, n=n, g=g, half=half, bank=bank: e.matmul(
                        bank[0:n, g * 64:(g + 1) * 64], lhsT=hid[1][g][:, half, c * 128:c * 128 + n], rhs=w2[1][:, half, :],
                        start=(half == 0), stop=False), r=[("hid", 1, g, half), ("w2", 1)], w=[bk])
                A("pe", lambda e, n=n, g=g, bank=bank: e.matmul(
                    bank[0:n, g * 64:(g + 1) * 64], lhsT=ones32[0:1, 0:n], rhs=b2row[1][0:1, g * 64:(g + 1) * 64],
                    start=False, stop=True), r=[("b2row", 1)], w=[bk])
            A("act", lambda e, c=c, n=n, bank=bank: e.copy(out=RCv[0:n, c, 0:64], in_=bank[0:n, 0:64]), r=[bk, "RCv"], w=[("RCv", c, 0)])
            A("act", lambda e, c=c, n=n, bank=bank: e.copy(out=RCv[0:n, c, 130:194], in_=bank[0:n, 64:128]), r=[bk, "RCv"], w=[("RCv", c, 1)])
            A("dve", lambda e, c=c, n=n: e.memset(RCv[0:n, c, 64:65], 1.0), r=["RCv"], w=[("RCv", c, 2)])
            A("dve", lambda e, c=c, n=n: e.memset(RCv[0:n, c, 66:67], 1.0), r=["RCv"], w=[("RCv", c, 3)])
        if "kcT" in dbgout:
            A("sp", lambda e: e.dma_start(out=dbgout["kcT"], in_=kcT[:]), r=["kcT"], dma="dbg")
            A("sp", lambda e: e.dma_start(out=dbgout["RCv"], in_=RCv[:]), r=[("RCv", c, i) for c in range(4) for i in range(4)], dma="dbg")
        ph.emit()


def _phase_B(nc, st, SB, PS, dr, gscr, oscr, P, dbgout):
    KsT, KwT, Vs, Vw, kcT, RCv, biasC = P["KsT"], P["KwT"], P["Vs"], P["Vw"], P["kcT"], P["RCv"], P["biasC"]
    ones32, gsb, vq = P["ones32"], P["gsb"], P["vq"]
    identf = SB(st, "identfB", [128, 128], F32)
    Wq = SB(st, "Wq", [128, 8, 1024], BF16)
    Wqr = SB(st, "Wqr", [128, 8, 1024], BF16)
    Wg = SB(st, "Wg", [128, 8, 48], BF16)
    EF = SB(st, "EF", [128, 16, 128], BF16)
    ovl = SB(st, "ovl", [128, 4, 129], BF16)
    xq1 = SB(st, "xq0", [128, D], F32)
    xq = [xq1, xq1]
    xnf = SB(st, "xnf", [128, D], F32)
    ssq = SB(st, "ssqB", [128, 1], F32)
    rstd = SB(st, "rstdB", [128, 1], F32)
    hTq = SB(st, "hTq", [128, 8, 128], BF16)
    cq1 = SB(st, "cq0", [128, 128], F32)
    sq1 = SB(st, "sq0", [128, 128], F32)
    cq = [cq1, cq1]
    sq = [sq1, sq1]
    bon3 = [SB(st, f"bon{i}", [128, 128], F32) for i in range(3)]
    mkb3 = [SB(st, f"mkb{i}", [128, 6, 128], BF16) for i in range(3)]
    QT = [SB(st, f"QT{i}", [128, 8, 128], BF16) for i in range(2)]
    gsig = [SB(st, f"gsig{i}", [48, 128], F32) for i in range(2)]
    selT = [[SB(st, f"selT{i}{g}", [128, 128], BF16) for g in range(2)] for i in range(2)]
    acc = [SB(st, f"acc{i}", [128, 8, 128], F32) for i in range(2)]
    Ec = [SB(st, f"Ec{c}", [128, 8, 128], BF16) for c in range(4)]
    NBUF = 5
    E = [SB(st, f"E{i}", [128, 8, 128], BF16) for i in range(NBUF)]
    Pb = [SB(st, f"Pb{i}", [128, 8, 128], BF16) for i in range(NBUF)]
    oasb = [SB(st, f"oasb{i}", [128, 1024], F32) for i in range(2)]
    msk = [SB(st, f"msk{i}", [128, 128], BF16) for i in range(NBUF)]
    t1 = SB(st, "t1B", [128, 8, 128], F32)
    t2 = SB(st, "t2B", [128, 8, 128], F32)
    dsb = SB(st, "dsb", [65, 1024], F32)
    rdb = SB(st, "rdb", [128, 4, 128], F32)
    tmpf = SB(st, "tmpf", [128, 4, 128], F32)
    grow = [SB(st, f"grow{i}", [65, 1024], F32) for i in range(2)]
    dsb2 = SB(st, "dsb2", [65, 1024], F32)
    cbc = SB(st, "cbc", [128, 1024], F32)
    crow = nc.dram_tensor("crow", [8, 1024], F32).ap()
    score = SB(st, "score", [128, 128], F32)
    work = SB(st, "work", [128, 128], F32)
    selq = SB(st, "selq", [128, 128], F32)
    m8a = SB(st, "m8a", [128, 8], F32)
    m8b = SB(st, "m8b", [128, 8], F32)
    thr = SB(st, "thr", [128, 1], F32)
    rc = SB(st, "rc", [128, 8], F32)
    obf = SB(st, "obf", [128, 8, 128], BF16)
    scA = PS(st, "scA", [128, 1024], F32)
    scB = PS(st, "scB", [128, 1024], F32)
    oa = PS(st, "oa", [128, 1024], F32)
    mx = PS(st, "mx", [128, 512], F32)
    msc = PS(st, "msc", [128, 512], F32)
    gs2 = gscr.rearrange("b r q -> b (r q)")
    print("[phase B] sbuf bytes remaining:", nc.sbuf_bytes_remaining)

    ph = Phase(nc, "B")
    A = ph.add
    A("pool", lambda e: e.dma_start(out=Wq[:], in_=dr["wq"].rearrange("(k p) n -> p k n", p=128)), w=["Wq"], dma="w")
    A("pool", lambda e: e.dma_start(out=Wg[:], in_=dr["wg"].rearrange("(k p) n -> p k n", p=128)), w=["Wg"], dma="w")
    A("sp", lambda e: e.dma_start(out=EF[:], in_=dr["ef"]), w=["EF"], dma="c")
    A("sp", lambda e: e.dma_start(out=ovl[:], in_=dr["ovl"]), w=["ovl"], dma="c")
    A("sp", lambda e: e.dma_start(out=identf[:], in_=dr["identf"]), w=["identf"], dma="c")
    for kc in range(8):
        v = Wq[:, kc, :].rearrange("p (h two d) -> p h two d", two=2, d=32)
        vr = Wqr[:, kc, :].rearrange("p (h two d) -> p h two d", two=2, d=32)
        A("dve", lambda e, v=v, vr=vr: e.tensor_scalar(out=vr[:, :, 0, :], in0=v[:, :, 1, :], scalar1=-1.0, scalar2=None, op0=ALU.mult),
          r=["Wq"], w=[("Wqr", kc, 0)])
        A("dve", lambda e, v=v, vr=vr: e.tensor_copy(out=vr[:, :, 1, :], in_=v[:, :, 0, :]), r=["Wq"], w=[("Wqr", kc, 1)])
    Wqrk = [("Wqr", kc, i) for kc in range(8) for i in range(2)]

    def v8(t, nq):
        return t[:].rearrange("p (h q) -> p h q", q=128)[:, :, 0:nq]

    def v4(t, u, nq, p0=0, p1=128):
        return t[p0:p1, u * 512:(u + 1) * 512].rearrange("p (h q) -> p h q", q=128)[:, :, 0:nq]

    def bc(ap2, n, nq):
        return ap2.unsqueeze(1).to_broadcast([ap2.shape[0], n, nq])

    def stage_load(bi):
        S, off, nq, col0 = BLK[bi]
        t0 = 128 * S + off
        b3 = bi % 3
        A("sp", lambda e: e.dma_start(out=xq1[0:nq, :], in_=dr["xs"][t0:t0 + nq, :]), w=["xqb"], dma="ldq")
        A("sp", lambda e: e.dma_start(out=cq1[:, 0:nq], in_=dr["cosQ"][:, bi * 128:bi * 128 + nq]), w=["cqb"], dma="ldq")
        A("sp", lambda e: e.dma_start(out=sq1[:, 0:nq], in_=dr["sinQ"][:, bi * 128:bi * 128 + nq]), w=["sqb"], dma="ldq")
        A("sp", lambda e: e.dma_start(out=bon3[b3][0:nq, :], in_=dr["bonus"][0:nq, bi, :]), w=[("bon", b3)], dma=("ldm", b3))
        A("sp", lambda e: e.dma_start(out=mkb3[b3][:], in_=dr["mk"][bi]), w=[("mkb", b3)], dma=("ldm", b3))

    def stage_q(bi):
        S, off, nq, col0 = BLK[bi]
        pb = bi % 2
        A("act", lambda e: e.activation(out=xnf[0:nq, :], in_=xq[pb][0:nq, :], func=AF.Square, accum_out=ssq[0:nq, :]),
          r=["xqb"], w=["xnf", "ssq"])
        A("act", lambda e: e.activation(out=rstd[0:nq, :], in_=ssq[0:nq, :], func=AF.Sqrt, bias=EPS, scale=1.0 / D), r=["ssq"], w=["rstd"])
        A("dve", lambda e: e.reciprocal(out=rstd[0:nq, :], in_=rstd[0:nq, :]), r=["rstd"], w=["rstd"])
        A("dve", lambda e: e.tensor_scalar(out=xnf[0:nq, :], in0=xq[pb][0:nq, :], scalar1=rstd[0:nq, :], scalar2=None, op0=ALU.mult),
          r=["xqb", "rstd"], w=["xnf"])
        for kc in range(8):
            A("pe", lambda e, kc=kc: e.transpose(out=scA[:, kc * 128:kc * 128 + nq], in_=xnf[0:nq, kc * 128:(kc + 1) * 128],
                                                 identity=identf[0:nq, 0:nq]), r=["xnf", "identf"], w=[("scA", kc // 4)])
        A("dve", lambda e: e.tensor_tensor(out=hTq[:, :, 0:nq], in0=v8(scA, nq), in1=gsb[:, 0, :].unsqueeze(2).to_broadcast([128, 8, nq]),
                                           op=ALU.mult), r=[("scA", 0), ("scA", 1)], w=["hTq"])
        for hl in range(8):
            for kc in range(8):
                A("pe", lambda e, hl=hl, kc=kc: e.matmul(scB[:, hl * 128:hl * 128 + nq], lhsT=Wq[:, kc, hl * 128:(hl + 1) * 128],
                                                         rhs=hTq[:, kc, 0:nq], start=(kc == 0), stop=(kc == 7)),
                  r=["Wq", "hTq"], w=[("scB", hl // 4)])
        for hl in range(8):
            for kc in range(8):
                A("pe", lambda e, hl=hl, kc=kc: e.matmul(oa[:, hl * 128:hl * 128 + nq], lhsT=Wqr[:, kc, hl * 128:(hl + 1) * 128],
                                                         rhs=hTq[:, kc, 0:nq], start=(kc == 0), stop=(kc == 7)),
                  r=Wqrk + ["hTq"], w=[("oa", hl // 4)])
        A("dve", lambda e: e.tensor_tensor(out=t1[:, :, 0:nq], in0=v8(scB, nq), in1=bc(cq[pb][:, 0:nq], 8, nq), op=ALU.mult),
          r=[("scB", 0), ("scB", 1), "cqb"], w=[("t1", 0), ("t1", 3), ("t1", 6)])
        A("dve", lambda e: e.tensor_tensor(out=t2[:, :, 0:nq], in0=v8(oa, nq), in1=bc(sq[pb][:, 0:nq], 8, nq), op=ALU.mult),
          r=[("oa", 0), ("oa", 1), "sqb"], w=["t2"])
        A("dve", lambda e: e.tensor_tensor(out=QT[pb][:, :, 0:nq], in0=t1[:, :, 0:nq], in1=t2[:, :, 0:nq], op=ALU.add),
          r=[("t1", 0), ("t1", 3), ("t1", 6), "t2"], w=[("QT", pb)])
        for kc in range(8):
            A("pe", lambda e, kc=kc: e.matmul(mx[0:48, 0:nq], lhsT=Wg[:, kc, :], rhs=hTq[:, kc, 0:nq], start=(kc == 0), stop=(kc == 7)),
              r=["Wg", "hTq"], w=MXALL)
        A("act", lambda e: e.activation(out=gsig[pb][:, 0:nq], in_=mx[0:48, 0:nq], func=AF.Sigmoid), r=MXALL, w=[("gsig", pb)])
        A("sp", lambda e: e.dma_start(out=gscr[bi, :, 0:nq], in_=gsig[pb][:, 0:nq]), r=[("gsig", pb)], w=[("gscr", bi)], dma=("gs", pb))

    growi = [0]
    grpi = [0]
    MXALL = ["mx"]

    DEFER = True
    pending = []
    stepc = [0]

    def flush(force=False):
        while pending and (force or pending[0][0] <= stepc[0]):
            pending.pop(0)[1]()

    fini = [0]

    def finalize(bi, g, br, first, src, srck, delay):
        S, off, nq, col0 = BLK[bi]
        pb = bi % 2
        p0 = 64 * g
        dp = 64 if g == 0 else 0
        flush(force=True)
        fi = fini[0]
        fini[0] += 1
        X, xk = (dsb, "dsb") if fi % 2 == 0 else (dsb2, "dsb2")
        ri = fi % 8
        gi = growi[0] % 2
        growi[0] += 1
        r0 = br * 16 + g * 8
        A("sp", lambda e: e.dma_start(out=grow[gi][dp:dp + 1, :], in_=gs2[bi:bi + 1, r0 * 128:r0 * 128 + 1024]),
          r=[("gscr", bi)], w=[("grow", gi)], dma=("gr", gi))
        A("dve", lambda e: e.tensor_scalar(out=X[dp:dp + 1, :], in0=src[dp:dp + 1, :], scalar1=1.0e-30, scalar2=None, op0=ALU.max),
          r=[srck(0), srck(1)], w=[xk])
        A("act", lambda e: e.activation(out=X[dp:dp + 1, :], in_=X[dp:dp + 1, :], func=AF.Ln), r=[xk], w=[xk])
        A("act", lambda e: e.activation(out=X[dp:dp + 1, :], in_=X[dp:dp + 1, :], func=AF.Exp, scale=-1.0), r=[xk], w=[xk])
        A("dve", lambda e: e.tensor_tensor(out=X[dp:dp + 1, :], in0=X[dp:dp + 1, :], in1=grow[gi][dp:dp + 1, :], op=ALU.mult),
          r=[xk, ("grow", gi)], w=[xk])
        A("sp", lambda e: e.dma_start(out=crow[ri:ri + 1, :], in_=X[dp:dp + 1, :]), r=[xk], w=[("crow", ri)], dma=("cr", ri % 2))
        A("sp", lambda e: e.dma_start(out=cbc[p0:p0 + 64, :], in_=crow[ri:ri + 1, :].partition_broadcast(64)),
          r=[("crow", ri)], w=[("cbc", g)], dma=("cb", g))
        def tail():
            for u in range(2):
                dst = acc[pb][p0:p0 + 64, 4 * u:4 * u + 4, 0:nq]
                if first:
                    A("dve", lambda e, u=u, dst=dst: e.tensor_tensor(out=dst, in0=v4(src, u, nq, p0, p0 + 64), in1=v4(cbc, u, nq, p0, p0 + 64),
                                                                     op=ALU.mult), r=[srck(u), ("cbc", g)], w=[("acc", pb, g, u)])
                else:
                    A("dve", lambda e, u=u: e.tensor_tensor(out=tmpf[p0:p0 + 64, :, 0:nq], in0=v4(src, u, nq, p0, p0 + 64),
                                                            in1=v4(cbc, u, nq, p0, p0 + 64), op=ALU.mult), r=[srck(u), ("cbc", g)], w=["tmpf"])
                    A("dve", lambda e, dst=dst: e.tensor_tensor(out=dst, in0=dst, in1=tmpf[p0:p0 + 64, :, 0:nq], op=ALU.add),
                      r=["tmpf", ("acc", pb, g, u)], w=[("acc", pb, g, u)])
        if DEFER:
            pending.append((stepc[0] + delay, tail))
        else:
            tail()

    def vaug(Vt, idx, g):
        return Vt[:, idx, 0:128] if g == 0 else Vt[:, idx, 66:194]

    def stage_cmp(bi):
        fins = [stage_cmp_g(bi, g) for g in range(2)]
        for g, (ob, obk) in enumerate(fins):
            finalize(bi, g, 0, True, ob, lambda u, obk=obk: obk + (u,), 4)

    def stage_cmp_g(bi, g):
        S, off, nq, col0 = BLK[bi]
        pb = bi % 2
        M = [128, 128]
        if True:
            for c in range(4):
                sc, sk = (scA, "scA") if c % 2 == 0 else (scB, "scB")
                for u in range(2):
                    A("pe", lambda e, c=c, u=u, sc=sc: e.matmul(v4(sc, u, nq), lhsT=kcT[64 * g:64 * g + 64, c * 128:(c + 1) * 128],
                                                              rhs=QT[pb][64 * g:64 * g + 64, 4 * u:4 * u + 4, 0:nq], start=True, stop=True),
                      r=["kcT", ("QT", pb)], w=[(sk, u)])
                A("act", lambda e, c=c, sc=sc: e.activation(out=Ec[c][:, :, 0:nq], in_=v8(sc, nq), func=AF.Exp, bias=biasC[:, c:c + 1],
                                                            scale=SCALE), r=[(sk, 0), (sk, 1), "biasC"], w=[("Ec", c)])
                A("dve", lambda e, c=c: e.tensor_tensor(out=Ec[c][:, :, 0:nq], in0=Ec[c][:, :, 0:nq],
                                                        in1=bc(mkb3[bi % 3][:, 2 + c, 0:nq], 8, nq), op=ALU.mult),
                  r=[("Ec", c), ("mkb", bi % 3)], w=[("Ec", c)])
            for u in range(2):
                for c in range(4):
                    A("pe", lambda e, c=c, u=u: e.matmul(v4(oa, u, nq, 0, M[g]), lhsT=vaug(RCv, c, g), rhs=Ec[c][:, 4 * u:4 * u + 4, 0:nq],
                                                         start=(c == 0), stop=(c == 3)), r=[("Ec", c), "RCv"], w=[("oa", u)])
            regs = []
            for hl in range(8):
                j, o = hl // 3, (hl % 3) * 129
                tt, tk = [(scA, ("scA", 0)), (scA, ("scA", 1)), (scB, ("scB", 0))][j]
                base = 512 if j == 1 else 0
                regs.append((tt, tk, base + o))
                for c in range(4):
                    A("pe", lambda e, hl=hl, c=c, tt=tt, base=base, o=o: e.matmul(
                        tt[0:nq, base + o:base + o + 129], lhsT=Ec[c][:, hl, 0:nq], rhs=ovl[:, c, :], start=(c == 0), stop=(c == 3)),
                      r=[("Ec", c), "ovl"], w=[tk])
            banks = [(scA, ("scA", 0), 0, 3, 0), (scA, ("scA", 1), 512, 3, 3), (scB, ("scB", 0), 0, 2, 6)]
            for tt, tk, base, nh, h0 in banks:
                A("dve", lambda e, tt=tt, base=base, nh=nh, h0=h0: e.tensor_scalar(
                    out=rc[0:nq, h0:h0 + nh], in0=tt[0:nq, base + 128:base + 128 + 129 * (nh - 1) + 1:129], scalar1=1.0e-30,
                    scalar2=None, op0=ALU.max), r=[tk], w=[("rc", h0)])
            A("dve", lambda e: e.reciprocal(out=rc[0:nq, :], in_=rc[0:nq, :]), r=[("rc", 0), ("rc", 3), ("rc", 6)], w=["rc"])
            for tt, tk, base, nh, h0 in banks:
                A("dve", lambda e, tt=tt, base=base, nh=nh, h0=h0: e.tensor_tensor(
                    out=t1[0:nq, h0:h0 + nh, :], in0=tt[0:nq, base:base + 129 * nh].rearrange("p (h c) -> p h c", c=129)[:, :, 0:128],
                    in1=rc[0:nq, h0:h0 + nh].unsqueeze(2).to_broadcast([nq, nh, 128]), op=ALU.mult), r=[tk, "rc"], w=[("t1", h0)])
            A("dve", lambda e: e.tensor_reduce(out=score[0:nq, :], in_=t1[0:nq, :, :].rearrange("p h s -> p s h"), axis=AX.X, op=ALU.add),
              r=[("t1", 0), ("t1", 3), ("t1", 6)], w=["score"])
            A("dve", lambda e: e.tensor_tensor(out=score[0:nq, :], in0=score[0:nq, :], in1=bon3[bi % 3][0:nq, :], op=ALU.add),
              r=["score", ("bon", bi % 3)], w=["score"])
            A("dve", lambda e: e.max(out=m8a[0:nq, :], in_=score[0:nq, :]), r=["score"], w=["m8a"])
            A("dve", lambda e: e.match_replace(out=work[0:nq, :], in_to_replace=m8a[0:nq, :], in_values=score[0:nq, :], imm_value=-3.0e38),
              r=["score", "m8a"], w=["work"])
            A("dve", lambda e: e.max(out=m8b[0:nq, :], in_=work[0:nq, :]), r=["work"], w=["m8b"])
            A("dve", lambda e: e.tensor_scalar(out=thr[0:nq, :], in0=m8b[0:nq, 7:8], scalar1=-1.0e29, scalar2=None, op0=ALU.max),
              r=["m8b"], w=["thr"])
            A("dve", lambda e: e.tensor_scalar(out=selq[0:nq, :], in0=score[0:nq, :], scalar1=thr[0:nq, :], scalar2=None, op0=ALU.is_ge),
              r=["score", "thr"], w=["selq"])
            A("pe", lambda e: e.transpose(out=mx[:, 0:nq], in_=selq[0:nq, :], identity=identf[0:nq, 0:nq]), r=["selq", "identf"], w=MXALL)
            A("act", lambda e, g=g: e.copy(out=selT[pb][g][:, 0:nq], in_=mx[:, 0:nq]), r=MXALL, w=[("selT", pb, g)])
            flush(force=True)
            ob = oasb[grpi[0] % 2]
            obk = ("oasb", grpi[0] % 2)
            grpi[0] += 1
            for u in range(2):
                A("act", lambda e, u=u, ob=ob: e.copy(out=ob[:, u * 512:(u + 1) * 512], in_=oa[:, u * 512:(u + 1) * 512]),
                  r=[("oa", u)], w=[obk + (u,)])
            return ob, obk

    LAG = 4
    FILL = 1
    WARM = 0
    XFILL = 16

    def stage_attn(bi):
        S, off, nq, col0 = BLK[bi]
        pb = bi % 2
        M = [128, 128]
        items = []
        gidx = []
        for g in range(2):
            for br in (1, 2):
                kts = list(range(0, S + 1)) if br == 1 else list(range(S - 4, S + 1))
                for idx, kt in enumerate(kts):
                    items.append((g, br, kt, idx == 0, idx == len(kts) - 1))
                    gidx.append(idx)
        N = len(items)
        srcs = [None] * N
        mids = [None] * N

        def front(i):
            g, br, kt, isfirst, islast = items[i]
            KT, Vt, koff = (KsT, Vs, 0) if br == 1 else (KwT, Vw, 40)
            sc, sk = (scA, "scA") if i % 2 == 0 else (scB, "scB")
            Eb, ek = E[i % NBUF], ("E", i % NBUF)
            Pq, pk_ = Pb[i % NBUF], ("Pb", i % NBUF)
            mb, mbk = msk[i % NBUF], ("msk", i % NBUF)
            masked = True
            if br == 1:
                a, v = kt // 16, kt % 16
                kw = dict(tile_position=(96, 0)) if a == 3 else {}
                A("pe", lambda e: e.matmul(mx[:, 0:nq], lhsT=EF[32 * a:32 * a + 32, v, :], rhs=selT[pb][g][32 * a:32 * a + 32, 0:nq],
                                           start=True, stop=True, **kw), r=["EF", ("selT", pb, g)], w=["mx"])
                if kt == S:
                    A("dve", lambda e: e.tensor_tensor(out=mb[:, 0:nq], in0=mx[:, 0:nq], in1=mkb3[bi % 3][:, 0, 0:nq], op=ALU.mult),
                      r=["mx", ("mkb", bi % 3)], w=[mbk])
                else:
                    A("dve", lambda e: e.tensor_copy(out=mb[:, 0:nq], in_=mx[:, 0:nq]), r=["mx"], w=[mbk])
                mask_ap, mask_r = mb[:, 0:nq], [mbk]
            else:
                if kt == S - 4:
                    mask_ap, mask_r = mkb3[bi % 3][:, 1, 0:nq], [("mkb", bi % 3)]
                elif kt == S:
                    mask_ap, mask_r = mkb3[bi % 3][:, 0, 0:nq], [("mkb", bi % 3)]
                else:
                    masked = False
            for u in range(2):
                A("pe", lambda e, u=u: e.matmul(v4(sc, u, nq), lhsT=KT[64 * g:64 * g + 64, (kt - koff) * 128:(kt - koff + 1) * 128],
                                                rhs=QT[pb][64 * g:64 * g + 64, 4 * u:4 * u + 4, 0:nq], start=True, stop=True),
                  r=[("QT", pb)], w=[(sk, u)])
            A("act", lambda e: e.activation(out=Eb[:, :, 0:nq], in_=v8(sc, nq), func=AF.Exp, scale=SCALE), r=[(sk, 0), (sk, 1)], w=[ek])
            for _ in range(FILL + (1 if gidx[i] < XFILL else 0)):
                A("pe", lambda e: e.matmul(msc[:], lhsT=Wq[:, 0, 0:128], rhs=Wq[:, 1, 0:512], start=True, stop=True), r=["Wq"], w=["msc"])
            if masked:
                mids[i] = (Eb, ek, Pq, pk_, mask_ap, mask_r)
                srcs[i] = (Pq, pk_)
            else:
                srcs[i] = (Eb, ek)

        def mid(i):
            if mids[i] is None:
                return
            Eb, ek, Pq, pk_, mask_ap, mask_r = mids[i]
            A("dve", lambda e: e.tensor_tensor(out=Pq[:, :, 0:nq], in0=Eb[:, :, 0:nq], in1=bc(mask_ap, 8, nq), op=ALU.mult),
              r=[ek] + mask_r, w=[pk_])

        def back(i):
            g, br, kt, isfirst, islast = items[i]
            KT, Vt, koff = (KsT, Vs, 0) if br == 1 else (KwT, Vw, 40)
            src, srck = srcs[i]
            for u in range(2):
                A("pe", lambda e, u=u: e.matmul(v4(oa, u, nq, 0, M[g]), lhsT=vaug(Vt, kt - koff, g), rhs=src[:, 4 * u:4 * u + 4, 0:nq],
                                                start=isfirst, stop=islast), r=[srck], w=[("oa", u)])
            if islast:
                flush(force=True)
                ob = oasb[grpi[0] % 2]
                obk = ("oasb", grpi[0] % 2)
                grpi[0] += 1
                for u in range(2):
                    A("act", lambda e, u=u: e.copy(out=ob[:, u * 512:(u + 1) * 512], in_=oa[:, u * 512:(u + 1) * 512]),
                      r=[("oa", u)], w=[obk + (u,)])
                finalize(bi, g, br, False, ob, lambda u: obk + (u,), 4)

        for _ in range(WARM):
            A("pe", lambda e: e.matmul(msc[:], lhsT=Wq[:, 0, 0:128], rhs=Wq[:, 1, 0:512], start=True, stop=True), r=["Wq"], w=["msc"])
        for i in range(N + LAG):
            stepc[0] += 1
            flush()
            if i < N:
                front(i)
            if 0 <= i - 1 < N:
                mid(i - 1)
            if i - LAG >= 0:
                back(i - LAG)
        if DEFER:
            pending.append((stepc[0] + 4, lambda: stage_store(bi)))
        else:
            stage_store(bi)

    def stage_store(bi):
        S, off, nq, col0 = BLK[bi]
        pb = bi % 2
        rk = [("acc", pb, g, u) for g in range(2) for u in range(2)]
        if bi == 0:
            A("dve", lambda e: e.tensor_scalar(out=obf[:, :, 0:nq], in0=acc[pb][:, :, 0:nq], scalar1=vq[:, 0:1], scalar2=None, op0=ALU.mult),
              r=rk + ["vq"], w=["obf"])
        else:
            A("dve", lambda e: e.tensor_copy(out=obf[:, :, 0:nq], in_=acc[pb][:, :, 0:nq]), r=rk, w=["obf"])
        A("sp", lambda e: e.dma_start(out=oscr[:, :, col0:col0 + nq], in_=obf[:, :, 0:nq]), r=["obf"], w=["oscr"], dma="os")
        if "selT" in dbgout and bi == dbgout["_blk"]:
            for g in range(2):
                A("sp", lambda e, g=g: e.dma_start(out=dbgout["selT"][g], in_=selT[pb][g][:]), r=[("selT", pb, g)], dma="dbg")
            A("sp", lambda e: e.dma_start(out=dbgout["QT"], in_=QT[pb][:]), r=[("QT", pb)], dma="dbg")
            A("sp", lambda e: e.dma_start(out=dbgout["acc"], in_=acc[pb][:]), r=rk, dma="dbg")

    nb = dbgout.get("_nblk", NBLK)
    stage_load(0)
    stage_q(0)
    if nb > 1:
        stage_load(1)
    stage_cmp(0)
    for bi in range(nb):
        if bi + 1 < nb:
            stage_q(bi + 1)
            if bi + 2 < nb:
                stage_load(bi + 2)
            stage_cmp(bi + 1)
        stage_attn(bi)
    flush(force=True)
    ph.emit()


def _norm_group(A, xres, c0, n, gcol, onesb, sqb, pn, rs, dst, dkey, tag):
    A("act", lambda e: e.activation(out=sqb[:, :, 0:n], in_=xres[:, :, c0:c0 + n], func=AF.Square), r=["xres"], w=["sqb"])
    for kc in range(8):
        A("pe", lambda e, kc=kc: e.matmul(pn[:, 0:n], lhsT=onesb[:], rhs=sqb[:, kc, 0:n], start=(kc == 0), stop=(kc == 7)),
          r=["sqb"], w=["pn"])
    A("act", lambda e: e.activation(out=rs[:, 0:n], in_=pn[:, 0:n], func=AF.Sqrt, bias=EPS, scale=1.0 / D), r=["pn"], w=["rs"])
    A("dve", lambda e: e.reciprocal(out=rs[:, 0:n], in_=rs[:, 0:n]), r=["rs"], w=["rs"])
    for kc in range(8):
        A("dve", lambda e, kc=kc: e.scalar_tensor_tensor(out=dst[:, kc, 0:n], in0=xres[:, kc, c0:c0 + n], scalar=gcol[:, kc:kc + 1],
                                                        in1=rs[:, 0:n], op0=ALU.mult, op1=ALU.mult), r=["xres", "rs"], w=[dkey])


def _phase_C(nc, st, SB, PS, dr, oscr, P, dbgout):
    xres, identf = P["xres"], P["identf"]
    Wo = SB(st, "Wo", [128, 8, 1024], BF16)
    xq = [SB(st, f"xqC{i}", [128, D], F32) for i in range(2)]
    ot = [SB(st, f"otC{i}", [128, 8, 512], BF16) for i in range(2)]
    pa = [PS(st, f"paC{i}", [128, 1024], F32) for i in range(2)]
    py = [PS(st, f"pyC{i}", [128, 512], F32) for i in range(2)]
    ph = Phase(nc, "C")
    A = ph.add
    A("pool", lambda e: e.dma_start(out=Wo[:], in_=dr["wout"].rearrange("(k p) n -> p k n", p=128)), w=["Wo"], dma="w")
    for bi, (S, off, nq, col0) in enumerate(BLK):
        pb = bi % 2
        t0 = 128 * S + off
        A("sp", lambda e, pb=pb, t0=t0, nq=nq: e.dma_start(out=xq[pb][0:nq, :], in_=dr["xs"][t0:t0 + nq, :]), w=[("xq", pb)], dma=("xq", pb))
        for kc in range(8):
            A("pe", lambda e, kc=kc, pb=pb, nq=nq: e.transpose(out=pa[pb][:, kc * 128:kc * 128 + nq], in_=xq[pb][0:nq, kc * 128:(kc + 1) * 128],
                                                              identity=identf[0:nq, 0:nq]), r=[("xq", pb)], w=[("pa", pb)])
        A("act", lambda e, pb=pb, nq=nq, col0=col0: e.copy(out=xres[:, :, col0:col0 + nq],
                                                           in_=pa[pb][:].rearrange("p (k q) -> p k q", q=128)[:, :, 0:nq]),
          r=[("pa", pb)], w=["xres"])
    for ti, (c0, n) in enumerate(TG):
        tb = ti % 2
        A("sp", lambda e, tb=tb, c0=c0, n=n: e.dma_start(out=ot[tb][:, :, 0:n], in_=oscr[:, :, c0:c0 + n]), w=[("ot", tb)], dma=("ot", tb))
        for dc in range(8):
            for hl in range(8):
                A("pe", lambda e, dc=dc, hl=hl, tb=tb, n=n: e.matmul(py[dc % 2][:, 0:n], lhsT=Wo[:, hl, dc * 128:(dc + 1) * 128],
                                                                    rhs=ot[tb][:, hl, 0:n], start=(hl == 0), stop=(hl == 7)),
                  r=["Wo", ("ot", tb)], w=[("py", dc % 2)])
            A("dve", lambda e, dc=dc, c0=c0, n=n: e.tensor_tensor(out=xres[:, dc, c0:c0 + n], in0=xres[:, dc, c0:c0 + n],
                                                                  in1=py[dc % 2][:, 0:n], op=ALU.add),
              r=[("py", dc % 2), "xres"], w=["xres"])
    if "x0mix" in dbgout:
        A("sp", lambda e: e.dma_start(out=dbgout["x0mix"], in_=xres[:]), r=["xres"], dma="dbg")
    ph.emit()


def _phase_ffn(nc, st, SB, PS, dr, L, P, dbgout):
    xres, onesb, gsb, vq = P["xres"], P["onesb"], P["gsb"], P["vq"]
    gcol = gsb[:, 1 + 2 * L, :]
    hTall = SB(st, f"hTall{L}", [128, 8, NTOK], BF16)
    with contextlib.ExitStack() as nst:
        sqb = SB(nst, f"sqbf{L}", [128, 8, 512], BF16)
        rs = SB(nst, f"rsf{L}", [128, 512], F32)
        pn = PS(nst, f"pnf{L}", [128, 512], F32)
        phn = Phase(nc, f"N{L}")
        for ti, (c0, n) in enumerate(TG):
            _norm_group(phn.add, xres, c0, n, gcol, onesb, sqb, pn, rs, hTall[:, :, c0:c0 + n], "hT", f"f{L}")
        phn.emit()
    wu = [SB(st, f"wu{L}{i}", [128, 8, 6, 256], BF16) for i in range(2)]
    wd = [SB(st, f"wd{L}{i}", [128, 6, 1024], BF16) for i in range(2)]
    cw = SB(st, f"cw{L}", [128, 3, 44], F32)
    cb = SB(st, f"cb{L}", [128, 44], F32)
    carry = SB(st, f"carry{L}", [128, 44, 2], F32)
    ub = [SB(st, f"ub{L}{i}", [128, 514], F32) for i in range(2)]
    cbuf = [SB(st, f"cbuf{L}{i}", [128, 512], F32) for i in range(2)]
    sg = SB(st, f"sg{L}", [128, 512], F32)
    act2 = [SB(st, f"act{L}{i}", [128, 6, 512], BF16) for i in range(2)]
    pu = [[PS(st, f"pu{L}{a}{b}", [128, 512], F32) for b in range(2)] for a in range(2)]
    py = [PS(st, f"pyf{L}{i}", [128, 512], F32) for i in range(2)]
    ph = Phase(nc, f"F{L}")
    A = ph.add
    A("sp", lambda e: e.dma_start(out=cw[:], in_=dr[f"cw{L}"]), w=["cw"], dma="c")
    A("sp", lambda e: e.dma_start(out=cb[:], in_=dr[f"cb{L}"]), w=["cb"], dma="c")
    A("dve", lambda e: e.memset(carry[:], 0.0), w=[("carry", ch) for ch in range(44)])

    def load_pass(p):
        wb = p % 2
        for i, fc in enumerate(FPASS[p]):
            A("pool", lambda e, i=i, fc=fc, wb=wb: e.dma_start(
                out=wu[wb][:, :, i, 0:128], in_=dr[f"wup{L}"][:, fc * 128:(fc + 1) * 128].rearrange("(k p) n -> p k n", p=128)),
              w=[("wu", wb, i)], dma=("w", wb))
            A("pool", lambda e, i=i, fc=fc, wb=wb: e.dma_start(
                out=wu[wb][:, :, i, 128:256], in_=dr[f"wup{L}"][:, DFF + fc * 128:DFF + (fc + 1) * 128].rearrange("(k p) n -> p k n", p=128)),
              w=[("wu", wb, i)], dma=("w", wb))
            A("pool", lambda e, i=i, fc=fc, wb=wb: e.dma_start(out=wd[wb][:, i, :], in_=dr[f"wdn{L}"][fc * 128:(fc + 1) * 128, :]),
              w=[("wd", wb, i)], dma=("w", wb))

    def up_stage(p, ti):
        wb = p % 2
        c0, n = TG[ti]
        ab = (p * len(TG) + ti) % 2
        for i, fc in enumerate(FPASS[p]):
            for part in range(2):
                bank = pu[part][i % 2]
                bk = ("pu", part, i % 2)
                ch = fc + 22 * part
                for kc in range(8):
                    A("pe", lambda e, kc=kc, i=i, part=part, bank=bank: e.matmul(
                        bank[:, 0:n], lhsT=wu[wb][:, kc, i, part * 128:(part + 1) * 128], rhs=hTall[:, kc, c0:c0 + n],
                        start=(kc == 0), stop=(kc == 7)), r=[("wu", wb, i)], w=[bk])
                A("act", lambda e, part=part, bank=bank: e.copy(out=ub[part][:, 2:2 + n], in_=bank[:, 0:n]), r=[bk], w=[("ub", part)])
                A("act", lambda e, part=part, bank=bank, ch=ch: e.activation(
                    out=cbuf[part][:, 0:n], in_=bank[:, 0:n], func=AF.Identity, bias=cb[:, ch:ch + 1], scale=cw[:, 2, ch:ch + 1]),
                  r=[bk, "cw", "cb"], w=[("cbuf", part)])
                A("act", lambda e, part=part, ch=ch: e.copy(out=ub[part][:, 0:2], in_=carry[:, ch, :]), r=[("carry", ch)], w=[("ubc", part)])
                for k in (1, 0):
                    A("dve", lambda e, part=part, ch=ch, k=k: e.scalar_tensor_tensor(
                        out=cbuf[part][:, 0:n], in0=ub[part][:, k:k + n], scalar=cw[:, k, ch:ch + 1], in1=cbuf[part][:, 0:n],
                        op0=ALU.mult, op1=ALU.add), r=[("ub", part), ("ubc", part), ("cbuf", part), "cw"], w=[("cbuf", part)])
                A("act", lambda e, part=part, ch=ch: e.copy(out=carry[:, ch, :], in_=ub[part][:, n:n + 2]), r=[("ub", part)], w=[("carry", ch)])
            A("act", lambda e: e.activation(out=sg[:, 0:n], in_=cbuf[0][:, 0:n], func=AF.Silu), r=[("cbuf", 0)], w=["sg"])
            A("dve", lambda e, i=i: e.tensor_tensor(out=act2[ab][:, i, 0:n], in0=sg[:, 0:n], in1=cbuf[1][:, 0:n], op=ALU.mult),
              r=["sg", ("cbuf", 1)], w=[("act", ab, i)])

    def down_stage(p, ti):
        wb = p % 2
        c0, n = TG[ti]
        ab = (p * len(TG) + ti) % 2
        nf = len(FPASS[p])
        for dc in range(8):
            for i in range(nf):
                A("pe", lambda e, dc=dc, i=i: e.matmul(py[dc % 2][:, 0:n], lhsT=wd[wb][:, i, dc * 128:(dc + 1) * 128], rhs=act2[ab][:, i, 0:n],
                                                       start=(i == 0), stop=(i == nf - 1)), r=[("wd", wb, i), ("act", ab, i)], w=[("py", dc % 2)])
            A("dve", lambda e, dc=dc: e.tensor_tensor(out=xres[:, dc, c0:c0 + n], in0=xres[:, dc, c0:c0 + n], in1=py[dc % 2][:, 0:n], op=ALU.add),
              r=[("py", dc % 2), "xres"], w=["xres"])

    steps = [(p, ti) for p in range(len(FPASS)) for ti in range(len(TG))]
    load_pass(0)
    load_pass(1)
    for k in range(len(steps) + 1):
        if k < len(steps):
            up_stage(*steps[k])
        if k >= 1:
            pp, pti = steps[k - 1]
            down_stage(pp, pti)
            if pti == len(TG) - 1 and pp + 2 < len(FPASS):
                load_pass(pp + 2)
    A("dve", lambda e: e.tensor_scalar(out=xres[:, :, 0:HALO], in0=xres[:, :, 0:HALO], scalar1=vq[:, 0:1], scalar2=None, op0=ALU.mult),
      r=["xres"], w=["xres"])
    if f"xffn{L}" in dbgout:
        A("sp", lambda e: e.dma_start(out=dbgout[f"xffn{L}"], in_=xres[:]), r=["xres"], dma="dbg")
    ph.emit()


def _phase_pool(nc, st, SB, PS, dr, P, dbgout):
    xres, onesb, gsb, vq = P["xres"], P["onesb"], P["gsb"], P["vq"]
    gcol = gsb[:, 2, :]
    hf = SB(st, "hf", [128, 8, NTOK], F32)
    pl = SB(st, "pl", [128, 8, NTOK], BF16)
    wa = SB(st, "wa", [128, NTOK], F32)
    wb_ = SB(st, "wb", [128, NTOK], F32)
    sqb = SB(st, "sqbp", [128, 8, 512], BF16)
    rs = SB(st, "rsp", [128, 512], F32)
    pw = SB(st, "pw", [128, 8, 256], BF16)
    pbias = SB(st, "pbias", [128, 8], F32)
    pscl = SB(st, "pscl", [128, 8], F32)
    psc16 = SB(st, "psc16", [128, 8, 16], F32)
    tmp16 = SB(st, "tmp16", [128, 16], F32)
    ytmp = SB(st, "ytmp", [128, 512], F32)
    pn = PS(st, "pnp", [128, 512], F32)
    py = [PS(st, f"pyp{i}", [128, 512], F32) for i in range(2)]
    ph = Phase(nc, "P")
    A = ph.add
    A("pool", lambda e: e.dma_start(out=pw[:], in_=dr["poolw"]), w=["pw"], dma="w")
    A("sp", lambda e: e.dma_start(out=pbias[:], in_=dr["poolb"]), w=["pbias"], dma="c")
    A("sp", lambda e: e.dma_start(out=pscl[:], in_=dr["pools"]), w=["pscl"], dma="c")
    A("sp", lambda e: e.dma_start(out=psc16[:], in_=dr["pscale"]), w=["psc16"], dma="c")
    for ti, (c0, n) in enumerate(TG):
        _norm_group(A, xres, c0, n, gcol, onesb, sqb, pn, rs, hf[:, :, c0:c0 + n], "hf", "p")
    for kc in range(8):
        nsteps = kc // 2 + 1
        w = 2 ** nsteps
        src, sk = hf[:, kc, :], "hf"
        bufs = [(wa, "wa"), (wb_, "wb")]
        for sidx in range(nsteps):
            d = 2 ** sidx
            dst, dk = bufs[sidx % 2]
            A("dve", lambda e, src=src, dst=dst, d=d: e.tensor_tensor(out=dst[:, d:NTOK], in0=src[:, d:NTOK], in1=src[:, 0:NTOK - d], op=ALU.add),
              r=[sk], w=[dk])
            A("act", lambda e, src=src, dst=dst, d=d: e.copy(out=dst[:, 0:d], in_=src[:, 0:d]), r=[sk], w=[dk])
            src, sk = dst[:], dk
        A("dve", lambda e, src=src, kc=kc, w=w: e.scalar_tensor_tensor(out=pl[:, kc, :], in0=src, scalar=1.0 / w, in1=hf[:, kc, :],
                                                                      op0=ALU.mult, op1=ALU.subtract), r=[sk, "hf"], w=[("pl", kc)])
        A("dve", lambda e, src=src, kc=kc: e.tensor_tensor(out=tmp16[:], in0=src[:, HALO:HALO + 16], in1=psc16[:, kc, :], op=ALU.mult),
          r=[sk, "psc16"], w=["tmp16"])
        A("dve", lambda e, kc=kc: e.tensor_tensor(out=pl[:, kc, HALO:HALO + 16], in0=tmp16[:], in1=hf[:, kc, HALO:HALO + 16], op=ALU.subtract),
          r=["tmp16", "hf", ("pl", kc)], w=[("pl", kc)])
    for ti, (c0, n) in enumerate(TG):
        for oc in range(8):
            g, oh = oc // 2, oc % 2
            for kh in range(2):
                A("pe", lambda e, oc=oc, g=g, oh=oh, kh=kh, c0=c0, n=n: e.matmul(
                    py[oc % 2][:, 0:n], lhsT=pw[:, g * 2 + kh, oh * 128:(oh + 1) * 128], rhs=pl[:, g * 2 + kh, c0:c0 + n],
                    start=(kh == 0), stop=(kh == 1)), r=["pw", ("pl", g * 2 + kh)], w=[("py", oc % 2)])
            A("dve", lambda e, oc=oc, n=n: e.tensor_scalar(out=ytmp[:, 0:n], in0=py[oc % 2][:, 0:n], scalar1=pbias[:, oc:oc + 1],
                                                          scalar2=pscl[:, oc:oc + 1], op0=ALU.add, op1=ALU.mult),
              r=[("py", oc % 2), "pbias", "pscl"], w=["ytmp"])
            A("dve", lambda e, oc=oc, c0=c0, n=n: e.tensor_tensor(out=xres[:, oc, c0:c0 + n], in0=xres[:, oc, c0:c0 + n], in1=ytmp[:, 0:n],
                                                                  op=ALU.add), r=["ytmp", "xres"], w=["xres"])
    A("dve", lambda e: e.tensor_scalar(out=xres[:, :, 0:HALO], in0=xres[:, :, 0:HALO], scalar1=vq[:, 0:1], scalar2=None, op0=ALU.mult),
      r=["xres"], w=["xres"])
    if "xpool" in dbgout:
        A("sp", lambda e: e.dma_start(out=dbgout["xpool"], in_=xres[:]), r=["xres"], dma="dbg")
    ph.emit()


def _phase_out(nc, st, SB, PS, dr, out, P, dbgout):
    xres, onesb, gsb, identf = P["xres"], P["onesb"], P["gsb"], P["identf"]
    gcol = gsb[:, 4, :]
    of = SB(st, "of", [128, 8, 512], F32)
    sqb = SB(st, "sqbo", [128, 8, 512], BF16)
    rs = SB(st, "rso", [128, 512], F32)
    ot = [SB(st, f"oto{i}", [128, D], F32) for i in range(2)]
    pn = PS(st, "pno", [128, 512], F32)
    pt = [PS(st, f"pto{i}", [128, 1024], F32) for i in range(2)]
    ph = Phase(nc, "O")
    A = ph.add
    for gi in range(4):
        c0 = HALO + 512 * gi
        _norm_group(A, xres, c0, 512, gcol, onesb, sqb, pn, rs, of, "of", "o")
        for tt in range(4):
            tb = tt % 2
            for kc in range(8):
                A("pe", lambda e, tt=tt, tb=tb, kc=kc: e.transpose(out=pt[tb][:, kc * 128:(kc + 1) * 128],
                                                                 in_=of[:, kc, tt * 128:(tt + 1) * 128], identity=identf[:]),
                  r=["of"], w=[("pt", tb)])
            A("act", lambda e, tb=tb: e.copy(out=ot[tb][:], in_=pt[tb][:]), r=[("pt", tb)], w=[("ot", tb)])
            row = (gi * 4 + tt) * 128
            A("sp", lambda e, tb=tb, row=row: e.dma_start(out=out[row:row + 128, :], in_=ot[tb][:]), r=[("ot", tb)], dma=("st", tb))
    ph.emit()


_CACHE = {}


def kernel(**inputs):
    inp = {k: np.asarray(v) for k, v in inputs.items()}
    if "nc" not in _CACHE:
        _CACHE["nc"] = build()
    nc = _CACHE["nc"]
    sh = _shared_inputs(inp)
    maps = []
    for c in range(8):
        m = dict(sh)
        m.update(_core_inputs(inp, c))
        maps.append(m)
    res = run_bass_kernel_spmd(nc, maps, core_ids=list(range(8)))
    outp = np.zeros((2, T, D), np.float32)
    for c in range(8):
        b, j = c // 4, c % 4
        outp[b, 2048 * j:2048 * (j + 1)] = np.asarray(res.results[c]["out"], dtype=np.float32)
    return outp
```

```python
import contextlib
import numpy as np
import ml_dtypes
import concourse.bass as bass
import concourse.mybir as mybir
from concourse.bass_utils import run_bass_kernel_spmd

F32 = mybir.dt.float32
BF16 = mybir.dt.bfloat16
AF = mybir.ActivationFunctionType
ALU = mybir.AluOpType
AX = mybir.AxisListType
NPBF = ml_dtypes.bfloat16

D = 1024
KC = 8
T = 8192
NSLOT = 64
OWN0 = 48
NOWN = 16
HALO = 20
NTOK = HALO + 128 * NOWN
DFF = 2816
NFC = 22
EPS = 1e-6
SCALE = 0.125
NEGB = -30000.0
BLK = [(47, 108, HALO, 0)] + [(OWN0 + m, 0, 128, HALO + 128 * m) for m in range(NOWN)]
NBLK = len(BLK)
TG = [(0, HALO)] + [(HALO + 512 * i, 512) for i in range(4)]
FPASS = [list(range(0, 6)), list(range(6, 12)), list(range(12, 17)), list(range(17, 22))]

ENGS = ("pe", "act", "dve", "pool", "sp")


class Op:
    __slots__ = ("eng", "fn", "dma", "waits", "signal", "sigidx", "idx")

    def __init__(self, eng, fn, dma):
        self.eng = eng
        self.fn = fn
        self.dma = dma
        self.waits = []
        self.signal = False
        self.sigidx = None


class Phase:
    def __init__(self, nc, name):
        self.nc = nc
        self.name = name
        self.ops = {e: [] for e in ENGS}
        self.lastw = {}
        self.readers = {}
        self.dma_count = {}
        self.n = 0

    def add(self, eng, fn, r=(), w=(), dma=None):
        op = Op(eng, fn, dma)
        op.idx = self.n
        self.n += 1
        deps = []
        for k in r:
            x = self.lastw.get(k)
            if x is not None:
                deps.append(x)
        for k in w:
            x = self.lastw.get(k)
            if x is not None:
                deps.append(x)
            deps.extend(self.readers.get(k, {}).values())
        seen = set()
        for d in deps:
            if d is op or id(d) in seen:
                continue
            seen.add(id(d))
            if d.dma is not None:
                op.waits.append(("dma", d.dma, 16 * self.dma_count[d.dma]))
            else:
                if d.eng == "pe" and eng == "pe" and dma is None:
                    continue
                d.signal = True
                op.waits.append(("eng", d, None))
        rk = eng if dma is None else ("dma", op.idx)
        for k in r:
            self.readers.setdefault(k, {})[rk] = op
        for k in w:
            self.lastw[k] = op
            self.readers[k] = {}
        if dma is not None:
            self.dma_count[dma] = self.dma_count.get(dma, 0) + 1
        self.ops[eng].append(op)
        return op

    def emit(self):
        nc = self.nc
        for e in ENGS:
            k = 0
            for op in self.ops[e]:
                if op.dma is None and op.signal:
                    k += 1
                    op.sigidx = k
        with contextlib.ExitStack() as st:
            esem = {e: st.enter_context(nc.semaphore(f"{self.name}_s_{e}")) for e in ENGS}
            dsem = {k: st.enter_context(nc.semaphore(f"{self.name}_d_{i}"))
                    for i, k in enumerate(self.dma_count)}
            block = st.enter_context(nc.Block())
            final_dma = dict(self.dma_count)

            def run(e, eng):
                seen = {}
                for op in self.ops[e]:
                    for kind, obj, val in op.waits:
                        if kind == "dma":
                            sem, v = dsem[obj], val
                        else:
                            sem, v = esem[obj.eng], obj.sigidx
                        if seen.get(id(sem), 0) >= v:
                            continue
                        seen[id(sem)] = v
                        eng.wait_ge(sem, v)
                    inst = op.fn(eng)
                    if op.dma is not None:
                        inst.then_inc(dsem[op.dma], 16)
                    elif op.signal:
                        inst.then_inc(esem[e], 1)
                mine = []
                for op in self.ops[e]:
                    if op.dma is not None and op.dma not in mine:
                        mine.append(op.dma)
                for k in mine:
                    v = 16 * final_dma[k]
                    if seen.get(id(dsem[k]), 0) < v:
                        eng.wait_ge(dsem[k], v)

            block.tensor(lambda eng: run("pe", eng))
            block.scalar(lambda eng: run("act", eng))
            block.vector(lambda eng: run("dve", eng))
            block.gpsimd(lambda eng: run("pool", eng))
            block.sync(lambda eng: run("sp", eng))


def _c(a, dt=np.float32):
    return np.ascontiguousarray(a).astype(dt, copy=False)


def _pk(v):
    return _c(np.asarray(v).reshape(-1, 128).T)


def _rope_tab(pos):
    inv = (1.0 / (10000.0 ** (np.arange(0, 64, 2, dtype=np.float32) / np.float32(64)))).astype(np.float32)
    ang = pos.astype(np.float32)[:, None] * inv[None, :]
    c = np.cos(ang).astype(np.float32)
    s = np.sin(ang).astype(np.float32)
    idx = np.arange(128) % 32
    return _c(c[:, idx].T), _c(s[:, idx].T)


def _shared_inputs(inp):
    sh = {}
    w_in = np.asarray(inp["nsa_w_in"])
    sh["wq"] = _c(w_in[:, :1024].reshape(1024, 2, 8, 64).transpose(0, 2, 1, 3).reshape(1024, 1024))
    sh["wkv"] = _c(w_in[:, 1024:1792])
    sh["wg"] = _c(w_in[:, 1792:1840].reshape(1024, 2, 8, 3).transpose(0, 3, 1, 2).reshape(1024, 48))
    sh["wout"] = _c(np.asarray(inp["nsa_w_out"]).reshape(2, 8, 64, 1024).transpose(1, 0, 2, 3).reshape(1024, 1024))
    for x in ("k", "v"):
        sh[f"c{x}_w1"] = _c(inp[f"cmp_{x}_w1"])
        sh[f"c{x}_w2"] = _c(inp[f"cmp_{x}_w2"])
        pos = np.asarray(inp[f"cmp_{x}_pos"])
        pt = np.zeros((128, 32, 2), np.float32)
        pt[0:64, :, 0] = pos.T
        pt[64:128, :, 0] = pos.T
        sh[f"c{x}_posT"] = _c(pt)
        sh[f"c{x}_b1"] = _c(np.asarray(inp[f"cmp_{x}_b1"]).reshape(2, 128).T)
        sh[f"c{x}_b2"] = _c(np.tile(np.asarray(inp[f"cmp_{x}_b2"]), 2)[None, :])
    for i in (0, 1):
        sh[f"gmix{i}"] = _pk(inp[f"norm_mix_{i}"])
        sh[f"gffn{i}"] = _pk(inp[f"norm_ffn_{i}"])
        sh[f"wup{i}"] = _c(inp[f"ffn_up_{i}"])
        sh[f"wdn{i}"] = _c(inp[f"ffn_down_{i}"])
        cw = np.asarray(inp[f"ffn_conv_w_{i}"])
        sh[f"cw{i}"] = _c(cw.reshape(3, 44, 128).transpose(2, 0, 1))
        sh[f"cb{i}"] = _c(np.asarray(inp[f"ffn_conv_b_{i}"]).reshape(44, 128).T)
    sh["gfin"] = _pk(inp["norm_final"])
    sh["poolw"] = _c(np.asarray(inp["pool_w"]).reshape(4, 2, 128, 256).transpose(2, 0, 1, 3).reshape(128, 8, 256))
    sh["poolb"] = _pk(np.asarray(inp["pool_b"]).reshape(-1))
    sh["pools"] = _pk(inp["pool_scale"])
    sh["identb"] = _c(np.eye(128), NPBF)
    sh["identf"] = _c(np.eye(128))
    ef = np.zeros((128, 16, 128), np.float32)
    for p in range(128):
        for v in range(16):
            if p % 32 == 2 * v:
                ef[p, v, 0:64] = 1
            if p % 32 == 2 * v + 1:
                ef[p, v, 64:128] = 1
    sh["ef"] = _c(ef, NPBF)
    n = np.arange(512)[:, None]
    s = np.arange(128)[None, :]
    lo = np.maximum(n * 16, s * 64)
    hi = np.minimum(n * 16 + 32, (s + 1) * 64)
    ov = np.clip(hi - lo, 0, None) / 32.0
    ova = np.ones((512, 129), np.float32)
    ova[:, :128] = ov
    sh["ovl"] = _c(ova.reshape(4, 128, 129).transpose(1, 0, 2), NPBF)
    mk = np.zeros((NBLK, 128, 6, 128), np.float32)
    k = np.arange(128)[:, None]
    q = np.arange(128)[None, :]
    for bi, (S, off, nq, col0) in enumerate(BLK):
        mk[bi, :, 0, :] = (k <= off + q)
        mk[bi, :, 1, :] = (k > off + q)
        tq = 128 * S + off + q
        for c in range(4):
            mk[bi, :, 2 + c, :] = (16 * (c * 128 + k) + 31 <= tq)
    sh["mk"] = _c(mk, NPBF)
    return sh


def _core_inputs(inp, core):
    b, j = core // 4, core % 4
    SH = OWN0 - 16 * j
    x = np.asarray(inp["x"])[b]
    ci = {}
    xs = np.zeros((T, D), np.float32)
    nreal = (NSLOT - SH) * 128
    xs[SH * 128:] = x[:nreal]
    ci["xs"] = xs
    tp = np.arange(T)
    pos = np.maximum(tp - 128 * SH, 0)
    ci["cosK"], ci["sinK"] = _rope_tab(pos)
    qpos = np.zeros((NBLK, 128), np.int64)
    bonus = np.zeros((NBLK, 128, 128), np.float32)
    sblk = np.arange(128)[None, :]
    for bi, (S, off, nq, col0) in enumerate(BLK):
        tq = 128 * S + off + np.arange(128)
        qpos[bi] = np.maximum(tq - 128 * SH, 0)
        cur = (tq // 64)[:, None]
        forced = (sblk == cur) | (sblk == cur - 1) | (sblk == 2 * SH)
        bonus[bi] = np.where(sblk <= cur, 1e4 * forced, -1e30)
    cq, sq = _rope_tab(qpos.reshape(-1))
    ci["cosQ"], ci["sinQ"] = cq, sq
    ci["bonus"] = _c(bonus.transpose(1, 0, 2))
    nn = np.arange(512)
    cend = np.maximum(16 * (nn - 8 * SH) + 31, 0)
    ci["cosC"], ci["sinC"] = _rope_tab(cend)
    validc = (nn >= 8 * SH) & (nn <= 510)
    ci["biasC"] = _c(np.where(validc, 0.0, NEGB).reshape(4, 128).T)
    vk = (np.arange(NSLOT) >= SH).astype(np.float32)
    ci["validk"] = _c(np.tile(vk[None, :], (128, 1)), NPBF)
    ci["vq"] = _c(np.full((128, 1), 0.0 if j == 0 else 1.0))
    ps = np.zeros((128, 8, 16), np.float32)
    for kc in range(8):
        w = [2, 4, 8, 16][kc // 2]
        for qq in range(16):
            t = 128 * 16 * j + qq
            ps[:, kc, qq] = 1.0 / min(t + 1, w)
    ci["pscale"] = ps
    return ci


def build(dbg=()):
    nc = bass.Bass("TRN2", target_bir_lowering=False)
    dr = {}

    def din(name, shape, dt=F32):
        dr[name] = nc.dram_tensor(name, list(shape), dt, kind="ExternalInput").ap()
        return dr[name]

    din("xs", [T, D])
    din("cosK", [128, T]); din("sinK", [128, T])
    din("cosQ", [128, NBLK * 128]); din("sinQ", [128, NBLK * 128])
    din("bonus", [128, NBLK, 128])
    din("cosC", [128, 512]); din("sinC", [128, 512])
    din("biasC", [128, 4]); din("validk", [128, NSLOT], BF16); din("vq", [128, 1]); din("pscale", [128, 8, 16])
    din("wq", [D, D]); din("wkv", [D, 768]); din("wg", [D, 48]); din("wout", [D, D])
    for x in ("k", "v"):
        din(f"c{x}_w1", [2048, 256]); din(f"c{x}_w2", [256, 64]); din(f"c{x}_posT", [128, 32, 2])
        din(f"c{x}_b1", [128, 2]); din(f"c{x}_b2", [1, 128])
    for i in (0, 1):
        din(f"gmix{i}", [128, 8]); din(f"gffn{i}", [128, 8])
        din(f"wup{i}", [D, 2 * DFF]); din(f"wdn{i}", [DFF, D])
        din(f"cw{i}", [128, 3, 44]); din(f"cb{i}", [128, 44])
    din("gfin", [128, 8]); din("poolw", [128, 8, 256]); din("poolb", [128, 8]); din("pools", [128, 8])
    din("identb", [128, 128], BF16); din("identf", [128, 128])
    din("ef", [128, 16, 128], BF16); din("ovl", [128, 4, 129], BF16); din("mk", [NBLK, 128, 6, 128], BF16)
    out = nc.dram_tensor("out", [128 * NOWN, D], F32, kind="ExternalOutput").ap()
    gscr = nc.dram_tensor("gscr", [NBLK, 48, 128], F32).ap()
    dbgout = {}
    for name, shape, dt in dbg:
        if shape is None:
            dbgout[name] = dt
            continue
        dbgout[name] = nc.dram_tensor("dbg_" + name, list(shape), dt, kind="ExternalOutput").ap()

    with contextlib.ExitStack() as top:
        def SB(st, name, shape, dt):
            return st.enter_context(nc.sbuf_tensor("s_" + name, list(shape), dt))

        def PS(st, name, shape, dt):
            return st.enter_context(nc.psum_tensor("p_" + name, list(shape), dt))

        identb = SB(top, "identb", [128, 128], BF16)
        identf = SB(top, "identf", [128, 128], F32)
        ones32 = SB(top, "ones32", [128, 512], F32)
        onesb = SB(top, "onesb", [128, 128], BF16)
        gsb = SB(top, "gsb", [128, 5, 8], F32)
        vq = SB(top, "vq", [128, 1], F32)
        ph = Phase(nc, "K")
        ph.add("sp", lambda e: e.dma_start(out=identb[:], in_=dr["identb"]), w=["identb"], dma="c")
        ph.add("sp", lambda e: e.dma_start(out=identf[:], in_=dr["identf"]), w=["identf"], dma="c")
        for i, nm in enumerate(("gmix0", "gffn0", "gmix1", "gffn1", "gfin")):
            ph.add("sp", lambda e, i=i, nm=nm: e.dma_start(out=gsb[:, i, :], in_=dr[nm]), w=["gsb"], dma="c")
        ph.add("sp", lambda e: e.dma_start(out=vq[:], in_=dr["vq"]), w=["vq"], dma="c")
        ph.add("dve", lambda e: e.memset(ones32[:], 1.0), w=["ones32"])
        ph.add("dve", lambda e: e.memset(onesb[:], 1.0), w=["onesb"])
        ph.emit()

        oscr = nc.dram_tensor("oscr", [128, 8, NTOK], BF16).ap()
        with contextlib.ExitStack() as att:
            KsT = SB(att, "KsT", [128, T], BF16)
            KwT = SB(att, "KwT", [128, 24 * 128], BF16)
            Vs = SB(att, "Vs", [128, NSLOT, 194], BF16)
            Vw = SB(att, "Vw", [128, 24, 194], BF16)
            kcT = SB(att, "kcT", [128, 512], BF16)
            RCv = SB(att, "RCv", [128, 4, 194], BF16)
            biasC = SB(att, "biasC", [128, 4], F32)
            P = dict(KsT=KsT, KwT=KwT, Vs=Vs, Vw=Vw, kcT=kcT, RCv=RCv, biasC=biasC,
                     identb=identb, ones32=ones32, gsb=gsb, vq=vq)
            with contextlib.ExitStack() as st:
                _phase_A(nc, st, SB, PS, dr, P, dbgout)
            if "stopA" not in dbgout:
                with contextlib.ExitStack() as st:
                    _phase_B(nc, st, SB, PS, dr, gscr, oscr, P, dbgout)
        if "stopA" in dbgout or "stopB" in dbgout:
            return nc
        with contextlib.ExitStack() as rest:
            xres = SB(rest, "xres", [128, 8, NTOK], F32)
            P = dict(xres=xres, onesb=onesb, gsb=gsb, vq=vq, identf=identf, identb=identb)
            with contextlib.ExitStack() as st:
                _phase_C(nc, st, SB, PS, dr, oscr, P, dbgout)
            with contextlib.ExitStack() as st:
                _phase_ffn(nc, st, SB, PS, dr, 0, P, dbgout)
            with contextlib.ExitStack() as st:
                _phase_pool(nc, st, SB, PS, dr, P, dbgout)
            with contextlib.ExitStack() as st:
                _phase_ffn(nc, st, SB, PS, dr, 1, P, dbgout)
            with contextlib.ExitStack() as st:
                _phase_out(nc, st, SB, PS, dr, out, P, dbgout)
    return nc


def _phase_A(nc, st, SB, PS, dr, P, dbgout):
    KsT, KwT, Vs, Vw, kcT, RCv = P["KsT"], P["KwT"], P["Vs"], P["Vw"], P["kcT"], P["RCv"]
    identb, ones32, gsb = P["identb"], P["ones32"], P["gsb"]
    rawK = SB(st, "rawK", [128, T], BF16)
    rawV = SB(st, "rawV", [128, T], BF16)
    t1a = SB(st, "t1a", [128, 512], F32)
    t1b = SB(st, "t1b", [128, 512], F32)
    ist = contextlib.ExitStack()
    WA = SB(ist, "WA", [128, 8, 768], BF16)
    WAr = SB(ist, "WAr", [128, 8, 256], BF16)
    validk = SB(ist, "validk", [128, NSLOT], BF16)
    xt = [SB(ist, f"xt{i}", [128, D], F32) for i in range(4)]
    junk = SB(ist, "junk", [128, D], BF16)
    ssq = [SB(ist, f"ssq{i}", [128, 1], F32) for i in range(4)]
    rstd = [SB(ist, f"rstd{i}", [128, 1], F32) for i in range(4)]
    xn = [SB(ist, f"xn{i}", [128, D], BF16) for i in range(8)]
    hT = [SB(ist, f"hT{i}", [128, 8, 512], BF16) for i in range(2)]
    cs = [SB(ist, f"cs{i}", [128, 512], F32) for i in range(2)]
    sn = [SB(ist, f"sn{i}", [128, 512], F32) for i in range(2)]
    pst = contextlib.ExitStack()
    tp = [PS(pst, f"tp{i}", [128, D], BF16) for i in range(2)]
    pk = [PS(pst, f"pk{i}", [128, 512], F32) for i in range(4)]
    pv = PS(pst, "pv", [128, 512], F32)

    ph = Phase(nc, "A")
    A = ph.add
    A("pool", lambda e: e.dma_start(out=WA[:], in_=dr["wkv"].rearrange("(k p) n -> p k n", p=128)), w=["WA"], dma="w")
    A("sp", lambda e: e.dma_start(out=validk[:], in_=dr["validk"]), w=["validk"], dma="c")
    A("sp", lambda e: e.dma_start(out=P["biasC"][:], in_=dr["biasC"]), w=["biasC"], dma="c")
    for i, c0 in enumerate((256, 512)):
        for g in range(2):
            A("dve", lambda e, i=i, c0=c0, g=g: e.tensor_scalar(
                out=WAr[:, :, i * 128 + g * 64: i * 128 + g * 64 + 32], in0=WA[:, :, c0 + g * 64 + 32: c0 + g * 64 + 64],
                scalar1=-1.0, scalar2=None, op0=ALU.mult), r=["WA"], w=[("WAr", i, g, 0)])
            A("dve", lambda e, i=i, c0=c0, g=g: e.tensor_copy(
                out=WAr[:, :, i * 128 + g * 64 + 32: i * 128 + g * 64 + 64], in_=WA[:, :, c0 + g * 64: c0 + g * 64 + 32]),
              r=["WA"], w=[("WAr", i, g, 1)])
    WArk = [("WAr", i, g, h) for i in range(2) for g in range(2) for h in range(2)]
    A("pool", lambda e: e.memset(Vs[:], 0.0), w=["Vs0"])
    A("pool", lambda e: e.memset(Vw[:], 0.0), w=["Vw0"])
    A("pool", lambda e: e.memset(RCv[:], 0.0), w=["RCv"])
    A("dve", lambda e: e.tensor_copy(out=Vs[:, :, 64:65], in_=validk[:].unsqueeze(2)), r=["validk", "Vs0"], w=["Vs1"])
    A("dve", lambda e: e.tensor_copy(out=Vs[:, :, 66:67], in_=validk[:].unsqueeze(2)), r=["validk", "Vs0"], w=["Vs2"])
    A("dve", lambda e: e.tensor_copy(out=Vw[:, :, 64:65], in_=validk[:, 40:64].unsqueeze(2)), r=["validk", "Vw0"], w=["Vw1"])
    A("dve", lambda e: e.tensor_copy(out=Vw[:, :, 66:67], in_=validk[:, 40:64].unsqueeze(2)), r=["validk", "Vw0"], w=["Vw2"])

    def norm_stage(G):
        A("sp", lambda e: e.dma_start(out=cs[G % 2][:], in_=dr["cosK"][:, G * 512:(G + 1) * 512]), w=[("cs", G % 2)], dma=("cs", G % 2))
        A("sp", lambda e: e.dma_start(out=sn[G % 2][:], in_=dr["sinK"][:, G * 512:(G + 1) * 512]), w=[("sn", G % 2)], dma=("cs", G % 2))
        for si in range(4):
            s_ = 4 * G + si
            b3 = si
            bx = (G % 2) * 4 + si
            A("sp", lambda e, s_=s_, b3=b3: e.dma_start(out=xt[b3][:], in_=dr["xs"][s_ * 128:(s_ + 1) * 128, :]),
              w=[("xt", b3)], dma=("x", b3))
            A("act", lambda e, b3=b3: e.activation(out=junk[:], in_=xt[b3][:], func=AF.Square, accum_out=ssq[b3][:]),
              r=[("xt", b3)], w=["junk", ("ssq", b3)])
            A("act", lambda e, b3=b3: e.activation(out=rstd[b3][:], in_=ssq[b3][:], func=AF.Sqrt, bias=EPS, scale=1.0 / D),
              r=[("ssq", b3)], w=[("rstd", b3)])
            A("dve", lambda e, b3=b3: e.reciprocal(out=rstd[b3][:], in_=rstd[b3][:]), r=[("rstd", b3)], w=[("rstd", b3)])
            A("dve", lambda e, b3=b3, bx=bx: e.tensor_scalar(out=xn[bx][:], in0=xt[b3][:], scalar1=rstd[b3][:], scalar2=None,
                                                           op0=ALU.mult), r=[("xt", b3), ("rstd", b3)], w=[("xn", bx)])

    norm_stage(0)
    for G in range(16):
        hb = hT[G % 2]
        hk = ("hT", G % 2)
        if G + 1 < 16:
            norm_stage(G + 1)
        for si in range(4):
            s = 4 * G + si
            b2 = s % 2
            bx = (G % 2) * 4 + si
            for kc in range(8):
                A("pe", lambda e, kc=kc, b2=b2, bx=bx: e.transpose(out=tp[b2][:, kc * 128:(kc + 1) * 128],
                                                                 in_=xn[bx][:, kc * 128:(kc + 1) * 128], identity=identb[:]),
                  r=[("xn", bx)], w=[("tp", b2)])
            A("dve", lambda e, b2=b2, hb=hb, si=si: e.tensor_tensor(
                out=hb[:, :, si * 128:(si + 1) * 128], in0=tp[b2][:].rearrange("p (k t) -> p k t", k=8),
                in1=gsb[:, 0, :].unsqueeze(2).to_broadcast([128, 8, 128]), op=ALU.mult),
              r=[("tp", b2)], w=[hk + (si,)])
        hks = [hk + (si,) for si in range(4)]

        def proj(W, c0, bank, bk, wk, hb=hb, hks=hks):
            for kc in range(8):
                A("pe", lambda e, kc=kc, W=W, c0=c0, bank=bank, hb=hb: e.matmul(bank[:], lhsT=W[:, kc, c0:c0 + 128], rhs=hb[:, kc, :],
                                                                          start=(kc == 0), stop=(kc == 7)),
                  r=hks + wk, w=[bk])
        proj(WA, 0, pk[0], "pk0", ["WA"])
        proj(WA, 128, pk[1], "pk1", ["WA"])
        proj(WA, 256, pk[2], "pk2", ["WA"])
        proj(WAr, 0, pk[3], "pk3", WArk)
        A("act", lambda e, G=G: e.copy(out=rawK[:, G * 512:(G + 1) * 512], in_=pk[0][:]), r=["pk0"], w=[("rawK", G)])
        A("act", lambda e, G=G: e.copy(out=rawV[:, G * 512:(G + 1) * 512], in_=pk[1][:]), r=["pk1"], w=[("rawV", G)])

        def ropeevac(b0, b0k, b1, b1k, dst, dk, G=G):
            A("dve", lambda e: e.tensor_tensor(out=t1a[:], in0=b0[:], in1=cs[G % 2][:], op=ALU.mult),
              r=[b0k, ("cs", G % 2)], w=["t1a"])
            A("dve", lambda e: e.tensor_tensor(out=t1b[:], in0=b1[:], in1=sn[G % 2][:], op=ALU.mult),
              r=[b1k, ("sn", G % 2)], w=["t1b"])
            A("dve", lambda e: e.tensor_tensor(out=dst, in0=t1a[:], in1=t1b[:], op=ALU.add), r=["t1a", "t1b"], w=[dk])
        ropeevac(pk[2], "pk2", pk[3], "pk3", KsT[:, G * 512:(G + 1) * 512], ("KsT", G))
        if G >= 10:
            proj(WA, 512, pk[0], "pk0", ["WA"])
            proj(WAr, 128, pk[1], "pk1", WArk)
            ropeevac(pk[0], "pk0", pk[1], "pk1", KwT[:, (G - 10) * 512:(G - 9) * 512], ("KwT", G))
        for si in range(4):
            s = 4 * G + si
            nv = 2 if G >= 10 else 1
            for vi in range(nv):
                c0 = 384 if vi == 0 else 640
                for kc in range(8):
                    A("pe", lambda e, kc=kc, si=si, vi=vi, c0=c0, hb=hb: e.matmul(
                        pv[:, vi * 128:(vi + 1) * 128], lhsT=hb[:, kc, si * 128:(si + 1) * 128], rhs=WA[:, kc, c0:c0 + 128],
                        start=(kc == 0), stop=(kc == 7)), r=[hk + (si,), "WA"], w=["pv"])
            A("act", lambda e, s=s: e.copy(out=Vs[:, s, 0:64], in_=pv[:, 0:64]), r=["pv", "Vs0"], w=[("Vs", s, 0)])
            A("act", lambda e, s=s: e.copy(out=Vs[:, s, 130:194], in_=pv[:, 64:128]), r=["pv", "Vs0"], w=[("Vs", s, 1)])
            if G >= 10:
                A("act", lambda e, s=s: e.copy(out=Vw[:, s - 40, 0:64], in_=pv[:, 128:192]), r=["pv", "Vw0"], w=[("Vw", s, 0)])
                A("act", lambda e, s=s: e.copy(out=Vw[:, s - 40, 130:194], in_=pv[:, 192:256]), r=["pv", "Vw0"], w=[("Vw", s, 1)])
    if "KsT" in dbgout:
        A("sp", lambda e: e.dma_start(out=dbgout["KsT"], in_=KsT[:]), r=[("KsT", G) for G in range(16)], dma="dbg")
        A("sp", lambda e: e.dma_start(out=dbgout["Vs"], in_=Vs[:]), r=[("Vs", s, i) for s in range(64) for i in range(2)] + ["Vs1", "Vs2"], dma="dbg")
        A("sp", lambda e: e.dma_start(out=dbgout["KwT"], in_=KwT[:]), r=[("KwT", G) for G in range(10, 16)], dma="dbg")
    ph.emit()
    pst.close()
    ist.close()
    _phase_A2(nc, st, SB, PS, dr, P, dbgout, rawK, rawV, t1a, t1b)


def _phase_A2(nc, st0, SB, PS, dr, P, dbgout, rawK, rawV, t1a, t1b):
    kcT, RCv, ones32 = P["kcT"], P["RCv"], P["ones32"]
    with contextlib.ExitStack() as st:
        w1 = [SB(st, f"w1{x}", [128, 32, 256], BF16) for x in range(2)]
        w2 = [SB(st, f"w2{x}", [128, 2, 64], BF16) for x in range(2)]
        posT = [SB(st, f"posT{x}", [128, 32, 2], BF16) for x in range(2)]
        b1 = [SB(st, f"b1{x}", [128, 2], F32) for x in range(2)]
        b1e = [SB(st, f"b1e{x}", [128, 2], F32) for x in range(2)]
        b2row = [SB(st, f"b2row{x}", [1, 128], F32) for x in range(2)]
        b2rrow = SB(st, "b2rrow", [1, 128], F32)
        w2pad = [SB(st, f"w2pad{g}", [128, 2, 128], BF16) for g in range(2)]
        w2padr = [SB(st, f"w2padr{g}", [128, 2, 128], BF16) for g in range(2)]
        hid = [[SB(st, f"hid{x}{g}", [128, 2, 512], BF16) for g in range(2)] for x in range(2)]
        u = SB(st, "u", [128, 512], F32)
        u2 = SB(st, "u2", [128, 512], F32)
        th = SB(st, "th", [128, 512], F32)
        csC = SB(st, "csC", [128, 512], F32)
        snC = SB(st, "snC", [128, 512], F32)
        pb = PS(st, "pb", [128, 512], F32)
        ph_ = [PS(st, f"ph{i}", [128, 512], F32) for i in range(2)]
        po = [PS(st, f"po{i}", [128, 512], F32) for i in range(2)]
        ph = Phase(nc, "A2")
        A = ph.add
        NCMP = 511
        for x, nm in enumerate(("k", "v")):
            src = dr[f"c{nm}_w1"].rearrange("(p d) h -> d p h", d=64)
            A("pool", lambda e, x=x, src=src: e.dma_start(out=w1[x][0:64], in_=src), w=[("w1", x, 0)], dma="w")
            A("pool", lambda e, x=x, src=src: e.dma_start(out=w1[x][64:128], in_=src), w=[("w1", x, 1)], dma="w")
            A("pool", lambda e, x=x, nm=nm: e.dma_start(out=w2[x][:], in_=dr[f"c{nm}_w2"].rearrange("(k p) d -> p k d", p=128)),
              w=[("w2", x)], dma="w")
            A("pool", lambda e, x=x, nm=nm: e.dma_start(out=posT[x][:], in_=dr[f"c{nm}_posT"]), w=[("posT", x)], dma="w")
            A("sp", lambda e, x=x, nm=nm: e.dma_start(out=b1[x][:], in_=dr[f"c{nm}_b1"]), w=[("b1", x)], dma="c")
            A("sp", lambda e, x=x, nm=nm: e.dma_start(out=b2row[x][:], in_=dr[f"c{nm}_b2"]), w=[("b2row", x)], dma="c")
        A("sp", lambda e: e.dma_start(out=csC[:], in_=dr["cosC"]), w=["csC"], dma="c")
        A("sp", lambda e: e.dma_start(out=snC[:], in_=dr["sinC"]), w=["snC"], dma="c")
        for g in range(2):
            A("dve", lambda e, g=g: e.memset(w2pad[g][:], 0.0), w=[("w2pad", g)])
            A("dve", lambda e, g=g: e.memset(w2padr[g][:], 0.0), w=[("w2padr", g)])
            A("dve", lambda e, g=g: e.tensor_copy(out=w2pad[g][:, :, g * 64:(g + 1) * 64], in_=w2[0][:]),
              r=[("w2", 0)], w=[("w2pad", g)])
            A("dve", lambda e, g=g: e.tensor_scalar(out=w2padr[g][:, :, g * 64:g * 64 + 32], in0=w2[0][:, :, 32:64],
                                                     scalar1=-1.0, scalar2=None, op0=ALU.mult), r=[("w2", 0)], w=[("w2padr", g)])
            A("dve", lambda e, g=g: e.tensor_copy(out=w2padr[g][:, :, g * 64 + 32:g * 64 + 64], in_=w2[0][:, :, 0:32]),
              r=[("w2", 0)], w=[("w2padr", g)])
            A("dve", lambda e, g=g: e.tensor_scalar(out=b2rrow[0:1, g * 64:g * 64 + 32], in0=b2row[0][0:1, g * 64 + 32:g * 64 + 64],
                                                     scalar1=-1.0, scalar2=None, op0=ALU.mult), r=[("b2row", 0)], w=["b2rrow"])
            A("dve", lambda e, g=g: e.tensor_copy(out=b2rrow[0:1, g * 64 + 32:g * 64 + 64], in_=b2row[0][0:1, g * 64:g * 64 + 32]),
              r=[("b2row", 0)], w=["b2rrow"])
        for x in range(2):
            raw = rawK if x == 0 else rawV
            for half in range(2):
                for p in range(32):
                    A("pe", lambda e, x=x, half=half, p=p: e.matmul(
                        pb[:, half * 2:half * 2 + 2], lhsT=w1[x][0:64, p, half * 128:(half + 1) * 128], rhs=posT[x][0:64, p, :],
                        start=(p == 0), stop=(p == 31)), r=[("w1", x, 0), ("posT", x)], w=["pb"])
            A("dve", lambda e, x=x: e.tensor_tensor(out=b1e[x][:], in0=pb[:, 0:4:2], in1=b1[x][:], op=ALU.add),
              r=["pb", ("b1", x)], w=[("b1e", x)])
            for half in range(2):
                for p in range(32):
                    for g in range(2):
                        A("pe", lambda e, x=x, g=g, half=half, p=p, raw=raw: e.matmul(
                            ph_[g][:, 0:NCMP], lhsT=w1[x][64 * g:64 * g + 64, p, half * 128:(half + 1) * 128],
                            rhs=raw[64 * g:64 * g + 64, p:p + 16 * (NCMP - 1) + 1:16],
                            start=(p == 0), stop=(p == 31)), r=[("w1", x, g)], w=[("ph", g)])
                for g in range(2):
                    bank = ph_[g]
                    A("act", lambda e, x=x, half=half, bank=bank: e.activation(out=u[:, 0:NCMP], in_=bank[:, 0:NCMP], func=AF.Identity,
                                                                               bias=b1e[x][:, half:half + 1], scale=1.0),
                      r=[("ph", g), ("b1e", x)], w=["u"])
                    A("dve", lambda e: e.tensor_tensor(out=u2[:, 0:NCMP], in0=u[:, 0:NCMP], in1=u[:, 0:NCMP], op=ALU.mult), r=["u"], w=["u2"])
                    A("dve", lambda e: e.tensor_scalar(out=u2[:, 0:NCMP], in0=u2[:, 0:NCMP], scalar1=0.044715, scalar2=1.0,
                                                        op0=ALU.mult, op1=ALU.add), r=["u2"], w=["u2"])
                    A("dve", lambda e: e.tensor_tensor(out=u2[:, 0:NCMP], in0=u2[:, 0:NCMP], in1=u[:, 0:NCMP], op=ALU.mult), r=["u2", "u"], w=["u2"])
                    A("act", lambda e: e.activation(out=th[:, 0:NCMP], in_=u2[:, 0:NCMP], func=AF.Tanh, scale=0.7978845608028654),
                      r=["u2"], w=["th"])
                    A("dve", lambda e: e.tensor_scalar(out=th[:, 0:NCMP], in0=th[:, 0:NCMP], scalar1=0.5, scalar2=0.5,
                                                        op0=ALU.mult, op1=ALU.add), r=["th"], w=["th"])
                    A("dve", lambda e, x=x, g=g, half=half: e.tensor_tensor(out=hid[x][g][:, half, 0:NCMP], in0=th[:, 0:NCMP],
                                                                             in1=u[:, 0:NCMP], op=ALU.mult),
                      r=["th", "u"], w=[("hid", x, g, half)])
        for r_, (pads, brow, bank, bk) in enumerate(((w2pad, b2row[0], po[0], "po0"), (w2padr, b2rrow, po[1], "po1"))):
            first = True
            for g in range(2):
                for half in range(2):
                    A("pe", lambda e, g=g, half=half, pads=pads, bank=bank, first=first: e.matmul(
                        bank[:, 0:NCMP], lhsT=pads[g][:, half, :], rhs=hid[0][g][:, half, 0:NCMP], start=first, stop=False),
                      r=[("hid", 0, g, half), ("w2pad", g), ("w2padr", g)], w=[bk])
                    first = False
            A("pe", lambda e, brow=brow, bank=bank: e.matmul(bank[:, 0:NCMP], lhsT=brow[0:1, 0:128], rhs=ones32[0:1, 0:NCMP],
                                                              start=False, stop=True), r=[("b2row", 0), "b2rrow"], w=[bk])
        A("dve", lambda e: e.tensor_tensor(out=t1a[:, 0:NCMP], in0=po[0][:, 0:NCMP], in1=csC[:, 0:NCMP], op=ALU.mult),
          r=["po0", "csC"], w=["t1a"])
        A("dve", lambda e: e.tensor_tensor(out=t1b[:, 0:NCMP], in0=po[1][:, 0:NCMP], in1=snC[:, 0:NCMP], op=ALU.mult),
          r=["po1", "snC"], w=["t1b"])
        A("dve", lambda e: e.memset(kcT[:], 0.0), w=["kcT"])
        A("dve", lambda e: e.tensor_tensor(out=kcT[:, 0:NCMP], in0=t1a[:, 0:NCMP], in1=t1b[:, 0:NCMP], op=ALU.add),
          r=["t1a", "t1b"], w=["kcT"])
        for c in range(4):
            n = 128 if c < 3 else NCMP - 384
            bank = po[c % 2]
            bk = f"po{c % 2}"
            for g in range(2):
                for half in range(2):
                    A("pe", lambda e, c=c, n=n, g=g, half=half, bank=bank: e.matmul(
                        bank[0:n, g * 64:(g + 1) * 64], lhsT=hid[1][g][:, half, c * 128:c * 128 + n], rhs=w2[1][:, half, :],
                        start=(half == 0), stop=False), r=[("hid", 1, g, half), ("w2", 1)], w=[bk])
                A("pe", lambda e, n=n, g=g, bank=bank: e.matmul(
                    bank[0:n, g * 64:(g + 1) * 64], lhsT=ones32[0:1, 0:n], rhs=b2row[1][0:1, g * 64:(g + 1) * 64],
                    start=False, stop=True), r=[("b2row", 1)], w=[bk])
            A("act", lambda e, c=c, n=n, bank=bank: e.copy(out=RCv[0:n, c, 0:64], in_=bank[0:n, 0:64]), r=[bk, "RCv"], w=[("RCv", c, 0)])
            A("act", lambda e, c=c, n=n, bank=bank: e.copy(out=RCv[0:n, c, 130:194], in_=bank[0:n, 64:128]), r=[bk, "RCv"], w=[("RCv", c, 1)])
            A("dve", lambda e, c=c, n=n: e.memset(RCv[0:n, c, 64:65], 1.0), r=["RCv"], w=[("RCv", c, 2)])
            A("dve", lambda e, c=c, n=n: e.memset(RCv[0:n, c, 66:67], 1.0), r=["RCv"], w=[("RCv", c, 3)])
        if "kcT" in dbgout:
            A("sp", lambda e: e.dma_start(out=dbgout["kcT"], in_=kcT[:]), r=["kcT"], dma="dbg")
            A("sp", lambda e: e.dma_start(out=dbgout["RCv"], in_=RCv[:]), r=[("RCv", c, i) for c in range(4) for i in range(4)], dma="dbg")
        ph.emit()


def _phase_B(nc, st, SB, PS, dr, gscr, oscr, P, dbgout):
    KsT, KwT, Vs, Vw, kcT, RCv, biasC = P["KsT"], P["KwT"], P["Vs"], P["Vw"], P["kcT"], P["RCv"], P["biasC"]
    ones32, gsb, vq = P["ones32"], P["gsb"], P["vq"]
    identf = SB(st, "identfB", [128, 128], F32)
    Wq = SB(st, "Wq", [128, 8, 1024], BF16)
    Wqr = SB(st, "Wqr", [128, 8, 1024], BF16)
    Wg = SB(st, "Wg", [128, 8, 48], BF16)
    EF = SB(st, "EF", [128, 16, 128], BF16)
    ovl = SB(st, "ovl", [128, 4, 129], BF16)
    xq1 = SB(st, "xq0", [128, D], F32)
    xq = [xq1, xq1]
    xnf = SB(st, "xnf", [128, D], F32)
    ssq = SB(st, "ssqB", [128, 1], F32)
    rstd = SB(st, "rstdB", [128, 1], F32)
    hTq = SB(st, "hTq", [128, 8, 128], BF16)
    cq1 = SB(st, "cq0", [128, 128], F32)
    sq1 = SB(st, "sq0", [128, 128], F32)
    cq = [cq1, cq1]
    sq = [sq1, sq1]
    bon3 = [SB(st, f"bon{i}", [128, 128], F32) for i in range(3)]
    mkb3 = [SB(st, f"mkb{i}", [128, 6, 128], BF16) for i in range(3)]
    QT = [SB(st, f"QT{i}", [128, 8, 128], BF16) for i in range(2)]
    gsig = [SB(st, f"gsig{i}", [48, 128], F32) for i in range(2)]
    selT = [[SB(st, f"selT{i}{g}", [128, 128], BF16) for g in range(2)] for i in range(2)]
    acc = [SB(st, f"acc{i}", [128, 8, 128], F32) for i in range(2)]
    Ec = [SB(st, f"Ec{c}", [128, 8, 128], BF16) for c in range(4)]
    NBUF = 5
    E = [SB(st, f"E{i}", [128, 8, 128], BF16) for i in range(NBUF)]
    Pb = [SB(st, f"Pb{i}", [128, 8, 128], BF16) for i in range(NBUF)]
    oasb = [SB(st, f"oasb{i}", [128, 1024], F32) for i in range(2)]
    msk = [SB(st, f"msk{i}", [128, 128], BF16) for i in range(NBUF)]
    t1 = SB(st, "t1B", [128, 8, 128], F32)
    t2 = SB(st, "t2B", [128, 8, 128], F32)
    dsb = SB(st, "dsb", [65, 1024], F32)
    rdb = SB(st, "rdb", [128, 4, 128], F32)
    tmpf = SB(st, "tmpf", [128, 4, 128], F32)
    grow = [SB(st, f"grow{i}", [65, 1024], F32) for i in range(2)]
    dsb2 = SB(st, "dsb2", [65, 1024], F32)
    cbc = SB(st, "cbc", [128, 1024], F32)
    crow = nc.dram_tensor("crow", [8, 1024], F32).ap()
    score = SB(st, "score", [128, 128], F32)
    work = SB(st, "work", [128, 128], F32)
    selq = SB(st, "selq", [128, 128], F32)
    m8a = SB(st, "m8a", [128, 8], F32)
    m8b = SB(st, "m8b", [128, 8], F32)
    thr = SB(st, "thr", [128, 1], F32)
    rc = SB(st, "rc", [128, 8], F32)
    obf = SB(st, "obf", [128, 8, 128], BF16)
    scA = PS(st, "scA", [128, 1024], F32)
    scB = PS(st, "scB", [128, 1024], F32)
    oa = PS(st, "oa", [128, 1024], F32)
    mx = PS(st, "mx", [128, 512], F32)
    msc = PS(st, "msc", [128, 512], F32)
    gs2 = gscr.rearrange("b r q -> b (r q)")
    print("[phase B] sbuf bytes remaining:", nc.sbuf_bytes_remaining)

    ph = Phase(nc, "B")
    A = ph.add
    A("pool", lambda e: e.dma_start(out=Wq[:], in_=dr["wq"].rearrange("(k p) n -> p k n", p=128)), w=["Wq"], dma="w")
    A("pool", lambda e: e.dma_start(out=Wg[:], in_=dr["wg"].rearrange("(k p) n -> p k n", p=128)), w=["Wg"], dma="w")
    A("sp", lambda e: e.dma_start(out=EF[:], in_=dr["ef"]), w=["EF"], dma="c")
    A("sp", lambda e: e.dma_start(out=ovl[:], in_=dr["ovl"]), w=["ovl"], dma="c")
    A("sp", lambda e: e.dma_start(out=identf[:], in_=dr["identf"]), w=["identf"], dma="c")
    for kc in range(8):
        v = Wq[:, kc, :].rearrange("p (h two d) -> p h two d", two=2, d=32)
        vr = Wqr[:, kc, :].rearrange("p (h two d) -> p h two d", two=2, d=32)
        A("dve", lambda e, v=v, vr=vr: e.tensor_scalar(out=vr[:, :, 0, :], in0=v[:, :, 1, :], scalar1=-1.0, scalar2=None, op0=ALU.mult),
          r=["Wq"], w=[("Wqr", kc, 0)])
        A("dve", lambda e, v=v, vr=vr: e.tensor_copy(out=vr[:, :, 1, :], in_=v[:, :, 0, :]), r=["Wq"], w=[("Wqr", kc, 1)])
    Wqrk = [("Wqr", kc, i) for kc in range(8) for i in range(2)]

    def v8(t, nq):
        return t[:].rearrange("p (h q) -> p h q", q=128)[:, :, 0:nq]

    def v4(t, u, nq, p0=0, p1=128):
        return t[p0:p1, u * 512:(u + 1) * 512].rearrange("p (h q) -> p h q", q=128)[:, :, 0:nq]

    def bc(ap2, n, nq):
        return ap2.unsqueeze(1).to_broadcast([ap2.shape[0], n, nq])

    def stage_load(bi):
        S, off, nq, col0 = BLK[bi]
        t0 = 128 * S + off
        b3 = bi % 3
        A("sp", lambda e: e.dma_start(out=xq1[0:nq, :], in_=dr["xs"][t0:t0 + nq, :]), w=["xqb"], dma="ldq")
        A("sp", lambda e: e.dma_start(out=cq1[:, 0:nq], in_=dr["cosQ"][:, bi * 128:bi * 128 + nq]), w=["cqb"], dma="ldq")
        A("sp", lambda e: e.dma_start(out=sq1[:, 0:nq], in_=dr["sinQ"][:, bi * 128:bi * 128 + nq]), w=["sqb"], dma="ldq")
        A("sp", lambda e: e.dma_start(out=bon3[b3][0:nq, :], in_=dr["bonus"][0:nq, bi, :]), w=[("bon", b3)], dma=("ldm", b3))
        A("sp", lambda e: e.dma_start(out=mkb3[b3][:], in_=dr["mk"][bi]), w=[("mkb", b3)], dma=("ldm", b3))

    def stage_q(bi):
        S, off, nq, col0 = BLK[bi]
        pb = bi % 2
        A("act", lambda e: e.activation(out=xnf[0:nq, :], in_=xq[pb][0:nq, :], func=AF.Square, accum_out=ssq[0:nq, :]),
          r=["xqb"], w=["xnf", "ssq"])
        A("act", lambda e: e.activation(out=rstd[0:nq, :], in_=ssq[0:nq, :], func=AF.Sqrt, bias=EPS, scale=1.0 / D), r=["ssq"], w=["rstd"])
        A("dve", lambda e: e.reciprocal(out=rstd[0:nq, :], in_=rstd[0:nq, :]), r=["rstd"], w=["rstd"])
        A("dve", lambda e: e.tensor_scalar(out=xnf[0:nq, :], in0=xq[pb][0:nq, :], scalar1=rstd[0:nq, :], scalar2=None, op0=ALU.mult),
          r=["xqb", "rstd"], w=["xnf"])
        for kc in range(8):
            A("pe", lambda e, kc=kc: e.transpose(out=scA[:, kc * 128:kc * 128 + nq], in_=xnf[0:nq, kc * 128:(kc + 1) * 128],
                                                 identity=identf[0:nq, 0:nq]), r=["xnf", "identf"], w=[("scA", kc // 4)])
        A("dve", lambda e: e.tensor_tensor(out=hTq[:, :, 0:nq], in0=v8(scA, nq), in1=gsb[:, 0, :].unsqueeze(2).to_broadcast([128, 8, nq]),
                                           op=ALU.mult), r=[("scA", 0), ("scA", 1)], w=["hTq"])
        for hl in range(8):
            for kc in range(8):
                A("pe", lambda e, hl=hl, kc=kc: e.matmul(scB[:, hl * 128:hl * 128 + nq], lhsT=Wq[:, kc, hl * 128:(hl + 1) * 128],
                                                         rhs=hTq[:, kc, 0:nq], start=(kc == 0), stop=(kc == 7)),
                  r=["Wq", "hTq"], w=[("scB", hl // 4)])
        for hl in range(8):
            for kc in range(8):
                A("pe", lambda e, hl=hl, kc=kc: e.matmul(oa[:, hl * 128:hl * 128 + nq], lhsT=Wqr[:, kc, hl * 128:(hl + 1) * 128],
                                                         rhs=hTq[:, kc, 0:nq], start=(kc == 0), stop=(kc == 7)),
                  r=Wqrk + ["hTq"], w=[("oa", hl // 4)])
        A("dve", lambda e: e.tensor_tensor(out=t1[:, :, 0:nq], in0=v8(scB, nq), in1=bc(cq[pb][:, 0:nq], 8, nq), op=ALU.mult),
          r=[("scB", 0), ("scB", 1), "cqb"], w=[("t1", 0), ("t1", 3), ("t1", 6)])
        A("dve", lambda e: e.tensor_tensor(out=t2[:, :, 0:nq], in0=v8(oa, nq), in1=bc(sq[pb][:, 0:nq], 8, nq), op=ALU.mult),
          r=[("oa", 0), ("oa", 1), "sqb"], w=["t2"])
        A("dve", lambda e: e.tensor_tensor(out=QT[pb][:, :, 0:nq], in0=t1[:, :, 0:nq], in1=t2[:, :, 0:nq], op=ALU.add),
          r=[("t1", 0), ("t1", 3), ("t1", 6), "t2"], w=[("QT", pb)])
        for kc in range(8):
            A("pe", lambda e, kc=kc: e.matmul(mx[0:48, 0:nq], lhsT=Wg[:, kc, :], rhs=hTq[:, kc, 0:nq], start=(kc == 0), stop=(kc == 7)),
              r=["Wg", "hTq"], w=MXALL)
        A("act", lambda e: e.activation(out=gsig[pb][:, 0:nq], in_=mx[0:48, 0:nq], func=AF.Sigmoid), r=MXALL, w=[("gsig", pb)])
        A("sp", lambda e: e.dma_start(out=gscr[bi, :, 0:nq], in_=gsig[pb][:, 0:nq]), r=[("gsig", pb)], w=[("gscr", bi)], dma=("gs", pb))

    growi = [0]
    grpi = [0]
    MXALL = ["mx"]

    DEFER = True
    pending = []
    stepc = [0]

    def flush(force=False):
        while pending and (force or pending[0][0] <= stepc[0]):
            pending.pop(0)[1]()

    fini = [0]

    def finalize(bi, g, br, first, src, srck, delay):
        S, off, nq, col0 = BLK[bi]
        pb = bi % 2
        p0 = 64 * g
        dp = 64 if g == 0 else 0
        flush(force=True)
        fi = fini[0]
        fini[0] += 1
        X, xk = (dsb, "dsb") if fi % 2 == 0 else (dsb2, "dsb2")
        ri = fi % 8
        gi = growi[0] % 2
        growi[0] += 1
        r0 = br * 16 + g * 8
        A("sp", lambda e: e.dma_start(out=grow[gi][dp:dp + 1, :], in_=gs2[bi:bi + 1, r0 * 128:r0 * 128 + 1024]),
          r=[("gscr", bi)], w=[("grow", gi)], dma=("gr", gi))
        A("dve", lambda e: e.tensor_scalar(out=X[dp:dp + 1, :], in0=src[dp:dp + 1, :], scalar1=1.0e-30, scalar2=None, op0=ALU.max),
          r=[srck(0), srck(1)], w=[xk])
        A("act", lambda e: e.activation(out=X[dp:dp + 1, :], in_=X[dp:dp + 1, :], func=AF.Ln), r=[xk], w=[xk])
        A("act", lambda e: e.activation(out=X[dp:dp + 1, :], in_=X[dp:dp + 1, :], func=AF.Exp, scale=-1.0), r=[xk], w=[xk])
        A("dve", lambda e: e.tensor_tensor(out=X[dp:dp + 1, :], in0=X[dp:dp + 1, :], in1=grow[gi][dp:dp + 1, :], op=ALU.mult),
          r=[xk, ("grow", gi)], w=[xk])
        A("sp", lambda e: e.dma_start(out=crow[ri:ri + 1, :], in_=X[dp:dp + 1, :]), r=[xk], w=[("crow", ri)], dma=("cr", ri % 2))
        A("sp", lambda e: e.dma_start(out=cbc[p0:p0 + 64, :], in_=crow[ri:ri + 1, :].partition_broadcast(64)),
          r=[("crow", ri)], w=[("cbc", g)], dma=("cb", g))
        def tail():
            for u in range(2):
                dst = acc[pb][p0:p0 + 64, 4 * u:4 * u + 4, 0:nq]
                if first:
                    A("dve", lambda e, u=u, dst=dst: e.tensor_tensor(out=dst, in0=v4(src, u, nq, p0, p0 + 64), in1=v4(cbc, u, nq, p0, p0 + 64),
                                                                     op=ALU.mult), r=[srck(u), ("cbc", g)], w=[("acc", pb, g, u)])
                else:
                    A("dve", lambda e, u=u: e.tensor_tensor(out=tmpf[p0:p0 + 64, :, 0:nq], in0=v4(src, u, nq, p0, p0 + 64),
                                                            in1=v4(cbc, u, nq, p0, p0 + 64), op=ALU.mult), r=[srck(u), ("cbc", g)], w=["tmpf"])
                    A("dve", lambda e, dst=dst: e.tensor_tensor(out=dst, in0=dst, in1=tmpf[p0:p0 + 64, :, 0:nq], op=ALU.add),
                      r=["tmpf", ("acc", pb, g, u)], w=[("acc", pb, g, u)])
        if DEFER:
            pending.append((stepc[0] + delay, tail))
        else:
            tail()

    def vaug(Vt, idx, g):
        return Vt[:, idx, 0:128] if g == 0 else Vt[:, idx, 66:194]

    def stage_cmp(bi):
        fins = [stage_cmp_g(bi, g) for g in range(2)]
        for g, (ob, obk) in enumerate(fins):
            finalize(bi, g, 0, True, ob, lambda u, obk=obk: obk + (u,), 4)

    def stage_cmp_g(bi, g):
        S, off, nq, col0 = BLK[bi]
        pb = bi % 2
        M = [128, 128]
        if True:
            for c in range(4):
                sc, sk = (scA, "scA") if c % 2 == 0 else (scB, "scB")
                for u in range(2):
                    A("pe", lambda e, c=c, u=u, sc=sc: e.matmul(v4(sc, u, nq), lhsT=kcT[64 * g:64 * g + 64, c * 128:(c + 1) * 128],
                                                              rhs=QT[pb][64 * g:64 * g + 64, 4 * u:4 * u + 4, 0:nq], start=True, stop=True),
                      r=["kcT", ("QT", pb)], w=[(sk, u)])
                A("act", lambda e, c=c, sc=sc: e.activation(out=Ec[c][:, :, 0:nq], in_=v8(sc, nq), func=AF.Exp, bias=biasC[:, c:c + 1],
                                                            scale=SCALE), r=[(sk, 0), (sk, 1), "biasC"], w=[("Ec", c)])
                A("dve", lambda e, c=c: e.tensor_tensor(out=Ec[c][:, :, 0:nq], in0=Ec[c][:, :, 0:nq],
                                                        in1=bc(mkb3[bi % 3][:, 2 + c, 0:nq], 8, nq), op=ALU.mult),
                  r=[("Ec", c), ("mkb", bi % 3)], w=[("Ec", c)])
            for u in range(2):
                for c in range(4):
                    A("pe", lambda e, c=c, u=u: e.matmul(v4(oa, u, nq, 0, M[g]), lhsT=vaug(RCv, c, g), rhs=Ec[c][:, 4 * u:4 * u + 4, 0:nq],
                                                         start=(c == 0), stop=(c == 3)), r=[("Ec", c), "RCv"], w=[("oa", u)])
            regs = []
            for hl in range(8):
                j, o = hl // 3, (hl % 3) * 129
                tt, tk = [(scA, ("scA", 0)), (scA, ("scA", 1)), (scB, ("scB", 0))][j]
                base = 512 if j == 1 else 0
                regs.append((tt, tk, base + o))
                for c in range(4):
                    A("pe", lambda e, hl=hl, c=c, tt=tt, base=base, o=o: e.matmul(
                        tt[0:nq, base + o:base + o + 129], lhsT=Ec[c][:, hl, 0:nq], rhs=ovl[:, c, :], start=(c == 0), stop=(c == 3)),
                      r=[("Ec", c), "ovl"], w=[tk])
            banks = [(scA, ("scA", 0), 0, 3, 0), (scA, ("scA", 1), 512, 3, 3), (scB, ("scB", 0), 0, 2, 6)]
            for tt, tk, base, nh, h0 in banks:
                A("dve", lambda e, tt=tt, base=base, nh=nh, h0=h0: e.tensor_scalar(
                    out=rc[0:nq, h0:h0 + nh], in0=tt[0:nq, base + 128:base + 128 + 129 * (nh - 1) + 1:129], scalar1=1.0e-30,
                    scalar2=None, op0=ALU.max), r=[tk], w=[("rc", h0)])
            A("dve", lambda e: e.reciprocal(out=rc[0:nq, :], in_=rc[0:nq, :]), r=[("rc", 0), ("rc", 3), ("rc", 6)], w=["rc"])
            for tt, tk, base, nh, h0 in banks:
                A("dve", lambda e, tt=tt, base=base, nh=nh, h0=h0: e.tensor_tensor(
                    out=t1[0:nq, h0:h0 + nh, :], in0=tt[0:nq, base:base + 129 * nh].rearrange("p (h c) -> p h c", c=129)[:, :, 0:128],
                    in1=rc[0:nq, h0:h0 + nh].unsqueeze(2).to_broadcast([nq, nh, 128]), op=ALU.mult), r=[tk, "rc"], w=[("t1", h0)])
            A("dve", lambda e: e.tensor_reduce(out=score[0:nq, :], in_=t1[0:nq, :, :].rearrange("p h s -> p s h"), axis=AX.X, op=ALU.add),
              r=[("t1", 0), ("t1", 3), ("t1", 6)], w=["score"])
            A("dve", lambda e: e.tensor_tensor(out=score[0:nq, :], in0=score[0:nq, :], in1=bon3[bi % 3][0:nq, :], op=ALU.add),
              r=["score", ("bon", bi % 3)], w=["score"])
            A("dve", lambda e: e.max(out=m8a[0:nq, :], in_=score[0:nq, :]), r=["score"], w=["m8a"])
            A("dve", lambda e: e.match_replace(out=work[0:nq, :], in_to_replace=m8a[0:nq, :], in_values=score[0:nq, :], imm_value=-3.0e38),
              r=["score", "m8a"], w=["work"])
            A("dve", lambda e: e.max(out=m8b[0:nq, :], in_=work[0:nq, :]), r=["work"], w=["m8b"])
            A("dve", lambda e: e.tensor_scalar(out=thr[0:nq, :], in0=m8b[0:nq, 7:8], scalar1=-1.0e29, scalar2=None, op0=ALU.max),
              r=["m8b"], w=["thr"])
            A("dve", lambda e: e.tensor_scalar(out=selq[0:nq, :], in0=score[0:nq, :], scalar1=thr[0:nq, :], scalar2=None, op0=ALU.is_ge),
              r=["score", "thr"], w=["selq"])
            A("pe", lambda e: e.transpose(out=mx[:, 0:nq], in_=selq[0:nq, :], identity=identf[0:nq, 0:nq]), r=["selq", "identf"], w=MXALL)
            A("act", lambda e, g=g: e.copy(out=selT[pb][g][:, 0:nq], in_=mx[:, 0:nq]), r=MXALL, w=[("selT", pb, g)])
            flush(force=True)
            ob = oasb[grpi[0] % 2]
            obk = ("oasb", grpi[0] % 2)
            grpi[0] += 1
            for u in range(2):
                A("act", lambda e, u=u, ob=ob: e.copy(out=ob[:, u * 512:(u + 1) * 512], in_=oa[:, u * 512:(u + 1) * 512]),
                  r=[("oa", u)], w=[obk + (u,)])
            return ob, obk

    LAG = 4
    FILL = 1
    WARM = 0
    XFILL = 16

    def stage_attn(bi):
        S, off, nq, col0 = BLK[bi]
        pb = bi % 2
        M = [128, 128]
        items = []
        gidx = []
        for g in range(2):
            for br in (1, 2):
                kts = list(range(0, S + 1)) if br == 1 else list(range(S - 4, S + 1))
                for idx, kt in enumerate(kts):
                    items.append((g, br, kt, idx == 0, idx == len(kts) - 1))
                    gidx.append(idx)
        N = len(items)
        srcs = [None] * N
        mids = [None] * N

        def front(i):
            g, br, kt, isfirst, islast = items[i]
            KT, Vt, koff = (KsT, Vs, 0) if br == 1 else (KwT, Vw, 40)
            sc, sk = (scA, "scA") if i % 2 == 0 else (scB, "scB")
            Eb, ek = E[i % NBUF], ("E", i % NBUF)
            Pq, pk_ = Pb[i % NBUF], ("Pb", i % NBUF)
            mb, mbk = msk[i % NBUF], ("msk", i % NBUF)
            masked = True
            if br == 1:
                a, v = kt // 16, kt % 16
                kw = dict(tile_position=(96, 0)) if a == 3 else {}
                A("pe", lambda e: e.matmul(mx[:, 0:nq], lhsT=EF[32 * a:32 * a + 32, v, :], rhs=selT[pb][g][32 * a:32 * a + 32, 0:nq],
                                           start=True, stop=True, **kw), r=["EF", ("selT", pb, g)], w=["mx"])
                if kt == S:
                    A("dve", lambda e: e.tensor_tensor(out=mb[:, 0:nq], in0=mx[:, 0:nq], in1=mkb3[bi % 3][:, 0, 0:nq], op=ALU.mult),
                      r=["mx", ("mkb", bi % 3)], w=[mbk])
                else:
                    A("dve", lambda e: e.tensor_copy(out=mb[:, 0:nq], in_=mx[:, 0:nq]), r=["mx"], w=[mbk])
                mask_ap, mask_r = mb[:, 0:nq], [mbk]
            else:
                if kt == S - 4:
                    mask_ap, mask_r = mkb3[bi % 3][:, 1, 0:nq], [("mkb", bi % 3)]
                elif kt == S:
                    mask_ap, mask_r = mkb3[bi % 3][:, 0, 0:nq], [("mkb", bi % 3)]
                else:
                    masked = False
            for u in range(2):
                A("pe", lambda e, u=u: e.matmul(v4(sc, u, nq), lhsT=KT[64 * g:64 * g + 64, (kt - koff) * 128:(kt - koff + 1) * 128],
                                                rhs=QT[pb][64 * g:64 * g + 64, 4 * u:4 * u + 4, 0:nq], start=True, stop=True),
                  r=[("QT", pb)], w=[(sk, u)])
            A("act", lambda e: e.activation(out=Eb[:, :, 0:nq], in_=v8(sc, nq), func=AF.Exp, scale=SCALE), r=[(sk, 0), (sk, 1)], w=[ek])
            for _ in range(FILL + (1 if gidx[i] < XFILL else 0)):
                A("pe", lambda e: e.matmul(msc[:], lhsT=Wq[:, 0, 0:128], rhs=Wq[:, 1, 0:512], start=True, stop=True), r=["Wq"], w=["msc"])
            if masked:
                mids[i] = (Eb, ek, Pq, pk_, mask_ap, mask_r)
                srcs[i] = (Pq, pk_)
            else:
                srcs[i] = (Eb, ek)

        def mid(i):
            if mids[i] is None:
                return
            Eb, ek, Pq, pk_, mask_ap, mask_r = mids[i]
            A("dve", lambda e: e.tensor_tensor(out=Pq[:, :, 0:nq], in0=Eb[:, :, 0:nq], in1=bc(mask_ap, 8, nq), op=ALU.mult),
              r=[ek] + mask_r, w=[pk_])

        def back(i):
            g, br, kt, isfirst, islast = items[i]
            KT, Vt, koff = (KsT, Vs, 0) if br == 1 else (KwT, Vw, 40)
            src, srck = srcs[i]
            for u in range(2):
                A("pe", lambda e, u=u: e.matmul(v4(oa, u, nq, 0, M[g]), lhsT=vaug(Vt, kt - koff, g), rhs=src[:, 4 * u:4 * u + 4, 0:nq],
                                                start=isfirst, stop=islast), r=[srck], w=[("oa", u)])
            if islast:
                flush(force=True)
                ob = oasb[grpi[0] % 2]
                obk = ("oasb", grpi[0] % 2)
                grpi[0] += 1
                for u in range(2):
                    A("act", lambda e, u=u: e.copy(out=ob[:, u * 512:(u + 1) * 512], in_=oa[:, u * 512:(u + 1) * 512]),
                      r=[("oa", u)], w=[obk + (u,)])
                finalize(bi, g, br, False, ob, lambda u: obk + (u,), 4)

        for _ in range(WARM):
            A("pe", lambda e: e.matmul(msc[:], lhsT=Wq[:, 0, 0:128], rhs=Wq[:, 1, 0:512], start=True, stop=True), r=["Wq"], w=["msc"])
        for i in range(N + LAG):
            stepc[0] += 1
            flush()
            if i < N:
                front(i)
            if 0 <= i - 1 < N:
                mid(i - 1)
            if i - LAG >= 0:
                back(i - LAG)
        if DEFER:
            pending.append((stepc[0] + 4, lambda: stage_store(bi)))
        else:
            stage_store(bi)

    def stage_store(bi):
        S, off, nq, col0 = BLK[bi]
        pb = bi % 2
        rk = [("acc", pb, g, u) for g in range(2) for u in range(2)]
        if bi == 0:
            A("dve", lambda e: e.tensor_scalar(out=obf[:, :, 0:nq], in0=acc[pb][:, :, 0:nq], scalar1=vq[:, 0:1], scalar2=None, op0=ALU.mult),
              r=rk + ["vq"], w=["obf"])
        else:
            A("dve", lambda e: e.tensor_copy(out=obf[:, :, 0:nq], in_=acc[pb][:, :, 0:nq]), r=rk, w=["obf"])
        A("sp", lambda e: e.dma_start(out=oscr[:, :, col0:col0 + nq], in_=obf[:, :, 0:nq]), r=["obf"], w=["oscr"], dma="os")
        if "selT" in dbgout and bi == dbgout["_blk"]:
            for g in range(2):
                A("sp", lambda e, g=g: e.dma_start(out=dbgout["selT"][g], in_=selT[pb][g][:]), r=[("selT", pb, g)], dma="dbg")
            A("sp", lambda e: e.dma_start(out=dbgout["QT"], in_=QT[pb][:]), r=[("QT", pb)], dma="dbg")
            A("sp", lambda e: e.dma_start(out=dbgout["acc"], in_=acc[pb][:]), r=rk, dma="dbg")

    nb = dbgout.get("_nblk", NBLK)
    stage_load(0)
    stage_q(0)
    if nb > 1:
        stage_load(1)
    stage_cmp(0)
    for bi in range(nb):
        if bi + 1 < nb:
            stage_q(bi + 1)
            if bi + 2 < nb:
                stage_load(bi + 2)
            stage_cmp(bi + 1)
        stage_attn(bi)
    flush(force=True)
    ph.emit()


def _norm_group(A, xres, c0, n, gcol, onesb, sqb, pn, rs, dst, dkey, tag):
    A("act", lambda e: e.activation(out=sqb[:, :, 0:n], in_=xres[:, :, c0:c0 + n], func=AF.Square), r=["xres"], w=["sqb"])
    for kc in range(8):
        A("pe", lambda e, kc=kc: e.matmul(pn[:, 0:n], lhsT=onesb[:], rhs=sqb[:, kc, 0:n], start=(kc == 0), stop=(kc == 7)),
          r=["sqb"], w=["pn"])
    A("act", lambda e: e.activation(out=rs[:, 0:n], in_=pn[:, 0:n], func=AF.Sqrt, bias=EPS, scale=1.0 / D), r=["pn"], w=["rs"])
    A("dve", lambda e: e.reciprocal(out=rs[:, 0:n], in_=rs[:, 0:n]), r=["rs"], w=["rs"])
    for kc in range(8):
        A("dve", lambda e, kc=kc: e.scalar_tensor_tensor(out=dst[:, kc, 0:n], in0=xres[:, kc, c0:c0 + n], scalar=gcol[:, kc:kc + 1],
                                                        in1=rs[:, 0:n], op0=ALU.mult, op1=ALU.mult), r=["xres", "rs"], w=[dkey])


def _phase_C(nc, st, SB, PS, dr, oscr, P, dbgout):
    xres, identf = P["xres"], P["identf"]
    Wo = SB(st, "Wo", [128, 8, 1024], BF16)
    xq = [SB(st, f"xqC{i}", [128, D], F32) for i in range(2)]
    ot = [SB(st, f"otC{i}", [128, 8, 512], BF16) for i in range(2)]
    pa = [PS(st, f"paC{i}", [128, 1024], F32) for i in range(2)]
    py = [PS(st, f"pyC{i}", [128, 512], F32) for i in range(2)]
    ph = Phase(nc, "C")
    A = ph.add
    A("pool", lambda e: e.dma_start(out=Wo[:], in_=dr["wout"].rearrange("(k p) n -> p k n", p=128)), w=["Wo"], dma="w")
    for bi, (S, off, nq, col0) in enumerate(BLK):
        pb = bi % 2
        t0 = 128 * S + off
        A("sp", lambda e, pb=pb, t0=t0, nq=nq: e.dma_start(out=xq[pb][0:nq, :], in_=dr["xs"][t0:t0 + nq, :]), w=[("xq", pb)], dma=("xq", pb))
        for kc in range(8):
            A("pe", lambda e, kc=kc, pb=pb, nq=nq: e.transpose(out=pa[pb][:, kc * 128:kc * 128 + nq], in_=xq[pb][0:nq, kc * 128:(kc + 1) * 128],
                                                              identity=identf[0:nq, 0:nq]), r=[("xq", pb)], w=[("pa", pb)])
        A("act", lambda e, pb=pb, nq=nq, col0=col0: e.copy(out=xres[:, :, col0:col0 + nq],
                                                           in_=pa[pb][:].rearrange("p (k q) -> p k q", q=128)[:, :, 0:nq]),
          r=[("pa", pb)], w=["xres"])
    for ti, (c0, n) in enumerate(TG):
        tb = ti % 2
        A("sp", lambda e, tb=tb, c0=c0, n=n: e.dma_start(out=ot[tb][:, :, 0:n], in_=oscr[:, :, c0:c0 + n]), w=[("ot", tb)], dma=("ot", tb))
        for dc in range(8):
            for hl in range(8):
                A("pe", lambda e, dc=dc, hl=hl, tb=tb, n=n: e.matmul(py[dc % 2][:, 0:n], lhsT=Wo[:, hl, dc * 128:(dc + 1) * 128],
                                                                    rhs=ot[tb][:, hl, 0:n], start=(hl == 0), stop=(hl == 7)),
                  r=["Wo", ("ot", tb)], w=[("py", dc % 2)])
            A("dve", lambda e, dc=dc, c0=c0, n=n: e.tensor_tensor(out=xres[:, dc, c0:c0 + n], in0=xres[:, dc, c0:c0 + n],
                                                                  in1=py[dc % 2][:, 0:n], op=ALU.add),
              r=[("py", dc % 2), "xres"], w=["xres"])
    if "x0mix" in dbgout:
        A("sp", lambda e: e.dma_start(out=dbgout["x0mix"], in_=xres[:]), r=["xres"], dma="dbg")
    ph.emit()


def _phase_ffn(nc, st, SB, PS, dr, L, P, dbgout):
    xres, onesb, gsb, vq = P["xres"], P["onesb"], P["gsb"], P["vq"]
    gcol = gsb[:, 1 + 2 * L, :]
    hTall = SB(st, f"hTall{L}", [128, 8, NTOK], BF16)
    with contextlib.ExitStack() as nst:
        sqb = SB(nst, f"sqbf{L}", [128, 8, 512], BF16)
        rs = SB(nst, f"rsf{L}", [128, 512], F32)
        pn = PS(nst, f"pnf{L}", [128, 512], F32)
        phn = Phase(nc, f"N{L}")
        for ti, (c0, n) in enumerate(TG):
            _norm_group(phn.add, xres, c0, n, gcol, onesb, sqb, pn, rs, hTall[:, :, c0:c0 + n], "hT", f"f{L}")
        phn.emit()
    wu = [SB(st, f"wu{L}{i}", [128, 8, 6, 256], BF16) for i in range(2)]
    wd = [SB(st, f"wd{L}{i}", [128, 6, 1024], BF16) for i in range(2)]
    cw = SB(st, f"cw{L}", [128, 3, 44], F32)
    cb = SB(st, f"cb{L}", [128, 44], F32)
    carry = SB(st, f"carry{L}", [128, 44, 2], F32)
    ub = [SB(st, f"ub{L}{i}", [128, 514], F32) for i in range(2)]
    cbuf = [SB(st, f"cbuf{L}{i}", [128, 512], F32) for i in range(2)]
    sg = SB(st, f"sg{L}", [128, 512], F32)
    act2 = [SB(st, f"act{L}{i}", [128, 6, 512], BF16) for i in range(2)]
    pu = [[PS(st, f"pu{L}{a}{b}", [128, 512], F32) for b in range(2)] for a in range(2)]
    py = [PS(st, f"pyf{L}{i}", [128, 512], F32) for i in range(2)]
    ph = Phase(nc, f"F{L}")
    A = ph.add
    A("sp", lambda e: e.dma_start(out=cw[:], in_=dr[f"cw{L}"]), w=["cw"], dma="c")
    A("sp", lambda e: e.dma_start(out=cb[:], in_=dr[f"cb{L}"]), w=["cb"], dma="c")
    A("dve", lambda e: e.memset(carry[:], 0.0), w=[("carry", ch) for ch in range(44)])

    def load_pass(p):
        wb = p % 2
        for i, fc in enumerate(FPASS[p]):
            A("pool", lambda e, i=i, fc=fc, wb=wb: e.dma_start(
                out=wu[wb][:, :, i, 0:128], in_=dr[f"wup{L}"][:, fc * 128:(fc + 1) * 128].rearrange("(k p) n -> p k n", p=128)),
              w=[("wu", wb, i)], dma=("w", wb))
            A("pool", lambda e, i=i, fc=fc, wb=wb: e.dma_start(
                out=wu[wb][:, :, i, 128:256], in_=dr[f"wup{L}"][:, DFF + fc * 128:DFF + (fc + 1) * 128].rearrange("(k p) n -> p k n", p=128)),
              w=[("wu", wb, i)], dma=("w", wb))
            A("pool", lambda e, i=i, fc=fc, wb=wb: e.dma_start(out=wd[wb][:, i, :], in_=dr[f"wdn{L}"][fc * 128:(fc + 1) * 128, :]),
              w=[("wd", wb, i)], dma=("w", wb))

    def up_stage(p, ti):
        wb = p % 2
        c0, n = TG[ti]
        ab = (p * len(TG) + ti) % 2
        for i, fc in enumerate(FPASS[p]):
            for part in range(2):
                bank = pu[part][i % 2]
                bk = ("pu", part, i % 2)
                ch = fc + 22 * part
                for kc in range(8):
                    A("pe", lambda e, kc=kc, i=i, part=part, bank=bank: e.matmul(
                        bank[:, 0:n], lhsT=wu[wb][:, kc, i, part * 128:(part + 1) * 128], rhs=hTall[:, kc, c0:c0 + n],
                        start=(kc == 0), stop=(kc == 7)), r=[("wu", wb, i)], w=[bk])
                A("act", lambda e, part=part, bank=bank: e.copy(out=ub[part][:, 2:2 + n], in_=bank[:, 0:n]), r=[bk], w=[("ub", part)])
                A("act", lambda e, part=part, bank=bank, ch=ch: e.activation(
                    out=cbuf[part][:, 0:n], in_=bank[:, 0:n], func=AF.Identity, bias=cb[:, ch:ch + 1], scale=cw[:, 2, ch:ch + 1]),
                  r=[bk, "cw", "cb"], w=[("cbuf", part)])
                A("act", lambda e, part=part, ch=ch: e.copy(out=ub[part][:, 0:2], in_=carry[:, ch, :]), r=[("carry", ch)], w=[("ubc", part)])
                for k in (1, 0):
                    A("dve", lambda e, part=part, ch=ch, k=k: e.scalar_tensor_tensor(
                        out=cbuf[part][:, 0:n], in0=ub[part][:, k:k + n], scalar=cw[:, k, ch:ch + 1], in1=cbuf[part][:, 0:n],
                        op0=ALU.mult, op1=ALU.add), r=[("ub", part), ("ubc", part), ("cbuf", part), "cw"], w=[("cbuf", part)])
                A("act", lambda e, part=part, ch=ch: e.copy(out=carry[:, ch, :], in_=ub[part][:, n:n + 2]), r=[("ub", part)], w=[("carry", ch)])
            A("act", lambda e: e.activation(out=sg[:, 0:n], in_=cbuf[0][:, 0:n], func=AF.Silu), r=[("cbuf", 0)], w=["sg"])
            A("dve", lambda e, i=i: e.tensor_tensor(out=act2[ab][:, i, 0:n], in0=sg[:, 0:n], in1=cbuf[1][:, 0:n], op=ALU.mult),
              r=["sg", ("cbuf", 1)], w=[("act", ab, i)])

    def down_stage(p, ti):
        wb = p % 2
        c0, n = TG[ti]
        ab = (p * len(TG) + ti) % 2
        nf = len(FPASS[p])
        for dc in range(8):
            for i in range(nf):
                A("pe", lambda e, dc=dc, i=i: e.matmul(py[dc % 2][:, 0:n], lhsT=wd[wb][:, i, dc * 128:(dc + 1) * 128], rhs=act2[ab][:, i, 0:n],
                                                       start=(i == 0), stop=(i == nf - 1)), r=[("wd", wb, i), ("act", ab, i)], w=[("py", dc % 2)])
            A("dve", lambda e, dc=dc: e.tensor_tensor(out=xres[:, dc, c0:c0 + n], in0=xres[:, dc, c0:c0 + n], in1=py[dc % 2][:, 0:n], op=ALU.add),
              r=[("py", dc % 2), "xres"], w=["xres"])

    steps = [(p, ti) for p in range(len(FPASS)) for ti in range(len(TG))]
    load_pass(0)
    load_pass(1)
    for k in range(len(steps) + 1):
        if k < len(steps):
            up_stage(*steps[k])
        if k >= 1:
            pp, pti = steps[k - 1]
            down_stage(pp, pti)
            if pti == len(TG) - 1 and pp + 2 < len(FPASS):
                load_pass(pp + 2)
    A("dve", lambda e: e.tensor_scalar(out=xres[:, :, 0:HALO], in0=xres[:, :, 0:HALO], scalar1=vq[:, 0:1], scalar2=None, op0=ALU.mult),
      r=["xres"], w=["xres"])
    if f"xffn{L}" in dbgout:
        A("sp", lambda e: e.dma_start(out=dbgout[f"xffn{L}"], in_=xres[:]), r=["xres"], dma="dbg")
    ph.emit()


def _phase_pool(nc, st, SB, PS, dr, P, dbgout):
    xres, onesb, gsb, vq = P["xres"], P["onesb"], P["gsb"], P["vq"]
    gcol = gsb[:, 2, :]
    hf = SB(st, "hf", [128, 8, NTOK], F32)
    pl = SB(st, "pl", [128, 8, NTOK], BF16)
    wa = SB(st, "wa", [128, NTOK], F32)
    wb_ = SB(st, "wb", [128, NTOK], F32)
    sqb = SB(st, "sqbp", [128, 8, 512], BF16)
    rs = SB(st, "rsp", [128, 512], F32)
    pw = SB(st, "pw", [128, 8, 256], BF16)
    pbias = SB(st, "pbias", [128, 8], F32)
    pscl = SB(st, "pscl", [128, 8], F32)
    psc16 = SB(st, "psc16", [128, 8, 16], F32)
    tmp16 = SB(st, "tmp16", [128, 16], F32)
    ytmp = SB(st, "ytmp", [128, 512], F32)
    pn = PS(st, "pnp", [128, 512], F32)
    py = [PS(st, f"pyp{i}", [128, 512], F32) for i in range(2)]
    ph = Phase(nc, "P")
    A = ph.add
    A("pool", lambda e: e.dma_start(out=pw[:], in_=dr["poolw"]), w=["pw"], dma="w")
    A("sp", lambda e: e.dma_start(out=pbias[:], in_=dr["poolb"]), w=["pbias"], dma="c")
    A("sp", lambda e: e.dma_start(out=pscl[:], in_=dr["pools"]), w=["pscl"], dma="c")
    A("sp", lambda e: e.dma_start(out=psc16[:], in_=dr["pscale"]), w=["psc16"], dma="c")
    for ti, (c0, n) in enumerate(TG):
        _norm_group(A, xres, c0, n, gcol, onesb, sqb, pn, rs, hf[:, :, c0:c0 + n], "hf", "p")
    for kc in range(8):
        nsteps = kc // 2 + 1
        w = 2 ** nsteps
        src, sk = hf[:, kc, :], "hf"
        bufs = [(wa, "wa"), (wb_, "wb")]
        for sidx in range(nsteps):
            d = 2 ** sidx
            dst, dk = bufs[sidx % 2]
            A("dve", lambda e, src=src, dst=dst, d=d: e.tensor_tensor(out=dst[:, d:NTOK], in0=src[:, d:NTOK], in1=src[:, 0:NTOK - d], op=ALU.add),
              r=[sk], w=[dk])
            A("act", lambda e, src=src, dst=dst, d=d: e.copy(out=dst[:, 0:d], in_=src[:, 0:d]), r=[sk], w=[dk])
            src, sk = dst[:], dk
        A("dve", lambda e, src=src, kc=kc, w=w: e.scalar_tensor_tensor(out=pl[:, kc, :], in0=src, scalar=1.0 / w, in1=hf[:, kc, :],
                                                                      op0=ALU.mult, op1=ALU.subtract), r=[sk, "hf"], w=[("pl", kc)])
        A("dve", lambda e, src=src, kc=kc: e.tensor_tensor(out=tmp16[:], in0=src[:, HALO:HALO + 16], in1=psc16[:, kc, :], op=ALU.mult),
          r=[sk, "psc16"], w=["tmp16"])
        A("dve", lambda e, kc=kc: e.tensor_tensor(out=pl[:, kc, HALO:HALO + 16], in0=tmp16[:], in1=hf[:, kc, HALO:HALO + 16], op=ALU.subtract),
          r=["tmp16", "hf", ("pl", kc)], w=[("pl", kc)])
    for ti, (c0, n) in enumerate(TG):
        for oc in range(8):
            g, oh = oc // 2, oc % 2
            for kh in range(2):
                A("pe", lambda e, oc=oc, g=g, oh=oh, kh=kh, c0=c0, n=n: e.matmul(
                    py[oc % 2][:, 0:n], lhsT=pw[:, g * 2 + kh, oh * 128:(oh + 1) * 128], rhs=pl[:, g * 2 + kh, c0:c0 + n],
                    start=(kh == 0), stop=(kh == 1)), r=["pw", ("pl", g * 2 + kh)], w=[("py", oc % 2)])
            A("dve", lambda e, oc=oc, n=n: e.tensor_scalar(out=ytmp[:, 0:n], in0=py[oc % 2][:, 0:n], scalar1=pbias[:, oc:oc + 1],
                                                          scalar2=pscl[:, oc:oc + 1], op0=ALU.add, op1=ALU.mult),
              r=[("py", oc % 2), "pbias", "pscl"], w=["ytmp"])
            A("dve", lambda e, oc=oc, c0=c0, n=n: e.tensor_tensor(out=xres[:, oc, c0:c0 + n], in0=xres[:, oc, c0:c0 + n], in1=ytmp[:, 0:n],
                                                                  op=ALU.add), r=["ytmp", "xres"], w=["xres"])
    A("dve", lambda e: e.tensor_scalar(out=xres[:, :, 0:HALO], in0=xres[:, :, 0:HALO], scalar1=vq[:, 0:1], scalar2=None, op0=ALU.mult),
      r=["xres"], w=["xres"])
    if "xpool" in dbgout:
        A("sp", lambda e: e.dma_start(out=dbgout["xpool"], in_=xres[:]), r=["xres"], dma="dbg")
    ph.emit()


def _phase_out(nc, st, SB, PS, dr, out, P, dbgout):
    xres, onesb, gsb, identf = P["xres"], P["onesb"], P["gsb"], P["identf"]
    gcol = gsb[:, 4, :]
    of = SB(st, "of", [128, 8, 512], F32)
    sqb = SB(st, "sqbo", [128, 8, 512], BF16)
    rs = SB(st, "rso", [128, 512], F32)
    ot = [SB(st, f"oto{i}", [128, D], F32) for i in range(2)]
    pn = PS(st, "pno", [128, 512], F32)
    pt = [PS(st, f"pto{i}", [128, 1024], F32) for i in range(2)]
    ph = Phase(nc, "O")
    A = ph.add
    for gi in range(4):
        c0 = HALO + 512 * gi
        _norm_group(A, xres, c0, 512, gcol, onesb, sqb, pn, rs, of, "of", "o")
        for tt in range(4):
            tb = tt % 2
            for kc in range(8):
                A("pe", lambda e, tt=tt, tb=tb, kc=kc: e.transpose(out=pt[tb][:, kc * 128:(kc + 1) * 128],
                                                                 in_=of[:, kc, tt * 128:(tt + 1) * 128], identity=identf[:]),
                  r=["of"], w=[("pt", tb)])
            A("act", lambda e, tb=tb: e.copy(out=ot[tb][:], in_=pt[tb][:]), r=[("pt", tb)], w=[("ot", tb)])
            row = (gi * 4 + tt) * 128
            A("sp", lambda e, tb=tb, row=row: e.dma_start(out=out[row:row + 128, :], in_=ot[tb][:]), r=[("ot", tb)], dma=("st", tb))
    ph.emit()


_CACHE = {}


def kernel(**inputs):
    inp = {k: np.asarray(v) for k, v in inputs.items()}
    if "nc" not in _CACHE:
        _CACHE["nc"] = build()
    nc = _CACHE["nc"]
    sh = _shared_inputs(inp)
    maps = []
    for c in range(8):
        m = dict(sh)
        m.update(_core_inputs(inp, c))
        maps.append(m)
    res = run_bass_kernel_spmd(nc, maps, core_ids=list(range(8)))
    outp = np.zeros((2, T, D), np.float32)
    for c in range(8):
        b, j = c // 4, c % 4
        outp[b, 2048 * j:2048 * (j + 1)] = np.asarray(res.results[c]["out"], dtype=np.float32)
    return outp
```

```python
import contextlib
import numpy as np
import ml_dtypes
import concourse.bass as bass
import concourse.mybir as mybir
from concourse.bass_utils import run_bass_kernel_spmd

F32 = mybir.dt.float32
BF16 = mybir.dt.bfloat16
AF = mybir.ActivationFunctionType
ALU = mybir.AluOpType
AX = mybir.AxisListType
NPBF = ml_dtypes.bfloat16

D = 1024
KC = 8
T = 8192
NSLOT = 64
OWN0 = 48
NOWN = 16
HALO = 20
NTOK = HALO + 128 * NOWN
DFF = 2816
NFC = 22
EPS = 1e-6
SCALE = 0.125
NEGB = -30000.0
BLK = [(47, 108, HALO, 0)] + [(OWN0 + m, 0, 128, HALO + 128 * m) for m in range(NOWN)]
NBLK = len(BLK)
TG = [(0, HALO)] + [(HALO + 512 * i, 512) for i in range(4)]
FPASS = [list(range(0, 6)), list(range(6, 12)), list(range(12, 17)), list(range(17, 22))]

ENGS = ("pe", "act", "dve", "pool", "sp")


class Op:
    __slots__ = ("eng", "fn", "dma", "waits", "signal", "sigidx", "idx")

    def __init__(self, eng, fn, dma):
        self.eng = eng
        self.fn = fn
        self.dma = dma
        self.waits = []
        self.signal = False
        self.sigidx = None


class Phase:
    def __init__(self, nc, name):
        self.nc = nc
        self.name = name
        self.ops = {e: [] for e in ENGS}
        self.lastw = {}
        self.readers = {}
        self.dma_count = {}
        self.n = 0

    def add(self, eng, fn, r=(), w=(), dma=None):
        op = Op(eng, fn, dma)
        op.idx = self.n
        self.n += 1
        deps = []
        for k in r:
            x = self.lastw.get(k)
            if x is not None:
                deps.append(x)
        for k in w:
            x = self.lastw.get(k)
            if x is not None:
                deps.append(x)
            deps.extend(self.readers.get(k, {}).values())
        seen = set()
        for d in deps:
            if d is op or id(d) in seen:
                continue
            seen.add(id(d))
            if d.dma is not None:
                op.waits.append(("dma", d.dma, 16 * self.dma_count[d.dma]))
            else:
                if d.eng == "pe" and eng == "pe" and dma is None:
                    continue
                d.signal = True
                op.waits.append(("eng", d, None))
        rk = eng if dma is None else ("dma", op.idx)
        for k in r:
            self.readers.setdefault(k, {})[rk] = op
        for k in w:
            self.lastw[k] = op
            self.readers[k] = {}
        if dma is not None:
            self.dma_count[dma] = self.dma_count.get(dma, 0) + 1
        self.ops[eng].append(op)
        return op

    def emit(self):
        nc = self.nc
        for e in ENGS:
            k = 0
            for op in self.ops[e]:
                if op.dma is None and op.signal:
                    k += 1
                    op.sigidx = k
        with contextlib.ExitStack() as st:
            esem = {e: st.enter_context(nc.semaphore(f"{self.name}_s_{e}")) for e in ENGS}
            dsem = {k: st.enter_context(nc.semaphore(f"{self.name}_d_{i}"))
                    for i, k in enumerate(self.dma_count)}
            block = st.enter_context(nc.Block())
            final_dma = dict(self.dma_count)

            def run(e, eng):
                seen = {}
                for op in self.ops[e]:
                    for kind, obj, val in op.waits:
                        if kind == "dma":
                            sem, v = dsem[obj], val
                        else:
                            sem, v = esem[obj.eng], obj.sigidx
                        if seen.get(id(sem), 0) >= v:
                            continue
                        seen[id(sem)] = v
                        eng.wait_ge(sem, v)
                    inst = op.fn(eng)
                    if op.dma is not None:
                        inst.then_inc(dsem[op.dma], 16)
                    elif op.signal:
                        inst.then_inc(esem[e], 1)
                mine = []
                for op in self.ops[e]:
                    if op.dma is not None and op.dma not in mine:
                        mine.append(op.dma)
                for k in mine:
                    v = 16 * final_dma[k]
                    if seen.get(id(dsem[k]), 0) < v:
                        eng.wait_ge(dsem[k], v)

            block.tensor(lambda eng: run("pe", eng))
            block.scalar(lambda eng: run("act", eng))
            block.vector(lambda eng: run("dve", eng))
            block.gpsimd(lambda eng: run("pool", eng))
            block.sync(lambda eng: run("sp", eng))


def _c(a, dt=np.float32):
    return np.ascontiguousarray(a).astype(dt, copy=False)


def _pk(v):
    return _c(np.asarray(v).reshape(-1, 128).T)


def _rope_tab(pos):
    inv = (1.0 / (10000.0 ** (np.arange(0, 64, 2, dtype=np.float32) / np.float32(64)))).astype(np.float32)
    ang = pos.astype(np.float32)[:, None] * inv[None, :]
    c = np.cos(ang).astype(np.float32)
    s = np.sin(ang).astype(np.float32)
    idx = np.arange(128) % 32
    return _c(c[:, idx].T), _c(s[:, idx].T)


def _shared_inputs(inp):
    sh = {}
    w_in = np.asarray(inp["nsa_w_in"])
    sh["wq"] = _c(w_in[:, :1024].reshape(1024, 2, 8, 64).transpose(0, 2, 1, 3).reshape(1024, 1024))
    sh["wkv"] = _c(w_in[:, 1024:1792])
    sh["wg"] = _c(w_in[:, 1792:1840].reshape(1024, 2, 8, 3).transpose(0, 3, 1, 2).reshape(1024, 48))
    sh["wout"] = _c(np.asarray(inp["nsa_w_out"]).reshape(2, 8, 64, 1024).transpose(1, 0, 2, 3).reshape(1024, 1024))
    for x in ("k", "v"):
        sh[f"c{x}_w1"] = _c(inp[f"cmp_{x}_w1"])
        sh[f"c{x}_w2"] = _c(inp[f"cmp_{x}_w2"])
        pos = np.asarray(inp[f"cmp_{x}_pos"])
        pt = np.zeros((128, 32, 2), np.float32)
        pt[0:64, :, 0] = pos.T
        pt[64:128, :, 0] = pos.T
        sh[f"c{x}_posT"] = _c(pt)
        sh[f"c{x}_b1"] = _c(np.asarray(inp[f"cmp_{x}_b1"]).reshape(2, 128).T)
        sh[f"c{x}_b2"] = _c(np.tile(np.asarray(inp[f"cmp_{x}_b2"]), 2)[None, :])
    for i in (0, 1):
        sh[f"gmix{i}"] = _pk(inp[f"norm_mix_{i}"])
        sh[f"gffn{i}"] = _pk(inp[f"norm_ffn_{i}"])
        sh[f"wup{i}"] = _c(inp[f"ffn_up_{i}"])
        sh[f"wdn{i}"] = _c(inp[f"ffn_down_{i}"])
        cw = np.asarray(inp[f"ffn_conv_w_{i}"])
        sh[f"cw{i}"] = _c(cw.reshape(3, 44, 128).transpose(2, 0, 1))
        sh[f"cb{i}"] = _c(np.asarray(inp[f"ffn_conv_b_{i}"]).reshape(44, 128).T)
    sh["gfin"] = _pk(inp["norm_final"])
    sh["poolw"] = _c(np.asarray(inp["pool_w"]).reshape(4, 2, 128, 256).transpose(2, 0, 1, 3).reshape(128, 8, 256))
    sh["poolb"] = _pk(np.asarray(inp["pool_b"]).reshape(-1))
    sh["pools"] = _pk(inp["pool_scale"])
    sh["identb"] = _c(np.eye(128), NPBF)
    sh["identf"] = _c(np.eye(128))
    ef = np.zeros((128, 16, 128), np.float32)
    for p in range(128):
        for v in range(16):
            if p % 32 == 2 * v:
                ef[p, v, 0:64] = 1
            if p % 32 == 2 * v + 1:
                ef[p, v, 64:128] = 1
    sh["ef"] = _c(ef, NPBF)
    n = np.arange(512)[:, None]
    s = np.arange(128)[None, :]
    lo = np.maximum(n * 16, s * 64)
    hi = np.minimum(n * 16 + 32, (s + 1) * 64)
    ov = np.clip(hi - lo, 0, None) / 32.0
    ova = np.ones((512, 129), np.float32)
    ova[:, :128] = ov
    sh["ovl"] = _c(ova.reshape(4, 128, 129).transpose(1, 0, 2), NPBF)
    mk = np.zeros((NBLK, 128, 6, 128), np.float32)
    k = np.arange(128)[:, None]
    q = np.arange(128)[None, :]
    for bi, (S, off, nq, col0) in enumerate(BLK):
        mk[bi, :, 0, :] = (k <= off + q)
        mk[bi, :, 1, :] = (k > off + q)
        tq = 128 * S + off + q
        for c in range(4):
            mk[bi, :, 2 + c, :] = (16 * (c * 128 + k) + 31 <= tq)
    sh["mk"] = _c(mk, NPBF)
    return sh


def _core_inputs(inp, core):
    b, j = core // 4, core % 4
    SH = OWN0 - 16 * j
    x = np.asarray(inp["x"])[b]
    ci = {}
    xs = np.zeros((T, D), np.float32)
    nreal = (NSLOT - SH) * 128
    xs[SH * 128:] = x[:nreal]
    ci["xs"] = xs
    tp = np.arange(T)
    pos = np.maximum(tp - 128 * SH, 0)
    ci["cosK"], ci["sinK"] = _rope_tab(pos)
    qpos = np.zeros((NBLK, 128), np.int64)
    bonus = np.zeros((NBLK, 128, 128), np.float32)
    sblk = np.arange(128)[None, :]
    for bi, (S, off, nq, col0) in enumerate(BLK):
        tq = 128 * S + off + np.arange(128)
        qpos[bi] = np.maximum(tq - 128 * SH, 0)
        cur = (tq // 64)[:, None]
        forced = (sblk == cur) | (sblk == cur - 1) | (sblk == 2 * SH)
        bonus[bi] = np.where(sblk <= cur, 1e4 * forced, -1e30)
    cq, sq = _rope_tab(qpos.reshape(-1))
    ci["cosQ"], ci["sinQ"] = cq, sq
    ci["bonus"] = _c(bonus.transpose(1, 0, 2))
    nn = np.arange(512)
    cend = np.maximum(16 * (nn - 8 * SH) + 31, 0)
    ci["cosC"], ci["sinC"] = _rope_tab(cend)
    validc = (nn >= 8 * SH) & (nn <= 510)
    ci["biasC"] = _c(np.where(validc, 0.0, NEGB).reshape(4, 128).T)
    vk = (np.arange(NSLOT) >= SH).astype(np.float32)
    ci["validk"] = _c(np.tile(vk[None, :], (128, 1)), NPBF)
    ci["vq"] = _c(np.full((128, 1), 0.0 if j == 0 else 1.0))
    ps = np.zeros((128, 8, 16), np.float32)
    for kc in range(8):
        w = [2, 4, 8, 16][kc // 2]
        for qq in range(16):
            t = 128 * 16 * j + qq
            ps[:, kc, qq] = 1.0 / min(t + 1, w)
    ci["pscale"] = ps
    return ci


def build(dbg=()):
    nc = bass.Bass("TRN2", target_bir_lowering=False)
    dr = {}

    def din(name, shape, dt=F32):
        dr[name] = nc.dram_tensor(name, list(shape), dt, kind="ExternalInput").ap()
        return dr[name]

    din("xs", [T, D])
    din("cosK", [128, T]); din("sinK", [128, T])
    din("cosQ", [128, NBLK * 128]); din("sinQ", [128, NBLK * 128])
    din("bonus", [128, NBLK, 128])
    din("cosC", [128, 512]); din("sinC", [128, 512])
    din("biasC", [128, 4]); din("validk", [128, NSLOT], BF16); din("vq", [128, 1]); din("pscale", [128, 8, 16])
    din("wq", [D, D]); din("wkv", [D, 768]); din("wg", [D, 48]); din("wout", [D, D])
    for x in ("k", "v"):
        din(f"c{x}_w1", [2048, 256]); din(f"c{x}_w2", [256, 64]); din(f"c{x}_posT", [128, 32, 2])
        din(f"c{x}_b1", [128, 2]); din(f"c{x}_b2", [1, 128])
    for i in (0, 1):
        din(f"gmix{i}", [128, 8]); din(f"gffn{i}", [128, 8])
        din(f"wup{i}", [D, 2 * DFF]); din(f"wdn{i}", [DFF, D])
        din(f"cw{i}", [128, 3, 44]); din(f"cb{i}", [128, 44])
    din("gfin", [128, 8]); din("poolw", [128, 8, 256]); din("poolb", [128, 8]); din("pools", [128, 8])
    din("identb", [128, 128], BF16); din("identf", [128, 128])
    din("ef", [128, 16, 128], BF16); din("ovl", [128, 4, 129], BF16); din("mk", [NBLK, 128, 6, 128], BF16)
    out = nc.dram_tensor("out", [128 * NOWN, D], F32, kind="ExternalOutput").ap()
    gscr = nc.dram_tensor("gscr", [NBLK, 48, 128], F32).ap()
    dbgout = {}
    for name, shape, dt in dbg:
        if shape is None:
            dbgout[name] = dt
            continue
        dbgout[name] = nc.dram_tensor("dbg_" + name, list(shape), dt, kind="ExternalOutput").ap()

    with contextlib.ExitStack() as top:
        def SB(st, name, shape, dt):
            return st.enter_context(nc.sbuf_tensor("s_" + name, list(shape), dt))

        def PS(st, name, shape, dt):
            return st.enter_context(nc.psum_tensor("p_" + name, list(shape), dt))

        identb = SB(top, "identb", [128, 128], BF16)
        identf = SB(top, "identf", [128, 128], F32)
        ones32 = SB(top, "ones32", [128, 512], F32)
        onesb = SB(top, "onesb", [128, 128], BF16)
        gsb = SB(top, "gsb", [128, 5, 8], F32)
        vq = SB(top, "vq", [128, 1], F32)
        ph = Phase(nc, "K")
        ph.add("sp", lambda e: e.dma_start(out=identb[:], in_=dr["identb"]), w=["identb"], dma="c")
        ph.add("sp", lambda e: e.dma_start(out=identf[:], in_=dr["identf"]), w=["identf"], dma="c")
        for i, nm in enumerate(("gmix0", "gffn0", "gmix1", "gffn1", "gfin")):
            ph.add("sp", lambda e, i=i, nm=nm: e.dma_start(out=gsb[:, i, :], in_=dr[nm]), w=["gsb"], dma="c")
        ph.add("sp", lambda e: e.dma_start(out=vq[:], in_=dr["vq"]), w=["vq"], dma="c")
        ph.add("dve", lambda e: e.memset(ones32[:], 1.0), w=["ones32"])
        ph.add("dve", lambda e: e.memset(onesb[:], 1.0), w=["onesb"])
        ph.emit()

        oscr = nc.dram_tensor("oscr", [128, 8, NTOK], BF16).ap()
        with contextlib.ExitStack() as att:
            KsT = SB(att, "KsT", [128, T], BF16)
            KwT = SB(att, "KwT", [128, 24 * 128], BF16)
            Vs = SB(att, "Vs", [128, NSLOT, 194], BF16)
            Vw = SB(att, "Vw", [128, 24, 194], BF16)
            kcT = SB(att, "kcT", [128, 512], BF16)
            RCv = SB(att, "RCv", [128, 4, 194], BF16)
            biasC = SB(att, "biasC", [128, 4], F32)
            P = dict(KsT=KsT, KwT=KwT, Vs=Vs, Vw=Vw, kcT=kcT, RCv=RCv, biasC=biasC,
                     identb=identb, ones32=ones32, gsb=gsb, vq=vq)
            with contextlib.ExitStack() as st:
                _phase_A(nc, st, SB, PS, dr, P, dbgout)
            if "stopA" not in dbgout:
                with contextlib.ExitStack() as st:
                    _phase_B(nc, st, SB, PS, dr, gscr, oscr, P, dbgout)
        if "stopA" in dbgout or "stopB" in dbgout:
            return nc
        with contextlib.ExitStack() as rest:
            xres = SB(rest, "xres", [128, 8, NTOK], F32)
            P = dict(xres=xres, onesb=onesb, gsb=gsb, vq=vq, identf=identf, identb=identb)
            with contextlib.ExitStack() as st:
                _phase_C(nc, st, SB, PS, dr, oscr, P, dbgout)
            with contextlib.ExitStack() as st:
                _phase_ffn(nc, st, SB, PS, dr, 0, P, dbgout)
            with contextlib.ExitStack() as st:
                _phase_pool(nc, st, SB, PS, dr, P, dbgout)
            with contextlib.ExitStack() as st:
                _phase_ffn(nc, st, SB, PS, dr, 1, P, dbgout)
            with contextlib.ExitStack() as st:
                _phase_out(nc, st, SB, PS, dr, out, P, dbgout)
    return nc


def _phase_A(nc, st, SB, PS, dr, P, dbgout):
    KsT, KwT, Vs, Vw, kcT, RCv = P["KsT"], P["KwT"], P["Vs"], P["Vw"], P["kcT"], P["RCv"]
    identb, ones32, gsb = P["identb"], P["ones32"], P["gsb"]
    rawK = SB(st, "rawK", [128, T], BF16)
    rawV = SB(st, "rawV", [128, T], BF16)
    t1a = SB(st, "t1a", [128, 512], F32)
    t1b = SB(st, "t1b", [128, 512], F32)
    ist = contextlib.ExitStack()
    WA = SB(ist, "WA", [128, 8, 768], BF16)
    WAr = SB(ist, "WAr", [128, 8, 256], BF16)
    validk = SB(ist, "validk", [128, NSLOT], BF16)
    xt = [SB(ist, f"xt{i}", [128, D], F32) for i in range(4)]
    junk = SB(ist, "junk", [128, D], BF16)
    ssq = [SB(ist, f"ssq{i}", [128, 1], F32) for i in range(4)]
    rstd = [SB(ist, f"rstd{i}", [128, 1], F32) for i in range(4)]
    xn = [SB(ist, f"xn{i}", [128, D], BF16) for i in range(8)]
    hT = [SB(ist, f"hT{i}", [128, 8, 512], BF16) for i in range(2)]
    cs = [SB(ist, f"cs{i}", [128, 512], F32) for i in range(2)]
    sn = [SB(ist, f"sn{i}", [128, 512], F32) for i in range(2)]
    pst = contextlib.ExitStack()
    tp = [PS(pst, f"tp{i}", [128, D], BF16) for i in range(2)]
    pk = [PS(pst, f"pk{i}", [128, 512], F32) for i in range(4)]
    pv = PS(pst, "pv", [128, 512], F32)

    ph = Phase(nc, "A")
    A = ph.add
    A("pool", lambda e: e.dma_start(out=WA[:], in_=dr["wkv"].rearrange("(k p) n -> p k n", p=128)), w=["WA"], dma="w")
    A("sp", lambda e: e.dma_start(out=validk[:], in_=dr["validk"]), w=["validk"], dma="c")
    A("sp", lambda e: e.dma_start(out=P["biasC"][:], in_=dr["biasC"]), w=["biasC"], dma="c")
    for i, c0 in enumerate((256, 512)):
        for g in range(2):
            A("dve", lambda e, i=i, c0=c0, g=g: e.tensor_scalar(
                out=WAr[:, :, i * 128 + g * 64: i * 128 + g * 64 + 32], in0=WA[:, :, c0 + g * 64 + 32: c0 + g * 64 + 64],
                scalar1=-1.0, scalar2=None, op0=ALU.mult), r=["WA"], w=[("WAr", i, g, 0)])
            A("dve", lambda e, i=i, c0=c0, g=g: e.tensor_copy(
                out=WAr[:, :, i * 128 + g * 64 + 32: i * 128 + g * 64 + 64], in_=WA[:, :, c0 + g * 64: c0 + g * 64 + 32]),
              r=["WA"], w=[("WAr", i, g, 1)])
    WArk = [("WAr", i, g, h) for i in range(2) for g in range(2) for h in range(2)]
    A("pool", lambda e: e.memset(Vs[:], 0.0), w=["Vs0"])
    A("pool", lambda e: e.memset(Vw[:], 0.0), w=["Vw0"])
    A("pool", lambda e: e.memset(RCv[:], 0.0), w=["RCv"])
    A("dve", lambda e: e.tensor_copy(out=Vs[:, :, 64:65], in_=validk[:].unsqueeze(2)), r=["validk", "Vs0"], w=["Vs1"])
    A("dve", lambda e: e.tensor_copy(out=Vs[:, :, 66:67], in_=validk[:].unsqueeze(2)), r=["validk", "Vs0"], w=["Vs2"])
    A("dve", lambda e: e.tensor_copy(out=Vw[:, :, 64:65], in_=validk[:, 40:64].unsqueeze(2)), r=["validk", "Vw0"], w=["Vw1"])
    A("dve", lambda e: e.tensor_copy(out=Vw[:, :, 66:67], in_=validk[:, 40:64].unsqueeze(2)), r=["validk", "Vw0"], w=["Vw2"])

    def norm_stage(G):
        A("sp", lambda e: e.dma_start(out=cs[G % 2][:], in_=dr["cosK"][:, G * 512:(G + 1) * 512]), w=[("cs", G % 2)], dma=("cs", G % 2))
        A("sp", lambda e: e.dma_start(out=sn[G % 2][:], in_=dr["sinK"][:, G * 512:(G + 1) * 512]), w=[("sn", G % 2)], dma=("cs", G % 2))
        for si in range(4):
            s_ = 4 * G + si
            b3 = si
            bx = (G % 2) * 4 + si
            A("sp", lambda e, s_=s_, b3=b3: e.dma_start(out=xt[b3][:], in_=dr["xs"][s_ * 128:(s_ + 1) * 128, :]),
              w=[("xt", b3)], dma=("x", b3))
            A("act", lambda e, b3=b3: e.activation(out=junk[:], in_=xt[b3][:], func=AF.Square, accum_out=ssq[b3][:]),
              r=[("xt", b3)], w=["junk", ("ssq", b3)])
            A("act", lambda e, b3=b3: e.activation(out=rstd[b3][:], in_=ssq[b3][:], func=AF.Sqrt, bias=EPS, scale=1.0 / D),
              r=[("ssq", b3)], w=[("rstd", b3)])
            A("dve", lambda e, b3=b3: e.reciprocal(out=rstd[b3][:], in_=rstd[b3][:]), r=[("rstd", b3)], w=[("rstd", b3)])
            A("dve", lambda e, b3=b3, bx=bx: e.tensor_scalar(out=xn[bx][:], in0=xt[b3][:], scalar1=rstd[b3][:], scalar2=None,
                                                           op0=ALU.mult), r=[("xt", b3), ("rstd", b3)], w=[("xn", bx)])

    norm_stage(0)
    for G in range(16):
        hb = hT[G % 2]
        hk = ("hT", G % 2)
        if G + 1 < 16:
            norm_stage(G + 1)
        for si in range(4):
            s = 4 * G + si
            b2 = s % 2
            bx = (G % 2) * 4 + si
            for kc in range(8):
                A("pe", lambda e, kc=kc, b2=b2, bx=bx: e.transpose(out=tp[b2][:, kc * 128:(kc + 1) * 128],
                                                                 in_=xn[bx][:, kc * 128:(kc + 1) * 128], identity=identb[:]),
                  r=[("xn", bx)], w=[("tp", b2)])
            A("dve", lambda e, b2=b2, hb=hb, si=si: e.tensor_tensor(
                out=hb[:, :, si * 128:(si + 1) * 128], in0=tp[b2][:].rearrange("p (k t) -> p k t", k=8),
                in1=gsb[:, 0, :].unsqueeze(2).to_broadcast([128, 8, 128]), op=ALU.mult),
              r=[("tp", b2)], w=[hk + (si,)])
        hks = [hk + (si,) for si in range(4)]

        def proj(W, c0, bank, bk, wk, hb=hb, hks=hks):
            for kc in range(8):
                A("pe", lambda e, kc=kc, W=W, c0=c0, bank=bank, hb=hb: e.matmul(bank[:], lhsT=W[:, kc, c0:c0 + 128], rhs=hb[:, kc, :],
                                                                          start=(kc == 0), stop=(kc == 7)),
                  r=hks + wk, w=[bk])
        proj(WA, 0, pk[0], "pk0", ["WA"])
        proj(WA, 128, pk[1], "pk1", ["WA"])
        proj(WA, 256, pk[2], "pk2", ["WA"])
        proj(WAr, 0, pk[3], "pk3", WArk)
        A("act", lambda e, G=G: e.copy(out=rawK[:, G * 512:(G + 1) * 512], in_=pk[0][:]), r=["pk0"], w=[("rawK", G)])
        A("act", lambda e, G=G: e.copy(out=rawV[:, G * 512:(G + 1) * 512], in_=pk[1][:]), r=["pk1"], w=[("rawV", G)])

        def ropeevac(b0, b0k, b1, b1k, dst, dk, G=G):
            A("dve", lambda e: e.tensor_tensor(out=t1a[:], in0=b0[:], in1=cs[G % 2][:], op=ALU.mult),
              r=[b0k, ("cs", G % 2)], w=["t1a"])
            A("dve", lambda e: e.tensor_tensor(out=t1b[:], in0=b1[:], in1=sn[G % 2][:], op=ALU.mult),
              r=[b1k, ("sn", G % 2)], w=["t1b"])
            A("dve", lambda e: e.tensor_tensor(out=dst, in0=t1a[:], in1=t1b[:], op=ALU.add), r=["t1a", "t1b"], w=[dk])
        ropeevac(pk[2], "pk2", pk[3], "pk3", KsT[:, G * 512:(G + 1) * 512], ("KsT", G))
        if G >= 10:
            proj(WA, 512, pk[0], "pk0", ["WA"])
            proj(WAr, 128, pk[1], "pk1", WArk)
            ropeevac(pk[0], "pk0", pk[1], "pk1", KwT[:, (G - 10) * 512:(G - 9) * 512], ("KwT", G))
        for si in range(4):
            s = 4 * G + si
            nv = 2 if G >= 10 else 1
            for vi in range(nv):
                c0 = 384 if vi == 0 else 640
                for kc in range(8):
                    A("pe", lambda e, kc=kc, si=si, vi=vi, c0=c0, hb=hb: e.matmul(
                        pv[:, vi * 128:(vi + 1) * 128], lhsT=hb[:, kc, si * 128:(si + 1) * 128], rhs=WA[:, kc, c0:c0 + 128],
                        start=(kc == 0), stop=(kc == 7)), r=[hk + (si,), "WA"], w=["pv"])
            A("act", lambda e, s=s: e.copy(out=Vs[:, s, 0:64], in_=pv[:, 0:64]), r=["pv", "Vs0"], w=[("Vs", s, 0)])
            A("act", lambda e, s=s: e.copy(out=Vs[:, s, 130:194], in_=pv[:, 64:128]), r=["pv", "Vs0"], w=[("Vs", s, 1)])
            if G >= 10:
                A("act", lambda e, s=s: e.copy(out=Vw[:, s - 40, 0:64], in_=pv[:, 128:192]), r=["pv", "Vw0"], w=[("Vw", s, 0)])
                A("act", lambda e, s=s: e.copy(out=Vw[:, s - 40, 130:194], in_=pv[:, 192:256]), r=["pv", "Vw0"], w=[("Vw", s, 1)])
    if "KsT" in dbgout:
        A("sp", lambda e: e.dma_start(out=dbgout["KsT"], in_=KsT[:]), r=[("KsT", G) for G in range(16)], dma="dbg")
        A("sp", lambda e: e.dma_start(out=dbgout["Vs"], in_=Vs[:]), r=[("Vs", s, i) for s in range(64) for i in range(2)] + ["Vs1", "Vs2"], dma="dbg")
        A("sp", lambda e: e.dma_start(out=dbgout["KwT"], in_=KwT[:]), r=[("KwT", G) for G in range(10, 16)], dma="dbg")
    ph.emit()
    pst.close()
    ist.close()
    _phase_A2(nc, st, SB, PS, dr, P, dbgout, rawK, rawV, t1a, t1b)


def _phase_A2(nc, st0, SB, PS, dr, P, dbgout, rawK, rawV, t1a, t1b):
    kcT, RCv, ones32 = P["kcT"], P["RCv"], P["ones32"]
    with contextlib.ExitStack() as st:
        w1 = [SB(st, f"w1{x}", [128, 32, 256], BF16) for x in range(2)]
        w2 = [SB(st, f"w2{x}", [128, 2, 64], BF16) for x in range(2)]
        posT = [SB(st, f"posT{x}", [128, 32, 2], BF16) for x in range(2)]
        b1 = [SB(st, f"b1{x}", [128, 2], F32) for x in range(2)]
        b1e = [SB(st, f"b1e{x}", [128, 2], F32) for x in range(2)]
        b2row = [SB(st, f"b2row{x}", [1, 128], F32) for x in range(2)]
        b2rrow = SB(st, "b2rrow", [1, 128], F32)
        w2pad = [SB(st, f"w2pad{g}", [128, 2, 128], BF16) for g in range(2)]
        w2padr = [SB(st, f"w2padr{g}", [128, 2, 128], BF16) for g in range(2)]
        hid = [[SB(st, f"hid{x}{g}", [128, 2, 512], BF16) for g in range(2)] for x in range(2)]
        u = SB(st, "u", [128, 512], F32)
        u2 = SB(st, "u2", [128, 512], F32)
        th = SB(st, "th", [128, 512], F32)
        csC = SB(st, "csC", [128, 512], F32)
        snC = SB(st, "snC", [128, 512], F32)
        pb = PS(st, "pb", [128, 512], F32)
        ph_ = [PS(st, f"ph{i}", [128, 512], F32) for i in range(2)]
        po = [PS(st, f"po{i}", [128, 512], F32) for i in range(2)]
        ph = Phase(nc, "A2")
        A = ph.add
        NCMP = 511
        for x, nm in enumerate(("k", "v")):
            src = dr[f"c{nm}_w1"].rearrange("(p d) h -> d p h", d=64)
            A("pool", lambda e, x=x, src=src: e.dma_start(out=w1[x][0:64], in_=src), w=[("w1", x, 0)], dma="w")
            A("pool", lambda e, x=x, src=src: e.dma_start(out=w1[x][64:128], in_=src), w=[("w1", x, 1)], dma="w")
            A("pool", lambda e, x=x, nm=nm: e.dma_start(out=w2[x][:], in_=dr[f"c{nm}_w2"].rearrange("(k p) d -> p k d", p=128)),
              w=[("w2", x)], dma="w")
            A("pool", lambda e, x=x, nm=nm: e.dma_start(out=posT[x][:], in_=dr[f"c{nm}_posT"]), w=[("posT", x)], dma="w")
            A("sp", lambda e, x=x, nm=nm: e.dma_start(out=b1[x][:], in_=dr[f"c{nm}_b1"]), w=[("b1", x)], dma="c")
            A("sp", lambda e, x=x, nm=nm: e.dma_start(out=b2row[x][:], in_=dr[f"c{nm}_b2"]), w=[("b2row", x)], dma="c")
        A("sp", lambda e: e.dma_start(out=csC[:], in_=dr["cosC"]), w=["csC"], dma="c")
        A("sp", lambda e: e.dma_start(out=snC[:], in_=dr["sinC"]), w=["snC"], dma="c")
        for g in range(2):
            A("dve", lambda e, g=g: e.memset(w2pad[g][:], 0.0), w=[("w2pad", g)])
            A("dve", lambda e, g=g: e.memset(w2padr[g][:], 0.0), w=[("w2padr", g)])
            A("dve", lambda e, g=g: e.tensor_copy(out=w2pad[g][:, :, g * 64:(g + 1) * 64], in_=w2[0][:]),
              r=[("w2", 0)], w=[("w2pad", g)])
            A("dve", lambda e, g=g: e.tensor_scalar(out=w2padr[g][:, :, g * 64:g * 64 + 32], in0=w2[0][:, :, 32:64],
                                                     scalar1=-1.0, scalar2=None, op0=ALU.mult), r=[("w2", 0)], w=[("w2padr", g)])
            A("dve", lambda e, g=g: e.tensor_copy(out=w2padr[g][:, :, g * 64 + 32:g * 64 + 64], in_=w2[0][:, :, 0:32]),
              r=[("w2", 0)], w=[("w2padr", g)])
            A("dve", lambda e, g=g: e.tensor_scalar(out=b2rrow[0:1, g * 64:g * 64 + 32], in0=b2row[0][0:1, g * 64 + 32:g * 64 + 64],
                                                     scalar1=-1.0, scalar2=None, op0=ALU.mult), r=[("b2row", 0)], w=["b2rrow"])
            A("dve", lambda e, g=g: e.tensor_copy(out=b2rrow[0:1, g * 64 + 32:g * 64 + 64], in_=b2row[0][0:1, g * 64:g * 64 + 32]),
              r=[("b2row", 0)], w=["b2rrow"])
        for x in range(2):
            raw = rawK if x == 0 else rawV
            for half in range(2):
                for p in range(32):
                    A("pe", lambda e, x=x, half=half, p=p: e.matmul(
                        pb[:, half * 2:half * 2 + 2], lhsT=w1[x][0:64, p, half * 128:(half + 1) * 128], rhs=posT[x][0:64, p, :],
                        start=(p == 0), stop=(p == 31)), r=[("w1", x, 0), ("posT", x)], w=["pb"])
            A("dve", lambda e, x=x: e.tensor_tensor(out=b1e[x][:], in0=pb[:, 0:4:2], in1=b1[x][:], op=ALU.add),
              r=["pb", ("b1", x)], w=[("b1e", x)])
            for half in range(2):
                for p in range(32):
                    for g in range(2):
                        A("pe", lambda e, x=x, g=g, half=half, p=p, raw=raw: e.matmul(
                            ph_[g][:, 0:NCMP], lhsT=w1[x][64 * g:64 * g + 64, p, half * 128:(half + 1) * 128],
                            rhs=raw[64 * g:64 * g + 64, p:p + 16 * (NCMP - 1) + 1:16],
                            start=(p == 0), stop=(p == 31)), r=[("w1", x, g)], w=[("ph", g)])
                for g in range(2):
                    bank = ph_[g]
                    A("act", lambda e, x=x, half=half, bank=bank: e.activation(out=u[:, 0:NCMP], in_=bank[:, 0:NCMP], func=AF.Identity,
                                                                               bias=b1e[x][:, half:half + 1], scale=1.0),
                      r=[("ph", g), ("b1e", x)], w=["u"])
                    A("dve", lambda e: e.tensor_tensor(out=u2[:, 0:NCMP], in0=u[:, 0:NCMP], in1=u[:, 0:NCMP], op=ALU.mult), r=["u"], w=["u2"])
                    A("dve", lambda e: e.tensor_scalar(out=u2[:, 0:NCMP], in0=u2[:, 0:NCMP], scalar1=0.044715, scalar2=1.0,
                                                        op0=ALU.mult, op1=ALU.add), r=["u2"], w=["u2"])
                    A("dve", lambda e: e.tensor_tensor(out=u2[:, 0:NCMP], in0=u2[:, 0:NCMP], in1=u[:, 0:NCMP], op=ALU.mult), r=["u2", "u"], w=["u2"])
                    A("act", lambda e: e.activation(out=th[:, 0:NCMP], in_=u2[:, 0:NCMP], func=AF.Tanh, scale=0.7978845608028654),
                      r=["u2"], w=["th"])
                    A("dve", lambda e: e.tensor_scalar(out=th[:, 0:NCMP], in0=th[:, 0:NCMP], scalar1=0.5, scalar2=0.5,
                                                        op0=ALU.mult, op1=ALU.add), r=["th"], w=["th"])
                    A("dve", lambda e, x=x, g=g, half=half: e.tensor_tensor(out=hid[x][g][:, half, 0:NCMP], in0=th[:, 0:NCMP],
                                                                             in1=u[:, 0:NCMP], op=ALU.mult),
                      r=["th", "u"], w=[("hid", x, g, half)])
        for r_, (pads, brow, bank, bk) in enumerate(((w2pad, b2row[0], po[0], "po0"), (w2padr, b2rrow, po[1], "po1"))):
            first = True
            for g in range(2):
                for half in range(2):
                    A("pe", lambda e, g=g, half=half, pads=pads, bank=bank, first=first: e.matmul(
                        bank[:, 0:NCMP], lhsT=pads[g][:, half, :], rhs=hid[0][g][:, half, 0:NCMP], start=first, stop=False),
                      r=[("hid", 0, g, half), ("w2pad", g), ("w2padr", g)], w=[bk])
                    first = False
            A("pe", lambda e, brow=brow, bank=bank: e.matmul(bank[:, 0:NCMP], lhsT=brow[0:1, 0:128], rhs=ones32[0:1, 0:NCMP],
                                                              start=False, stop=True), r=[("b2row", 0), "b2rrow"], w=[bk])
        A("dve", lambda e: e.tensor_tensor(out=t1a[:, 0:NCMP], in0=po[0][:, 0:NCMP], in1=csC[:, 0:NCMP], op=ALU.mult),
          r=["po0", "csC"], w=["t1a"])
        A("dve", lambda e: e.tensor_tensor(out=t1b[:, 0:NCMP], in0=po[1][:, 0:NCMP], in1=snC[:, 0:NCMP], op=ALU.mult),
          r=["po1", "snC"], w=["t1b"])
        A("dve", lambda e: e.memset(kcT[:], 0.0), w=["kcT"])
        A("dve", lambda e: e.tensor_tensor(out=kcT[:, 0:NCMP], in0=t1a[:, 0:NCMP], in1=t1b[:, 0:NCMP], op=ALU.add),
          r=["t1a", "t1b"], w=["kcT"])
        for c in range(4):
            n = 128 if c < 3 else NCMP - 384
            bank = po[c % 2]
            bk = f"po{c % 2}"
            for g in range(2):
                for half in range(2):
                    A("pe", lambda e, c=c, n=n, g=g, half=half, bank=bank: e.matmul(
                        bank[0:n, g * 64:(g + 1) * 64], lhsT=hid[1][g][:, half, c * 128:c * 128 + n], rhs=w2[1][:, half, :],
                        start=(half == 0), stop=False), r=[("hid", 1, g, half), ("w2", 1)], w=[bk])
                A("pe", lambda e, n=n, g=g, bank=bank: e.matmul(
                    bank[0:n, g * 64:(g + 1) * 64], lhsT=ones32[0:1, 0:n], rhs=b2row[1][0:1, g * 64:(g + 1) * 64],
                    start=False, stop=True), r=[("b2row", 1)], w=[bk])
            A("act", lambda e, c=c, n=n, bank=bank: e.copy(out=RCv[0:n, c, 0:64], in_=bank[0:n, 0:64]), r=[bk, "RCv"], w=[("RCv", c, 0)])
            A("act", lambda e, c=c, n=n, bank=bank: e.copy(out=RCv[0:n, c, 130:194], in_=bank[0:n, 64:128]), r=[bk, "RCv"], w=[("RCv", c, 1)])
            A("dve", lambda e, c=c, n=n: e.memset(RCv[0:n, c, 64:65], 1.0), r=["RCv"], w=[("RCv", c, 2)])
            A("dve", lambda e, c=c, n=n: e.memset(RCv[0:n, c, 66:67], 1.0), r=["RCv"], w=[("RCv", c, 3)])
        if "kcT" in dbgout:
            A("sp", lambda e: e.dma_start(out=dbgout["kcT"], in_=kcT[:]), r=["kcT"], dma="dbg")
            A("sp", lambda e: e.dma_start(out=dbgout["RCv"], in_=RCv[:]), r=[("RCv", c, i) for c in range(4) for i in range(4)], dma="dbg")
        ph.emit()


def _phase_B(nc, st, SB, PS, dr, gscr, oscr, P, dbgout):
    KsT, KwT, Vs, Vw, kcT, RCv, biasC = P["KsT"], P["KwT"], P["Vs"], P["Vw"], P["kcT"], P["RCv"], P["biasC"]
    ones32, gsb, vq = P["ones32"], P["gsb"], P["vq"]
    identf = SB(st, "identfB", [128, 128], F32)
    Wq = SB(st, "Wq", [128, 8, 1024], BF16)
    Wqr = SB(st, "Wqr", [128, 8, 1024], BF16)
    Wg = SB(st, "Wg", [128, 8, 48], BF16)
    EF = SB(st, "EF", [128, 16, 128], BF16)
    ovl = SB(st, "ovl", [128, 4, 129], BF16)
    xq1 = SB(st, "xq0", [128, D], F32)
    xq = [xq1, xq1]
    xnf = SB(st, "xnf", [128, D], F32)
    ssq = SB(st, "ssqB", [128, 1], F32)
    rstd = SB(st, "rstdB", [128, 1], F32)
    hTq = SB(st, "hTq", [128, 8, 128], BF16)
    cq1 = SB(st, "cq0", [128, 128], F32)
    sq1 = SB(st, "sq0", [128, 128], F32)
    cq = [cq1, cq1]
    sq = [sq1, sq1]
    bon3 = [SB(st, f"bon{i}", [128, 128], F32) for i in range(3)]
    mkb3 = [SB(st, f"mkb{i}", [128, 6, 128], BF16) for i in range(3)]
    QT = [SB(st, f"QT{i}", [128, 8, 128], BF16) for i in range(2)]
    gsig = [SB(st, f"gsig{i}", [48, 128], F32) for i in range(2)]
    selT = [[SB(st, f"selT{i}{g}", [128, 128], BF16) for g in range(2)] for i in range(2)]
    acc = [SB(st, f"acc{i}", [128, 8, 128], F32) for i in range(2)]
    Ec = [SB(st, f"Ec{c}", [128, 8, 128], BF16) for c in range(4)]
    NBUF = 5
    E = [SB(st, f"E{i}", [128, 8, 128], BF16) for i in range(NBUF)]
    Pb = [SB(st, f"Pb{i}", [128, 8, 128], BF16) for i in range(NBUF)]
    oasb = [SB(st, f"oasb{i}", [128, 1024], F32) for i in range(2)]
    msk = [SB(st, f"msk{i}", [128, 128], BF16) for i in range(NBUF)]
    t1 = SB(st, "t1B", [128, 8, 128], F32)
    t2 = SB(st, "t2B", [128, 8, 128], F32)
    dsb = SB(st, "dsb", [65, 1024], F32)
    rdb = SB(st, "rdb", [128, 4, 128], F32)
    tmpf = SB(st, "tmpf", [128, 4, 128], F32)
    grow = [SB(st, f"grow{i}", [65, 1024], F32) for i in range(2)]
    dsb2 = SB(st, "dsb2", [65, 1024], F32)
    cbc = SB(st, "cbc", [128, 1024], F32)
    crow = nc.dram_tensor("crow", [8, 1024], F32).ap()
    score = SB(st, "score", [128, 128], F32)
    work = SB(st, "work", [128, 128], F32)
    selq = SB(st, "selq", [128, 128], F32)
    m8a = SB(st, "m8a", [128, 8], F32)
    m8b = SB(st, "m8b", [128, 8], F32)
    thr = SB(st, "thr", [128, 1], F32)
    rc = SB(st, "rc", [128, 8], F32)
    obf = SB(st, "obf", [128, 8, 128], BF16)
    scA = PS(st, "scA", [128, 1024], F32)
    scB = PS(st, "scB", [128, 1024], F32)
    oa = PS(st, "oa", [128, 1024], F32)
    mx = PS(st, "mx", [128, 512], F32)
    msc = PS(st, "msc", [128, 512], F32)
    gs2 = gscr.rearrange("b r q -> b (r q)")
    print("[phase B] sbuf bytes remaining:", nc.sbuf_bytes_remaining)

    ph = Phase(nc, "B")
    A = ph.add
    A("pool", lambda e: e.dma_start(out=Wq[:], in_=dr["wq"].rearrange("(k p) n -> p k n", p=128)), w=["Wq"], dma="w")
    A("pool", lambda e: e.dma_start(out=Wg[:], in_=dr["wg"].rearrange("(k p) n -> p k n", p=128)), w=["Wg"], dma="w")
    A("sp", lambda e: e.dma_start(out=EF[:], in_=dr["ef"]), w=["EF"], dma="c")
    A("sp", lambda e: e.dma_start(out=ovl[:], in_=dr["ovl"]), w=["ovl"], dma="c")
    A("sp", lambda e: e.dma_start(out=identf[:], in_=dr["identf"]), w=["identf"], dma="c")
    for kc in range(8):
        v = Wq[:, kc, :].rearrange("p (h two d) -> p h two d", two=2, d=32)
        vr = Wqr[:, kc, :].rearrange("p (h two d) -> p h two d", two=2, d=32)
        A("dve", lambda e, v=v, vr=vr: e.tensor_scalar(out=vr[:, :, 0, :], in0=v[:, :, 1, :], scalar1=-1.0, scalar2=None, op0=ALU.mult),
          r=["Wq"], w=[("Wqr", kc, 0)])
        A("dve", lambda e, v=v, vr=vr: e.tensor_copy(out=vr[:, :, 1, :], in_=v[:, :, 0, :]), r=["Wq"], w=[("Wqr", kc, 1)])
    Wqrk = [("Wqr", kc, i) for kc in range(8) for i in range(2)]

    def v8(t, nq):
        return t[:].rearrange("p (h q) -> p h q", q=128)[:, :, 0:nq]

    def v4(t, u, nq, p0=0, p1=128):
        return t[p0:p1, u * 512:(u + 1) * 512].rearrange("p (h q) -> p h q", q=128)[:, :, 0:nq]

    def bc(ap2, n, nq):
        return ap2.unsqueeze(1).to_broadcast([ap2.shape[0], n, nq])

    def stage_load(bi):
        S, off, nq, col0 = BLK[bi]
        t0 = 128 * S + off
        b3 = bi % 3
        A("sp", lambda e: e.dma_start(out=xq1[0:nq, :], in_=dr["xs"][t0:t0 + nq, :]), w=["xqb"], dma="ldq")
        A("sp", lambda e: e.dma_start(out=cq1[:, 0:nq], in_=dr["cosQ"][:, bi * 128:bi * 128 + nq]), w=["cqb"], dma="ldq")
        A("sp", lambda e: e.dma_start(out=sq1[:, 0:nq], in_=dr["sinQ"][:, bi * 128:bi * 128 + nq]), w=["sqb"], dma="ldq")
        A("sp", lambda e: e.dma_start(out=bon3[b3][0:nq, :], in_=dr["bonus"][0:nq, bi, :]), w=[("bon", b3)], dma=("ldm", b3))
        A("sp", lambda e: e.dma_start(out=mkb3[b3][:], in_=dr["mk"][bi]), w=[("mkb", b3)], dma=("ldm", b3))

    def stage_q(bi):
        S, off, nq, col0 = BLK[bi]
        pb = bi % 2
        A("act", lambda e: e.activation(out=xnf[0:nq, :], in_=xq[pb][0:nq, :], func=AF.Square, accum_out=ssq[0:nq, :]),
          r=["xqb"], w=["xnf", "ssq"])
        A("act", lambda e: e.activation(out=rstd[0:nq, :], in_=ssq[0:nq, :], func=AF.Sqrt, bias=EPS, scale=1.0 / D), r=["ssq"], w=["rstd"])
        A("dve", lambda e: e.reciprocal(out=rstd[0:nq, :], in_=rstd[0:nq, :]), r=["rstd"], w=["rstd"])
        A("dve", lambda e: e.tensor_scalar(out=xnf[0:nq, :], in0=xq[pb][0:nq, :], scalar1=rstd[0:nq, :], scalar2=None, op0=ALU.mult),
          r=["xqb", "rstd"], w=["xnf"])
        for kc in range(8):
            A("pe", lambda e, kc=kc: e.transpose(out=scA[:, kc * 128:kc * 128 + nq], in_=xnf[0:nq, kc * 128:(kc + 1) * 128],
                                                 identity=identf[0:nq, 0:nq]), r=["xnf", "identf"], w=[("scA", kc // 4)])
        A("dve", lambda e: e.tensor_tensor(out=hTq[:, :, 0:nq], in0=v8(scA, nq), in1=gsb[:, 0, :].unsqueeze(2).to_broadcast([128, 8, nq]),
                                           op=ALU.mult), r=[("scA", 0), ("scA", 1)], w=["hTq"])
        for hl in range(8):
            for kc in range(8):
                A("pe", lambda e, hl=hl, kc=kc: e.matmul(scB[:, hl * 128:hl * 128 + nq], lhsT=Wq[:, kc, hl * 128:(hl + 1) * 128],
                                                         rhs=hTq[:, kc, 0:nq], start=(kc == 0), stop=(kc == 7)),
                  r=["Wq", "hTq"], w=[("scB", hl // 4)])
        for hl in range(8):
            for kc in range(8):
                A("pe", lambda e, hl=hl, kc=kc: e.matmul(oa[:, hl * 128:hl * 128 + nq], lhsT=Wqr[:, kc, hl * 128:(hl + 1) * 128],
                                                         rhs=hTq[:, kc, 0:nq], start=(kc == 0), stop=(kc == 7)),
                  r=Wqrk + ["hTq"], w=[("oa", hl // 4)])
        A("dve", lambda e: e.tensor_tensor(out=t1[:, :, 0:nq], in0=v8(scB, nq), in1=bc(cq[pb][:, 0:nq], 8, nq), op=ALU.mult),
          r=[("scB", 0), ("scB", 1), "cqb"], w=[("t1", 0), ("t1", 3), ("t1", 6)])
        A("dve", lambda e: e.tensor_tensor(out=t2[:, :, 0:nq], in0=v8(oa, nq), in1=bc(sq[pb][:, 0:nq], 8, nq), op=ALU.mult),
          r=[("oa", 0), ("oa", 1), "sqb"], w=["t2"])
        A("dve", lambda e: e.tensor_tensor(out=QT[pb][:, :, 0:nq], in0=t1[:, :, 0:nq], in1=t2[:, :, 0:nq], op=ALU.add),
          r=[("t1", 0), ("t1", 3), ("t1", 6), "t2"], w=[("QT", pb)])
        for kc in range(8):
            A("pe", lambda e, kc=kc: e.matmul(mx[0:48, 0:nq], lhsT=Wg[:, kc, :], rhs=hTq[:, kc, 0:nq], start=(kc == 0), stop=(kc == 7)),
              r=["Wg", "hTq"], w=MXALL)
        A("act", lambda e: e.activation(out=gsig[pb][:, 0:nq], in_=mx[0:48, 0:nq], func=AF.Sigmoid), r=MXALL, w=[("gsig", pb)])
        A("sp", lambda e: e.dma_start(out=gscr[bi, :, 0:nq], in_=gsig[pb][:, 0:nq]), r=[("gsig", pb)], w=[("gscr", bi)], dma=("gs", pb))

    growi = [0]
    grpi = [0]
    MXALL = ["mx"]

    DEFER = True
    pending = []
    stepc = [0]

    def flush(force=False):
        while pending and (force or pending[0][0] <= stepc[0]):
            pending.pop(0)[1]()

    fini = [0]

    def finalize(bi, g, br, first, src, srck, delay):
        S, off, nq, col0 = BLK[bi]
        pb = bi % 2
        p0 = 64 * g
        dp = 64 if g == 0 else 0
        flush(force=True)
        fi = fini[0]
        fini[0] += 1
        X, xk = (dsb, "dsb") if fi % 2 == 0 else (dsb2, "dsb2")
        ri = fi % 8
        gi = growi[0] % 2
        growi[0] += 1
        r0 = br * 16 + g * 8
        A("sp", lambda e: e.dma_start(out=grow[gi][dp:dp + 1, :], in_=gs2[bi:bi + 1, r0 * 128:r0 * 128 + 1024]),
          r=[("gscr", bi)], w=[("grow", gi)], dma=("gr", gi))
        A("dve", lambda e: e.tensor_scalar(out=X[dp:dp + 1, :], in0=src[dp:dp + 1, :], scalar1=1.0e-30, scalar2=None, op0=ALU.max),
          r=[srck(0), srck(1)], w=[xk])
        A("act", lambda e: e.activation(out=X[dp:dp + 1, :], in_=X[dp:dp + 1, :], func=AF.Ln), r=[xk], w=[xk])
        A("act", lambda e: e.activation(out=X[dp:dp + 1, :], in_=X[dp:dp + 1, :], func=AF.Exp, scale=-1.0), r=[xk], w=[xk])
        A("dve", lambda e: e.tensor_tensor(out=X[dp:dp + 1, :], in0=X[dp:dp + 1, :], in1=grow[gi][dp:dp + 1, :], op=ALU.mult),
          r=[xk, ("grow", gi)], w=[xk])
        A("sp", lambda e: e.dma_start(out=crow[ri:ri + 1, :], in_=X[dp:dp + 1, :]), r=[xk], w=[("crow", ri)], dma=("cr", ri % 2))
        A("sp", lambda e: e.dma_start(out=cbc[p0:p0 + 64, :], in_=crow[ri:ri + 1, :].partition_broadcast(64)),
          r=[("crow", ri)], w=[("cbc", g)], dma=("cb", g))
        def tail():
            for u in range(2):
                dst = acc[pb][p0:p0 + 64, 4 * u:4 * u + 4, 0:nq]
                if first:
                    A("dve", lambda e, u=u, dst=dst: e.tensor_tensor(out=dst, in0=v4(src, u, nq, p0, p0 + 64), in1=v4(cbc, u, nq, p0, p0 + 64),
                                                                     op=ALU.mult), r=[srck(u), ("cbc", g)], w=[("acc", pb, g, u)])
                else:
                    A("dve", lambda e, u=u: e.tensor_tensor(out=tmpf[p0:p0 + 64, :, 0:nq], in0=v4(src, u, nq, p0, p0 + 64),
                                                            in1=v4(cbc, u, nq, p0, p0 + 64), op=ALU.mult), r=[srck(u), ("cbc", g)], w=["tmpf"])
                    A("dve", lambda e, dst=dst: e.tensor_tensor(out=dst, in0=dst, in1=tmpf[p0:p0 + 64, :, 0:nq], op=ALU.add),
                      r=["tmpf", ("acc", pb, g, u)], w=[("acc", pb, g, u)])
        if DEFER:
            pending.append((stepc[0] + delay, tail))
        else:
            tail()

    def vaug(Vt, idx, g):
        return Vt[:, idx, 0:128] if g == 0 else Vt[:, idx, 66:194]

    def stage_cmp(bi):
        fins = [stage_cmp_g(bi, g) for g in range(2)]
        for g, (ob, obk) in enumerate(fins):
            finalize(bi, g, 0, True, ob, lambda u, obk=obk: obk + (u,), 4)

    def stage_cmp_g(bi, g):
        S, off, nq, col0 = BLK[bi]
        pb = bi % 2
        M = [128, 128]
        if True:
            for c in range(4):
                sc, sk = (scA, "scA") if c % 2 == 0 else (scB, "scB")
                for u in range(2):
                    A("pe", lambda e, c=c, u=u, sc=sc: e.matmul(v4(sc, u, nq), lhsT=kcT[64 * g:64 * g + 64, c * 128:(c + 1) * 128],
                                                              rhs=QT[pb][64 * g:64 * g + 64, 4 * u:4 * u + 4, 0:nq], start=True, stop=True),
                      r=["kcT", ("QT", pb)], w=[(sk, u)])
                A("act", lambda e, c=c, sc=sc: e.activation(out=Ec[c][:, :, 0:nq], in_=v8(sc, nq), func=AF.Exp, bias=biasC[:, c:c + 1],
                                                            scale=SCALE), r=[(sk, 0), (sk, 1), "biasC"], w=[("Ec", c)])
                A("dve", lambda e, c=c: e.tensor_tensor(out=Ec[c][:, :, 0:nq], in0=Ec[c][:, :, 0:nq],
                                                        in1=bc(mkb3[bi % 3][:, 2 + c, 0:nq], 8, nq), op=ALU.mult),
                  r=[("Ec", c), ("mkb", bi % 3)], w=[("Ec", c)])
            for u in range(2):
                for c in range(4):
                    A("pe", lambda e, c=c, u=u: e.matmul(v4(oa, u, nq, 0, M[g]), lhsT=vaug(RCv, c, g), rhs=Ec[c][:, 4 * u:4 * u + 4, 0:nq],
                                                         start=(c == 0), stop=(c == 3)), r=[("Ec", c), "RCv"], w=[("oa", u)])
            regs = []
            for hl in range(8):
                j, o = hl // 3, (hl % 3) * 129
                tt, tk = [(scA, ("scA", 0)), (scA, ("scA", 1)), (scB, ("scB", 0))][j]
                base = 512 if j == 1 else 0
                regs.append((tt, tk, base + o))
                for c in range(4):
                    A("pe", lambda e, hl=hl, c=c, tt=tt, base=base, o=o: e.matmul(
                        tt[0:nq, base + o:base + o + 129], lhsT=Ec[c][:, hl, 0:nq], rhs=ovl[:, c, :], start=(c == 0), stop=(c == 3)),
                      r=[("Ec", c), "ovl"], w=[tk])
            banks = [(scA, ("scA", 0), 0, 3, 0), (scA, ("scA", 1), 512, 3, 3), (scB, ("scB", 0), 0, 2, 6)]
            for tt, tk, base, nh, h0 in banks:
                A("dve", lambda e, tt=tt, base=base, nh=nh, h0=h0: e.tensor_scalar(
                    out=rc[0:nq, h0:h0 + nh], in0=tt[0:nq, base + 128:base + 128 + 129 * (nh - 1) + 1:129], scalar1=1.0e-30,
                    scalar2=None, op0=ALU.max), r=[tk], w=[("rc", h0)])
            A("dve", lambda e: e.reciprocal(out=rc[0:nq, :], in_=rc[0:nq, :]), r=[("rc", 0), ("rc", 3), ("rc", 6)], w=["rc"])
            for tt, tk, base, nh, h0 in banks:
                A("dve", lambda e, tt=tt, base=base, nh=nh, h0=h0: e.tensor_tensor(
                    out=t1[0:nq, h0:h0 + nh, :], in0=tt[0:nq, base:base + 129 * nh].rearrange("p (h c) -> p h c", c=129)[:, :, 0:128],
                    in1=rc[0:nq, h0:h0 + nh].unsqueeze(2).to_broadcast([nq, nh, 128]), op=ALU.mult), r=[tk, "rc"], w=[("t1", h0)])
            A("dve", lambda e: e.tensor_reduce(out=score[0:nq, :], in_=t1[0:nq, :, :].rearrange("p h s -> p s h"), axis=AX.X, op=ALU.add),
              r=[("t1", 0), ("t1", 3), ("t1", 6)], w=["score"])
            A("dve", lambda e: e.tensor_tensor(out=score[0:nq, :], in0=score[0:nq, :], in1=bon3[bi % 3][0:nq, :], op=ALU.add),
              r=["score", ("bon", bi % 3)], w=["score"])
            A("dve", lambda e: e.max(out=m8a[0:nq, :], in_=score[0:nq, :]), r=["score"], w=["m8a"])
            A("dve", lambda e: e.match_replace(out=work[0:nq, :], in_to_replace=m8a[0:nq, :], in_values=score[0:nq, :], imm_value=-3.0e38),
              r=["score", "m8a"], w=["work"])
            A("dve", lambda e: e.max(out=m8b[0:nq, :], in_=work[0:nq, :]), r=["work"], w=["m8b"])
            A("dve", lambda e: e.tensor_scalar(out=thr[0:nq, :], in0=m8b[0:nq, 7:8], scalar1=-1.0e29, scalar2=None, op0=ALU.max),
              r=["m8b"], w=["thr"])
            A("dve", lambda e: e.tensor_scalar(out=selq[0:nq, :], in0=score[0:nq, :], scalar1=thr[0:nq, :], scalar2=None, op0=ALU.is_ge),
              r=["score", "thr"], w=["selq"])
            A("pe", lambda e: e.transpose(out=mx[:, 0:nq], in_=selq[0:nq, :], identity=identf[0:nq, 0:nq]), r=["selq", "identf"], w=MXALL)
            A("act", lambda e, g=g: e.copy(out=selT[pb][g][:, 0:nq], in_=mx[:, 0:nq]), r=MXALL, w=[("selT", pb, g)])
            flush(force=True)
            ob = oasb[grpi[0] % 2]
            obk = ("oasb", grpi[0] % 2)
            grpi[0] += 1
            for u in range(2):
                A("act", lambda e, u=u, ob=ob: e.copy(out=ob[:, u * 512:(u + 1) * 512], in_=oa[:, u * 512:(u + 1) * 512]),
                  r=[("oa", u)], w=[obk + (u,)])
            return ob, obk

    LAG = 4
    FILL = 1
    WARM = 0
    XFILL = 16

    def stage_attn(bi):
        S, off, nq, col0 = BLK[bi]
        pb = bi % 2
        M = [128, 128]
        items = []
        gidx = []
        for g in range(2):
            for br in (1, 2):
                kts = list(range(0, S + 1)) if br == 1 else list(range(S - 4, S + 1))
                for idx, kt in enumerate(kts):
                    items.append((g, br, kt, idx == 0, idx == len(kts) - 1))
                    gidx.append(idx)
        N = len(items)
        srcs = [None] * N
        mids = [None] * N

        def front(i):
            g, br, kt, isfirst, islast = items[i]
            KT, Vt, koff = (KsT, Vs, 0) if br == 1 else (KwT, Vw, 40)
            sc, sk = (scA, "scA") if i % 2 == 0 else (scB, "scB")
            Eb, ek = E[i % NBUF], ("E", i % NBUF)
            Pq, pk_ = Pb[i % NBUF], ("Pb", i % NBUF)
            mb, mbk = msk[i % NBUF], ("msk", i % NBUF)
            masked = True
            if br == 1:
                a, v = kt // 16, kt % 16
                kw = dict(tile_position=(96, 0)) if a == 3 else {}
                A("pe", lambda e: e.matmul(mx[:, 0:nq], lhsT=EF[32 * a:32 * a + 32, v, :], rhs=selT[pb][g][32 * a:32 * a + 32, 0:nq],
                                           start=True, stop=True, **kw), r=["EF", ("selT", pb, g)], w=["mx"])
                if kt == S:
                    A("dve", lambda e: e.tensor_tensor(out=mb[:, 0:nq], in0=mx[:, 0:nq], in1=mkb3[bi % 3][:, 0, 0:nq], op=ALU.mult),
                      r=["mx", ("mkb", bi % 3)], w=[mbk])
                else:
                    A("dve", lambda e: e.tensor_copy(out=mb[:, 0:nq], in_=mx[:, 0:nq]), r=["mx"], w=[mbk])
                mask_ap, mask_r = mb[:, 0:nq], [mbk]
            else:
                if kt == S - 4:
                    mask_ap, mask_r = mkb3[bi % 3][:, 1, 0:nq], [("mkb", bi % 3)]
                elif kt == S:
                    mask_ap, mask_r = mkb3[bi % 3][:, 0, 0:nq], [("mkb", bi % 3)]
                else:
                    masked = False
            for u in range(2):
                A("pe", lambda e, u=u: e.matmul(v4(sc, u, nq), lhsT=KT[64 * g:64 * g + 64, (kt - koff) * 128:(kt - koff + 1) * 128],
                                                rhs=QT[pb][64 * g:64 * g + 64, 4 * u:4 * u + 4, 0:nq], start=True, stop=True),
                  r=[("QT", pb)], w=[(sk, u)])
            A("act", lambda e: e.activation(out=Eb[:, :, 0:nq], in_=v8(sc, nq), func=AF.Exp, scale=SCALE), r=[(sk, 0), (sk, 1)], w=[ek])
            for _ in range(FILL + (1 if gidx[i] < XFILL else 0)):
                A("pe", lambda e: e.matmul(msc[:], lhsT=Wq[:, 0, 0:128], rhs=Wq[:, 1, 0:512], start=True, stop=True), r=["Wq"], w=["msc"])
            if masked:
                mids[i] = (Eb, ek, Pq, pk_, mask_ap, mask_r)
                srcs[i] = (Pq, pk_)
            else:
                srcs[i] = (Eb, ek)

        def mid(i):
            if mids[i] is None:
                return
            Eb, ek, Pq, pk_, mask_ap, mask_r = mids[i]
            A("dve", lambda e: e.tensor_tensor(out=Pq[:, :, 0:nq], in0=Eb[:, :, 0:nq], in1=bc(mask_ap, 8, nq), op=ALU.mult),
              r=[ek] + mask_r, w=[pk_])

        def back(i):
            g, br, kt, isfirst, islast = items[i]
            KT, Vt, koff = (KsT, Vs, 0) if br == 1 else (KwT, Vw, 40)
            src, srck = srcs[i]
            for u in range(2):
                A("pe", lambda e, u=u: e.matmul(v4(oa, u, nq, 0, M[g]), lhsT=vaug(Vt, kt - koff, g), rhs=src[:, 4 * u:4 * u + 4, 0:nq],
                                                start=isfirst, stop=islast), r=[srck], w=[("oa", u)])
            if islast:
                flush(force=True)
                ob = oasb[grpi[0] % 2]
                obk = ("oasb", grpi[0] % 2)
                grpi[0] += 1
                for u in range(2):
                    A("act", lambda e, u=u: e.copy(out=ob[:, u * 512:(u + 1) * 512], in_=oa[:, u * 512:(u + 1) * 512]),
                      r=[("oa", u)], w=[obk + (u,)])
                finalize(bi, g, br, False, ob, lambda u: obk + (u,), 4)

        for _ in range(WARM):
            A("pe", lambda e: e.matmul(msc[:], lhsT=Wq[:, 0, 0:128], rhs=Wq[:, 1, 0:512], start=True, stop=True), r=["Wq"], w=["msc"])
        for i in range(N + LAG):
            stepc[0] += 1
            flush()
            if i < N:
                front(i)
            if 0 <= i - 1 < N:
                mid(i - 1)
            if i - LAG >= 0:
                back(i - LAG)
        if DEFER:
            pending.append((stepc[0] + 4, lambda: stage_store(bi)))
        else:
            stage_store(bi)

    def stage_store(bi):
        S, off, nq, col0 = BLK[bi]
        pb = bi % 2
        rk = [("acc", pb, g, u) for g in range(2) for u in range(2)]
        if bi == 0:
            A("dve", lambda e: e.tensor_scalar(out=obf[:, :, 0:nq], in0=acc[pb][:, :, 0:nq], scalar1=vq[:, 0:1], scalar2=None, op0=ALU.mult),
              r=rk + ["vq"], w=["obf"])
        else:
            A("dve", lambda e: e.tensor_copy(out=obf[:, :, 0:nq], in_=acc[pb][:, :, 0:nq]), r=rk, w=["obf"])
        A("sp", lambda e: e.dma_start(out=oscr[:, :, col0:col0 + nq], in_=obf[:, :, 0:nq]), r=["obf"], w=["oscr"], dma="os")
        if "selT" in dbgout and bi == dbgout["_blk"]:
            for g in range(2):
                A("sp", lambda e, g=g: e.dma_start(out=dbgout["selT"][g], in_=selT[pb][g][:]), r=[("selT", pb, g)], dma="dbg")
            A("sp", lambda e: e.dma_start(out=dbgout["QT"], in_=QT[pb][:]), r=[("QT", pb)], dma="dbg")
            A("sp", lambda e: e.dma_start(out=dbgout["acc"], in_=acc[pb][:]), r=rk, dma="dbg")

    nb = dbgout.get("_nblk", NBLK)
    stage_load(0)
    stage_q(0)
    if nb > 1:
        stage_load(1)
    stage_cmp(0)
    for bi in range(nb):
        if bi + 1 < nb:
            stage_q(bi + 1)
            if bi + 2 < nb:
                stage_load(bi + 2)
            stage_cmp(bi + 1)
        stage_attn(bi)
    flush(force=True)
    ph.emit()


def _norm_group(A, xres, c0, n, gcol, onesb, sqb, pn, rs, dst, dkey, tag):
    A("act", lambda e: e.activation(out=sqb[:, :, 0:n], in_=xres[:, :, c0:c0 + n], func=AF.Square), r=["xres"], w=["sqb"])
    for kc in range(8):
        A("pe", lambda e, kc=kc: e.matmul(pn[:, 0:n], lhsT=onesb[:], rhs=sqb[:, kc, 0:n], start=(kc == 0), stop=(kc == 7)),
          r=["sqb"], w=["pn"])
    A("act", lambda e: e.activation(out=rs[:, 0:n], in_=pn[:, 0:n], func=AF.Sqrt, bias=EPS, scale=1.0 / D), r=["pn"], w=["rs"])
    A("dve", lambda e: e.reciprocal(out=rs[:, 0:n], in_=rs[:, 0:n]), r=["rs"], w=["rs"])
    for kc in range(8):
        A("dve", lambda e, kc=kc: e.scalar_tensor_tensor(out=dst[:, kc, 0:n], in0=xres[:, kc, c0:c0 + n], scalar=gcol[:, kc:kc + 1],
                                                        in1=rs[:, 0:n], op0=ALU.mult, op1=ALU.mult), r=["xres", "rs"], w=[dkey])


def _phase_C(nc, st, SB, PS, dr, oscr, P, dbgout):
    xres, identf = P["xres"], P["identf"]
    Wo = SB(st, "Wo", [128, 8, 1024], BF16)
    xq = [SB(st, f"xqC{i}", [128, D], F32) for i in range(2)]
    ot = [SB(st, f"otC{i}", [128, 8, 512], BF16) for i in range(2)]
    pa = [PS(st, f"paC{i}", [128, 1024], F32) for i in range(2)]
    py = [PS(st, f"pyC{i}", [128, 512], F32) for i in range(2)]
    ph = Phase(nc, "C")
    A = ph.add
    A("pool", lambda e: e.dma_start(out=Wo[:], in_=dr["wout"].rearrange("(k p) n -> p k n", p=128)), w=["Wo"], dma="w")
    for bi, (S, off, nq, col0) in enumerate(BLK):
        pb = bi % 2
        t0 = 128 * S + off
        A("sp", lambda e, pb=pb, t0=t0, nq=nq: e.dma_start(out=xq[pb][0:nq, :], in_=dr["xs"][t0:t0 + nq, :]), w=[("xq", pb)], dma=("xq", pb))
        for kc in range(8):
            A("pe", lambda e, kc=kc, pb=pb, nq=nq: e.transpose(out=pa[pb][:, kc * 128:kc * 128 + nq], in_=xq[pb][0:nq, kc * 128:(kc + 1) * 128],
                                                              identity=identf[0:nq, 0:nq]), r=[("xq", pb)], w=[("pa", pb)])
        A("act", lambda e, pb=pb, nq=nq, col0=col0: e.copy(out=xres[:, :, col0:col0 + nq],
                                                           in_=pa[pb][:].rearrange("p (k q) -> p k q", q=128)[:, :, 0:nq]),
          r=[("pa", pb)], w=["xres"])
    for ti, (c0, n) in enumerate(TG):
        tb = ti % 2
        A("sp", lambda e, tb=tb, c0=c0, n=n: e.dma_start(out=ot[tb][:, :, 0:n], in_=oscr[:, :, c0:c0 + n]), w=[("ot", tb)], dma=("ot", tb))
        for dc in range(8):
            for hl in range(8):
                A("pe", lambda e, dc=dc, hl=hl, tb=tb, n=n: e.matmul(py[dc % 2][:, 0:n], lhsT=Wo[:, hl, dc * 128:(dc + 1) * 128],
                                                                    rhs=ot[tb][:, hl, 0:n], start=(hl == 0), stop=(hl == 7)),
                  r=["Wo", ("ot", tb)], w=[("py", dc % 2)])
            A("dve", lambda e, dc=dc, c0=c0, n=n: e.tensor_tensor(out=xres[:, dc, c0:c0 + n], in0=xres[:, dc, c0:c0 + n],
                                                                  in1=py[dc % 2][:, 0:n], op=ALU.add),
              r=[("py", dc % 2), "xres"], w=["xres"])
    if "x0mix" in dbgout:
        A("sp", lambda e: e.dma_start(out=dbgout["x0mix"], in_=xres[:]), r=["xres"], dma="dbg")
    ph.emit()


def _phase_ffn(nc, st, SB, PS, dr, L, P, dbgout):
    xres, onesb, gsb, vq = P["xres"], P["onesb"], P["gsb"], P["vq"]
    gcol = gsb[:, 1 + 2 * L, :]
    hTall = SB(st, f"hTall{L}", [128, 8, NTOK], BF16)
    with contextlib.ExitStack() as nst:
        sqb = SB(nst, f"sqbf{L}", [128, 8, 512], BF16)
        rs = SB(nst, f"rsf{L}", [128, 512], F32)
        pn = PS(nst, f"pnf{L}", [128, 512], F32)
        phn = Phase(nc, f"N{L}")
        for ti, (c0, n) in enumerate(TG):
            _norm_group(phn.add, xres, c0, n, gcol, onesb, sqb, pn, rs, hTall[:, :, c0:c0 + n], "hT", f"f{L}")
        phn.emit()
    wu = [SB(st, f"wu{L}{i}", [128, 8, 6, 256], BF16) for i in range(2)]
    wd = [SB(st, f"wd{L}{i}", [128, 6, 1024], BF16) for i in range(2)]
    cw = SB(st, f"cw{L}", [128, 3, 44], F32)
    cb = SB(st, f"cb{L}", [128, 44], F32)
    carry = SB(st, f"carry{L}", [128, 44, 2], F32)
    ub = [SB(st, f"ub{L}{i}", [128, 514], F32) for i in range(2)]
    cbuf = [SB(st, f"cbuf{L}{i}", [128, 512], F32) for i in range(2)]
    sg = SB(st, f"sg{L}", [128, 512], F32)
    act2 = [SB(st, f"act{L}{i}", [128, 6, 512], BF16) for i in range(2)]
    pu = [[PS(st, f"pu{L}{a}{b}", [128, 512], F32) for b in range(2)] for a in range(2)]
    py = [PS(st, f"pyf{L}{i}", [128, 512], F32) for i in range(2)]
    ph = Phase(nc, f"F{L}")
    A = ph.add
    A("sp", lambda e: e.dma_start(out=cw[:], in_=dr[f"cw{L}"]), w=["cw"], dma="c")
    A("sp", lambda e: e.dma_start(out=cb[:], in_=dr[f"cb{L}"]), w=["cb"], dma="c")
    A("dve", lambda e: e.memset(carry[:], 0.0), w=[("carry", ch) for ch in range(44)])

    def load_pass(p):
        wb = p % 2
        for i, fc in enumerate(FPASS[p]):
            A("pool", lambda e, i=i, fc=fc, wb=wb: e.dma_start(
                out=wu[wb][:, :, i, 0:128], in_=dr[f"wup{L}"][:, fc * 128:(fc + 1) * 128].rearrange("(k p) n -> p k n", p=128)),
              w=[("wu", wb, i)], dma=("w", wb))
            A("pool", lambda e, i=i, fc=fc, wb=wb: e.dma_start(
                out=wu[wb][:, :, i, 128:256], in_=dr[f"wup{L}"][:, DFF + fc * 128:DFF + (fc + 1) * 128].rearrange("(k p) n -> p k n", p=128)),
              w=[("wu", wb, i)], dma=("w", wb))
            A("pool", lambda e, i=i, fc=fc, wb=wb: e.dma_start(out=wd[wb][:, i, :], in_=dr[f"wdn{L}"][fc * 128:(fc + 1) * 128, :]),
              w=[("wd", wb, i)], dma=("w", wb))

    def up_stage(p, ti):
        wb = p % 2
        c0, n = TG[ti]
        ab = (p * len(TG) + ti) % 2
        for i, fc in enumerate(FPASS[p]):
            for part in range(2):
                bank = pu[part][i % 2]
                bk = ("pu", part, i % 2)
                ch = fc + 22 * part
                for kc in range(8):
                    A("pe", lambda e, kc=kc, i=i, part=part, bank=bank: e.matmul(
                        bank[:, 0:n], lhsT=wu[wb][:, kc, i, part * 128:(part + 1) * 128], rhs=hTall[:, kc, c0:c0 + n],
                        start=(kc == 0), stop=(kc == 7)), r=[("wu", wb, i)], w=[bk])
                A("act", lambda e, part=part, bank=bank: e.copy(out=ub[part][:, 2:2 + n], in_=bank[:, 0:n]), r=[bk], w=[("ub", part)])
                A("act", lambda e, part=part, bank=bank, ch=ch: e.activation(
                    out=cbuf[part][:, 0:n], in_=bank[:, 0:n], func=AF.Identity, bias=cb[:, ch:ch + 1], scale=cw[:, 2, ch:ch + 1]),
                  r=[bk, "cw", "cb"], w=[("cbuf", part)])
                A("act", lambda e, part=part, ch=ch: e.copy(out=ub[part][:, 0:2], in_=carry[:, ch, :]), r=[("carry", ch)], w=[("ubc", part)])
                for k in (1, 0):
                    A("dve", lambda e, part=part, ch=ch, k=k: e.scalar_tensor_tensor(
                        out=cbuf[part][:, 0:n], in0=ub[part][:, k:k + n], scalar=cw[:, k, ch:ch + 1], in1=cbuf[part][:, 0:n],
                        op0=ALU.mult, op1=ALU.add), r=[("ub", part), ("ubc", part), ("cbuf", part), "cw"], w=[("cbuf", part)])
                A("act", lambda e, part=part, ch=ch: e.copy(out=carry[:, ch, :], in_=ub[part][:, n:n + 2]), r=[("ub", part)], w=[("carry", ch)])
            A("act", lambda e: e.activation(out=sg[:, 0:n], in_=cbuf[0][:, 0:n], func=AF.Silu), r=[("cbuf", 0)], w=["sg"])
            A("dve", lambda e, i=i: e.tensor_tensor(out=act2[ab][:, i, 0:n], in0=sg[:, 0:n], in1=cbuf[1][:, 0:n], op=ALU.mult),
              r=["sg", ("cbuf", 1)], w=[("act", ab, i)])

    def down_stage(p, ti):
        wb = p % 2
        c0, n = TG[ti]
        ab = (p * len(TG) + ti) % 2
        nf = len(FPASS[p])
        for dc in range(8):
            for i in range(nf):
                A("pe", lambda e, dc=dc, i=i: e.matmul(py[dc % 2][:, 0:n], lhsT=wd[wb][:, i, dc * 128:(dc + 1) * 128], rhs=act2[ab][:, i, 0:n],
                                                       start=(i == 0), stop=(i == nf - 1)), r=[("wd", wb, i), ("act", ab, i)], w=[("py", dc % 2)])
            A("dve", lambda e, dc=dc: e.tensor_tensor(out=xres[:, dc, c0:c0 + n], in0=xres[:, dc, c0:c0 + n], in1=py[dc % 2][:, 0:n], op=ALU.add),
              r=[("py", dc % 2), "xres"], w=["xres"])

    steps = [(p, ti) for p in range(len(FPASS)) for ti in range(len(TG))]
    load_pass(0)
    load_pass(1)
    for k in range(len(steps) + 1):
        if k < len(steps):
            up_stage(*steps[k])
        if k >= 1:
            pp, pti = steps[k - 1]
            down_stage(pp, pti)
            if pti == len(TG) - 1 and pp + 2 < len(FPASS):
                load_pass(pp + 2)
    A("dve", lambda e: e.tensor_scalar(out=xres[:, :, 0:HALO], in0=xres[:, :, 0:HALO], scalar1=vq[:, 0:1], scalar2=None, op0=ALU.mult),
      r=["xres"], w=["xres"])
    if f"xffn{L}" in dbgout:
        A("sp", lambda e: e.dma_start(out=dbgout[f"xffn{L}"], in_=xres[:]), r=["xres"], dma="dbg")
    ph.emit()


def _phase_pool(nc, st, SB, PS, dr, P, dbgout):
    xres, onesb, gsb, vq = P["xres"], P["onesb"], P["gsb"], P["vq"]
    gcol = gsb[:, 2, :]
    hf = SB(st, "hf", [128, 8, NTOK], F32)
    pl = SB(st, "pl", [128, 8, NTOK], BF16)
    wa = SB(st, "wa", [128, NTOK], F32)
    wb_ = SB(st, "wb", [128, NTOK], F32)
    sqb = SB(st, "sqbp", [128, 8, 512], BF16)
    rs = SB(st, "rsp", [128, 512], F32)
    pw = SB(st, "pw", [128, 8, 256], BF16)
    pbias = SB(st, "pbias", [128, 8], F32)
    pscl = SB(st, "pscl", [128, 8], F32)
    pbs = SB(st, "pbs", [128, 8], F32)
    psc16 = SB(st, "psc16", [128, 8, 16], F32)
    tmp16 = SB(st, "tmp16", [128, 16], F32)
    ytmp = SB(st, "ytmp", [128, 512], F32)
    pn = PS(st, "pnp", [128, 512], F32)
    py = [PS(st, f"pyp{i}", [128, 512], F32) for i in range(2)]
    ph = Phase(nc, "P")
    A = ph.add
    A("pool", lambda e: e.dma_start(out=pw[:], in_=dr["poolw"]), w=["pw"], dma="w")
    A("sp", lambda e: e.dma_start(out=pbias[:], in_=dr["poolb"]), w=["pbias"], dma="c")
    A("sp", lambda e: e.dma_start(out=pscl[:], in_=dr["pools"]), w=["pscl"], dma="c")
    A("sp", lambda e: e.dma_start(out=psc16[:], in_=dr["pscale"]), w=["psc16"], dma="c")
    A("dve", lambda e: e.tensor_tensor(out=pbs[:], in0=pbias[:], in1=pscl[:], op=ALU.mult), r=["pbias", "pscl"], w=["pbs"])
    for ti, (c0, n) in enumerate(TG):
        _norm_group(A, xres, c0, n, gcol, onesb, sqb, pn, rs, hf[:, :, c0:c0 + n], "hf", "p")
    for kc in range(8):
        nsteps = kc // 2 + 1
        w = 2 ** nsteps
        src, sk = hf[:, kc, :], "hf"
        bufs = [(wa, "wa"), (wb_, "wb")]
        for sidx in range(nsteps):
            d = 2 ** sidx
            dst, dk = bufs[sidx % 2]
            A("dve", lambda e, src=src, dst=dst, d=d: e.tensor_tensor(out=dst[:, d:NTOK], in0=src[:, d:NTOK], in1=src[:, 0:NTOK - d], op=ALU.add),
              r=[sk], w=[dk])
            A("act", lambda e, src=src, dst=dst, d=d: e.copy(out=dst[:, 0:d], in_=src[:, 0:d]), r=[sk], w=[dk])
            src, sk = dst[:], dk
        A("dve", lambda e, src=src, kc=kc, w=w: e.scalar_tensor_tensor(out=pl[:, kc, :], in0=src, scalar=1.0 / w, in1=hf[:, kc, :],
                                                                      op0=ALU.mult, op1=ALU.subtract), r=[sk, "hf"], w=[("pl", kc)])
        A("dve", lambda e, src=src, kc=kc: e.tensor_tensor(out=tmp16[:], in0=src[:, HALO:HALO + 16], in1=psc16[:, kc, :], op=ALU.mult),
          r=[sk, "psc16"], w=["tmp16"])
        A("dve", lambda e, kc=kc: e.tensor_tensor(out=pl[:, kc, HALO:HALO + 16], in0=tmp16[:], in1=hf[:, kc, HALO:HALO + 16], op=ALU.subtract),
          r=["tmp16", "hf", ("pl", kc)], w=[("pl", kc)])
    for ti, (c0, n) in enumerate(TG):
        for oc in range(8):
            g, oh = oc // 2, oc % 2
            for kh in range(2):
                A("pe", lambda e, oc=oc, g=g, oh=oh, kh=kh, c0=c0, n=n: e.matmul(
                    py[oc % 2][:, 0:n], lhsT=pw[:, g * 2 + kh, oh * 128:(oh + 1) * 128], rhs=pl[:, g * 2 + kh, c0:c0 + n],
                    start=(kh == 0), stop=(kh == 1)), r=["pw", ("pl", g * 2 + kh)], w=[("py", oc % 2)])
            A("act", lambda e, oc=oc, n=n: e.activation(out=ytmp[:, 0:n], in_=py[oc % 2][:, 0:n], func=AF.Identity,
                                                       bias=pbs[:, oc:oc + 1], scale=pscl[:, oc:oc + 1]),
              r=[("py", oc % 2), "pbs", "pscl"], w=["ytmp"])
            A("dve", lambda e, oc=oc, c0=c0, n=n: e.tensor_tensor(out=xres[:, oc, c0:c0 + n], in0=xres[:, oc, c0:c0 + n], in1=ytmp[:, 0:n],
                                                                  op=ALU.add), r=["ytmp", "xres"], w=["xres"])
    A("dve", lambda e: e.tensor_scalar(out=xres[:, :, 0:HALO], in0=xres[:, :, 0:HALO], scalar1=vq[:, 0:1], scalar2=None, op0=ALU.mult),
      r=["xres"], w=["xres"])
    if "xpool" in dbgout:
        A("sp", lambda e: e.dma_start(out=dbgout["xpool"], in_=xres[:]), r=["xres"], dma="dbg")
    ph.emit()


def _phase_out(nc, st, SB, PS, dr, out, P, dbgout):
    xres, onesb, gsb, identf = P["xres"], P["onesb"], P["gsb"], P["identf"]
    gcol = gsb[:, 4, :]
    of = SB(st, "of", [128, 8, 512], F32)
    sqb = SB(st, "sqbo", [128, 8, 512], BF16)
    rs = SB(st, "rso", [128, 512], F32)
    ot = [SB(st, f"oto{i}", [128, D], F32) for i in range(2)]
    pn = PS(st, "pno", [128, 512], F32)
    pt = [PS(st, f"pto{i}", [128, 1024], F32) for i in range(2)]
    ph = Phase(nc, "O")
    A = ph.add
    for gi in range(4):
        c0 = HALO + 512 * gi
        _norm_group(A, xres, c0, 512, gcol, onesb, sqb, pn, rs, of, "of", "o")
        for tt in range(4):
            tb = tt % 2
            for kc in range(8):
                A("pe", lambda e, tt=tt, tb=tb, kc=kc: e.transpose(out=pt[tb][:, kc * 128:(kc + 1) * 128],
                                                                 in_=of[:, kc, tt * 128:(tt + 1) * 128], identity=identf[:]),
                  r=["of"], w=[("pt", tb)])
            A("act", lambda e, tb=tb: e.copy(out=ot[tb][:], in_=pt[tb][:]), r=[("pt", tb)], w=[("ot", tb)])
            row = (gi * 4 + tt) * 128
            A("sp", lambda e, tb=tb, row=row: e.dma_start(out=out[row:row + 128, :], in_=ot[tb][:]), r=[("ot", tb)], dma=("st", tb))
    ph.emit()


_CACHE = {}


def kernel(**inputs):
    inp = {k: np.asarray(v) for k, v in inputs.items()}
    if "nc" not in _CACHE:
        _CACHE["nc"] = build()
    nc = _CACHE["nc"]
    sh = _shared_inputs(inp)
    maps = []
    for c in range(8):
        m = dict(sh)
        m.update(_core_inputs(inp, c))
        maps.append(m)
    res = run_bass_kernel_spmd(nc, maps, core_ids=list(range(8)))
    outp = np.zeros((2, T, D), np.float32)
    for c in range(8):
        b, j = c // 4, c % 4
        outp[b, 2048 * j:2048 * (j + 1)] = np.asarray(res.results[c]["out"], dtype=np.float32)
    return outp
```
